# Optimizing a Trainium2 kernel written in Bass

```python
import math
import jax, jax.numpy as jnp
from jax import lax
import numpy as np

D_MODEL = 1024
BATCH = 4
SEQ = 4096
DEPTH = 1

N_Q_HEADS = 8
N_KV_HEADS = 2
HEAD_DIM = 64
ATTN_WIDTH = N_Q_HEADS * HEAD_DIM
KV_WIDTH = N_KV_HEADS * HEAD_DIM
WINDOW = 128
BLOCK = 128
ROPE_DIM = HEAD_DIM // 4
ROPE_THETA = 500000.0
SSM_WIDTH = D_MODEL // 2
SSM_GROUP = 16
N_SSM_GROUPS = SSM_WIDTH // SSM_GROUP
SSM_STATE = 64
N_DIRS = 2
DT_MIN = 1e-3
DT_MAX = 1e-1
MIX_WIDTH = ATTN_WIDTH + SSM_WIDTH
IN_WIDTH = ATTN_WIDTH + 2 * KV_WIDTH + SSM_WIDTH
D_FF = 2816
CONV_WIDTH = 3
EPS = 1e-6

kernel_name = "hymba_s5_swa_convffn_encoder"

F32 = jnp.float32


def rms_norm(x, g):
    xf = x.astype(F32)
    y = xf * lax.rsqrt(jnp.mean(xf * xf, axis=-1, keepdims=True) + EPS)
    return (y * g.astype(F32)).astype(x.dtype)


def partial_rope(t, pos):
    half = ROPE_DIM // 2
    inv_freq = jnp.power(ROPE_THETA, -jnp.arange(half, dtype=F32) / half)
    ang = pos.astype(F32)[:, None] * inv_freq[None, :]
    cos = jnp.cos(ang)[None, :, None, :]
    sin = jnp.sin(ang)[None, :, None, :]
    tf = t.astype(F32)
    t1 = tf[..., :half]
    t2 = tf[..., half:ROPE_DIM]
    rest = tf[..., ROPE_DIM:]
    out = jnp.concatenate([t1 * cos - t2 * sin, t2 * cos + t1 * sin, rest], axis=-1)
    return out.astype(t.dtype)


def window_attention(q, k, v, sink):
    b, l = q.shape[0], q.shape[1]
    nb = l // BLOCK
    grp = N_Q_HEADS // N_KV_HEADS
    qb = q.astype(F32).reshape(b, nb, BLOCK, N_KV_HEADS, grp, HEAD_DIM)

    def band(t):
        tp = jnp.pad(t.astype(F32), ((0, 0), (BLOCK, BLOCK), (0, 0), (0, 0)))
        tb = tp.reshape(b, nb + 2, BLOCK, N_KV_HEADS, HEAD_DIM)
        return jnp.concatenate([tb[:, :-2], tb[:, 1:-1], tb[:, 2:]], axis=2)

    kw = band(k)
    vw = band(v)
    s = jnp.einsum('bnqhgd,bnkhd->bnhgqk', qb, kw) * (HEAD_DIM ** -0.5)
    qpos = jnp.arange(nb)[:, None] * BLOCK + jnp.arange(BLOCK)[None, :]
    kpos = (jnp.arange(nb)[:, None] - 1) * BLOCK + jnp.arange(3 * BLOCK)[None, :]
    rel = kpos[:, None, :] - qpos[:, :, None]
    valid = (jnp.abs(rel) <= WINDOW) & (kpos[:, None, :] >= 0) & (kpos[:, None, :] < l)
    s = jnp.where(valid[None, :, None, None], s, -jnp.inf)
    sink_l = sink.astype(F32).reshape(N_KV_HEADS, grp)[None, None, :, :, None, None]
    m = jnp.maximum(jnp.max(s, axis=-1, keepdims=True), sink_l)
    p = jnp.exp(s - m)
    denom = jnp.sum(p, axis=-1, keepdims=True) + jnp.exp(sink_l - m)
    o = jnp.einsum('bnhgqk,bnkhd->bnqhgd', p / denom, vw)
    return o.reshape(b, l, ATTN_WIDTH)


def _scan_op(e1, e2):
    a1, x1 = e1
    a2, x2 = e2
    return a1 * a2, a2 * x1 + x2


def s5_bidirectional(u, a_re, a_im, log_step, b_re, b_im, c_re, c_im, d_skip):
    bsz, l = u.shape[0], u.shape[1]
    ug = u.astype(F32).reshape(bsz, l, N_SSM_GROUPS, SSM_GROUP)
    lam = lax.complex(a_re.astype(F32), a_im.astype(F32))
    step = jnp.exp(log_step.astype(F32))[..., None]
    lam_bar = jnp.exp(lam * step)
    bmat = lax.complex(b_re.astype(F32), b_im.astype(F32))
    b_bar = ((lam_bar - 1.0) / lam)[..., None] * bmat
    cmat = lax.complex(c_re.astype(F32), c_im.astype(F32))
    y = d_skip.astype(F32)[None, None] * ug
    for direction, rev in ((0, False), (1, True)):
        bu = jnp.einsum('blgc,gpc->blgp', ug, b_bar[direction])
        a = jnp.broadcast_to(lam_bar[direction][None, None], bu.shape)
        _, states = lax.associative_scan(_scan_op, (a, bu), reverse=rev, axis=1)
        y = y + jnp.real(jnp.einsum('gcp,blgp->blgc', cmat[direction], states))
    return y.reshape(bsz, l, SSM_WIDTH)


def depthwise_conv(t, w, bias):
    c = t.shape[-1]
    pad = CONV_WIDTH // 2
    out = lax.conv_general_dilated(t, w[:, None, :].astype(t.dtype), window_strides=(1,),
                                   padding=((pad, pad),), dimension_numbers=('NWC', 'WIO', 'NWC'),
                                   feature_group_count=c)
    return out + bias.astype(out.dtype)


def setup_inputs(seed: int = 0) -> dict:
    key = jax.random.key(seed)
    ks = jax.random.split(key, 24)
    G, P, C = N_SSM_GROUPS, SSM_STATE, SSM_GROUP
    nrm = lambda k, shape: jax.random.normal(k, shape, dtype=F32)
    x = nrm(ks[0], (BATCH, SEQ, D_MODEL))
    norm_mix_g = 1.0 + 0.02 * nrm(ks[1], (DEPTH, D_MODEL))
    w_in = nrm(ks[2], (DEPTH, D_MODEL, IN_WIDTH)) * D_MODEL ** -0.5
    n_idx = jnp.arange(P, dtype=F32)
    a_re = -0.5 + 0.01 * nrm(ks[3], (DEPTH, N_DIRS, G, P))
    a_im = math.pi * n_idx + 0.01 * nrm(ks[4], (DEPTH, N_DIRS, G, P))
    log_step = jax.random.uniform(ks[5], (DEPTH, N_DIRS, G), dtype=F32,
                                  minval=math.log(DT_MIN), maxval=math.log(DT_MAX))
    b_re = nrm(ks[6], (DEPTH, N_DIRS, G, P, C)) * (2.0 * C) ** -0.5
    b_im = nrm(ks[7], (DEPTH, N_DIRS, G, P, C)) * (2.0 * C) ** -0.5
    c_re = nrm(ks[8], (DEPTH, N_DIRS, G, C, P)) * P ** -0.5
    c_im = nrm(ks[9], (DEPTH, N_DIRS, G, C, P)) * P ** -0.5
    d_skip = 0.5 * nrm(ks[10], (DEPTH, G, C))
    w_glu = nrm(ks[11], (DEPTH, SSM_WIDTH, SSM_WIDTH)) * SSM_WIDTH ** -0.5
    sink = 0.5 * nrm(ks[12], (DEPTH, N_Q_HEADS))
    norm_attn_g = 1.0 + 0.02 * nrm(ks[13], (DEPTH, ATTN_WIDTH))
    norm_ssm_g = 1.0 + 0.02 * nrm(ks[14], (DEPTH, SSM_WIDTH))
    w_out = nrm(ks[15], (DEPTH, MIX_WIDTH, D_MODEL)) * MIX_WIDTH ** -0.5
    norm_ffn_g = 1.0 + 0.02 * nrm(ks[16], (DEPTH, D_MODEL))
    w_up = nrm(ks[17], (DEPTH, D_MODEL, 2 * D_FF)) * D_MODEL ** -0.5
    conv_w = nrm(ks[18], (DEPTH, CONV_WIDTH, 2 * D_FF)) * CONV_WIDTH ** -0.5
    conv_b = 0.02 * nrm(ks[19], (DEPTH, 2 * D_FF))
    w_down = nrm(ks[20], (DEPTH, D_FF, D_MODEL)) * D_FF ** -0.5
    norm_final_g = 1.0 + 0.02 * nrm(ks[21], (D_MODEL,))
    return {"x": x, "norm_mix_g": norm_mix_g, "w_in": w_in, "a_re": a_re, "a_im": a_im,
            "log_step": log_step, "b_re": b_re, "b_im": b_im, "c_re": c_re, "c_im": c_im,
            "d_skip": d_skip, "w_glu": w_glu, "sink": sink, "norm_attn_g": norm_attn_g,
            "norm_ssm_g": norm_ssm_g, "w_out": w_out, "norm_ffn_g": norm_ffn_g, "w_up": w_up,
            "conv_w": conv_w, "conv_b": conv_b, "w_down": w_down, "norm_final_g": norm_final_g}


def reference(x, norm_mix_g, w_in, a_re, a_im, log_step, b_re, b_im, c_re, c_im, d_skip, w_glu,
              sink, norm_attn_g, norm_ssm_g, w_out, norm_ffn_g, w_up, conv_w, conv_b, w_down,
              norm_final_g):
    bsz, l = x.shape[0], x.shape[1]
    pos = jnp.arange(l)
    for i in range(DEPTH):
        h = rms_norm(x, norm_mix_g[i])
        proj = h @ w_in[i]
        q = proj[..., :ATTN_WIDTH].reshape(bsz, l, N_Q_HEADS, HEAD_DIM)
        k = proj[..., ATTN_WIDTH:ATTN_WIDTH + KV_WIDTH].reshape(bsz, l, N_KV_HEADS, HEAD_DIM)
        v = proj[..., ATTN_WIDTH + KV_WIDTH:ATTN_WIDTH + 2 * KV_WIDTH].reshape(bsz, l, N_KV_HEADS, HEAD_DIM)
        u = proj[..., ATTN_WIDTH + 2 * KV_WIDTH:]
        q = partial_rope(q, pos)
        k = partial_rope(k, pos)
        attn = window_attention(q, k, v, sink[i])
        ys = s5_bidirectional(u, a_re[i], a_im[i], log_step[i], b_re[i], b_im[i],
                              c_re[i], c_im[i], d_skip[i])
        ys = jax.nn.gelu(ys, approximate=False)
        ys = ys * jax.nn.sigmoid(ys @ w_glu[i].astype(F32))
        mixed = jnp.concatenate([rms_norm(attn, norm_attn_g[i]), rms_norm(ys, norm_ssm_g[i])], axis=-1)
        x = x + (mixed.astype(x.dtype) @ w_out[i]).astype(x.dtype)
        h = rms_norm(x, norm_ffn_g[i])
        up = depthwise_conv(h @ w_up[i], conv_w[i], conv_b[i])
        gate = up[..., :D_FF]
        val = up[..., D_FF:]
        x = x + ((jax.nn.silu(gate) * val) @ w_down[i]).astype(x.dtype)
    return rms_norm(x, norm_final_g)
```

```python
import numpy as np
import concourse.bass as bass
import concourse.mybir as mybir
from concourse.bass_utils import run_bass_kernel_spmd

F32 = mybir.dt.float32
BF16 = mybir.dt.bfloat16
I32 = mybir.dt.int32
ALU = mybir.AluOpType
AF = mybir.ActivationFunctionType

D = 1024
T_ALL = 4096
T_OWN = 2048
T_EXT = 2176
T_QKV = 2560
NQKV_G = 5
DFF = 2816
NPAIR = 22
EPS = 1e-6
TWO_PI = float(2.0 * np.pi)
SB_BASE = 16512
import os as _os
_C = lambda k, d: float(_os.environ.get(k, d))
C_ACT_F, C_ACT_E = _C("KS_ACT_F", 0.2), _C("KS_ACT_E", 1000.0)
C_DVE_F, C_DVE_E = _C("KS_DVE_F", 0.15), _C("KS_DVE_E", 960.0)
C_PE_F, C_PE_E = _C("KS_PE_F", 0.3), _C("KS_PE_E", 2400.0)
C_DMA_F, C_DMA_B = _C("KS_DMA_F", 2.0), _C("KS_DMA_B", 150e3)
SB_END = 229376


class Buf:
    __slots__ = ("name", "writer", "readers", "excl")

    def __init__(self, name, excl=False):
        self.name = name
        self.writer = None
        self.readers = {}
        self.excl = excl


class Stream:
    def __init__(self, name, eng_name, inc, sem):
        self.name, self.eng_name, self.inc, self.sem, self.count = name, eng_name, inc, sem, 0


class Op:
    __slots__ = ("stream", "fn", "preds", "cost", "prog", "end", "start", "sidx", "nsucc", "bar_counts")

    def __init__(self, stream, fn, preds, cost, prog):
        self.stream, self.fn, self.preds, self.cost, self.prog = stream, fn, preds, cost, prog
        self.end = self.start = None
        self.sidx = None
        self.bar_counts = None


class Sched:
    ENGS = ("tensor", "vector", "scalar", "gpsimd", "sync")
    NDS = 12
    HOP = _C("KS_HOP", 0.45)
    NDP = 4

    def __init__(self, sems):
        names = [("pe", "tensor", 1), ("dve", "vector", 1), ("act", "scalar", 1), ("pool", "gpsimd", 1)]
        names += [(f"dsync{i}", "sync", 16) for i in range(self.NDS)]
        names += [(f"dpool{i}", "gpsimd", 16) for i in range(self.NDP)]
        assert len(sems) == len(names)
        self.streams = {n: Stream(n, e, inc, s) for (n, e, inc), s in zip(names, sems)}
        self.slots = {n: Buf("slot_" + n) for n in self.streams if n.startswith("ds") or n.startswith("dp")}
        self.rr = {"dsync": 0, "dpool": 0}
        self.ops = []
        self.last_barrier = None
        self.since_barrier = []

    def op(self, stream, fn, reads=(), writes=(), cost=0.5):
        if stream in self.rr:
            k = self.rr[stream]
            self.rr[stream] = (k + 1) % (self.NDS if stream == "dsync" else self.NDP)
            stream = f"{stream}{k}"
            writes = list(writes) + [self.slots[stream]]
        ex = [b for b in reads if b.excl]
        if ex:
            reads = [b for b in reads if not b.excl]
            writes = list(writes) + ex
        preds = set()
        for b in reads:
            if b.writer is not None:
                preds.add(b.writer)
        for b in writes:
            if b.writer is not None:
                preds.add(b.writer)
            for o in b.readers.values():
                preds.add(o)
        if self.last_barrier is not None:
            preds.add(self.last_barrier)
        o = Op(stream, fn, list(preds), cost, len(self.ops))
        self.ops.append(o)
        self.since_barrier.append(o)
        for b in reads:
            b.readers[id(o)] = o
        for b in writes:
            b.writer = o
            b.readers = {}
        return o

    def barrier(self):
        if not self.since_barrier:
            return
        b = Op(None, None, list(self.since_barrier) + ([self.last_barrier] if self.last_barrier else []), 0.0, len(self.ops))
        self.ops.append(b)
        self.last_barrier = b
        self.since_barrier = []

    def schedule(self):
        import heapq
        ops = self.ops
        npred = {id(o): len(o.preds) for o in ops}
        succ = {id(o): [] for o in ops}
        for o in ops:
            for p in o.preds:
                succ[id(p)].append(o)
        eng_free = {e: 0.0 for e in self.ENGS}
        eng_of = lambda o: self.streams[o.stream].eng_name
        heap = []

        def ready_time(o):
            return max([p.end for p in o.preds], default=0.0) + self.HOP

        def push(o):
            if o.stream is None:
                o.start = o.end = ready_time(o)
                release(o)
                return
            rt = ready_time(o)
            heapq.heappush(heap, (max(rt, eng_free[eng_of(o)]), o.prog, rt, o))

        def release(o):
            for s_ in succ[id(o)]:
                npred[id(s_)] -= 1
                if npred[id(s_)] == 0:
                    push(s_)
        for o in ops:
            if npred[id(o)] == 0:
                push(o)
        nsched = 0
        while heap:
            key, prog, rt, o = heapq.heappop(heap)
            e = eng_of(o)
            st_ = max(rt, eng_free[e])
            if st_ > key + 1e-9:
                heapq.heappush(heap, (st_, prog, rt, o))
                continue
            o.start = st_
            is_dma = self.streams[o.stream].inc == 16
            o.end = st_ + o.cost
            eng_free[e] = st_ + (0.08 if is_dma else o.cost)
            nsched += 1
            release(o)
        assert all(o.start is not None for o in ops), "scheduler: unscheduled ops (cycle?)"
        self.makespan = max(o.end for o in ops)

    def emit(self, block):
        self.barrier()
        self.schedule()
        per_eng = {e: [] for e in self.ENGS}
        for o in self.ops:
            if o.stream is not None:
                per_eng[self.streams[o.stream].eng_name].append(o)
        for e in self.ENGS:
            per_eng[e].sort(key=lambda o: (o.start, o.prog))
            for o in per_eng[e]:
                st = self.streams[o.stream]
                st.count += 1
                o.sidx = st.count
        cnt = {n: 0 for n in self.streams}
        for o in self.ops:
            if o.stream is None:
                o.bar_counts = dict(cnt)
            else:
                cnt[o.stream] += 1
        final_counts = dict(cnt)
        progs = {e: [] for e in self.ENGS}
        for e in self.ENGS:
            seen = {}
            for o in per_eng[e]:
                need = {}
                for p in o.preds:
                    if p.stream is None:
                        for sn, c in p.bar_counts.items():
                            if c and need.get(sn, 0) < c:
                                need[sn] = c
                    else:
                        if p.stream == "pe" and o.stream == "pe":
                            continue
                        if need.get(p.stream, 0) < p.sidx:
                            need[p.stream] = p.sidx
                waits = []
                for sn, c in need.items():
                    if seen.get(sn, 0) >= c:
                        continue
                    seen[sn] = c
                    waits.append((self.streams[sn].sem, c * self.streams[sn].inc))
                progs[e].append((waits, o.fn, self.streams[o.stream].sem, self.streams[o.stream].inc))
            waits = [(self.streams[sn].sem, c * self.streams[sn].inc) for sn, c in final_counts.items()
                     if c and seen.get(sn, 0) < c]
            progs[e].append((waits, None, None, 0))

        def run(engname):
            def body(eh):
                for waits, fn, sem, inc in progs[engname]:
                    for (ws, wv) in waits:
                        eh.wait_ge(ws, wv)
                    if fn is not None:
                        fn(eh).then_inc(sem, inc)
            return body
        block.tensor(run("tensor"))
        block.vector(run("vector"))
        block.scalar(run("scalar"))
        block.gpsimd(run("gpsimd"))
        block.sync(run("sync"))


def build_program(dbg=False, stop=99):
    nc = bass.Bass("TRN2", target_bir_lowering=False)

    def din(name, shape, dt=F32):
        return nc.dram_tensor(name, list(shape), dt, kind="ExternalInput").ap()

    x_d = din("x", [T_ALL, D])
    pos_d = din("pos", [1, T_QKV])
    win_d = din("w_in", [D, 1280])
    gains_d = din("gains", [1, 4096])
    are_d = din("are", [128, 32])
    aim_d = din("aim", [128, 32])
    ls_d = din("ls", [128, 32])
    bre_d = din("bre", [128, 512])
    bim_d = din("bim", [128, 512])
    cre_d = din("cre", [128, 512])
    cim_d = din("cim", [128, 512])
    dsk_d = din("dsk", [128, 4])
    wglu_d = din("w_glu", [512, 512])
    sink_d = din("sink", [1, 8])
    wout_d = din("w_out", [D, D])
    wup_d = din("w_up", [D, 2 * DFF])
    cw_d = din("cw", [128, 3 * 44])
    cb_d = din("cb", [128, 44])
    wdn_d = din("w_down", [DFF, D])
    ident_d = din("ident", [128, 128])
    rmat_d = din("rmat", [128, 128])
    ifr_d = din("ifr", [128, 1])
    msk_d = din("msk", [128, 256])
    bmask_d = din("bmask", [128, 128])
    sidx_d = din("sidx", [1, 512])
    y_d = nc.dram_tensor("y", [T_OWN, D], F32, kind="ExternalOutput").ap()
    x1_d = nc.dram_tensor("x1_scr", [T_EXT, D], F32, kind="Internal").ap()
    qs_d = nc.dram_tensor("q_scr", [128, 4, T_QKV], BF16, kind="Internal").ap()
    ks_d = nc.dram_tensor("k_scr", [128, T_QKV], BF16, kind="Internal").ap()
    vs_d = nc.dram_tensor("v_scr", [128, 20, 130], BF16, kind="Internal").ap()
    dbg_d = None
    if dbg:
        dbg_d = nc.dram_tensor("dbg", [8, 128, 2560], F32, kind="ExternalOutput").ap()

    import contextlib
    _es = contextlib.ExitStack()
    sems = [_es.enter_context(nc.semaphore(f"sem{i}")) for i in range(4 + Sched.NDS + Sched.NDP)]
    S = Sched(sems)

    _sbn = [0]
    def finish():
        with nc.Block() as block:
            S.emit(block)
        _es.close()
        return nc

    class Arena:
        def __init__(self, lo, hi):
            self.lo, self.hi, self.cur, self.n = lo, hi, lo, 0

        def alloc(self, shape, dt):
            nbytes = int(np.prod(shape[1:])) * (4 if dt in (F32, I32) else 2)
            nbytes = (nbytes + 63) // 64 * 64
            off = self.cur
            assert off + nbytes <= self.hi, (shape, off, nbytes, self.hi)
            self.cur += nbytes
            _sbn[0] += 1
            return nc.alloc_sbuf_tensor_at(f"sb{_sbn[0]}", list(shape), dt, offset=off)

        def mark(self):
            return self.cur

        def reset(self, m):
            self.cur = m

    def region(lo_kb, hi_kb):
        return Arena(SB_BASE + int(lo_kb * 1024), min(SB_END, SB_BASE + int(hi_kb * 1024)))

    AR = region(0, 21.5)
    AR_YS = region(21.5, 55.5)
    _psn = [0]

    def psum(dt=F32):
        _psn[0] += 1
        return nc.alloc_psum_tensor(f"ps{_psn[0]}", [128, 512 if dt == F32 else 1024], dt)

    PSF = [(psum(F32), Buf(f"psf{i}", excl=True)) for i in range(6)]
    PSB = [(psum(BF16), Buf(f"psb{i}", excl=True)) for i in range(2)]
    _rr = {"f": 0, "b": 0}

    def next_psf():
        _rr["f"] = (_rr["f"] + 1) % len(PSF)
        return PSF[_rr["f"]]

    def next_psb():
        _rr["b"] = (_rr["b"] + 1) % len(PSB)
        return PSB[_rr["b"]]

    def fsz(ap):
        n = 1
        for d_ in ap.shape[1:]:
            n *= int(d_)
        return n

    def dma(q, out, in_, reads=(), writes=()):
        nbytes = fsz(out) * int(out.shape[0]) * 4
        S.op(q, lambda e: e.dma_start(out=out, in_=in_), reads, writes, cost=C_DMA_F + nbytes / C_DMA_B)

    def tcopy(st, out, in_, reads=(), writes=()):
        S.op(st, lambda e: e.tensor_copy(out=out, in_=in_), reads, writes,
             cost=(C_DVE_F + fsz(out) / C_DVE_E) * (4 if st == "pool" else 1))

    def acopy(out, in_, reads=(), writes=(), func=AF.Copy, scale=1.0, bias=None, accum=None):
        def f(e):
            kw = {}
            if bias is not None:
                kw["bias"] = bias
            if accum is not None:
                kw["accum_out"] = accum
            return e.activation(out=out, in_=in_, func=func, scale=scale, **kw)
        S.op("act", f, reads, writes, cost=C_ACT_F + fsz(out) / C_ACT_E)

    def tt(st, out, in0, in1, op, reads=(), writes=()):
        S.op(st, lambda e: e.tensor_tensor(out=out, in0=in0, in1=in1, op=op), reads, writes,
             cost=(C_DVE_F + fsz(out) / C_DVE_E) * (4 if st == "pool" else 1))

    def ts(st, out, in0, s1, s2, op0, op1=None, reads=(), writes=()):
        c_ = (C_DVE_F + fsz(out) / C_DVE_E) * (4 if st == "pool" else 1)
        if op1 is None:
            S.op(st, lambda e: e.tensor_scalar(out=out, in0=in0, scalar1=s1, scalar2=None, op0=op0), reads, writes, cost=c_)
        else:
            S.op(st, lambda e: e.tensor_scalar(out=out, in0=in0, scalar1=s1, scalar2=s2, op0=op0, op1=op1), reads, writes,
                 cost=c_)

    def stt(out, in0, scalar, in1, op0, op1, reads=(), writes=()):
        S.op("dve", lambda e: e.scalar_tensor_tensor(out=out, in0=in0, scalar=scalar, in1=in1, op0=op0, op1=op1),
             reads, writes, cost=C_DVE_F + fsz(out) / (C_DVE_E * 0.83))

    def mm(out, pairs, reads=(), writes=()):
        def f(e):
            ins = None
            n = len(pairs)
            for i, (l, r) in enumerate(pairs):
                ins = e.matmul(out, l, r, start=(i == 0), stop=(i == n - 1))
            return ins
        S.op("pe", f, reads, writes, cost=C_PE_F + sum(max(64, fsz(r)) / C_PE_E + 0.01 for (_, r) in pairs))

    def transposes(outs_ins, ident, reads=(), writes=()):
        def f(e):
            ins = None
            for (o, i_) in outs_ins:
                ins = e.transpose(o, i_, ident)
            return ins
        S.op("pe", f, reads, writes, cost=0.3 + 0.1 * len(outs_ins))

    gains = AR.alloc([128, 4096], F32); b_gains = Buf("gains")
    ident_f = AR.alloc([128, 128], F32)
    ident = AR.alloc([128, 128], BF16); b_ident = Buf("ident")
    rmat_f = AR.alloc([128, 128], F32)
    rmat = AR.alloc([128, 128], BF16); b_rmat = Buf("rmat")
    msk_f = AR.alloc([128, 256], F32)
    msk = AR.alloc([128, 256], BF16); b_msk = Buf("msk")
    ifr = AR.alloc([128, 1], F32); b_ifr = Buf("ifr")
    dsk = AR.alloc([128, 4], F32); b_dsk = Buf("dsk")
    esink = AR.alloc([128, 8], F32); b_esink = Buf("esink")
    cw = AR.alloc([128, 132], F32); b_cw = Buf("cw")
    cb = AR.alloc([128, 44], F32); b_cb = Buf("cb")
    epsc = AR.alloc([128, 1], F32); b_epsc = Buf("epsc")
    onec2 = AR.alloc([128, 1], F32); b_onec2 = Buf("onec2")
    stat = AR.alloc([128, 64], F32)
    b_stat = [Buf(f"stat{i}") for i in range(16)]
    _st = [0]

    def next_stat():
        _st[0] = (_st[0] + 1) % 16
        return stat[:, 4 * _st[0]:4 * _st[0] + 4], b_stat[_st[0]]

    b_tmp = Buf("ldtmp")
    S.op("dve", lambda e: e.memset(epsc[:], EPS), (), [b_epsc])
    S.op("dve", lambda e: e.memset(onec2[:], 1.0), (), [b_onec2])
    dma("dsync", gains[:], gains_d.partition_broadcast(128), writes=[b_gains])
    dma("dsync", ident_f[:], ident_d[:, :], writes=[b_tmp])
    dma("dsync", rmat_f[:], rmat_d[:, :], writes=[b_tmp])
    dma("dsync", msk_f[:], msk_d[:, :], writes=[b_tmp])
    dma("dsync", ifr[:], ifr_d[:, :], writes=[b_ifr])
    dma("dsync", dsk[:], dsk_d[:, :], writes=[b_dsk])
    dma("dsync", esink[:], sink_d.partition_broadcast(128), writes=[b_esink])
    dma("dsync", cw[:], cw_d[:, :], writes=[b_cw])
    dma("dsync", cb[:], cb_d[:, :], writes=[b_cb])
    tcopy("dve", ident[:], ident_f[:], [b_tmp], [b_ident])
    tcopy("dve", rmat[:], rmat_f[:], [b_tmp], [b_rmat])
    tcopy("dve", msk[:], msk_f[:], [b_tmp], [b_msk])
    acopy(esink[:], esink[:], [b_esink], [b_esink], func=AF.Exp)
    G_MIX, G_FFN, G_FIN, G_ATT, G_SSM = 0, 1024, 2048, 3072, 3584

    ysJ = AR_YS.alloc([128, 4, 8, T_EXT // 8], F32); b_ys = [Buf(f"ys{c}") for c in range(4)]

    def norm_transpose_group(tiles, exp_set=False):
        for (src, go, xt, b_xt, hb, b_hb, hT, b_hT, col0) in tiles:
            if src is not None:
                dma("dsync", xt[:], src, writes=[b_xt])
        scs = [next_stat() for _ in tiles]
        if exp_set:
            for (src, go, xt, b_xt, hb, b_hb, hT, b_hT, col0), (sc, b_sc) in zip(tiles, scs):
                S.op("dve", ssq_dve(hb[:], xt[:], sc), [b_xt], [b_hb, b_sc])
                rstd_exp(sc, b_sc, D)
        else:
            for (src, go, xt, b_xt, hb, b_hb, hT, b_hT, col0), (sc, b_sc) in zip(tiles, scs):
                acopy(hb[:], xt[:], [b_xt], [b_hb, b_sc], func=AF.Square, accum=sc[:, 0:1])
            for (sc, b_sc) in scs:
                ts("dve", sc[:, 1:2], sc[:, 0:1], 1.0 / D, EPS, ALU.mult, ALU.add, [b_sc], [b_sc])
            for (sc, b_sc) in scs:
                acopy(sc[:, 2:3], sc[:, 1:2], [b_sc], [b_sc], func=AF.Sqrt)
            for (sc, b_sc) in scs:
                S.op("dve", lambda e, sc=sc: e.reciprocal(out=sc[:, 3:4], in_=sc[:, 2:3]), [b_sc], [b_sc])
        for (src, go, xt, b_xt, hb, b_hb, hT, b_hT, col0), (sc, b_sc) in zip(tiles, scs):
            stt(hb[:], xt[:], sc[:, 3:4], gains[:, go:go + D], ALU.mult, ALU.mult, [b_xt, b_sc, b_gains], [b_hb])
        for (src, go, xt, b_xt, hb, b_hb, hT, b_hT, col0) in tiles:
            pb, b_pb = next_psb()
            transposes([(pb[:, k * 128:(k + 1) * 128], hb[:, k * 128:(k + 1) * 128]) for k in range(8)],
                       ident[:], [b_hb, b_ident], [b_pb])
            if not exp_set and (col0 // 128) % 2 == 1:
                tcopy("dve", hT[:, :, col0:col0 + 128], pb[:].rearrange("p (k t) -> p k t", k=8), [b_pb], [b_hT])
            else:
                acopy(hT[:, :, col0:col0 + 128], pb[:].rearrange("p (k t) -> p k t", k=8), [b_pb], [b_hT])

    def norm_transpose(src_ap, src_reads, gain_off, xt, b_xt, hb, b_hb, hT, b_hT, col0, from_dram=True, stop=99):
        norm_transpose_group([(src_ap if from_dram else None, gain_off, xt, b_xt, hb, b_hb, hT, b_hT, col0)],
                             exp_set=not from_dram)

    AR_P = region(196.25, 207.8)
    prm = AR_P.alloc([128, 21, 32], F32); b_prm = Buf("prm")
    (P_ARE, P_AIM, P_DT, P_ER, P_TH, P_C, P_S, P_LR, P_LI, P_T0, P_T1, P_T2, P_CR, P_CI, P_R8, P_T3,
     P_T4, P_T5, P_T6, P_T7, P_F8) = range(21)
    pri = AR_P.alloc([128, 32], I32)
    PW = AR_P.alloc([128, 9, 2, 32], F32); b_PW = Buf("PW")
    craw = AR_P.alloc([128, 2, 512], F32); b_craw = Buf("craw")
    sidx = AR_P.alloc([128, 512], F32); b_sidx = Buf("sidx")
    onec = AR_P.alloc([128, 1], F32); b_onec = Buf("onec")

    def P(i):
        return prm[:, i, :]

    dma("dsync", P(P_ARE), are_d[:, :], writes=[b_prm])
    dma("dsync", P(P_AIM), aim_d[:, :], writes=[b_prm])
    dma("dsync", P(P_DT), ls_d[:, :], writes=[b_prm])
    dma("dsync", craw[:, 0, :], cre_d[:, :], writes=[b_craw])
    dma("dsync", craw[:, 1, :], cim_d[:, :], writes=[b_craw])
    dma("dsync", sidx[:], sidx_d.partition_broadcast(128), writes=[b_sidx])
    S.op("pool", lambda e: e.memset(onec[:], 1.0), (), [b_onec])
    R, W_ = [b_prm], [b_prm]
    acopy(P(P_DT), P(P_DT), R, W_, func=AF.Exp)
    tt("dve", P(P_T0), P(P_ARE), P(P_DT), ALU.mult, R, W_)
    acopy(P(P_ER), P(P_T0), R, W_, func=AF.Exp)
    tt("dve", P(P_TH), P(P_AIM), P(P_DT), ALU.mult, R, W_)
    ts("dve", P(P_T0), P(P_TH), 1.0 / TWO_PI, None, ALU.mult, reads=R, writes=W_)
    tcopy("dve", pri[:], P(P_T0), R, W_)
    tcopy("dve", P(P_T0), pri[:], R, W_)
    stt(P(P_T1), P(P_T0), -TWO_PI, P(P_TH), ALU.mult, ALU.add, R, W_)
    ts("dve", P(P_T1), P(P_T1), float(np.pi), float(-np.pi), ALU.min, ALU.max, R, W_)
    acopy(P(P_S), P(P_T1), R, W_, func=AF.Sin)
    acopy(P(P_T2), P(P_T1), R, W_, func=AF.Sin, scale=0.5)
    tt("dve", P(P_T2), P(P_T2), P(P_T2), ALU.mult, R, W_)
    ts("dve", P(P_C), P(P_T2), -2.0, 1.0, ALU.mult, ALU.add, R, W_)
    tt("dve", P(P_LR), P(P_ER), P(P_C), ALU.mult, R, W_)
    tt("dve", P(P_LI), P(P_ER), P(P_S), ALU.mult, R, W_)
    tt("dve", P(P_T0), P(P_ER), P(P_ER), ALU.mult, R, W_)
    tt("dve", P(P_T0), P(P_T0), P(P_T0), ALU.mult, R, W_)
    tt("dve", P(P_R8), P(P_T0), P(P_T0), ALU.mult, R, W_)
    ts("dve", P(P_T0), P(P_LR), -1.0, None, ALU.add, reads=R, writes=W_)
    tt("dve", P(P_T1), P(P_ARE), P(P_ARE), ALU.mult, R, W_)
    tt("dve", P(P_T2), P(P_AIM), P(P_AIM), ALU.mult, R, W_)
    tt("dve", P(P_T1), P(P_T1), P(P_T2), ALU.add, R, W_)
    S.op("dve", lambda e: e.reciprocal(out=P(P_T3), in_=P(P_T1)), R, W_)
    tt("dve", P(P_T1), P(P_T0), P(P_ARE), ALU.mult, R, W_)
    tt("dve", P(P_T2), P(P_LI), P(P_AIM), ALU.mult, R, W_)
    tt("dve", P(P_T1), P(P_T1), P(P_T2), ALU.add, R, W_)
    tt("dve", P(P_CR), P(P_T1), P(P_T3), ALU.mult, R, W_)
    tt("dve", P(P_T1), P(P_LI), P(P_ARE), ALU.mult, R, W_)
    tt("dve", P(P_T2), P(P_T0), P(P_AIM), ALU.mult, R, W_)
    tt("dve", P(P_T1), P(P_T1), P(P_T2), ALU.subtract, R, W_)
    tt("dve", P(P_CI), P(P_T1), P(P_T3), ALU.mult, R, W_)
    ts("dve", P(P_T4), P(P_TH), 8.0 / TWO_PI, None, ALU.mult, reads=R, writes=W_)
    tcopy("dve", pri[:], P(P_T4), R, W_)
    tcopy("dve", P(P_T5), pri[:], R, W_)
    tt("dve", P(P_F8), P(P_T4), P(P_T5), ALU.subtract, R, W_)
    RP = [b_prm, b_PW]
    S.op("dve", lambda e: e.memset(PW[:, 0, 0, :], 1.0), (), [b_PW])
    S.op("dve", lambda e: e.memset(PW[:, 0, 1, :], 0.0), (), [b_PW])
    tcopy("dve", PW[:, 1, 0, :], P(P_LR), RP, [b_PW])
    tcopy("dve", PW[:, 1, 1, :], P(P_LI), RP, [b_PW])
    for k in range(2, 9):
        ar_, ai_ = PW[:, k - 1, 0, :], PW[:, k - 1, 1, :]
        tt("dve", P(P_T4), ar_, P(P_LR), ALU.mult, RP, W_)
        tt("dve", P(P_T5), ai_, P(P_LI), ALU.mult, RP, W_)
        tt("dve", PW[:, k, 0, :], P(P_T4), P(P_T5), ALU.subtract, RP, [b_PW])
        tt("dve", P(P_T6), ar_, P(P_LI), ALU.mult, RP, W_)
        tt("dve", P(P_T7), ai_, P(P_LR), ALU.mult, RP, W_)
        tt("dve", PW[:, k, 1, :], P(P_T6), P(P_T7), ALU.add, RP, [b_PW])


    if stop <= 0:
        return finish()
    AR_U = region(55.5, 87.5)
    uJ = AR_U.alloc([128, 4, 8, T_ALL // 8], BF16); b_uT = [Buf(f"uT{c}") for c in range(4)]
    AR = region(87.5, 196.25)
    w_in_bf = AR.alloc([128, 8, 1280], BF16); b_win = Buf("w_in")
    wst = [AR.alloc([128, 1280], F32) for _ in range(2)]; b_wst = [Buf("wst0"), Buf("wst1")]
    xts = [AR.alloc([128, D], F32) for _ in range(4)]; b_xts = [Buf(f"xt{i}") for i in range(4)]
    hbs = [AR.alloc([128, D], BF16) for _ in range(4)]; b_hbs = [Buf(f"hb{i}") for i in range(4)]
    hTs = [AR.alloc([128, 8, 512], BF16) for _ in range(2)]; b_hTs = [Buf("hT0"), Buf("hT1")]
    T_RP = 2304
    AR_T = region(21.5, 55.5)
    cosT = AR_T.alloc([128, T_RP], F32); sinT = AR_T.alloc([128, T_RP], F32); b_cs = Buf("cossin")
    angb = AR.alloc([128, 512], F32); angi = AR.alloc([128, 512], I32); b_ang = Buf("ang")
    qb = [AR.alloc([128, 512], BF16) for _ in range(2)]; b_qb = [Buf("qb0"), Buf("qb1")]
    rt2 = [[AR.alloc([128, 512], F32) for _ in range(2)] for _ in range(2)]
    b_rt2 = [[Buf(f"rt{a_}{b_}") for b_ in range(2)] for a_ in range(2)]
    qst = [AR.alloc([128, 4, 512], BF16) for _ in range(2)]; b_qst = [Buf("qst0"), Buf("qst1")]
    kst = [AR.alloc([128, 512], BF16) for _ in range(2)]; b_kst = [Buf("kst0"), Buf("kst1")]
    vst = [AR.alloc([128, 4, 130], BF16) for _ in range(2)]; b_vst = [Buf("vst0"), Buf("vst1")]
    b_qsd = Buf("q_scr"); b_ksd = Buf("k_scr"); b_vsd = Buf("v_scr")

    def load_w_in():
        for kc in range(8):
            dma("dsync", wst[kc % 2][:], win_d[kc * 128:(kc + 1) * 128, :], writes=[b_wst[kc % 2]])
            if kc % 2 == 0:
                acopy(w_in_bf[:, kc, :], wst[kc % 2][:], [b_wst[kc % 2]], [b_win])
            else:
                tcopy("dve", w_in_bf[:, kc, :], wst[kc % 2][:], [b_wst[kc % 2]], [b_win])

    load_w_in()
    for blk in range(5):
        c0 = blk * 512
        nb = min(512, T_RP - c0)
        cs_, sn_b = cosT[:, c0:c0 + nb], sinT[:, c0:c0 + nb]
        dma("dsync", angb[:, 0:nb], pos_d[:, c0:c0 + nb].partition_broadcast(128), writes=[b_ang])
        ts("dve", angb[:, 0:nb], angb[:, 0:nb], ifr[:, 0:1], None, ALU.mult, reads=[b_ang, b_ifr], writes=[b_ang])
        ts("dve", sn_b, angb[:, 0:nb], 1.0 / TWO_PI, None, ALU.mult, reads=[b_ang], writes=[b_cs])
        tcopy("dve", angi[:, 0:nb], sn_b, [b_cs], [b_ang])
        tcopy("dve", sn_b, angi[:, 0:nb], [b_ang], [b_cs])
        stt(angb[:, 0:nb], sn_b, -TWO_PI, angb[:, 0:nb], ALU.mult, ALU.add, [b_ang, b_cs], [b_ang])
        ts("dve", angb[:, 0:nb], angb[:, 0:nb], float(np.pi), float(-np.pi), ALU.min, ALU.max, [b_ang], [b_ang])
        acopy(sn_b, angb[:, 0:nb], [b_ang], [b_cs], func=AF.Sin)
        acopy(cs_, angb[:, 0:nb], [b_ang], [b_cs], func=AF.Sin, scale=0.5)
        tt("dve", cs_, cs_, cs_, ALU.mult, [b_cs], [b_cs])
        ts("dve", cs_, cs_, -2.0, 1.0, ALU.mult, ALU.add, [b_cs], [b_cs])
    for i_ in range(2):
        S.op("pool", lambda e, i_=i_: e.memset(vst[i_][:], 1.0), (), [b_vst[i_]])
    if stop <= 0.2:
        return finish()
    for g4 in range(8):
        hT, b_hT = hTs[g4 % 2], b_hTs[g4 % 2]
        norm_transpose_group([(x_d[(g4 * 4 + j) * 128:(g4 * 4 + j + 1) * 128, :], G_MIX, xts[j], b_xts[j], hbs[j], b_hbs[j],
                               hT, b_hT, j * 128) for j in range(4)])
        if stop <= 0.7:
            return finish()
        for c in range(4):
            pf, b_pf = next_psf()
            mm(pf[:], [(w_in_bf[:, kc, 768 + c * 128:768 + (c + 1) * 128], hT[:, kc, :]) for kc in range(8)],
               [b_win, b_hT], [b_pf])
            if stop <= 0.75:
                return finish()
            acopy(uJ[:, c, :, g4 * 64:(g4 + 1) * 64], pf[:].rearrange("p (n j) -> p j n", j=8), [b_pf], [b_uT[c]])
            if stop <= 0.8:
                return finish()
            if stop <= 0.85:
                return finish()
        if g4 < NQKV_G:
            nt = 4 if g4 < 4 else 2
            nn = nt * 128
            cols = slice(g4 * 512, g4 * 512 + nn)
            sp = g4 % 2
            for c in range(5):
                pf, b_pf = next_psf()
                mm(pf[:, 0:nn], [(w_in_bf[:, kc, c * 128:(c + 1) * 128], hT[:, kc, 0:nn]) for kc in range(8)],
                   [b_win, b_hT], [b_pf])
                i2 = c % 2
                acopy(qb[i2][:, 0:nn], pf[:, 0:nn], [b_pf], [b_qb[i2]])
                pr_, b_pr_ = next_psf()
                mm(pr_[:, 0:nn], [(rmat[:], qb[i2][:, 0:nn])], [b_rmat, b_qb[i2]], [b_pr_])
                ra, rb_, b_ra, b_rb = rt2[i2][0], rt2[i2][1], b_rt2[i2][0], b_rt2[i2][1]
                tt("dve", ra[:, 0:nn], pf[:, 0:nn], cosT[:, cols], ALU.mult, [b_pf, b_cs], [b_ra])
                tt("dve", rb_[:, 0:nn], pr_[:, 0:nn], sinT[:, cols], ALU.mult, [b_pr_, b_cs], [b_rb])
                if c < 4:
                    tt("dve", qst[sp][:, c, 0:nn], ra[:, 0:nn], rb_[:, 0:nn], ALU.add, [b_ra, b_rb], [b_qst[sp]])
                else:
                    tt("dve", kst[sp][:, 0:nn], ra[:, 0:nn], rb_[:, 0:nn], ALU.add, [b_ra, b_rb], [b_kst[sp]])
            for j in range(nt):
                pf, b_pf = next_psf()
                mm(pf[:, 0:128], [(hT[:, kc, j * 128:(j + 1) * 128], w_in_bf[:, kc, 640:768]) for kc in range(8)],
                   [b_win, b_hT], [b_pf])
                acopy(vst[sp][:, j, :].rearrange("p (g d) -> p g d", g=2)[:, :, 0:64],
                      pf[:, 0:128].rearrange("p (g d) -> p g d", g=2), [b_pf], [b_vst[sp]])
            dma("dpool", qs_d[:, :, cols], qst[sp][:, :, 0:nn], reads=[b_qst[sp]], writes=[b_qsd])
            dma("dpool", ks_d[:, cols], kst[sp][:, 0:nn], reads=[b_kst[sp]], writes=[b_ksd])
            dma("dpool", vs_d[:, g4 * 4:g4 * 4 + nt, :], vst[sp][:, 0:nt, :], reads=[b_vst[sp]], writes=[b_vsd])
        if stop <= 0.9:
            return finish()

    if stop <= 1:
        return finish()
    S.barrier()
    AR = region(87.5, 196.25)
    NO, NA = T_EXT // 8, T_ALL // 8
    Tz = AR.alloc([128, 4, 15, 128], BF16); b_Tz = Buf("Tz")
    BbTc = AR.alloc([128, 8, 2, 8, 128], BF16); b_BbTc = Buf("BbTc")
    CbTc = AR.alloc([128, 8, 2, 16, 32], BF16); b_CbTc = Buf("CbTc")
    mB = AR.mark()
    bbar = AR.alloc([128, 2, 512], F32); b_bbar = Buf("bbar")
    braw = AR.alloc([128, 2, 512], F32); b_braw = Buf("braw")
    tmpbs = [AR.alloc([128, 2, 512], F32) for _ in range(2)]; b_tmpbs = [Buf("tmpb0"), Buf("tmpb1")]
    bpows = [AR.alloc([128, 2, 512], F32) for _ in range(2)]; b_bpows = [Buf("bpow0"), Buf("bpow1")]
    Mps = [AR.alloc([128, 2, 8, 128], BF16) for _ in range(2)]; b_Mps = [Buf("Mp0"), Buf("Mp1")]
    tmpb, b_tmpb = tmpbs[0], b_tmpbs[0]
    Cp = AR.alloc([128, 2, 8, 128], BF16); b_Cp = Buf("Cp")
    bmask = AR.alloc([128, 128], F32); b_bmask = Buf("bmask")
    diagD = AR.alloc([128, 4, 128], F32); b_diagD = Buf("diagD")
    tzt = AR.alloc([128, 128], F32); b_tzt = Buf("tzt")

    dma("dsync", braw[:, 0, :], bre_d[:, :], writes=[b_braw])
    dma("dsync", braw[:, 1, :], bim_d[:, :], writes=[b_braw])
    dma("dsync", bmask[:], bmask_d[:, :], writes=[b_bmask])
    def v3(ap2):
        return ap2.rearrange("p (a c) -> p a c", c=16)

    def bc32(ap32):
        return ap32.unsqueeze(2).to_broadcast([128, 32, 16])

    def cmul(dst, b_dst, src, b_src, cre, cim, rd, tmpb=tmpb, b_tmpb=b_tmpb):
        RB = [b_src, b_tmpb] + rd
        tt("dve", v3(tmpb[:, 0, :]), v3(src[:, 0, :]), bc32(cre), ALU.mult, RB, [b_tmpb])
        tt("dve", v3(tmpb[:, 1, :]), v3(src[:, 1, :]), bc32(cim), ALU.mult, RB, [b_tmpb])
        tt("dve", dst[:, 0, :], tmpb[:, 0, :], tmpb[:, 1, :], ALU.subtract, [b_tmpb], [b_dst])
        tt("dve", v3(tmpb[:, 0, :]), v3(src[:, 1, :]), bc32(cre), ALU.mult, RB, [b_tmpb])
        tt("dve", v3(tmpb[:, 1, :]), v3(src[:, 0, :]), bc32(cim), ALU.mult, RB, [b_tmpb])
        tt("dve", dst[:, 1, :], tmpb[:, 0, :], tmpb[:, 1, :], ALU.add, [b_tmpb], [b_dst])

    cmul(bbar, b_bbar, braw, b_braw, P(P_CR), P(P_CI), [b_prm])

    def pack(dst, b_dst, src, b_src, neg_im):
        for ri in range(2):
            for e_ in range(2):
                ps_ = slice(e_ * 64, (e_ + 1) * 64)
                s_ = src[ps_, ri, :].rearrange("p (k r c) -> p k r c", r=4, c=16)
                d_ap = dst[ps_, ri, :, :].rearrange("p k (r x) -> p k r x", x=32)[:, :, :, 16 * e_:16 * e_ + 16]
                if neg_im and ri == 1:
                    acopy(d_ap, s_, [b_src], [b_dst], scale=-1.0)
                else:
                    acopy(d_ap, s_, [b_src], [b_dst])

    for Mp, b_Mp in zip(Mps, b_Mps):
        S.op("pool", lambda e, Mp=Mp: e.memset(Mp[:], 0.0), (), [b_Mp])
    S.op("pool", lambda e: e.memset(Cp[:], 0.0), (), [b_Cp])
    pack(Cp, b_Cp, craw, b_craw, True)
    for c in range(4):
        ts("dve", diagD[:, c, :], ident_f[:], dsk[:, c:c + 1], None, ALU.mult, reads=[b_tmp, b_dsk], writes=[b_diagD])
    for k in range(8):
        bpow, b_bpow, Mp, b_Mp = bpows[k % 2], b_bpows[k % 2], Mps[k % 2], b_Mps[k % 2]
        cmul(bpow, b_bpow, bbar, b_bbar, PW[:, k, 0, :], PW[:, k, 1, :], [b_PW], tmpb=tmpbs[k % 2], b_tmpb=b_tmpbs[k % 2])
        pack(Mp, b_Mp, bpow, b_bpow, False)
        for c in range(4):
            if k == 0:
                pf, b_pf = next_psf()
                mm(pf[:, 0:128], [(Mp[:, ri, 4 * d_ + c, :], Cp[:, ri, 4 * d_ + c, :]) for d_ in range(2) for ri in range(2)],
                   [b_Mp, b_Cp], [b_pf])
                tt("dve", tzt[:], pf[:, 0:128], bmask[:], ALU.mult, [b_pf, b_bmask], [b_tzt])
                tt("dve", Tz[:, c, 7, :], tzt[:], diagD[:, c, :], ALU.add, [b_tzt, b_diagD], [b_Tz])
            else:
                for d_ in range(2):
                    pf, b_pf = next_psf()
                    mm(pf[:, 0:128], [(Mp[:, ri, 4 * d_ + c, :], Cp[:, ri, 4 * d_ + c, :]) for ri in range(2)],
                       [b_Mp, b_Cp], [b_pf])
                    idx = 7 + k if d_ == 0 else 7 - k
                    tt("dve", Tz[:, c, idx, :], pf[:, 0:128], bmask[:], ALU.mult, [b_pf, b_bmask], [b_Tz])
        for d_ in range(2):
            j = 7 - k if d_ == 0 else k
            pb, b_pb = next_psb()
            transposes([(pb[:, (ri * 4 + kk) * 128:(ri * 4 + kk + 1) * 128], Mp[:, ri, 4 * d_ + kk, :])
                        for ri in range(2) for kk in range(4)], ident[:], [b_Mp, b_ident], [b_pb])
            acopy(BbTc[:, j, :, 4 * d_:4 * d_ + 4, :], pb[:].rearrange("p (r k x) -> p r k x", r=2, k=4), [b_pb], [b_BbTc])

    if stop <= 2:
        return finish()
    S.barrier()
    AR.reset(mB)
    Xd = AR.alloc([128, 2, 16, NO], BF16); b_Xd = Buf("Xd")
    Cts = [AR.alloc([128, NA], F32) for _ in range(2)]; Sns = [AR.alloc([128, NA], F32) for _ in range(2)]
    b_tabs = [Buf("tab0"), Buf("tab1")]
    bufA = AR.alloc([128, NA], F32); bufI = AR.alloc([128, NA], I32); b_bufA = Buf("bufA"); b_bufI = Buf("bufI")

    def gen_table(cq, Nd, par):
        Ct_, Sn_, b_t = Cts[par], Sns[par], b_tabs[par]
        acopy(bufA[:, 0:Nd], sidx[:, 0:Nd], [b_sidx, b_prm], [b_bufA], func=AF.Copy, scale=prm[:, P_F8, cq:cq + 1])
        acopy(bufI[:, 0:Nd], bufA[:, 0:Nd], [b_bufA], [b_bufI])
        tt("dve", bufA[:, 0:Nd], bufA[:, 0:Nd], bufI[:, 0:Nd], ALU.subtract, [b_bufA, b_bufI], [b_bufA])
        acopy(Sn_[:, 0:Nd], bufA[:, 0:Nd], [b_bufA], [b_t], func=AF.Sin, scale=6.283185)
        acopy(Ct_[:, 0:Nd], bufA[:, 0:Nd], [b_bufA], [b_t], func=AF.Sin, scale=3.141592)
        acopy(Ct_[:, 0:Nd], Ct_[:, 0:Nd], [b_t], [b_t], func=AF.Square)
        acopy(Ct_[:, 0:Nd], Ct_[:, 0:Nd], [b_t, b_onec], [b_t], func=AF.Identity, scale=-2.0, bias=onec[:, 0:1])
    Wr = AR.alloc([128, NA], BF16); Wi = AR.alloc([128, NA], BF16); b_W = Buf("W")
    Zr = AR.alloc([128, NA], BF16); Zi = AR.alloc([128, NA], BF16); b_Z = Buf("Z")
    mt = [AR.alloc([128, 512], F32) for _ in range(4)]; b_mt = [Buf(f"mt{i}") for i in range(4)]
    cl = AR.alloc([128, 2, 256], F32); b_cl = Buf("cl")
    clt = AR.alloc([128, 2, 256], F32); b_clt = Buf("clt")

    def v3h(ap2):
        return ap2.rearrange("p (a c) -> p a c", c=16)

    b_ysi = [[Buf(f"ysJ{c}_{i}") for i in range(8)] for c in range(4)]
    for c in range(4):
        for i in range(8):
            py, b_py = next_psf()
            mm(py[:, 0:NO], [(Tz[:, c, i - j + 7, :], uJ[:, c, j, 0:NO]) for j in range(8)], [b_Tz, b_uT[c]], [b_py])
            acopy(ysJ[:, c, i, :], py[:, 0:NO], [b_py], [b_ysi[c][i]])
    for d_ in range(2):
        Nd = NO if d_ == 0 else NA
        S.op("pool", lambda e: e.memset(CbTc[:], 0.0), (), [b_CbTc])
        for i in range(8):
            pw = i + 1 if d_ == 0 else 8 - i
            lr = PW[:, pw, 0, 16 * d_:16 * d_ + 16].unsqueeze(2).to_broadcast([128, 16, 16])
            li = PW[:, pw, 1, 16 * d_:16 * d_ + 16].unsqueeze(2).to_broadcast([128, 16, 16])
            c_re = v3h(craw[:, 0, 256 * d_:256 * d_ + 256]); c_im = v3h(craw[:, 1, 256 * d_:256 * d_ + 256])
            RC = [b_craw, b_PW, b_clt]
            tt("dve", v3h(clt[:, 0, :]), c_re, lr, ALU.mult, RC, [b_clt])
            tt("dve", v3h(clt[:, 1, :]), c_im, li, ALU.mult, RC, [b_clt])
            tt("dve", cl[:, 0, :], clt[:, 0, :], clt[:, 1, :], ALU.subtract, [b_clt], [b_cl])
            tt("dve", v3h(clt[:, 0, :]), c_re, li, ALU.mult, RC, [b_clt])
            tt("dve", v3h(clt[:, 1, :]), c_im, lr, ALU.mult, RC, [b_clt])
            stt(cl[:, 1, :], clt[:, 0, :], -1.0, clt[:, 1, :], ALU.mult, ALU.subtract, [b_clt], [b_cl])
            for ri in range(2):
                for e_ in range(2):
                    ps_ = slice(e_ * 64, (e_ + 1) * 64)
                    acopy(CbTc[ps_, i, ri, :, 16 * e_:16 * e_ + 16], v3h(cl[ps_, ri, :]), [b_cl], [b_CbTc])
        if d_ == 0:
            S.op("pool", lambda e: e.memset(Xd[:, :, :, 0:1], 0.0), (), [b_Xd])
        for q in range(16):
            cq = d_ * 16 + q
            ch, r4 = q // 4, q % 4
            rows = slice(32 * r4, 32 * r4 + 32)
            par = q % 2
            if q == 0:
                gen_table(cq, Nd, par)
            Ct, Sn, b_tab = Cts[par], Sns[par], b_tabs[par]
            pr, b_pr = next_psf()
            pi_, b_pi = next_psf()
            for ri, (pv, b_pv) in enumerate(((pr, b_pr), (pi_, b_pi))):
                def f(e, ri=ri, pv=pv, rows=rows, ch=ch, d_=d_, Nd=Nd, r4=r4):
                    ins = None
                    for j in range(8):
                        ins = e.matmul(pv[:, 0:Nd], BbTc[rows, j, ri, 4 * d_ + ch, :], uJ[rows, ch, j, 0:Nd],
                                       start=(j == 0), stop=(j == 7), tile_position=(32 * r4, 0))
                    return ins
                S.op("pe", f, [b_BbTc, b_uT[ch]], [b_pv], cost=0.3 + 8 * Nd / 2400.0)
            if d_ == 0:
                cs, sn_, wr, wi = Ct[:, 0:Nd], Sn[:, 0:Nd], Wr[:, 0:Nd], Wi[:, 0:Nd]
            else:
                cs, sn_ = Ct[:, 0:Nd][:, ::-1], Sn[:, 0:Nd][:, ::-1]
                wr, wi = Wr[:, 0:Nd][:, ::-1], Wi[:, 0:Nd][:, ::-1]
            tt("dve", mt[0][:, 0:Nd], pr[:, 0:Nd], cs, ALU.mult, [b_pr, b_tab], [b_mt[0]])
            tt("dve", mt[1][:, 0:Nd], pi_[:, 0:Nd], sn_, ALU.mult, [b_pi, b_tab], [b_mt[1]])
            tt("dve", mt[2][:, 0:Nd], pi_[:, 0:Nd], cs, ALU.mult, [b_pi, b_tab], [b_mt[2]])
            tt("dve", mt[3][:, 0:Nd], pr[:, 0:Nd], sn_, ALU.mult, [b_pr, b_tab], [b_mt[3]])
            tt("dve", wr, mt[0][:, 0:Nd], mt[1][:, 0:Nd], ALU.add, [b_mt[0], b_mt[1]], [b_W])
            tt("dve", wi, mt[2][:, 0:Nd], mt[3][:, 0:Nd], ALU.subtract, [b_mt[2], b_mt[3]], [b_W])
            if q + 1 < 16:
                gen_table(cq + 1, Nd, (q + 1) % 2)
            rho = prm[:, P_R8, cq:cq + 1].to_broadcast([128, Nd])
            S.op("dve", lambda e, rho=rho, Nd=Nd: e.tensor_tensor_scan(out=Zr[:, 0:Nd], data0=rho, data1=Wr[:, 0:Nd],
                 initial=0.0, op0=ALU.mult, op1=ALU.add), [b_W, b_prm], [b_Z], cost=0.15 + 2 * Nd / 960.0)
            S.op("dve", lambda e, rho=rho, Nd=Nd: e.tensor_tensor_scan(out=Zi[:, 0:Nd], data0=rho, data1=Wi[:, 0:Nd],
                 initial=0.0, op0=ALU.mult, op1=ALU.add), [b_W, b_prm], [b_Z], cost=0.15 + 2 * Nd / 960.0)
            if d_ == 0:
                n = NO - 1
                zr, zi, cs, sn_ = Zr[:, 0:n], Zi[:, 0:n], Ct[:, 0:n], Sn[:, 0:n]
                xr, xi = Xd[:, 0, q, 1:NO], Xd[:, 1, q, 1:NO]
            else:
                n = NO
                lo = NA - 1 - NO
                zr, zi = Zr[:, lo:lo + n][:, ::-1], Zi[:, lo:lo + n][:, ::-1]
                cs, sn_ = Ct[:, lo:lo + n][:, ::-1], Sn[:, lo:lo + n][:, ::-1]
                xr, xi = Xd[:, 0, q, 0:NO], Xd[:, 1, q, 0:NO]
            RZ = [b_Z, b_tab]
            tt("dve", mt[3][:, 0:n], zr, sn_, ALU.mult, RZ, [b_mt[3]])
            tt("dve", mt[0][:, 0:n], zr, cs, ALU.mult, RZ, [b_mt[0]])
            tt("dve", mt[1][:, 0:n], zi, sn_, ALU.mult, RZ, [b_mt[1]])
            tt("dve", mt[2][:, 0:n], zi, cs, ALU.mult, RZ, [b_mt[2]])
            tt("dve", xr, mt[0][:, 0:n], mt[1][:, 0:n], ALU.subtract, [b_mt[0], b_mt[1]], [b_Xd])
            tt("dve", xi, mt[2][:, 0:n], mt[3][:, 0:n], ALU.add, [b_mt[2], b_mt[3]], [b_Xd])
        for c in range(4):
            for i in range(8):
                py, b_py = next_psf()

                def f(e, c=c, i=i, py=py):
                    ins = None
                    for r4 in range(4):
                        for ri in range(2):
                            ins = e.matmul(py[32 * r4:32 * r4 + 32, 0:NO], CbTc[:, i, ri, 4 * c + r4, :],
                                           Xd[:, ri, 4 * c + r4, 0:NO], start=(ri == 0), stop=(ri == 1),
                                           tile_position=(0, 32 * r4), skip_group_check=True)
                    return ins
                S.op("pe", f, [b_CbTc, b_Xd], [b_py], cost=0.3 + 8 * NO / 2400.0)
                tt("dve", ysJ[:, c, i, :], ysJ[:, c, i, :], py[:, 0:NO], ALU.add, [b_py, b_ysi[c][i]], [b_ysi[c][i]])

    if dbg:
        for c in range(4):
            dma("dpool", dbg_d[c, :, 0:T_EXT], ysJ[:, c, :, :], reads=[b_ys[c]])

    if stop <= 3:
        return finish()
    if stop <= 4:
        return finish()
    AR_U = region(55.5, 87.5)
    qT = AR_U.alloc([128, 4, T_QKV], BF16); b_qT = Buf("qT")
    kT = AR_U.alloc([128, T_QKV], BF16); b_kT = Buf("kT")
    vaug = AR_U.alloc([128, 20, 2, 65], BF16); b_v = Buf("vaug")
    dma("dsync", kT[:, 0:T_RP], ks_d[:, 0:T_RP], reads=[b_ksd], writes=[b_kT] + b_uT)
    dma("dsync", vaug[:, 0:18, :, :].rearrange("p t g d -> p t (g d)"), vs_d[:, 0:18, :], reads=[b_vsd], writes=[b_v] + b_uT)
    for c in range(4):
        dma("dsync", qT[:, c, 0:T_EXT], qs_d[:, c, 0:T_EXT], reads=[b_qsd], writes=[b_qT] + b_uT)
    S.barrier()
    AR_H = region(173.5, 207.8)
    h2T = AR_H.alloc([128, 8, T_EXT], BF16); b_h2T = Buf("h2T")
    AR = region(87.5, 173.5)
    wout_bf = AR.alloc([128, 8, D], BF16); b_wout = Buf("wout")
    wglu_bf = AR.alloc([128, 4, 512], BF16); b_wglu = Buf("wglu")
    _w2 = AR.alloc([128, D], F32); _bw2 = Buf("wst2")
    wst2 = [_w2, _w2]; b_wst2 = [_bw2, _bw2]
    for kc in range(8):
        dma("dsync", wst2[kc % 2][:], wout_d[kc * 128:(kc + 1) * 128, :], writes=[b_wst2[kc % 2]])
        acopy(wout_bf[:, kc, :], wst2[kc % 2][:], [b_wst2[kc % 2]], [b_wout])
    for kc in range(4):
        dma("dsync", wst2[kc % 2][:, 0:512], wglu_d[kc * 128:(kc + 1) * 128, :], writes=[b_wst2[kc % 2]])
        acopy(wglu_bf[:, kc, :], wst2[kc % 2][:, 0:512], [b_wst2[kc % 2]], [b_wglu])
    PT = [[AR.alloc([128, 512], BF16) for _ in range(6)] for _ in range(2)]
    b_PT = [[Buf(f"PT{h}{i}") for i in range(6)] for h in range(2)]
    rden = [AR.alloc([128, 8], F32) for _ in range(2)]; b_rden = [Buf("rden0"), Buf("rden1")]
    attn_ = [AR.alloc([128, 512], F32) for _ in range(2)]; b_attn_ = [Buf("attn0"), Buf("attn1")]
    mixed_ = [AR.alloc([128, D], BF16) for _ in range(2)]; b_mixa = [Buf("mixa0"), Buf("mixa1")]; b_mixs = [Buf("mixs0"), Buf("mixs1")]
    mixT_ = [AR.alloc([128, 8, 128], BF16) for _ in range(2)]; b_mixT_ = [Buf("mixT0"), Buf("mixT1")]
    ysg_ = [AR.alloc([128, 4, 128], F32) for _ in range(2)]; b_ysg_ = [Buf("ysg0"), Buf("ysg1")]
    ysgb_ = [AR.alloc([128, 4, 128], BF16) for _ in range(2)]; b_ysgb_ = [Buf("ysgb0"), Buf("ysgb1")]
    sig_ = [AR.alloc([128, 4, 128], F32) for _ in range(2)]; b_sig_ = [Buf("sig0"), Buf("sig1")]
    ys2_ = [AR.alloc([128, 4, 128], BF16) for _ in range(2)]; b_ys2_ = [Buf("ys20"), Buf("ys21")]
    junk_ = [AR.alloc([128, 512], BF16) for _ in range(2)]; b_junk_ = [Buf("junk0"), Buf("junk1")]
    ysT_ = [AR.alloc([128, 512], BF16) for _ in range(2)]; b_ysT_ = [Buf("ysT0"), Buf("ysT1")]
    xc = [AR.alloc([128, D], F32) for _ in range(2)]; b_xc = [Buf("xc0"), Buf("xc1")]
    x1 = [AR.alloc([128, D], F32) for _ in range(2)]; b_x1 = [Buf("x1a"), Buf("x1b")]
    h2b = [AR.alloc([128, D], BF16) for _ in range(2)]; b_h2b = [Buf("h2b0"), Buf("h2b1")]
    b_x1d = [Buf(f"x1d{i}") for i in range(17)]
    po_ = {}

    def rstd_exp(sc, b_sc, n_feat):
        acopy(sc[:, 1:2], sc[:, 0:1], [b_sc, b_epsc], [b_sc], func=AF.Ln, scale=1.0 / n_feat, bias=epsc[:, 0:1])
        acopy(sc[:, 3:4], sc[:, 1:2], [b_sc], [b_sc], func=AF.Exp, scale=-0.5)

    def ssq_dve(junk, src, sc):
        return lambda e: e.scalar_tensor_tensor(out=junk, in0=src, scalar=1.0, in1=src, op0=ALU.mult, op1=ALU.mult,
                                                accum_out=sc[:, 0:1])

    def rms_to(dst, src, b_src_list, gain_off, junk, b_junk, b_dst):
        sc, b_sc = next_stat()
        S.op("dve", ssq_dve(junk[:, 0:512], src, sc), b_src_list, [b_junk, b_sc])
        rstd_exp(sc, b_sc, 512)
        stt(dst, src, sc[:, 3:4], gains[:, gain_off:gain_off + 512], ALU.mult, ALU.mult,
            b_src_list + [b_sc, b_gains], [b_dst])

    def s_scores(n):
        h = n % 2
        cols = slice(n * 128, (n + 1) * 128)
        for g in range(2):
            rows = slice(64 * g, 64 * g + 64)
            for kb in (n - 1, n, n + 1):
                if kb < 0:
                    continue
                slot = g * 3 + (kb - n + 1)
                pf, b_pf = next_psf()
                mm(pf[:].rearrange("p (j t) -> p j t", j=4),
                   [(kT[rows, kb * 128:(kb + 1) * 128], qT[rows, :, cols])], [b_kT, b_qT], [b_pf])
                acopy(PT[h][slot][:], pf[:], [b_pf], [b_PT[h][slot]], func=AF.Exp, scale=0.125)
                if kb != n:
                    m_ = msk[:, 0:128] if kb == n - 1 else msk[:, 128:256]
                    tt("dve", PT[h][slot][:].rearrange("p (j t) -> p j t", j=4),
                       PT[h][slot][:].rearrange("p (j t) -> p j t", j=4),
                       m_.unsqueeze(1).to_broadcast([128, 4, 128]), ALU.mult, [b_PT[h][slot], b_msk], [b_PT[h][slot]])

    def s_pv(n):
        h = n % 2
        for g in range(2):
            pts = [(kb, g * 3 + (kb - n + 1)) for kb in (n - 1, n, n + 1) if kb >= 0]
            pog, b_pog = next_psf()
            for j in range(4):
                mm(pog[:, j * 65:(j + 1) * 65],
                   [(PT[h][slot][:, j * 128:(j + 1) * 128], vaug[:, kb, g, :]) for (kb, slot) in pts],
                   [b_PT[h][s_] for (_, s_) in pts] + [b_v], [b_pog])
            o3 = pog[:, 0:260].rearrange("p (j d) -> p j d", j=4)
            tt("dve", rden[h][:, 4 * g:4 * g + 4], o3[:, :, 64], esink[:, 4 * g:4 * g + 4], ALU.add,
               [b_pog, b_esink], [b_rden[h]])
            S.op("dve", lambda e, g=g, h=h: e.reciprocal(out=rden[h][:, 4 * g:4 * g + 4], in_=rden[h][:, 4 * g:4 * g + 4]),
                 [b_rden[h]], [b_rden[h]])
            tt("dve", attn_[h][:, 256 * g:256 * g + 256].rearrange("p (j d) -> p j d", j=4), o3[:, :, 0:64],
               rden[h][:, 4 * g:4 * g + 4].unsqueeze(2).to_broadcast([128, 4, 64]), ALU.mult,
               [b_pog, b_rden[h]], [b_attn_[h]])
        rms_to(mixed_[h][:, 0:512], attn_[h][:], [b_attn_[h]], G_ATT, junk_[h], b_junk_[h], b_mixa[h])

    def s_ssm(n):
        h = n % 2
        for c in range(4):
            acopy(ysg_[h][:, c, :].rearrange("p (n i) -> p i n", i=8), ysJ[:, c, :, 16 * n:16 * n + 16], [b_ys[c]],
                  [b_ysg_[h]])
        acopy(ysgb_[h][:], ysg_[h][:], [b_ysg_[h]], [b_ysgb_[h]])
        for co in range(4):
            pf, b_pf = next_psf()
            mm(pf[:, 0:128], [(wglu_bf[:, kc, co * 128:(co + 1) * 128], ysgb_[h][:, kc, :]) for kc in range(4)],
               [b_wglu, b_ysgb_[h]], [b_pf])
            acopy(sig_[h][:, co, :], pf[:, 0:128], [b_pf], [b_sig_[h]], func=AF.Exp, scale=-1.0)
        acopy(sig_[h][:], sig_[h][:], [b_sig_[h], b_onec2], [b_sig_[h]], func=AF.Ln, bias=onec2[:, 0:1])
        acopy(sig_[h][:], sig_[h][:], [b_sig_[h]], [b_sig_[h]], func=AF.Exp, scale=-1.0)
        tt("dve", ys2_[h][:], ysg_[h][:], sig_[h][:], ALU.mult, [b_ysg_[h], b_sig_[h]], [b_ys2_[h]])
        pb, b_pb = next_psb()
        transposes([(pb[:, c * 128:(c + 1) * 128], ys2_[h][:, c, :]) for c in range(4)], ident[:],
                   [b_ys2_[h], b_ident], [b_pb])
        acopy(ysT_[h][:], pb[:, 0:512], [b_pb], [b_ysT_[h]])
        rms_to(mixed_[h][:, 512:1024], ysT_[h][:], [b_ysT_[h]], G_SSM, junk_[h], b_junk_[h], b_mixs[h])

    def s_out(n):
        h = n % 2
        pb, b_pb = next_psb()
        transposes([(pb[:, k * 128:(k + 1) * 128], mixed_[h][:, k * 128:(k + 1) * 128]) for k in range(8)],
                   ident[:], [b_mixa[h], b_mixs[h], b_ident], [b_pb])
        acopy(mixT_[h][:], pb[:].rearrange("p (k t) -> p k t", k=8), [b_pb], [b_mixT_[h]])
        dma("dsync", xc[h][:], x_d[n * 128:(n + 1) * 128, :], writes=[b_xc[h]])
        for hf in range(2):
            pf, b_pf = next_psf()
            mm(pf[:], [(mixT_[h][:, kc, :], wout_bf[:, kc, hf * 512:(hf + 1) * 512]) for kc in range(8)],
               [b_mixT_[h], b_wout], [b_pf])
            tt("dve", x1[h][:, hf * 512:(hf + 1) * 512], pf[:], xc[h][:, hf * 512:(hf + 1) * 512], ALU.add,
               [b_pf, b_xc[h]], [b_x1[h]])
        dma("dpool", x1_d[n * 128:(n + 1) * 128, :], x1[h][:], reads=[b_x1[h]], writes=[b_x1d[n]])
        norm_transpose(None, (), G_FFN, x1[h], b_x1[h], h2b[h], b_h2b[h], h2T, b_h2T, n * 128, from_dram=False)

    for c in range(4):
        acopy(ysJ[:, c, :, :], ysJ[:, c, :, :], [b_ys[c]] + b_ysi[c], [b_ys[c]], func=AF.Gelu)
    s_scores(0)
    s_ssm(0)
    for n in range(17):
        if n + 1 < 17:
            s_scores(n + 1)
        s_pv(n)
        if n + 1 < 17:
            s_ssm(n + 1)
        s_out(n)


    if stop <= 5:
        return finish()
    S.barrier()
    AR_ACT = region(21.5, 110)
    actT = AR_ACT.alloc([128, NPAIR, T_OWN], BF16); b_actT = Buf("actT")
    AR = region(110, 173.5)
    HT = T_OWN // 2
    NU = HT + 2
    up = [[AR.alloc([128, NU], F32) for _ in range(2)] for _ in range(2)]
    b_up = [[Buf(f"up{h}{g}") for g in range(2)] for h in range(2)]
    cv = [[AR.alloc([128, HT], F32) for _ in range(2)] for _ in range(2)]
    b_cv = [[Buf(f"cv{h}{g}") for g in range(2)] for h in range(2)]
    wus = [AR.alloc([128, 8, 128], F32) for _ in range(4)]; b_wus = [Buf(f"wus{i}") for i in range(4)]
    wub = [AR.alloc([128, 8, 128], BF16) for _ in range(4)]; b_wub = [Buf(f"wub{i}") for i in range(4)]
    for g in range(2):
        S.op("pool", lambda e, g=g: e.memset(up[0][g][:, 0:1], 0.0), (), [b_up[0][g]])
    def load_pair(p):
        for gv in range(2):
            wi = (p % 2) * 2 + gv
            c0 = gv * DFF + p * 128
            dma("dsync", wus[wi][:], wup_d[:, c0:c0 + 128].rearrange("(k p) f -> p k f", p=128), writes=[b_wus[wi]])
            if gv == 0:
                acopy(wub[wi][:], wus[wi][:], [b_wus[wi]], [b_wub[wi]])
            else:
                tcopy("dve", wub[wi][:], wus[wi][:], [b_wus[wi]], [b_wub[wi]])

    def stage_a(p, hf):
        for gv in range(2):
            wi = (p % 2) * 2 + gv
            ch = gv * NPAIR + p
            u_, b_u = up[hf][gv], b_up[hf][gv]
            if hf == 0:
                segs = [(0, 512, 1), (512, 512, 513), (1024, 1, 1025)]
            else:
                segs = [(1023, 1, 0), (1024, 512, 1), (1536, 512, 513), (2048, 1, 1025)]
            for si, (t0, n, dc) in enumerate(segs):
                pf, b_pf = next_psf()
                mm(pf[:, 0:n], [(wub[wi][:, kc, :], h2T[:, kc, t0:t0 + n]) for kc in range(8)],
                   [b_wub[wi], b_h2T], [b_pf])
                acopy(u_[:, dc:dc + n], pf[:, 0:n], [b_pf], [b_u])
            c_, b_c = cv[hf][gv], b_cv[hf][gv]
            acopy(c_[:], u_[:, 1:1 + HT], [b_u, b_cw, b_cb], [b_c], func=AF.Identity,
                  scale=cw[:, 44 + ch:44 + ch + 1], bias=cb[:, ch:ch + 1])
            stt(c_[:], u_[:, 0:HT], cw[:, ch:ch + 1], c_[:], ALU.mult, ALU.add, [b_u, b_cw, b_c], [b_c])
            stt(c_[:], u_[:, 2:2 + HT], cw[:, 88 + ch:88 + ch + 1], c_[:], ALU.mult, ALU.add, [b_u, b_cw, b_c], [b_c])

    def stage_b(p, hf):
        acopy(cv[hf][0][:], cv[hf][0][:], [b_cv[hf][0]], [b_cv[hf][0]], func=AF.Silu)
        tt("dve", actT[:, p, hf * HT:(hf + 1) * HT], cv[hf][0][:], cv[hf][1][:], ALU.mult,
           [b_cv[hf][0], b_cv[hf][1]], [b_actT])

    load_pair(0)
    for p in range(NPAIR):
        if p + 1 < NPAIR:
            load_pair(p + 1)
        stage_a(p, 0)
        if p > 0:
            stage_b(p - 1, 1)
        stage_a(p, 1)
        stage_b(p, 0)
    stage_b(NPAIR - 1, 1)

    S.barrier()
    AR = region(110, 207.8)
    wdn_bf = AR.alloc([128, NPAIR, D], BF16); b_wdnp = [Buf(f"wdn{p}") for p in range(NPAIR)]
    wds = [AR.alloc([128, D], F32) for _ in range(4)]; b_wds = [Buf(f"wds{i}") for i in range(4)]
    for p in range(NPAIR):
        dma("dsync", wds[p % 4][:], wdn_d[p * 128:(p + 1) * 128, :], writes=[b_wds[p % 4]])
        if p % 2 == 0:
            acopy(wdn_bf[:, p, :], wds[p % 4][:], [b_wds[p % 4]], [b_wdnp[p]])
        else:
            tcopy("dve", wdn_bf[:, p, :], wds[p % 4][:], [b_wds[p % 4]], [b_wdnp[p]])
    x1r = [AR.alloc([128, D], F32) for _ in range(2)]; b_x1r = [Buf("x1r0"), Buf("x1r1")]
    x2 = [AR.alloc([128, D], F32) for _ in range(2)]; b_x2 = [Buf("x2a"), Buf("x2b")]
    yo = [AR.alloc([128, D], F32) for _ in range(2)]; b_yo = [Buf("yo0"), Buf("yo1")]
    jk = AR.alloc([128, D], BF16); b_jk = Buf("jk")
    NE = 3
    early = [[next_psf() for _ in range(2)] for _ in range(NE)]
    for p in range(NPAIR):
        for n in range(NE):
            for hf in range(2):
                pf, b_pf = early[n][hf]
                S.op("pe", lambda e, pf=pf, p=p, n=n, hf=hf: e.matmul(
                    pf[:], actT[:, p, n * 128:(n + 1) * 128], wdn_bf[:, p, hf * 512:(hf + 1) * 512],
                    start=(p == 0), stop=(p == NPAIR - 1)), [b_actT, b_wdnp[p]], [b_pf], cost=0.25)
    for n in range(16):
        i2 = n % 2
        dma("dsync", x1r[i2][:], x1_d[n * 128:(n + 1) * 128, :], reads=[b_x1d[n]], writes=[b_x1r[i2]])
        for hf in range(2):
            if n < NE:
                pf, b_pf = early[n][hf]
            else:
                pf, b_pf = next_psf()
                mm(pf[:], [(actT[:, p, n * 128:(n + 1) * 128], wdn_bf[:, p, hf * 512:(hf + 1) * 512])
                           for p in range(NPAIR)], [b_actT] + b_wdnp, [b_pf])
            tt("dve", x2[i2][:, hf * 512:(hf + 1) * 512], pf[:], x1r[i2][:, hf * 512:(hf + 1) * 512], ALU.add,
               [b_pf, b_x1r[i2]], [b_x2[i2]])
        sc, b_sc = next_stat()
        acopy(jk[:], x2[i2][:], [b_x2[i2]], [b_jk, b_sc], func=AF.Square, accum=sc[:, 0:1])
        ts("dve", sc[:, 1:2], sc[:, 0:1], 1.0 / D, EPS, ALU.mult, ALU.add, [b_sc], [b_sc])
        acopy(sc[:, 2:3], sc[:, 1:2], [b_sc], [b_sc], func=AF.Sqrt)
        S.op("dve", lambda e, sc=sc: e.reciprocal(out=sc[:, 3:4], in_=sc[:, 2:3]), [b_sc], [b_sc])
        stt(yo[i2][:], x2[i2][:], sc[:, 3:4], gains[:, G_FIN:G_FIN + D], ALU.mult, ALU.mult,
            [b_x2[i2], b_sc, b_gains], [b_yo[i2]])
        dma("dpool", y_d[n * 128:(n + 1) * 128, :], yo[i2][:], reads=[b_yo[i2]])

    return finish()


def _consts():
    ident = np.eye(128, dtype=np.float32)
    R = np.zeros((128, 128), np.float32)
    for hh in range(2):
        for d in range(8):
            R[hh * 64 + d, hh * 64 + d + 8] = -1.0
            R[hh * 64 + d + 8, hh * 64 + d] = 1.0
    rmat = np.ascontiguousarray(R.T)
    ifr = np.zeros((128, 1), np.float32)
    base = np.power(np.float32(500000.0), -np.arange(8, dtype=np.float32) / np.float32(8.0)).astype(np.float32)
    for hh in range(2):
        for d in range(16):
            ifr[hh * 64 + d, 0] = base[d % 8]
    kk = np.arange(128)[:, None]
    qq = np.arange(128)[None, :]
    msk = np.concatenate([(kk >= qq), (kk <= qq)], axis=1).astype(np.float32)
    bm = (np.arange(128)[:, None] // 16 == np.arange(128)[None, :] // 16).astype(np.float32)
    return ident, rmat, ifr, msk, bm


_NC_CACHE = {}


def _prep_inputs(inp, dbg=False):
    f = lambda a: np.ascontiguousarray(np.asarray(a, dtype=np.float32))
    x = f(inp["x"])
    w_in = f(inp["w_in"][0])
    qcols = np.concatenate([np.r_[j * 64:(j + 1) * 64, (4 + j) * 64:(5 + j) * 64] for j in range(4)])
    w_in_r = np.ascontiguousarray(np.concatenate([w_in[:, qcols], w_in[:, 512:]], axis=1))
    gains = np.concatenate([f(inp["norm_mix_g"][0]), f(inp["norm_ffn_g"][0]), f(inp["norm_final_g"]),
                            f(inp["norm_attn_g"][0]), f(inp["norm_ssm_g"][0])])[None, :]
    ident, rmat, ifr, msk, bm = _consts()
    a_re, a_im = f(inp["a_re"][0]), f(inp["a_im"][0])
    lst = np.broadcast_to(f(inp["log_step"][0])[:, :, None], (2, 32, 64))
    b_re, b_im = f(inp["b_re"][0]), f(inp["b_im"][0])
    c_re, c_im = f(inp["c_re"][0]), f(inp["c_im"][0])
    cwf = f(inp["conv_w"][0])
    shared = dict(
        w_in=w_in_r, gains=np.ascontiguousarray(gains),
        dsk=np.ascontiguousarray(f(inp["d_skip"][0]).reshape(4, 128).T),
        w_glu=f(inp["w_glu"][0]), sink=f(inp["sink"]), w_out=f(inp["w_out"][0]), w_up=f(inp["w_up"][0]),
        cb=np.ascontiguousarray(f(inp["conv_b"][0]).reshape(44, 128).T),
        w_down=f(inp["w_down"][0]), ident=ident, rmat=rmat, ifr=ifr, msk=msk, bmask=bm,
        sidx=np.arange(512, dtype=np.float32)[None, :])

    def ep(a):
        return np.ascontiguousarray(a.reshape(2, 16, 2, 64).transpose(2, 3, 0, 1).reshape(128, 32))

    def ep_b(a):
        return np.ascontiguousarray(a.reshape(2, 16, 2, 64, 16).transpose(2, 3, 0, 1, 4).reshape(128, 512))

    def ep_c(a):
        return np.ascontiguousarray(a.reshape(2, 16, 2, 16, 64).transpose(2, 4, 0, 1, 3).reshape(128, 512))

    per_half = []
    for h in range(2):
        sl = slice(None) if h == 0 else slice(None, None, -1)
        cwh = cwf if h == 0 else cwf[::-1]
        pos = np.arange(T_QKV, dtype=np.float32) if h == 0 else (T_ALL - 1 - np.arange(T_QKV)).astype(np.float32)
        per_half.append(dict(
            are=ep(a_re[sl]), aim=ep(a_im[sl]), ls=ep(lst[sl]),
            bre=ep_b(b_re[sl]), bim=ep_b(b_im[sl]), cre=ep_c(c_re[sl]), cim=ep_c(c_im[sl]),
            cw=np.ascontiguousarray(cwh.reshape(3, 44, 128).transpose(2, 0, 1).reshape(128, 132)),
            pos=np.ascontiguousarray(pos[None, :])))
    in_maps = []
    for c in range(8):
        b, h = c // 2, c % 2
        xs = x[b] if h == 0 else x[b][::-1]
        m = dict(shared)
        m.update(per_half[h])
        m["x"] = np.ascontiguousarray(xs)
        in_maps.append(m)
    return in_maps


def kernel(**inputs):
    in_maps = _prep_inputs(inputs)
    if "nc" not in _NC_CACHE:
        _NC_CACHE["nc"] = build_program()
    nc = _NC_CACHE["nc"]
    res = run_bass_kernel_spmd(nc, in_maps, core_ids=list(range(8)))
    out = np.empty((4, T_ALL, D), np.float32)
    for c in range(8):
        b, h = c // 2, c % 2
        y = np.asarray(res.results[c]["y"], dtype=np.float32)
        if h == 0:
            out[b, :T_OWN] = y
        else:
            out[b, T_OWN:] = y[::-1]
    return out
```

```python
import numpy as np
import concourse.bass as bass
import concourse.mybir as mybir
from concourse.bass_utils import run_bass_kernel_spmd

F32 = mybir.dt.float32
BF16 = mybir.dt.bfloat16
I32 = mybir.dt.int32
ALU = mybir.AluOpType
AF = mybir.ActivationFunctionType

D = 1024
T_ALL = 4096
T_OWN = 2048
T_EXT = 2176
T_QKV = 2560
NQKV_G = 5
DFF = 2816
NPAIR = 22
EPS = 1e-6
TWO_PI = float(2.0 * np.pi)
SB_BASE = 16512
import os as _os
_C = lambda k, d: float(_os.environ.get(k, d))
C_ACT_F, C_ACT_E = _C("KS_ACT_F", 0.2), _C("KS_ACT_E", 1000.0)
C_DVE_F, C_DVE_E = _C("KS_DVE_F", 0.15), _C("KS_DVE_E", 960.0)
C_PE_F, C_PE_E = _C("KS_PE_F", 0.3), _C("KS_PE_E", 2400.0)
C_DMA_F, C_DMA_B = _C("KS_DMA_F", 2.0), _C("KS_DMA_B", 150e3)
SB_END = 229376


class Buf:
    __slots__ = ("name", "writer", "readers", "excl")

    def __init__(self, name, excl=False):
        self.name = name
        self.writer = None
        self.readers = {}
        self.excl = excl


class Stream:
    def __init__(self, name, eng_name, inc, sem):
        self.name, self.eng_name, self.inc, self.sem, self.count = name, eng_name, inc, sem, 0


class Op:
    __slots__ = ("stream", "fn", "preds", "cost", "prog", "end", "start", "sidx", "nsucc", "bar_counts")

    def __init__(self, stream, fn, preds, cost, prog):
        self.stream, self.fn, self.preds, self.cost, self.prog = stream, fn, preds, cost, prog
        self.end = self.start = None
        self.sidx = None
        self.bar_counts = None


class Sched:
    ENGS = ("tensor", "vector", "scalar", "gpsimd", "sync")
    NDS = 12
    HOP = _C("KS_HOP", 0.45)
    NDP = 4

    def __init__(self, sems):
        names = [("pe", "tensor", 1), ("dve", "vector", 1), ("act", "scalar", 1), ("pool", "gpsimd", 1)]
        names += [(f"dsync{i}", "sync", 16) for i in range(self.NDS)]
        names += [(f"dpool{i}", "gpsimd", 16) for i in range(self.NDP)]
        assert len(sems) == len(names)
        self.streams = {n: Stream(n, e, inc, s) for (n, e, inc), s in zip(names, sems)}
        self.slots = {n: Buf("slot_" + n) for n in self.streams if n.startswith("ds") or n.startswith("dp")}
        self.rr = {"dsync": 0, "dpool": 0}
        self.ops = []
        self.last_barrier = None
        self.since_barrier = []

    def op(self, stream, fn, reads=(), writes=(), cost=0.5):
        if stream in self.rr:
            k = self.rr[stream]
            self.rr[stream] = (k + 1) % (self.NDS if stream == "dsync" else self.NDP)
            stream = f"{stream}{k}"
            writes = list(writes) + [self.slots[stream]]
        ex = [b for b in reads if b.excl]
        if ex:
            reads = [b for b in reads if not b.excl]
            writes = list(writes) + ex
        preds = set()
        for b in reads:
            if b.writer is not None:
                preds.add(b.writer)
        for b in writes:
            if b.writer is not None:
                preds.add(b.writer)
            for o in b.readers.values():
                preds.add(o)
        if self.last_barrier is not None:
            preds.add(self.last_barrier)
        o = Op(stream, fn, list(preds), cost, len(self.ops))
        self.ops.append(o)
        self.since_barrier.append(o)
        for b in reads:
            b.readers[id(o)] = o
        for b in writes:
            b.writer = o
            b.readers = {}
        return o

    def barrier(self):
        if not self.since_barrier:
            return
        b = Op(None, None, list(self.since_barrier) + ([self.last_barrier] if self.last_barrier else []), 0.0, len(self.ops))
        self.ops.append(b)
        self.last_barrier = b
        self.since_barrier = []

    def schedule(self):
        import heapq
        ops = self.ops
        npred = {id(o): len(o.preds) for o in ops}
        succ = {id(o): [] for o in ops}
        for o in ops:
            for p in o.preds:
                succ[id(p)].append(o)
        eng_free = {e: 0.0 for e in self.ENGS}
        eng_of = lambda o: self.streams[o.stream].eng_name
        heap = []

        def ready_time(o):
            return max([p.end for p in o.preds], default=0.0) + self.HOP

        def push(o):
            if o.stream is None:
                o.start = o.end = ready_time(o)
                release(o)
                return
            rt = ready_time(o)
            heapq.heappush(heap, (max(rt, eng_free[eng_of(o)]), o.prog, rt, o))

        def release(o):
            for s_ in succ[id(o)]:
                npred[id(s_)] -= 1
                if npred[id(s_)] == 0:
                    push(s_)
        for o in ops:
            if npred[id(o)] == 0:
                push(o)
        nsched = 0
        while heap:
            key, prog, rt, o = heapq.heappop(heap)
            e = eng_of(o)
            st_ = max(rt, eng_free[e])
            if st_ > key + 1e-9:
                heapq.heappush(heap, (st_, prog, rt, o))
                continue
            o.start = st_
            is_dma = self.streams[o.stream].inc == 16
            o.end = st_ + o.cost
            eng_free[e] = st_ + (0.08 if is_dma else o.cost)
            nsched += 1
            release(o)
        assert all(o.start is not None for o in ops), "scheduler: unscheduled ops (cycle?)"
        self.makespan = max(o.end for o in ops)

    def emit(self, block):
        self.barrier()
        self.schedule()
        per_eng = {e: [] for e in self.ENGS}
        for o in self.ops:
            if o.stream is not None:
                per_eng[self.streams[o.stream].eng_name].append(o)
        for e in self.ENGS:
            per_eng[e].sort(key=lambda o: (o.start, o.prog))
            for o in per_eng[e]:
                st = self.streams[o.stream]
                st.count += 1
                o.sidx = st.count
        cnt = {n: 0 for n in self.streams}
        for o in self.ops:
            if o.stream is None:
                o.bar_counts = dict(cnt)
            else:
                cnt[o.stream] += 1
        final_counts = dict(cnt)
        progs = {e: [] for e in self.ENGS}
        for e in self.ENGS:
            seen = {}
            for o in per_eng[e]:
                need = {}
                for p in o.preds:
                    if p.stream is None:
                        for sn, c in p.bar_counts.items():
                            if c and need.get(sn, 0) < c:
                                need[sn] = c
                    else:
                        if p.stream == "pe" and o.stream == "pe":
                            continue
                        if need.get(p.stream, 0) < p.sidx:
                            need[p.stream] = p.sidx
                waits = []
                for sn, c in need.items():
                    if seen.get(sn, 0) >= c:
                        continue
                    seen[sn] = c
                    waits.append((self.streams[sn].sem, c * self.streams[sn].inc))
                progs[e].append((waits, o.fn, self.streams[o.stream].sem, self.streams[o.stream].inc))
            waits = [(self.streams[sn].sem, c * self.streams[sn].inc) for sn, c in final_counts.items()
                     if c and seen.get(sn, 0) < c]
            progs[e].append((waits, None, None, 0))

        def run(engname):
            def body(eh):
                for waits, fn, sem, inc in progs[engname]:
                    for (ws, wv) in waits:
                        eh.wait_ge(ws, wv)
                    if fn is not None:
                        fn(eh).then_inc(sem, inc)
            return body
        block.tensor(run("tensor"))
        block.vector(run("vector"))
        block.scalar(run("scalar"))
        block.gpsimd(run("gpsimd"))
        block.sync(run("sync"))


def build_program(dbg=False, stop=99):
    nc = bass.Bass("TRN2", target_bir_lowering=False)

    def din(name, shape, dt=F32):
        return nc.dram_tensor(name, list(shape), dt, kind="ExternalInput").ap()

    x_d = din("x", [T_ALL, D])
    pos_d = din("pos", [1, T_QKV])
    win_d = din("w_in", [D, 1280])
    gains_d = din("gains", [1, 4096])
    are_d = din("are", [128, 32])
    aim_d = din("aim", [128, 32])
    ls_d = din("ls", [128, 32])
    bre_d = din("bre", [128, 512])
    bim_d = din("bim", [128, 512])
    cre_d = din("cre", [128, 512])
    cim_d = din("cim", [128, 512])
    dsk_d = din("dsk", [128, 4])
    wglu_d = din("w_glu", [512, 512])
    sink_d = din("sink", [1, 8])
    wout_d = din("w_out", [D, D])
    wup_d = din("w_up", [D, 2 * DFF])
    cw_d = din("cw", [128, 3 * 44])
    cb_d = din("cb", [128, 44])
    wdn_d = din("w_down", [DFF, D])
    ident_d = din("ident", [128, 128])
    rmat_d = din("rmat", [128, 128])
    ifr_d = din("ifr", [128, 1])
    msk_d = din("msk", [128, 256])
    bmask_d = din("bmask", [128, 128])
    sidx_d = din("sidx", [1, 512])
    y_d = nc.dram_tensor("y", [T_OWN, D], F32, kind="ExternalOutput").ap()
    x1_d = nc.dram_tensor("x1_scr", [T_EXT, D], F32, kind="Internal").ap()
    qs_d = nc.dram_tensor("q_scr", [128, 4, T_QKV], BF16, kind="Internal").ap()
    ks_d = nc.dram_tensor("k_scr", [128, T_QKV], BF16, kind="Internal").ap()
    vs_d = nc.dram_tensor("v_scr", [128, 20, 130], BF16, kind="Internal").ap()
    dbg_d = None
    if dbg:
        dbg_d = nc.dram_tensor("dbg", [8, 128, 2560], F32, kind="ExternalOutput").ap()

    import contextlib
    _es = contextlib.ExitStack()
    sems = [_es.enter_context(nc.semaphore(f"sem{i}")) for i in range(4 + Sched.NDS + Sched.NDP)]
    S = Sched(sems)

    _sbn = [0]
    def finish():
        with nc.Block() as block:
            S.emit(block)
        _es.close()
        return nc

    class Arena:
        def __init__(self, lo, hi):
            self.lo, self.hi, self.cur, self.n = lo, hi, lo, 0

        def alloc(self, shape, dt):
            nbytes = int(np.prod(shape[1:])) * (4 if dt in (F32, I32) else 2)
            nbytes = (nbytes + 63) // 64 * 64
            off = self.cur
            assert off + nbytes <= self.hi, (shape, off, nbytes, self.hi)
            self.cur += nbytes
            _sbn[0] += 1
            return nc.alloc_sbuf_tensor_at(f"sb{_sbn[0]}", list(shape), dt, offset=off)

        def mark(self):
            return self.cur

        def reset(self, m):
            self.cur = m

    def region(lo_kb, hi_kb):
        return Arena(SB_BASE + int(lo_kb * 1024), min(SB_END, SB_BASE + int(hi_kb * 1024)))

    AR = region(0, 21.5)
    AR_YS = region(21.5, 55.5)
    _psn = [0]

    def psum(dt=F32):
        _psn[0] += 1
        return nc.alloc_psum_tensor(f"ps{_psn[0]}", [128, 512 if dt == F32 else 1024], dt)

    PSF = [(psum(F32), Buf(f"psf{i}", excl=True)) for i in range(6)]
    PSB = [(psum(BF16), Buf(f"psb{i}", excl=True)) for i in range(2)]
    _rr = {"f": 0, "b": 0}

    def next_psf():
        _rr["f"] = (_rr["f"] + 1) % len(PSF)
        return PSF[_rr["f"]]

    def next_psb():
        _rr["b"] = (_rr["b"] + 1) % len(PSB)
        return PSB[_rr["b"]]

    def fsz(ap):
        n = 1
        for d_ in ap.shape[1:]:
            n *= int(d_)
        return n

    def dma(q, out, in_, reads=(), writes=()):
        nbytes = fsz(out) * int(out.shape[0]) * 4
        S.op(q, lambda e: e.dma_start(out=out, in_=in_), reads, writes, cost=C_DMA_F + nbytes / C_DMA_B)

    def tcopy(st, out, in_, reads=(), writes=()):
        S.op(st, lambda e: e.tensor_copy(out=out, in_=in_), reads, writes,
             cost=(C_DVE_F + fsz(out) / C_DVE_E) * (4 if st == "pool" else 1))

    def acopy(out, in_, reads=(), writes=(), func=AF.Copy, scale=1.0, bias=None, accum=None):
        def f(e):
            kw = {}
            if bias is not None:
                kw["bias"] = bias
            if accum is not None:
                kw["accum_out"] = accum
            return e.activation(out=out, in_=in_, func=func, scale=scale, **kw)
        S.op("act", f, reads, writes, cost=C_ACT_F + fsz(out) / C_ACT_E)

    def tt(st, out, in0, in1, op, reads=(), writes=()):
        S.op(st, lambda e: e.tensor_tensor(out=out, in0=in0, in1=in1, op=op), reads, writes,
             cost=(C_DVE_F + fsz(out) / C_DVE_E) * (4 if st == "pool" else 1))

    def ts(st, out, in0, s1, s2, op0, op1=None, reads=(), writes=()):
        c_ = (C_DVE_F + fsz(out) / C_DVE_E) * (4 if st == "pool" else 1)
        if op1 is None:
            S.op(st, lambda e: e.tensor_scalar(out=out, in0=in0, scalar1=s1, scalar2=None, op0=op0), reads, writes, cost=c_)
        else:
            S.op(st, lambda e: e.tensor_scalar(out=out, in0=in0, scalar1=s1, scalar2=s2, op0=op0, op1=op1), reads, writes,
                 cost=c_)

    def stt(out, in0, scalar, in1, op0, op1, reads=(), writes=()):
        S.op("dve", lambda e: e.scalar_tensor_tensor(out=out, in0=in0, scalar=scalar, in1=in1, op0=op0, op1=op1),
             reads, writes, cost=C_DVE_F + fsz(out) / (C_DVE_E * 0.83))

    def mm(out, pairs, reads=(), writes=()):
        def f(e):
            ins = None
            n = len(pairs)
            for i, (l, r) in enumerate(pairs):
                ins = e.matmul(out, l, r, start=(i == 0), stop=(i == n - 1))
            return ins
        S.op("pe", f, reads, writes, cost=C_PE_F + sum(max(64, fsz(r)) / C_PE_E + 0.01 for (_, r) in pairs))

    def transposes(outs_ins, ident, reads=(), writes=()):
        def f(e):
            ins = None
            for (o, i_) in outs_ins:
                ins = e.transpose(o, i_, ident)
            return ins
        S.op("pe", f, reads, writes, cost=0.3 + 0.1 * len(outs_ins))

    gains = AR.alloc([128, 4096], F32); b_gains = Buf("gains")
    ident_f = AR.alloc([128, 128], F32)
    ident = AR.alloc([128, 128], BF16); b_ident = Buf("ident")
    rmat_f = AR.alloc([128, 128], F32)
    rmat = AR.alloc([128, 128], BF16); b_rmat = Buf("rmat")
    msk_f = AR.alloc([128, 256], F32)
    msk = AR.alloc([128, 256], BF16); b_msk = Buf("msk")
    ifr = AR.alloc([128, 1], F32); b_ifr = Buf("ifr")
    dsk = AR.alloc([128, 4], F32); b_dsk = Buf("dsk")
    esink = AR.alloc([128, 8], F32); b_esink = Buf("esink")
    cw = AR.alloc([128, 132], F32); b_cw = Buf("cw")
    cb = AR.alloc([128, 44], F32); b_cb = Buf("cb")
    epsc = AR.alloc([128, 1], F32); b_epsc = Buf("epsc")
    onec2 = AR.alloc([128, 1], F32); b_onec2 = Buf("onec2")
    stat = AR.alloc([128, 64], F32)
    b_stat = [Buf(f"stat{i}") for i in range(16)]
    _st = [0]

    def next_stat():
        _st[0] = (_st[0] + 1) % 16
        return stat[:, 4 * _st[0]:4 * _st[0] + 4], b_stat[_st[0]]

    b_tmp = Buf("ldtmp")
    S.op("dve", lambda e: e.memset(epsc[:], EPS), (), [b_epsc])
    S.op("dve", lambda e: e.memset(onec2[:], 1.0), (), [b_onec2])
    dma("dsync", gains[:], gains_d.partition_broadcast(128), writes=[b_gains])
    dma("dsync", ident_f[:], ident_d[:, :], writes=[b_tmp])
    dma("dsync", rmat_f[:], rmat_d[:, :], writes=[b_tmp])
    dma("dsync", msk_f[:], msk_d[:, :], writes=[b_tmp])
    dma("dsync", ifr[:], ifr_d[:, :], writes=[b_ifr])
    dma("dsync", dsk[:], dsk_d[:, :], writes=[b_dsk])
    dma("dsync", esink[:], sink_d.partition_broadcast(128), writes=[b_esink])
    dma("dsync", cw[:], cw_d[:, :], writes=[b_cw])
    dma("dsync", cb[:], cb_d[:, :], writes=[b_cb])
    tcopy("dve", ident[:], ident_f[:], [b_tmp], [b_ident])
    tcopy("dve", rmat[:], rmat_f[:], [b_tmp], [b_rmat])
    tcopy("dve", msk[:], msk_f[:], [b_tmp], [b_msk])
    acopy(esink[:], esink[:], [b_esink], [b_esink], func=AF.Exp)
    G_MIX, G_FFN, G_FIN, G_ATT, G_SSM = 0, 1024, 2048, 3072, 3584

    ysJ = AR_YS.alloc([128, 4, 8, T_EXT // 8], F32); b_ys = [Buf(f"ys{c}") for c in range(4)]

    def norm_transpose_group(tiles, exp_set=False):
        for (src, go, xt, b_xt, hb, b_hb, hT, b_hT, col0) in tiles:
            if src is not None:
                dma("dsync", xt[:], src, writes=[b_xt])
        scs = [next_stat() for _ in tiles]
        if exp_set:
            for (src, go, xt, b_xt, hb, b_hb, hT, b_hT, col0), (sc, b_sc) in zip(tiles, scs):
                S.op("dve", ssq_dve(hb[:], xt[:], sc), [b_xt], [b_hb, b_sc])
                rstd_exp(sc, b_sc, D)
        else:
            for (src, go, xt, b_xt, hb, b_hb, hT, b_hT, col0), (sc, b_sc) in zip(tiles, scs):
                acopy(hb[:], xt[:], [b_xt], [b_hb, b_sc], func=AF.Square, accum=sc[:, 0:1])
            for (sc, b_sc) in scs:
                ts("dve", sc[:, 1:2], sc[:, 0:1], 1.0 / D, EPS, ALU.mult, ALU.add, [b_sc], [b_sc])
            for (sc, b_sc) in scs:
                acopy(sc[:, 2:3], sc[:, 1:2], [b_sc], [b_sc], func=AF.Sqrt)
            for (sc, b_sc) in scs:
                S.op("dve", lambda e, sc=sc: e.reciprocal(out=sc[:, 3:4], in_=sc[:, 2:3]), [b_sc], [b_sc])
        for (src, go, xt, b_xt, hb, b_hb, hT, b_hT, col0), (sc, b_sc) in zip(tiles, scs):
            stt(hb[:], xt[:], sc[:, 3:4], gains[:, go:go + D], ALU.mult, ALU.mult, [b_xt, b_sc, b_gains], [b_hb])
        for (src, go, xt, b_xt, hb, b_hb, hT, b_hT, col0) in tiles:
            pb, b_pb = next_psb()
            transposes([(pb[:, k * 128:(k + 1) * 128], hb[:, k * 128:(k + 1) * 128]) for k in range(8)],
                       ident[:], [b_hb, b_ident], [b_pb])
            if not exp_set and (col0 // 128) % 2 == 1:
                tcopy("dve", hT[:, :, col0:col0 + 128], pb[:].rearrange("p (k t) -> p k t", k=8), [b_pb], [b_hT])
            else:
                acopy(hT[:, :, col0:col0 + 128], pb[:].rearrange("p (k t) -> p k t", k=8), [b_pb], [b_hT])

    def norm_transpose(src_ap, src_reads, gain_off, xt, b_xt, hb, b_hb, hT, b_hT, col0, from_dram=True, stop=99):
        norm_transpose_group([(src_ap if from_dram else None, gain_off, xt, b_xt, hb, b_hb, hT, b_hT, col0)],
                             exp_set=not from_dram)

    AR_P = region(196.25, 207.8)
    prm = AR_P.alloc([128, 21, 32], F32); b_prm = Buf("prm")
    (P_ARE, P_AIM, P_DT, P_ER, P_TH, P_C, P_S, P_LR, P_LI, P_T0, P_T1, P_T2, P_CR, P_CI, P_R8, P_T3,
     P_T4, P_T5, P_T6, P_T7, P_F8) = range(21)
    pri = AR_P.alloc([128, 32], I32)
    PW = AR_P.alloc([128, 9, 2, 32], F32); b_PW = Buf("PW")
    craw = AR_P.alloc([128, 2, 512], F32); b_craw = Buf("craw")
    sidx = AR_P.alloc([128, 512], F32); b_sidx = Buf("sidx")
    onec = AR_P.alloc([128, 1], F32); b_onec = Buf("onec")

    def P(i):
        return prm[:, i, :]

    dma("dsync", P(P_ARE), are_d[:, :], writes=[b_prm])
    dma("dsync", P(P_AIM), aim_d[:, :], writes=[b_prm])
    dma("dsync", P(P_DT), ls_d[:, :], writes=[b_prm])
    dma("dsync", craw[:, 0, :], cre_d[:, :], writes=[b_craw])
    dma("dsync", craw[:, 1, :], cim_d[:, :], writes=[b_craw])
    dma("dsync", sidx[:], sidx_d.partition_broadcast(128), writes=[b_sidx])
    S.op("pool", lambda e: e.memset(onec[:], 1.0), (), [b_onec])
    R, W_ = [b_prm], [b_prm]
    acopy(P(P_DT), P(P_DT), R, W_, func=AF.Exp)
    tt("dve", P(P_T0), P(P_ARE), P(P_DT), ALU.mult, R, W_)
    acopy(P(P_ER), P(P_T0), R, W_, func=AF.Exp)
    tt("dve", P(P_TH), P(P_AIM), P(P_DT), ALU.mult, R, W_)
    ts("dve", P(P_T0), P(P_TH), 1.0 / TWO_PI, None, ALU.mult, reads=R, writes=W_)
    tcopy("dve", pri[:], P(P_T0), R, W_)
    tcopy("dve", P(P_T0), pri[:], R, W_)
    stt(P(P_T1), P(P_T0), -TWO_PI, P(P_TH), ALU.mult, ALU.add, R, W_)
    ts("dve", P(P_T1), P(P_T1), float(np.pi), float(-np.pi), ALU.min, ALU.max, R, W_)
    acopy(P(P_S), P(P_T1), R, W_, func=AF.Sin)
    acopy(P(P_T2), P(P_T1), R, W_, func=AF.Sin, scale=0.5)
    tt("dve", P(P_T2), P(P_T2), P(P_T2), ALU.mult, R, W_)
    ts("dve", P(P_C), P(P_T2), -2.0, 1.0, ALU.mult, ALU.add, R, W_)
    tt("dve", P(P_LR), P(P_ER), P(P_C), ALU.mult, R, W_)
    tt("dve", P(P_LI), P(P_ER), P(P_S), ALU.mult, R, W_)
    tt("dve", P(P_T0), P(P_ER), P(P_ER), ALU.mult, R, W_)
    tt("dve", P(P_T0), P(P_T0), P(P_T0), ALU.mult, R, W_)
    tt("dve", P(P_R8), P(P_T0), P(P_T0), ALU.mult, R, W_)
    ts("dve", P(P_T0), P(P_LR), -1.0, None, ALU.add, reads=R, writes=W_)
    tt("dve", P(P_T1), P(P_ARE), P(P_ARE), ALU.mult, R, W_)
    tt("dve", P(P_T2), P(P_AIM), P(P_AIM), ALU.mult, R, W_)
    tt("dve", P(P_T1), P(P_T1), P(P_T2), ALU.add, R, W_)
    S.op("dve", lambda e: e.reciprocal(out=P(P_T3), in_=P(P_T1)), R, W_)
    tt("dve", P(P_T1), P(P_T0), P(P_ARE), ALU.mult, R, W_)
    tt("dve", P(P_T2), P(P_LI), P(P_AIM), ALU.mult, R, W_)
    tt("dve", P(P_T1), P(P_T1), P(P_T2), ALU.add, R, W_)
    tt("dve", P(P_CR), P(P_T1), P(P_T3), ALU.mult, R, W_)
    tt("dve", P(P_T1), P(P_LI), P(P_ARE), ALU.mult, R, W_)
    tt("dve", P(P_T2), P(P_T0), P(P_AIM), ALU.mult, R, W_)
    tt("dve", P(P_T1), P(P_T1), P(P_T2), ALU.subtract, R, W_)
    tt("dve", P(P_CI), P(P_T1), P(P_T3), ALU.mult, R, W_)
    ts("dve", P(P_T4), P(P_TH), 8.0 / TWO_PI, None, ALU.mult, reads=R, writes=W_)
    tcopy("dve", pri[:], P(P_T4), R, W_)
    tcopy("dve", P(P_T5), pri[:], R, W_)
    tt("dve", P(P_F8), P(P_T4), P(P_T5), ALU.subtract, R, W_)
    RP = [b_prm, b_PW]
    S.op("dve", lambda e: e.memset(PW[:, 0, 0, :], 1.0), (), [b_PW])
    S.op("dve", lambda e: e.memset(PW[:, 0, 1, :], 0.0), (), [b_PW])
    tcopy("dve", PW[:, 1, 0, :], P(P_LR), RP, [b_PW])
    tcopy("dve", PW[:, 1, 1, :], P(P_LI), RP, [b_PW])
    for k in range(2, 9):
        ar_, ai_ = PW[:, k - 1, 0, :], PW[:, k - 1, 1, :]
        tt("dve", P(P_T4), ar_, P(P_LR), ALU.mult, RP, W_)
        tt("dve", P(P_T5), ai_, P(P_LI), ALU.mult, RP, W_)
        tt("dve", PW[:, k, 0, :], P(P_T4), P(P_T5), ALU.subtract, RP, [b_PW])
        tt("dve", P(P_T6), ar_, P(P_LI), ALU.mult, RP, W_)
        tt("dve", P(P_T7), ai_, P(P_LR), ALU.mult, RP, W_)
        tt("dve", PW[:, k, 1, :], P(P_T6), P(P_T7), ALU.add, RP, [b_PW])


    if stop <= 0:
        return finish()
    AR_U = region(55.5, 87.5)
    uJ = AR_U.alloc([128, 4, 8, T_ALL // 8], BF16); b_uT = [Buf(f"uT{c}") for c in range(4)]
    AR = region(87.5, 196.25)
    w_in_bf = AR.alloc([128, 8, 1280], BF16); b_win = Buf("w_in")
    wst = [AR.alloc([128, 1280], F32) for _ in range(2)]; b_wst = [Buf("wst0"), Buf("wst1")]
    xts = [AR.alloc([128, D], F32) for _ in range(4)]; b_xts = [Buf(f"xt{i}") for i in range(4)]
    hbs = [AR.alloc([128, D], BF16) for _ in range(4)]; b_hbs = [Buf(f"hb{i}") for i in range(4)]
    hTs = [AR.alloc([128, 8, 512], BF16) for _ in range(2)]; b_hTs = [Buf("hT0"), Buf("hT1")]
    T_RP = 2304
    AR_T = region(21.5, 55.5)
    cosT = AR_T.alloc([128, T_RP], F32); sinT = AR_T.alloc([128, T_RP], F32); b_cs = Buf("cossin")
    angb = AR.alloc([128, 512], F32); angi = AR.alloc([128, 512], I32); b_ang = Buf("ang")
    qb = [AR.alloc([128, 512], BF16) for _ in range(2)]; b_qb = [Buf("qb0"), Buf("qb1")]
    rt2 = [[AR.alloc([128, 512], F32) for _ in range(2)] for _ in range(2)]
    b_rt2 = [[Buf(f"rt{a_}{b_}") for b_ in range(2)] for a_ in range(2)]
    qst = [AR.alloc([128, 4, 512], BF16) for _ in range(2)]; b_qst = [Buf("qst0"), Buf("qst1")]
    kst = [AR.alloc([128, 512], BF16) for _ in range(2)]; b_kst = [Buf("kst0"), Buf("kst1")]
    vst = [AR.alloc([128, 4, 130], BF16) for _ in range(2)]; b_vst = [Buf("vst0"), Buf("vst1")]
    b_qsd = Buf("q_scr"); b_ksd = Buf("k_scr"); b_vsd = Buf("v_scr")

    def load_w_in():
        for kc in range(8):
            dma("dsync", wst[kc % 2][:], win_d[kc * 128:(kc + 1) * 128, :], writes=[b_wst[kc % 2]])
            if kc % 2 == 0:
                acopy(w_in_bf[:, kc, :], wst[kc % 2][:], [b_wst[kc % 2]], [b_win])
            else:
                tcopy("dve", w_in_bf[:, kc, :], wst[kc % 2][:], [b_wst[kc % 2]], [b_win])

    load_w_in()
    for blk in range(5):
        c0 = blk * 512
        nb = min(512, T_RP - c0)
        cs_, sn_b = cosT[:, c0:c0 + nb], sinT[:, c0:c0 + nb]
        dma("dsync", angb[:, 0:nb], pos_d[:, c0:c0 + nb].partition_broadcast(128), writes=[b_ang])
        ts("dve", angb[:, 0:nb], angb[:, 0:nb], ifr[:, 0:1], None, ALU.mult, reads=[b_ang, b_ifr], writes=[b_ang])
        ts("dve", sn_b, angb[:, 0:nb], 1.0 / TWO_PI, None, ALU.mult, reads=[b_ang], writes=[b_cs])
        tcopy("dve", angi[:, 0:nb], sn_b, [b_cs], [b_ang])
        tcopy("dve", sn_b, angi[:, 0:nb], [b_ang], [b_cs])
        stt(angb[:, 0:nb], sn_b, -TWO_PI, angb[:, 0:nb], ALU.mult, ALU.add, [b_ang, b_cs], [b_ang])
        ts("dve", angb[:, 0:nb], angb[:, 0:nb], float(np.pi), float(-np.pi), ALU.min, ALU.max, [b_ang], [b_ang])
        acopy(sn_b, angb[:, 0:nb], [b_ang], [b_cs], func=AF.Sin)
        acopy(cs_, angb[:, 0:nb], [b_ang], [b_cs], func=AF.Sin, scale=0.5)
        tt("dve", cs_, cs_, cs_, ALU.mult, [b_cs], [b_cs])
        ts("dve", cs_, cs_, -2.0, 1.0, ALU.mult, ALU.add, [b_cs], [b_cs])
    for i_ in range(2):
        S.op("pool", lambda e, i_=i_: e.memset(vst[i_][:], 1.0), (), [b_vst[i_]])
    if stop <= 0.2:
        return finish()
    for g4 in range(8):
        hT, b_hT = hTs[g4 % 2], b_hTs[g4 % 2]
        norm_transpose_group([(x_d[(g4 * 4 + j) * 128:(g4 * 4 + j + 1) * 128, :], G_MIX, xts[j], b_xts[j], hbs[j], b_hbs[j],
                               hT, b_hT, j * 128) for j in range(4)])
        if stop <= 0.7:
            return finish()
        for c in range(4):
            pf, b_pf = next_psf()
            mm(pf[:], [(w_in_bf[:, kc, 768 + c * 128:768 + (c + 1) * 128], hT[:, kc, :]) for kc in range(8)],
               [b_win, b_hT], [b_pf])
            if stop <= 0.75:
                return finish()
            acopy(uJ[:, c, :, g4 * 64:(g4 + 1) * 64], pf[:].rearrange("p (n j) -> p j n", j=8), [b_pf], [b_uT[c]])
            if stop <= 0.8:
                return finish()
            if stop <= 0.85:
                return finish()
        if g4 < NQKV_G:
            nt = 4 if g4 < 4 else 2
            nn = nt * 128
            cols = slice(g4 * 512, g4 * 512 + nn)
            sp = g4 % 2
            for c in range(5):
                pf, b_pf = next_psf()
                mm(pf[:, 0:nn], [(w_in_bf[:, kc, c * 128:(c + 1) * 128], hT[:, kc, 0:nn]) for kc in range(8)],
                   [b_win, b_hT], [b_pf])
                i2 = c % 2
                acopy(qb[i2][:, 0:nn], pf[:, 0:nn], [b_pf], [b_qb[i2]])
                pr_, b_pr_ = next_psf()
                mm(pr_[:, 0:nn], [(rmat[:], qb[i2][:, 0:nn])], [b_rmat, b_qb[i2]], [b_pr_])
                ra, rb_, b_ra, b_rb = rt2[i2][0], rt2[i2][1], b_rt2[i2][0], b_rt2[i2][1]
                tt("dve", ra[:, 0:nn], pf[:, 0:nn], cosT[:, cols], ALU.mult, [b_pf, b_cs], [b_ra])
                tt("dve", rb_[:, 0:nn], pr_[:, 0:nn], sinT[:, cols], ALU.mult, [b_pr_, b_cs], [b_rb])
                if c < 4:
                    tt("dve", qst[sp][:, c, 0:nn], ra[:, 0:nn], rb_[:, 0:nn], ALU.add, [b_ra, b_rb], [b_qst[sp]])
                else:
                    tt("dve", kst[sp][:, 0:nn], ra[:, 0:nn], rb_[:, 0:nn], ALU.add, [b_ra, b_rb], [b_kst[sp]])
            for j in range(nt):
                pf, b_pf = next_psf()
                mm(pf[:, 0:128], [(hT[:, kc, j * 128:(j + 1) * 128], w_in_bf[:, kc, 640:768]) for kc in range(8)],
                   [b_win, b_hT], [b_pf])
                acopy(vst[sp][:, j, :].rearrange("p (g d) -> p g d", g=2)[:, :, 0:64],
                      pf[:, 0:128].rearrange("p (g d) -> p g d", g=2), [b_pf], [b_vst[sp]])
            dma("dpool", qs_d[:, :, cols], qst[sp][:, :, 0:nn], reads=[b_qst[sp]], writes=[b_qsd])
            dma("dpool", ks_d[:, cols], kst[sp][:, 0:nn], reads=[b_kst[sp]], writes=[b_ksd])
            dma("dpool", vs_d[:, g4 * 4:g4 * 4 + nt, :], vst[sp][:, 0:nt, :], reads=[b_vst[sp]], writes=[b_vsd])
        if stop <= 0.9:
            return finish()

    if stop <= 1:
        return finish()
    S.barrier()
    AR = region(87.5, 196.25)
    NO, NA = T_EXT // 8, T_ALL // 8
    Tz = AR.alloc([128, 4, 15, 128], BF16); b_Tz = Buf("Tz")
    BbTc = AR.alloc([128, 8, 2, 8, 128], BF16); b_BbTc = Buf("BbTc")
    CbTc = AR.alloc([128, 8, 2, 16, 32], BF16); b_CbTc = Buf("CbTc")
    mB = AR.mark()
    bbar = AR.alloc([128, 2, 512], F32); b_bbar = Buf("bbar")
    braw = AR.alloc([128, 2, 512], F32); b_braw = Buf("braw")
    tmpbs = [AR.alloc([128, 2, 512], F32) for _ in range(2)]; b_tmpbs = [Buf("tmpb0"), Buf("tmpb1")]
    bpows = [AR.alloc([128, 2, 512], F32) for _ in range(2)]; b_bpows = [Buf("bpow0"), Buf("bpow1")]
    Mps = [AR.alloc([128, 2, 8, 128], BF16) for _ in range(2)]; b_Mps = [Buf("Mp0"), Buf("Mp1")]
    tmpb, b_tmpb = tmpbs[0], b_tmpbs[0]
    Cp = AR.alloc([128, 2, 8, 128], BF16); b_Cp = Buf("Cp")
    bmask = AR.alloc([128, 128], F32); b_bmask = Buf("bmask")
    diagD = AR.alloc([128, 4, 128], F32); b_diagD = Buf("diagD")
    tzt = AR.alloc([128, 128], F32); b_tzt = Buf("tzt")

    dma("dsync", braw[:, 0, :], bre_d[:, :], writes=[b_braw])
    dma("dsync", braw[:, 1, :], bim_d[:, :], writes=[b_braw])
    dma("dsync", bmask[:], bmask_d[:, :], writes=[b_bmask])
    def v3(ap2):
        return ap2.rearrange("p (a c) -> p a c", c=16)

    def bc32(ap32):
        return ap32.unsqueeze(2).to_broadcast([128, 32, 16])

    def cmul(dst, b_dst, src, b_src, cre, cim, rd, tmpb=tmpb, b_tmpb=b_tmpb):
        RB = [b_src, b_tmpb] + rd
        tt("dve", v3(tmpb[:, 0, :]), v3(src[:, 0, :]), bc32(cre), ALU.mult, RB, [b_tmpb])
        tt("dve", v3(tmpb[:, 1, :]), v3(src[:, 1, :]), bc32(cim), ALU.mult, RB, [b_tmpb])
        tt("dve", dst[:, 0, :], tmpb[:, 0, :], tmpb[:, 1, :], ALU.subtract, [b_tmpb], [b_dst])
        tt("dve", v3(tmpb[:, 0, :]), v3(src[:, 1, :]), bc32(cre), ALU.mult, RB, [b_tmpb])
        tt("dve", v3(tmpb[:, 1, :]), v3(src[:, 0, :]), bc32(cim), ALU.mult, RB, [b_tmpb])
        tt("dve", dst[:, 1, :], tmpb[:, 0, :], tmpb[:, 1, :], ALU.add, [b_tmpb], [b_dst])

    cmul(bbar, b_bbar, braw, b_braw, P(P_CR), P(P_CI), [b_prm])

    def pack(dst, b_dst, src, b_src, neg_im):
        for ri in range(2):
            for e_ in range(2):
                ps_ = slice(e_ * 64, (e_ + 1) * 64)
                s_ = src[ps_, ri, :].rearrange("p (k r c) -> p k r c", r=4, c=16)
                d_ap = dst[ps_, ri, :, :].rearrange("p k (r x) -> p k r x", x=32)[:, :, :, 16 * e_:16 * e_ + 16]
                if neg_im and ri == 1:
                    acopy(d_ap, s_, [b_src], [b_dst], scale=-1.0)
                else:
                    acopy(d_ap, s_, [b_src], [b_dst])

    for Mp, b_Mp in zip(Mps, b_Mps):
        S.op("pool", lambda e, Mp=Mp: e.memset(Mp[:], 0.0), (), [b_Mp])
    S.op("pool", lambda e: e.memset(Cp[:], 0.0), (), [b_Cp])
    pack(Cp, b_Cp, craw, b_craw, True)
    for c in range(4):
        ts("dve", diagD[:, c, :], ident_f[:], dsk[:, c:c + 1], None, ALU.mult, reads=[b_tmp, b_dsk], writes=[b_diagD])
    for k in range(8):
        bpow, b_bpow, Mp, b_Mp = bpows[k % 2], b_bpows[k % 2], Mps[k % 2], b_Mps[k % 2]
        cmul(bpow, b_bpow, bbar, b_bbar, PW[:, k, 0, :], PW[:, k, 1, :], [b_PW], tmpb=tmpbs[k % 2], b_tmpb=b_tmpbs[k % 2])
        pack(Mp, b_Mp, bpow, b_bpow, False)
        for c in range(4):
            if k == 0:
                pf, b_pf = next_psf()
                mm(pf[:, 0:128], [(Mp[:, ri, 4 * d_ + c, :], Cp[:, ri, 4 * d_ + c, :]) for d_ in range(2) for ri in range(2)],
                   [b_Mp, b_Cp], [b_pf])
                tt("dve", tzt[:], pf[:, 0:128], bmask[:], ALU.mult, [b_pf, b_bmask], [b_tzt])
                tt("dve", Tz[:, c, 7, :], tzt[:], diagD[:, c, :], ALU.add, [b_tzt, b_diagD], [b_Tz])
            else:
                for d_ in range(2):
                    pf, b_pf = next_psf()
                    mm(pf[:, 0:128], [(Mp[:, ri, 4 * d_ + c, :], Cp[:, ri, 4 * d_ + c, :]) for ri in range(2)],
                       [b_Mp, b_Cp], [b_pf])
                    idx = 7 + k if d_ == 0 else 7 - k
                    tt("dve", Tz[:, c, idx, :], pf[:, 0:128], bmask[:], ALU.mult, [b_pf, b_bmask], [b_Tz])
        for d_ in range(2):
            j = 7 - k if d_ == 0 else k
            pb, b_pb = next_psb()
            transposes([(pb[:, (ri * 4 + kk) * 128:(ri * 4 + kk + 1) * 128], Mp[:, ri, 4 * d_ + kk, :])
                        for ri in range(2) for kk in range(4)], ident[:], [b_Mp, b_ident], [b_pb])
            acopy(BbTc[:, j, :, 4 * d_:4 * d_ + 4, :], pb[:].rearrange("p (r k x) -> p r k x", r=2, k=4), [b_pb], [b_BbTc])

    if stop <= 2:
        return finish()
    S.barrier()
    AR.reset(mB)
    Xd = AR.alloc([128, 2, 16, NO], BF16); b_Xdc = [Buf(f"Xd{c}") for c in range(4)]
    Cts = [AR.alloc([128, NA], F32) for _ in range(2)]; Sns = [AR.alloc([128, NA], F32) for _ in range(2)]
    b_tabs = [Buf("tab0"), Buf("tab1")]
    bufA = AR.alloc([128, NA], F32); bufI = AR.alloc([128, NA], I32); b_bufA = Buf("bufA"); b_bufI = Buf("bufI")

    def gen_table(cq, Nd, par):
        Ct_, Sn_, b_t = Cts[par], Sns[par], b_tabs[par]
        acopy(bufA[:, 0:Nd], sidx[:, 0:Nd], [b_sidx, b_prm], [b_bufA], func=AF.Copy, scale=prm[:, P_F8, cq:cq + 1])
        acopy(bufI[:, 0:Nd], bufA[:, 0:Nd], [b_bufA], [b_bufI])
        tt("dve", bufA[:, 0:Nd], bufA[:, 0:Nd], bufI[:, 0:Nd], ALU.subtract, [b_bufA, b_bufI], [b_bufA])
        acopy(Sn_[:, 0:Nd], bufA[:, 0:Nd], [b_bufA], [b_t], func=AF.Sin, scale=6.283185)
        acopy(Ct_[:, 0:Nd], bufA[:, 0:Nd], [b_bufA], [b_t], func=AF.Sin, scale=3.141592)
        acopy(Ct_[:, 0:Nd], Ct_[:, 0:Nd], [b_t], [b_t], func=AF.Square)
        acopy(Ct_[:, 0:Nd], Ct_[:, 0:Nd], [b_t, b_onec], [b_t], func=AF.Identity, scale=-2.0, bias=onec[:, 0:1])
    Wr = AR.alloc([128, NA], BF16); Wi = AR.alloc([128, NA], BF16); b_W = Buf("W")
    Zr = AR.alloc([128, NA], BF16); Zi = AR.alloc([128, NA], BF16); b_Z = Buf("Z")
    mt = [AR.alloc([128, 512], F32) for _ in range(4)]; b_mt = [Buf(f"mt{i}") for i in range(4)]
    cl = AR.alloc([128, 2, 256], F32); b_cl = Buf("cl")
    clt = AR.alloc([128, 2, 256], F32); b_clt = Buf("clt")

    def v3h(ap2):
        return ap2.rearrange("p (a c) -> p a c", c=16)

    b_ysi = [[Buf(f"ysJ{c}_{i}") for i in range(8)] for c in range(4)]
    for c in range(4):
        for i in range(8):
            py, b_py = next_psf()
            mm(py[:, 0:NO], [(Tz[:, c, i - j + 7, :], uJ[:, c, j, 0:NO]) for j in range(8)], [b_Tz, b_uT[c]], [b_py])
            acopy(ysJ[:, c, i, :], py[:, 0:NO], [b_py], [b_ysi[c][i]])
    for d_ in range(2):
        Nd = NO if d_ == 0 else NA
        S.op("pool", lambda e: e.memset(CbTc[:], 0.0), (), [b_CbTc])
        for i in range(8):
            pw = i + 1 if d_ == 0 else 8 - i
            lr = PW[:, pw, 0, 16 * d_:16 * d_ + 16].unsqueeze(2).to_broadcast([128, 16, 16])
            li = PW[:, pw, 1, 16 * d_:16 * d_ + 16].unsqueeze(2).to_broadcast([128, 16, 16])
            c_re = v3h(craw[:, 0, 256 * d_:256 * d_ + 256]); c_im = v3h(craw[:, 1, 256 * d_:256 * d_ + 256])
            RC = [b_craw, b_PW, b_clt]
            tt("dve", v3h(clt[:, 0, :]), c_re, lr, ALU.mult, RC, [b_clt])
            tt("dve", v3h(clt[:, 1, :]), c_im, li, ALU.mult, RC, [b_clt])
            tt("dve", cl[:, 0, :], clt[:, 0, :], clt[:, 1, :], ALU.subtract, [b_clt], [b_cl])
            tt("dve", v3h(clt[:, 0, :]), c_re, li, ALU.mult, RC, [b_clt])
            tt("dve", v3h(clt[:, 1, :]), c_im, lr, ALU.mult, RC, [b_clt])
            stt(cl[:, 1, :], clt[:, 0, :], -1.0, clt[:, 1, :], ALU.mult, ALU.subtract, [b_clt], [b_cl])
            for ri in range(2):
                for e_ in range(2):
                    ps_ = slice(e_ * 64, (e_ + 1) * 64)
                    acopy(CbTc[ps_, i, ri, :, 16 * e_:16 * e_ + 16], v3h(cl[ps_, ri, :]), [b_cl], [b_CbTc])
        if d_ == 0:
            S.op("pool", lambda e: e.memset(Xd[:, :, :, 0:1], 0.0), (), list(b_Xdc))
        for q in range(16):
            cq = d_ * 16 + q
            ch, r4 = q // 4, q % 4
            rows = slice(32 * r4, 32 * r4 + 32)
            par = q % 2
            if q == 0:
                gen_table(cq, Nd, par)
            Ct, Sn, b_tab = Cts[par], Sns[par], b_tabs[par]
            pr, b_pr = next_psf()
            pi_, b_pi = next_psf()
            for ri, (pv, b_pv) in enumerate(((pr, b_pr), (pi_, b_pi))):
                def f(e, ri=ri, pv=pv, rows=rows, ch=ch, d_=d_, Nd=Nd, r4=r4):
                    ins = None
                    for j in range(8):
                        ins = e.matmul(pv[:, 0:Nd], BbTc[rows, j, ri, 4 * d_ + ch, :], uJ[rows, ch, j, 0:Nd],
                                       start=(j == 0), stop=(j == 7), tile_position=(32 * r4, 0))
                    return ins
                S.op("pe", f, [b_BbTc, b_uT[ch]], [b_pv], cost=0.3 + 8 * Nd / 2400.0)
            if d_ == 0:
                cs, sn_, wr, wi = Ct[:, 0:Nd], Sn[:, 0:Nd], Wr[:, 0:Nd], Wi[:, 0:Nd]
            else:
                cs, sn_ = Ct[:, 0:Nd][:, ::-1], Sn[:, 0:Nd][:, ::-1]
                wr, wi = Wr[:, 0:Nd][:, ::-1], Wi[:, 0:Nd][:, ::-1]
            tt("dve", mt[0][:, 0:Nd], pr[:, 0:Nd], cs, ALU.mult, [b_pr, b_tab], [b_mt[0]])
            tt("dve", mt[1][:, 0:Nd], pi_[:, 0:Nd], sn_, ALU.mult, [b_pi, b_tab], [b_mt[1]])
            tt("dve", mt[2][:, 0:Nd], pi_[:, 0:Nd], cs, ALU.mult, [b_pi, b_tab], [b_mt[2]])
            tt("dve", mt[3][:, 0:Nd], pr[:, 0:Nd], sn_, ALU.mult, [b_pr, b_tab], [b_mt[3]])
            tt("dve", wr, mt[0][:, 0:Nd], mt[1][:, 0:Nd], ALU.add, [b_mt[0], b_mt[1]], [b_W])
            tt("dve", wi, mt[2][:, 0:Nd], mt[3][:, 0:Nd], ALU.subtract, [b_mt[2], b_mt[3]], [b_W])
            if q + 1 < 16:
                gen_table(cq + 1, Nd, (q + 1) % 2)
            rho = prm[:, P_R8, cq:cq + 1].to_broadcast([128, Nd])
            S.op("dve", lambda e, rho=rho, Nd=Nd: e.tensor_tensor_scan(out=Zr[:, 0:Nd], data0=rho, data1=Wr[:, 0:Nd],
                 initial=0.0, op0=ALU.mult, op1=ALU.add), [b_W, b_prm], [b_Z], cost=0.15 + 2 * Nd / 960.0)
            S.op("dve", lambda e, rho=rho, Nd=Nd: e.tensor_tensor_scan(out=Zi[:, 0:Nd], data0=rho, data1=Wi[:, 0:Nd],
                 initial=0.0, op0=ALU.mult, op1=ALU.add), [b_W, b_prm], [b_Z], cost=0.15 + 2 * Nd / 960.0)
            if d_ == 0:
                n = NO - 1
                zr, zi, cs, sn_ = Zr[:, 0:n], Zi[:, 0:n], Ct[:, 0:n], Sn[:, 0:n]
                xr, xi = Xd[:, 0, q, 1:NO], Xd[:, 1, q, 1:NO]
            else:
                n = NO
                lo = NA - 1 - NO
                zr, zi = Zr[:, lo:lo + n][:, ::-1], Zi[:, lo:lo + n][:, ::-1]
                cs, sn_ = Ct[:, lo:lo + n][:, ::-1], Sn[:, lo:lo + n][:, ::-1]
                xr, xi = Xd[:, 0, q, 0:NO], Xd[:, 1, q, 0:NO]
            RZ = [b_Z, b_tab]
            tt("dve", mt[3][:, 0:n], zr, sn_, ALU.mult, RZ, [b_mt[3]])
            tt("dve", mt[0][:, 0:n], zr, cs, ALU.mult, RZ, [b_mt[0]])
            tt("dve", mt[1][:, 0:n], zi, sn_, ALU.mult, RZ, [b_mt[1]])
            tt("dve", mt[2][:, 0:n], zi, cs, ALU.mult, RZ, [b_mt[2]])
            tt("dve", xr, mt[0][:, 0:n], mt[1][:, 0:n], ALU.subtract, [b_mt[0], b_mt[1]], [b_Xdc[ch]])
            tt("dve", xi, mt[2][:, 0:n], mt[3][:, 0:n], ALU.add, [b_mt[2], b_mt[3]], [b_Xdc[ch]])
        for c in range(4):
            for i in range(8):
                py, b_py = next_psf()

                def f(e, c=c, i=i, py=py):
                    ins = None
                    for r4 in range(4):
                        for ri in range(2):
                            ins = e.matmul(py[32 * r4:32 * r4 + 32, 0:NO], CbTc[:, i, ri, 4 * c + r4, :],
                                           Xd[:, ri, 4 * c + r4, 0:NO], start=(ri == 0), stop=(ri == 1),
                                           tile_position=(0, 32 * r4), skip_group_check=True)
                    return ins
                S.op("pe", f, [b_CbTc, b_Xdc[c]], [b_py], cost=0.3 + 8 * NO / 2400.0)
                tt("dve", ysJ[:, c, i, :], ysJ[:, c, i, :], py[:, 0:NO], ALU.add, [b_py, b_ysi[c][i]], [b_ysi[c][i]])

    if dbg:
        for c in range(4):
            dma("dpool", dbg_d[c, :, 0:T_EXT], ysJ[:, c, :, :], reads=[b_ys[c]])

    if stop <= 3:
        return finish()
    if stop <= 4:
        return finish()
    AR_U = region(55.5, 87.5)
    qT = AR_U.alloc([128, 4, T_QKV], BF16); b_qT = Buf("qT")
    kT = AR_U.alloc([128, T_QKV], BF16); b_kT = Buf("kT")
    vaug = AR_U.alloc([128, 20, 2, 65], BF16); b_v = Buf("vaug")
    dma("dsync", kT[:, 0:T_RP], ks_d[:, 0:T_RP], reads=[b_ksd], writes=[b_kT] + b_uT)
    dma("dsync", vaug[:, 0:18, :, :].rearrange("p t g d -> p t (g d)"), vs_d[:, 0:18, :], reads=[b_vsd], writes=[b_v] + b_uT)
    for c in range(4):
        dma("dsync", qT[:, c, 0:T_EXT], qs_d[:, c, 0:T_EXT], reads=[b_qsd], writes=[b_qT] + b_uT)
    S.barrier()
    AR_H = region(173.5, 207.8)
    h2T = AR_H.alloc([128, 8, T_EXT], BF16); b_h2T = Buf("h2T")
    AR = region(87.5, 173.5)
    wout_bf = AR.alloc([128, 8, D], BF16); b_wout = Buf("wout")
    wglu_bf = AR.alloc([128, 4, 512], BF16); b_wglu = Buf("wglu")
    _w2 = AR.alloc([128, D], F32); _bw2 = Buf("wst2")
    wst2 = [_w2, _w2]; b_wst2 = [_bw2, _bw2]
    for kc in range(8):
        dma("dsync", wst2[kc % 2][:], wout_d[kc * 128:(kc + 1) * 128, :], writes=[b_wst2[kc % 2]])
        acopy(wout_bf[:, kc, :], wst2[kc % 2][:], [b_wst2[kc % 2]], [b_wout])
    for kc in range(4):
        dma("dsync", wst2[kc % 2][:, 0:512], wglu_d[kc * 128:(kc + 1) * 128, :], writes=[b_wst2[kc % 2]])
        acopy(wglu_bf[:, kc, :], wst2[kc % 2][:, 0:512], [b_wst2[kc % 2]], [b_wglu])
    PT = [[AR.alloc([128, 512], BF16) for _ in range(6)] for _ in range(2)]
    b_PT = [[Buf(f"PT{h}{i}") for i in range(6)] for h in range(2)]
    rden = [AR.alloc([128, 8], F32) for _ in range(2)]; b_rden = [Buf("rden0"), Buf("rden1")]
    attn_ = [AR.alloc([128, 512], F32) for _ in range(2)]; b_attn_ = [Buf("attn0"), Buf("attn1")]
    mixed_ = [AR.alloc([128, D], BF16) for _ in range(2)]; b_mixa = [Buf("mixa0"), Buf("mixa1")]; b_mixs = [Buf("mixs0"), Buf("mixs1")]
    mixT_ = [AR.alloc([128, 8, 128], BF16) for _ in range(2)]; b_mixT_ = [Buf("mixT0"), Buf("mixT1")]
    ysg_ = [AR.alloc([128, 4, 128], F32) for _ in range(2)]; b_ysg_ = [Buf("ysg0"), Buf("ysg1")]
    ysgb_ = [AR.alloc([128, 4, 128], BF16) for _ in range(2)]; b_ysgb_ = [Buf("ysgb0"), Buf("ysgb1")]
    sig_ = [AR.alloc([128, 4, 128], F32) for _ in range(2)]; b_sig_ = [Buf("sig0"), Buf("sig1")]
    ys2_ = [AR.alloc([128, 4, 128], BF16) for _ in range(2)]; b_ys2_ = [Buf("ys20"), Buf("ys21")]
    junk_ = [AR.alloc([128, 512], BF16) for _ in range(2)]; b_junk_ = [Buf("junk0"), Buf("junk1")]
    ysT_ = [AR.alloc([128, 512], BF16) for _ in range(2)]; b_ysT_ = [Buf("ysT0"), Buf("ysT1")]
    xc = [AR.alloc([128, D], F32) for _ in range(2)]; b_xc = [Buf("xc0"), Buf("xc1")]
    x1 = [AR.alloc([128, D], F32) for _ in range(2)]; b_x1 = [Buf("x1a"), Buf("x1b")]
    h2b = [AR.alloc([128, D], BF16) for _ in range(2)]; b_h2b = [Buf("h2b0"), Buf("h2b1")]
    b_x1d = [Buf(f"x1d{i}") for i in range(17)]
    po_ = {}

    def rstd_exp(sc, b_sc, n_feat):
        acopy(sc[:, 1:2], sc[:, 0:1], [b_sc, b_epsc], [b_sc], func=AF.Ln, scale=1.0 / n_feat, bias=epsc[:, 0:1])
        acopy(sc[:, 3:4], sc[:, 1:2], [b_sc], [b_sc], func=AF.Exp, scale=-0.5)

    def ssq_dve(junk, src, sc):
        return lambda e: e.scalar_tensor_tensor(out=junk, in0=src, scalar=1.0, in1=src, op0=ALU.mult, op1=ALU.mult,
                                                accum_out=sc[:, 0:1])

    def rms_to(dst, src, b_src_list, gain_off, junk, b_junk, b_dst):
        sc, b_sc = next_stat()
        S.op("dve", ssq_dve(junk[:, 0:512], src, sc), b_src_list, [b_junk, b_sc])
        rstd_exp(sc, b_sc, 512)
        stt(dst, src, sc[:, 3:4], gains[:, gain_off:gain_off + 512], ALU.mult, ALU.mult,
            b_src_list + [b_sc, b_gains], [b_dst])

    def s_scores(n):
        h = n % 2
        cols = slice(n * 128, (n + 1) * 128)
        for g in range(2):
            rows = slice(64 * g, 64 * g + 64)
            for kb in (n - 1, n, n + 1):
                if kb < 0:
                    continue
                slot = g * 3 + (kb - n + 1)
                pf, b_pf = next_psf()
                mm(pf[:].rearrange("p (j t) -> p j t", j=4),
                   [(kT[rows, kb * 128:(kb + 1) * 128], qT[rows, :, cols])], [b_kT, b_qT], [b_pf])
                acopy(PT[h][slot][:], pf[:], [b_pf], [b_PT[h][slot]], func=AF.Exp, scale=0.125)
                if kb != n:
                    m_ = msk[:, 0:128] if kb == n - 1 else msk[:, 128:256]
                    tt("dve", PT[h][slot][:].rearrange("p (j t) -> p j t", j=4),
                       PT[h][slot][:].rearrange("p (j t) -> p j t", j=4),
                       m_.unsqueeze(1).to_broadcast([128, 4, 128]), ALU.mult, [b_PT[h][slot], b_msk], [b_PT[h][slot]])

    def s_pv(n):
        h = n % 2
        for g in range(2):
            pts = [(kb, g * 3 + (kb - n + 1)) for kb in (n - 1, n, n + 1) if kb >= 0]
            pog, b_pog = next_psf()
            for j in range(4):
                mm(pog[:, j * 65:(j + 1) * 65],
                   [(PT[h][slot][:, j * 128:(j + 1) * 128], vaug[:, kb, g, :]) for (kb, slot) in pts],
                   [b_PT[h][s_] for (_, s_) in pts] + [b_v], [b_pog])
            o3 = pog[:, 0:260].rearrange("p (j d) -> p j d", j=4)
            tt("dve", rden[h][:, 4 * g:4 * g + 4], o3[:, :, 64], esink[:, 4 * g:4 * g + 4], ALU.add,
               [b_pog, b_esink], [b_rden[h]])
            S.op("dve", lambda e, g=g, h=h: e.reciprocal(out=rden[h][:, 4 * g:4 * g + 4], in_=rden[h][:, 4 * g:4 * g + 4]),
                 [b_rden[h]], [b_rden[h]])
            tt("dve", attn_[h][:, 256 * g:256 * g + 256].rearrange("p (j d) -> p j d", j=4), o3[:, :, 0:64],
               rden[h][:, 4 * g:4 * g + 4].unsqueeze(2).to_broadcast([128, 4, 64]), ALU.mult,
               [b_pog, b_rden[h]], [b_attn_[h]])
        rms_to(mixed_[h][:, 0:512], attn_[h][:], [b_attn_[h]], G_ATT, junk_[h], b_junk_[h], b_mixa[h])

    def s_ssm(n):
        h = n % 2
        for c in range(4):
            acopy(ysg_[h][:, c, :].rearrange("p (n i) -> p i n", i=8), ysJ[:, c, :, 16 * n:16 * n + 16], [b_ys[c]],
                  [b_ysg_[h]])
        acopy(ysgb_[h][:], ysg_[h][:], [b_ysg_[h]], [b_ysgb_[h]])
        for co in range(4):
            pf, b_pf = next_psf()
            mm(pf[:, 0:128], [(wglu_bf[:, kc, co * 128:(co + 1) * 128], ysgb_[h][:, kc, :]) for kc in range(4)],
               [b_wglu, b_ysgb_[h]], [b_pf])
            acopy(sig_[h][:, co, :], pf[:, 0:128], [b_pf], [b_sig_[h]], func=AF.Exp, scale=-1.0)
        acopy(sig_[h][:], sig_[h][:], [b_sig_[h], b_onec2], [b_sig_[h]], func=AF.Ln, bias=onec2[:, 0:1])
        acopy(sig_[h][:], sig_[h][:], [b_sig_[h]], [b_sig_[h]], func=AF.Exp, scale=-1.0)
        tt("dve", ys2_[h][:], ysg_[h][:], sig_[h][:], ALU.mult, [b_ysg_[h], b_sig_[h]], [b_ys2_[h]])
        pb, b_pb = next_psb()
        transposes([(pb[:, c * 128:(c + 1) * 128], ys2_[h][:, c, :]) for c in range(4)], ident[:],
                   [b_ys2_[h], b_ident], [b_pb])
        acopy(ysT_[h][:], pb[:, 0:512], [b_pb], [b_ysT_[h]])
        rms_to(mixed_[h][:, 512:1024], ysT_[h][:], [b_ysT_[h]], G_SSM, junk_[h], b_junk_[h], b_mixs[h])

    def s_out(n):
        h = n % 2
        pb, b_pb = next_psb()
        transposes([(pb[:, k * 128:(k + 1) * 128], mixed_[h][:, k * 128:(k + 1) * 128]) for k in range(8)],
                   ident[:], [b_mixa[h], b_mixs[h], b_ident], [b_pb])
        acopy(mixT_[h][:], pb[:].rearrange("p (k t) -> p k t", k=8), [b_pb], [b_mixT_[h]])
        dma("dsync", xc[h][:], x_d[n * 128:(n + 1) * 128, :], writes=[b_xc[h]])
        for hf in range(2):
            pf, b_pf = next_psf()
            mm(pf[:], [(mixT_[h][:, kc, :], wout_bf[:, kc, hf * 512:(hf + 1) * 512]) for kc in range(8)],
               [b_mixT_[h], b_wout], [b_pf])
            tt("dve", x1[h][:, hf * 512:(hf + 1) * 512], pf[:], xc[h][:, hf * 512:(hf + 1) * 512], ALU.add,
               [b_pf, b_xc[h]], [b_x1[h]])
        dma("dpool", x1_d[n * 128:(n + 1) * 128, :], x1[h][:], reads=[b_x1[h]], writes=[b_x1d[n]])
        norm_transpose(None, (), G_FFN, x1[h], b_x1[h], h2b[h], b_h2b[h], h2T, b_h2T, n * 128, from_dram=False)

    for c in range(4):
        acopy(ysJ[:, c, :, :], ysJ[:, c, :, :], [b_ys[c]] + b_ysi[c], [b_ys[c]], func=AF.Gelu)
    s_scores(0)
    s_ssm(0)
    for n in range(17):
        if n + 1 < 17:
            s_scores(n + 1)
        s_pv(n)
        if n + 1 < 17:
            s_ssm(n + 1)
        s_out(n)


    if stop <= 5:
        return finish()
    S.barrier()
    AR_ACT = region(21.5, 110)
    actT = AR_ACT.alloc([128, NPAIR, T_OWN], BF16); b_actT = Buf("actT")
    AR = region(110, 173.5)
    HT = T_OWN // 2
    NU = HT + 2
    up = [[AR.alloc([128, NU], F32) for _ in range(2)] for _ in range(2)]
    b_up = [[Buf(f"up{h}{g}") for g in range(2)] for h in range(2)]
    cv = [[AR.alloc([128, HT], F32) for _ in range(2)] for _ in range(2)]
    b_cv = [[Buf(f"cv{h}{g}") for g in range(2)] for h in range(2)]
    wus = [AR.alloc([128, 8, 128], F32) for _ in range(4)]; b_wus = [Buf(f"wus{i}") for i in range(4)]
    wub = [AR.alloc([128, 8, 128], BF16) for _ in range(4)]; b_wub = [Buf(f"wub{i}") for i in range(4)]
    for g in range(2):
        S.op("pool", lambda e, g=g: e.memset(up[0][g][:, 0:1], 0.0), (), [b_up[0][g]])
    def load_pair(p):
        for gv in range(2):
            wi = (p % 2) * 2 + gv
            c0 = gv * DFF + p * 128
            dma("dsync", wus[wi][:], wup_d[:, c0:c0 + 128].rearrange("(k p) f -> p k f", p=128), writes=[b_wus[wi]])
            if gv == 0:
                acopy(wub[wi][:], wus[wi][:], [b_wus[wi]], [b_wub[wi]])
            else:
                tcopy("dve", wub[wi][:], wus[wi][:], [b_wus[wi]], [b_wub[wi]])

    def stage_a(p, hf):
        for gv in range(2):
            wi = (p % 2) * 2 + gv
            ch = gv * NPAIR + p
            u_, b_u = up[hf][gv], b_up[hf][gv]
            if hf == 0:
                segs = [(0, 512, 1), (512, 512, 513), (1024, 1, 1025)]
            else:
                segs = [(1023, 1, 0), (1024, 512, 1), (1536, 512, 513), (2048, 1, 1025)]
            for si, (t0, n, dc) in enumerate(segs):
                pf, b_pf = next_psf()
                mm(pf[:, 0:n], [(wub[wi][:, kc, :], h2T[:, kc, t0:t0 + n]) for kc in range(8)],
                   [b_wub[wi], b_h2T], [b_pf])
                acopy(u_[:, dc:dc + n], pf[:, 0:n], [b_pf], [b_u])
            c_, b_c = cv[hf][gv], b_cv[hf][gv]
            acopy(c_[:], u_[:, 1:1 + HT], [b_u, b_cw, b_cb], [b_c], func=AF.Identity,
                  scale=cw[:, 44 + ch:44 + ch + 1], bias=cb[:, ch:ch + 1])
            stt(c_[:], u_[:, 0:HT], cw[:, ch:ch + 1], c_[:], ALU.mult, ALU.add, [b_u, b_cw, b_c], [b_c])
            stt(c_[:], u_[:, 2:2 + HT], cw[:, 88 + ch:88 + ch + 1], c_[:], ALU.mult, ALU.add, [b_u, b_cw, b_c], [b_c])

    def stage_b(p, hf):
        acopy(cv[hf][0][:], cv[hf][0][:], [b_cv[hf][0]], [b_cv[hf][0]], func=AF.Silu)
        tt("dve", actT[:, p, hf * HT:(hf + 1) * HT], cv[hf][0][:], cv[hf][1][:], ALU.mult,
           [b_cv[hf][0], b_cv[hf][1]], [b_actT])

    load_pair(0)
    for p in range(NPAIR):
        if p + 1 < NPAIR:
            load_pair(p + 1)
        stage_a(p, 0)
        if p > 0:
            stage_b(p - 1, 1)
        stage_a(p, 1)
        stage_b(p, 0)
    stage_b(NPAIR - 1, 1)

    S.barrier()
    AR = region(110, 207.8)
    wdn_bf = AR.alloc([128, NPAIR, D], BF16); b_wdnp = [Buf(f"wdn{p}") for p in range(NPAIR)]
    wds = [AR.alloc([128, D], F32) for _ in range(4)]; b_wds = [Buf(f"wds{i}") for i in range(4)]
    for p in range(NPAIR):
        dma("dsync", wds[p % 4][:], wdn_d[p * 128:(p + 1) * 128, :], writes=[b_wds[p % 4]])
        if p % 2 == 0:
            acopy(wdn_bf[:, p, :], wds[p % 4][:], [b_wds[p % 4]], [b_wdnp[p]])
        else:
            tcopy("dve", wdn_bf[:, p, :], wds[p % 4][:], [b_wds[p % 4]], [b_wdnp[p]])
    x1r = [AR.alloc([128, D], F32) for _ in range(2)]; b_x1r = [Buf("x1r0"), Buf("x1r1")]
    x2 = [AR.alloc([128, D], F32) for _ in range(2)]; b_x2 = [Buf("x2a"), Buf("x2b")]
    yo = [AR.alloc([128, D], F32) for _ in range(2)]; b_yo = [Buf("yo0"), Buf("yo1")]
    jk = AR.alloc([128, D], BF16); b_jk = Buf("jk")
    NE = 3
    early = [[next_psf() for _ in range(2)] for _ in range(NE)]
    for p in range(NPAIR):
        for n in range(NE):
            for hf in range(2):
                pf, b_pf = early[n][hf]
                S.op("pe", lambda e, pf=pf, p=p, n=n, hf=hf: e.matmul(
                    pf[:], actT[:, p, n * 128:(n + 1) * 128], wdn_bf[:, p, hf * 512:(hf + 1) * 512],
                    start=(p == 0), stop=(p == NPAIR - 1)), [b_actT, b_wdnp[p]], [b_pf], cost=0.25)
    for n in range(16):
        i2 = n % 2
        dma("dsync", x1r[i2][:], x1_d[n * 128:(n + 1) * 128, :], reads=[b_x1d[n]], writes=[b_x1r[i2]])
        for hf in range(2):
            if n < NE:
                pf, b_pf = early[n][hf]
            else:
                pf, b_pf = next_psf()
                mm(pf[:], [(actT[:, p, n * 128:(n + 1) * 128], wdn_bf[:, p, hf * 512:(hf + 1) * 512])
                           for p in range(NPAIR)], [b_actT] + b_wdnp, [b_pf])
            tt("dve", x2[i2][:, hf * 512:(hf + 1) * 512], pf[:], x1r[i2][:, hf * 512:(hf + 1) * 512], ALU.add,
               [b_pf, b_x1r[i2]], [b_x2[i2]])
        sc, b_sc = next_stat()
        acopy(jk[:], x2[i2][:], [b_x2[i2]], [b_jk, b_sc], func=AF.Square, accum=sc[:, 0:1])
        ts("dve", sc[:, 1:2], sc[:, 0:1], 1.0 / D, EPS, ALU.mult, ALU.add, [b_sc], [b_sc])
        acopy(sc[:, 2:3], sc[:, 1:2], [b_sc], [b_sc], func=AF.Sqrt)
        S.op("dve", lambda e, sc=sc: e.reciprocal(out=sc[:, 3:4], in_=sc[:, 2:3]), [b_sc], [b_sc])
        stt(yo[i2][:], x2[i2][:], sc[:, 3:4], gains[:, G_FIN:G_FIN + D], ALU.mult, ALU.mult,
            [b_x2[i2], b_sc, b_gains], [b_yo[i2]])
        dma("dpool", y_d[n * 128:(n + 1) * 128, :], yo[i2][:], reads=[b_yo[i2]])

    return finish()


def _consts():
    ident = np.eye(128, dtype=np.float32)
    R = np.zeros((128, 128), np.float32)
    for hh in range(2):
        for d in range(8):
            R[hh * 64 + d, hh * 64 + d + 8] = -1.0
            R[hh * 64 + d + 8, hh * 64 + d] = 1.0
    rmat = np.ascontiguousarray(R.T)
    ifr = np.zeros((128, 1), np.float32)
    base = np.power(np.float32(500000.0), -np.arange(8, dtype=np.float32) / np.float32(8.0)).astype(np.float32)
    for hh in range(2):
        for d in range(16):
            ifr[hh * 64 + d, 0] = base[d % 8]
    kk = np.arange(128)[:, None]
    qq = np.arange(128)[None, :]
    msk = np.concatenate([(kk >= qq), (kk <= qq)], axis=1).astype(np.float32)
    bm = (np.arange(128)[:, None] // 16 == np.arange(128)[None, :] // 16).astype(np.float32)
    return ident, rmat, ifr, msk, bm


_NC_CACHE = {}


def _prep_inputs(inp, dbg=False):
    f = lambda a: np.ascontiguousarray(np.asarray(a, dtype=np.float32))
    x = f(inp["x"])
    w_in = f(inp["w_in"][0])
    qcols = np.concatenate([np.r_[j * 64:(j + 1) * 64, (4 + j) * 64:(5 + j) * 64] for j in range(4)])
    w_in_r = np.ascontiguousarray(np.concatenate([w_in[:, qcols], w_in[:, 512:]], axis=1))
    gains = np.concatenate([f(inp["norm_mix_g"][0]), f(inp["norm_ffn_g"][0]), f(inp["norm_final_g"]),
                            f(inp["norm_attn_g"][0]), f(inp["norm_ssm_g"][0])])[None, :]
    ident, rmat, ifr, msk, bm = _consts()
    a_re, a_im = f(inp["a_re"][0]), f(inp["a_im"][0])
    lst = np.broadcast_to(f(inp["log_step"][0])[:, :, None], (2, 32, 64))
    b_re, b_im = f(inp["b_re"][0]), f(inp["b_im"][0])
    c_re, c_im = f(inp["c_re"][0]), f(inp["c_im"][0])
    cwf = f(inp["conv_w"][0])
    shared = dict(
        w_in=w_in_r, gains=np.ascontiguousarray(gains),
        dsk=np.ascontiguousarray(f(inp["d_skip"][0]).reshape(4, 128).T),
        w_glu=f(inp["w_glu"][0]), sink=f(inp["sink"]), w_out=f(inp["w_out"][0]), w_up=f(inp["w_up"][0]),
        cb=np.ascontiguousarray(f(inp["conv_b"][0]).reshape(44, 128).T),
        w_down=f(inp["w_down"][0]), ident=ident, rmat=rmat, ifr=ifr, msk=msk, bmask=bm,
        sidx=np.arange(512, dtype=np.float32)[None, :])

    def ep(a):
        return np.ascontiguousarray(a.reshape(2, 16, 2, 64).transpose(2, 3, 0, 1).reshape(128, 32))

    def ep_b(a):
        return np.ascontiguousarray(a.reshape(2, 16, 2, 64, 16).transpose(2, 3, 0, 1, 4).reshape(128, 512))

    def ep_c(a):
        return np.ascontiguousarray(a.reshape(2, 16, 2, 16, 64).transpose(2, 4, 0, 1, 3).reshape(128, 512))

    per_half = []
    for h in range(2):
        sl = slice(None) if h == 0 else slice(None, None, -1)
        cwh = cwf if h == 0 else cwf[::-1]
        pos = np.arange(T_QKV, dtype=np.float32) if h == 0 else (T_ALL - 1 - np.arange(T_QKV)).astype(np.float32)
        per_half.append(dict(
            are=ep(a_re[sl]), aim=ep(a_im[sl]), ls=ep(lst[sl]),
            bre=ep_b(b_re[sl]), bim=ep_b(b_im[sl]), cre=ep_c(c_re[sl]), cim=ep_c(c_im[sl]),
            cw=np.ascontiguousarray(cwh.reshape(3, 44, 128).transpose(2, 0, 1).reshape(128, 132)),
            pos=np.ascontiguousarray(pos[None, :])))
    in_maps = []
    for c in range(8):
        b, h = c // 2, c % 2
        xs = x[b] if h == 0 else x[b][::-1]
        m = dict(shared)
        m.update(per_half[h])
        m["x"] = np.ascontiguousarray(xs)
        in_maps.append(m)
    return in_maps


def kernel(**inputs):
    in_maps = _prep_inputs(inputs)
    if "nc" not in _NC_CACHE:
        _NC_CACHE["nc"] = build_program()
    nc = _NC_CACHE["nc"]
    res = run_bass_kernel_spmd(nc, in_maps, core_ids=list(range(8)))
    out = np.empty((4, T_ALL, D), np.float32)
    for c in range(8):
        b, h = c // 2, c % 2
        y = np.asarray(res.results[c]["y"], dtype=np.float32)
        if h == 0:
            out[b, :T_OWN] = y
        else:
            out[b, T_OWN:] = y[::-1]
    return out
```

```python
import numpy as np
import concourse.bass as bass
import concourse.mybir as mybir
from concourse.bass_utils import run_bass_kernel_spmd

F32 = mybir.dt.float32
BF16 = mybir.dt.bfloat16
I32 = mybir.dt.int32
ALU = mybir.AluOpType
AF = mybir.ActivationFunctionType

D = 1024
T_ALL = 4096
T_OWN = 2048
T_EXT = 2176
T_QKV = 2560
NQKV_G = 5
DFF = 2816
NPAIR = 22
EPS = 1e-6
TWO_PI = float(2.0 * np.pi)
SB_BASE = 16512
import os as _os
_C = lambda k, d: float(_os.environ.get(k, d))
C_ACT_F, C_ACT_E = _C("KS_ACT_F", 0.2), _C("KS_ACT_E", 1000.0)
C_DVE_F, C_DVE_E = _C("KS_DVE_F", 0.15), _C("KS_DVE_E", 960.0)
C_PE_F, C_PE_E = _C("KS_PE_F", 0.3), _C("KS_PE_E", 2400.0)
C_DMA_F, C_DMA_B = _C("KS_DMA_F", 2.0), _C("KS_DMA_B", 150e3)
SB_END = 229376


class Buf:
    __slots__ = ("name", "writer", "readers", "excl")

    def __init__(self, name, excl=False):
        self.name = name
        self.writer = None
        self.readers = {}
        self.excl = excl


class Stream:
    def __init__(self, name, eng_name, inc, sem):
        self.name, self.eng_name, self.inc, self.sem, self.count = name, eng_name, inc, sem, 0


class Op:
    __slots__ = ("stream", "fn", "preds", "cost", "prog", "end", "start", "sidx", "nsucc", "bar_counts")

    def __init__(self, stream, fn, preds, cost, prog):
        self.stream, self.fn, self.preds, self.cost, self.prog = stream, fn, preds, cost, prog
        self.end = self.start = None
        self.sidx = None
        self.bar_counts = None


class Sched:
    ENGS = ("tensor", "vector", "scalar", "gpsimd", "sync")
    NDS = 12
    HOP = _C("KS_HOP", 0.45)
    NDP = 4

    def __init__(self, sems):
        names = [("pe", "tensor", 1), ("dve", "vector", 1), ("act", "scalar", 1), ("pool", "gpsimd", 1)]
        names += [(f"dsync{i}", "sync", 16) for i in range(self.NDS)]
        names += [(f"dpool{i}", "gpsimd", 16) for i in range(self.NDP)]
        assert len(sems) == len(names)
        self.streams = {n: Stream(n, e, inc, s) for (n, e, inc), s in zip(names, sems)}
        self.slots = {n: Buf("slot_" + n) for n in self.streams if n.startswith("ds") or n.startswith("dp")}
        self.rr = {"dsync": 0, "dpool": 0}
        self.ops = []
        self.last_barrier = None
        self.since_barrier = []

    def op(self, stream, fn, reads=(), writes=(), cost=0.5):
        if stream in self.rr:
            k = self.rr[stream]
            self.rr[stream] = (k + 1) % (self.NDS if stream == "dsync" else self.NDP)
            stream = f"{stream}{k}"
            writes = list(writes) + [self.slots[stream]]
        ex = [b for b in reads if b.excl]
        if ex:
            reads = [b for b in reads if not b.excl]
            writes = list(writes) + ex
        preds = set()
        for b in reads:
            if b.writer is not None:
                preds.add(b.writer)
        for b in writes:
            if b.writer is not None:
                preds.add(b.writer)
            for o in b.readers.values():
                preds.add(o)
        if self.last_barrier is not None:
            preds.add(self.last_barrier)
        o = Op(stream, fn, sorted(preds, key=lambda p_: p_.prog), cost, len(self.ops))
        self.ops.append(o)
        self.since_barrier.append(o)
        for b in reads:
            b.readers[id(o)] = o
        for b in writes:
            b.writer = o
            b.readers = {}
        return o

    def barrier(self):
        if not self.since_barrier:
            return
        b = Op(None, None, list(self.since_barrier) + ([self.last_barrier] if self.last_barrier else []), 0.0, len(self.ops))
        self.ops.append(b)
        self.last_barrier = b
        self.since_barrier = []

    def schedule(self):
        import heapq
        ops = self.ops
        npred = {id(o): len(o.preds) for o in ops}
        succ = {id(o): [] for o in ops}
        for o in ops:
            for p in o.preds:
                succ[id(p)].append(o)
        eng_free = {e: 0.0 for e in self.ENGS}
        eng_of = lambda o: self.streams[o.stream].eng_name
        heap = []

        def ready_time(o):
            return max([p.end for p in o.preds], default=0.0) + self.HOP

        def push(o):
            if o.stream is None:
                o.start = o.end = ready_time(o)
                release(o)
                return
            rt = ready_time(o)
            heapq.heappush(heap, (max(rt, eng_free[eng_of(o)]), o.prog, rt, o))

        def release(o):
            for s_ in succ[id(o)]:
                npred[id(s_)] -= 1
                if npred[id(s_)] == 0:
                    push(s_)
        for o in ops:
            if npred[id(o)] == 0:
                push(o)
        nsched = 0
        while heap:
            key, prog, rt, o = heapq.heappop(heap)
            e = eng_of(o)
            st_ = max(rt, eng_free[e])
            if st_ > key + 1e-9:
                heapq.heappush(heap, (st_, prog, rt, o))
                continue
            o.start = st_
            is_dma = self.streams[o.stream].inc == 16
            o.end = st_ + o.cost
            eng_free[e] = st_ + (0.08 if is_dma else o.cost)
            nsched += 1
            release(o)
        assert all(o.start is not None for o in ops), "scheduler: unscheduled ops (cycle?)"
        self.makespan = max(o.end for o in ops)

    def emit(self, block):
        self.barrier()
        self.schedule()
        per_eng = {e: [] for e in self.ENGS}
        for o in self.ops:
            if o.stream is not None:
                per_eng[self.streams[o.stream].eng_name].append(o)
        for e in self.ENGS:
            per_eng[e].sort(key=lambda o: (o.start, o.prog))
            for o in per_eng[e]:
                st = self.streams[o.stream]
                st.count += 1
                o.sidx = st.count
        cnt = {n: 0 for n in self.streams}
        for o in self.ops:
            if o.stream is None:
                o.bar_counts = dict(cnt)
            else:
                cnt[o.stream] += 1
        final_counts = dict(cnt)
        progs = {e: [] for e in self.ENGS}
        for e in self.ENGS:
            seen = {}
            for o in per_eng[e]:
                need = {}
                for p in o.preds:
                    if p.stream is None:
                        for sn, c in p.bar_counts.items():
                            if c and need.get(sn, 0) < c:
                                need[sn] = c
                    else:
                        if p.stream == "pe" and o.stream == "pe":
                            continue
                        if need.get(p.stream, 0) < p.sidx:
                            need[p.stream] = p.sidx
                waits = []
                for sn, c in need.items():
                    if seen.get(sn, 0) >= c:
                        continue
                    seen[sn] = c
                    waits.append((self.streams[sn].sem, c * self.streams[sn].inc))
                progs[e].append((waits, o.fn, self.streams[o.stream].sem, self.streams[o.stream].inc))
            waits = [(self.streams[sn].sem, c * self.streams[sn].inc) for sn, c in final_counts.items()
                     if c and seen.get(sn, 0) < c]
            progs[e].append((waits, None, None, 0))

        def run(engname):
            def body(eh):
                for waits, fn, sem, inc in progs[engname]:
                    for (ws, wv) in waits:
                        eh.wait_ge(ws, wv)
                    if fn is not None:
                        fn(eh).then_inc(sem, inc)
            return body
        block.tensor(run("tensor"))
        block.vector(run("vector"))
        block.scalar(run("scalar"))
        block.gpsimd(run("gpsimd"))
        block.sync(run("sync"))


def build_program(dbg=False, stop=99):
    nc = bass.Bass("TRN2", target_bir_lowering=False)

    def din(name, shape, dt=F32):
        return nc.dram_tensor(name, list(shape), dt, kind="ExternalInput").ap()

    x_d = din("x", [T_ALL, D])
    pos_d = din("pos", [1, T_QKV])
    win_d = din("w_in", [D, 1280])
    gains_d = din("gains", [1, 4096])
    are_d = din("are", [128, 32])
    aim_d = din("aim", [128, 32])
    ls_d = din("ls", [128, 32])
    bre_d = din("bre", [128, 512])
    bim_d = din("bim", [128, 512])
    cre_d = din("cre", [128, 512])
    cim_d = din("cim", [128, 512])
    dsk_d = din("dsk", [128, 4])
    wglu_d = din("w_glu", [512, 512])
    sink_d = din("sink", [1, 8])
    wout_d = din("w_out", [D, D])
    wup_d = din("w_up", [D, 2 * DFF])
    cw_d = din("cw", [128, 3 * 44])
    cb_d = din("cb", [128, 44])
    wdn_d = din("w_down", [DFF, D])
    ident_d = din("ident", [128, 128])
    rmat_d = din("rmat", [128, 128])
    ifr_d = din("ifr", [128, 1])
    msk_d = din("msk", [128, 256])
    bmask_d = din("bmask", [128, 128])
    sidx_d = din("sidx", [1, 512])
    y_d = nc.dram_tensor("y", [T_OWN, D], F32, kind="ExternalOutput").ap()
    x1_d = nc.dram_tensor("x1_scr", [T_EXT, D], F32, kind="Internal").ap()
    qs_d = nc.dram_tensor("q_scr", [128, 4, T_QKV], BF16, kind="Internal").ap()
    ks_d = nc.dram_tensor("k_scr", [128, T_QKV], BF16, kind="Internal").ap()
    vs_d = nc.dram_tensor("v_scr", [128, 20, 130], BF16, kind="Internal").ap()
    dbg_d = None
    if dbg:
        dbg_d = nc.dram_tensor("dbg", [8, 128, 2560], F32, kind="ExternalOutput").ap()

    import contextlib
    _es = contextlib.ExitStack()
    sems = [_es.enter_context(nc.semaphore(f"sem{i}")) for i in range(4 + Sched.NDS + Sched.NDP)]
    S = Sched(sems)

    _sbn = [0]
    def finish():
        with nc.Block() as block:
            S.emit(block)
        _es.close()
        return nc

    class Arena:
        def __init__(self, lo, hi):
            self.lo, self.hi, self.cur, self.n = lo, hi, lo, 0

        def alloc(self, shape, dt):
            nbytes = int(np.prod(shape[1:])) * (4 if dt in (F32, I32) else 2)
            nbytes = (nbytes + 63) // 64 * 64
            off = self.cur
            assert off + nbytes <= self.hi, (shape, off, nbytes, self.hi)
            self.cur += nbytes
            _sbn[0] += 1
            return nc.alloc_sbuf_tensor_at(f"sb{_sbn[0]}", list(shape), dt, offset=off)

        def mark(self):
            return self.cur

        def reset(self, m):
            self.cur = m

    def region(lo_kb, hi_kb):
        return Arena(SB_BASE + int(lo_kb * 1024), min(SB_END, SB_BASE + int(hi_kb * 1024)))

    AR = region(0, 21.5)
    AR_YS = region(21.5, 55.5)
    _psn = [0]

    def psum(dt=F32):
        _psn[0] += 1
        return nc.alloc_psum_tensor(f"ps{_psn[0]}", [128, 512 if dt == F32 else 1024], dt)

    PSF = [(psum(F32), Buf(f"psf{i}", excl=True)) for i in range(6)]
    PSB = [(psum(BF16), Buf(f"psb{i}", excl=True)) for i in range(2)]
    _rr = {"f": 0, "b": 0}

    def next_psf():
        _rr["f"] = (_rr["f"] + 1) % len(PSF)
        return PSF[_rr["f"]]

    def next_psb():
        _rr["b"] = (_rr["b"] + 1) % len(PSB)
        return PSB[_rr["b"]]

    def fsz(ap):
        n = 1
        for d_ in ap.shape[1:]:
            n *= int(d_)
        return n

    def dma(q, out, in_, reads=(), writes=()):
        nbytes = fsz(out) * int(out.shape[0]) * 4
        S.op(q, lambda e: e.dma_start(out=out, in_=in_), reads, writes, cost=C_DMA_F + nbytes / C_DMA_B)

    def tcopy(st, out, in_, reads=(), writes=()):
        S.op(st, lambda e: e.tensor_copy(out=out, in_=in_), reads, writes,
             cost=(C_DVE_F + fsz(out) / C_DVE_E) * (4 if st == "pool" else 1))

    def acopy(out, in_, reads=(), writes=(), func=AF.Copy, scale=1.0, bias=None, accum=None):
        def f(e):
            kw = {}
            if bias is not None:
                kw["bias"] = bias
            if accum is not None:
                kw["accum_out"] = accum
            return e.activation(out=out, in_=in_, func=func, scale=scale, **kw)
        S.op("act", f, reads, writes, cost=C_ACT_F + fsz(out) / C_ACT_E)

    def tt(st, out, in0, in1, op, reads=(), writes=()):
        S.op(st, lambda e: e.tensor_tensor(out=out, in0=in0, in1=in1, op=op), reads, writes,
             cost=(C_DVE_F + fsz(out) / C_DVE_E) * (4 if st == "pool" else 1))

    def ts(st, out, in0, s1, s2, op0, op1=None, reads=(), writes=()):
        c_ = (C_DVE_F + fsz(out) / C_DVE_E) * (4 if st == "pool" else 1)
        if op1 is None:
            S.op(st, lambda e: e.tensor_scalar(out=out, in0=in0, scalar1=s1, scalar2=None, op0=op0), reads, writes, cost=c_)
        else:
            S.op(st, lambda e: e.tensor_scalar(out=out, in0=in0, scalar1=s1, scalar2=s2, op0=op0, op1=op1), reads, writes,
                 cost=c_)

    def stt(out, in0, scalar, in1, op0, op1, reads=(), writes=()):
        S.op("dve", lambda e: e.scalar_tensor_tensor(out=out, in0=in0, scalar=scalar, in1=in1, op0=op0, op1=op1),
             reads, writes, cost=C_DVE_F + fsz(out) / (C_DVE_E * 0.83))

    def mm(out, pairs, reads=(), writes=()):
        def f(e):
            ins = None
            n = len(pairs)
            for i, (l, r) in enumerate(pairs):
                ins = e.matmul(out, l, r, start=(i == 0), stop=(i == n - 1))
            return ins
        S.op("pe", f, reads, writes, cost=C_PE_F + sum(max(64, fsz(r)) / C_PE_E + 0.01 for (_, r) in pairs))

    def transposes(outs_ins, ident, reads=(), writes=()):
        def f(e):
            ins = None
            for (o, i_) in outs_ins:
                ins = e.transpose(o, i_, ident)
            return ins
        S.op("pe", f, reads, writes, cost=0.3 + 0.1 * len(outs_ins))

    gains = AR.alloc([128, 4096], F32); b_gains = Buf("gains")
    ident_f = AR.alloc([128, 128], F32)
    ident = AR.alloc([128, 128], BF16); b_ident = Buf("ident")
    rmat_f = AR.alloc([128, 128], F32)
    rmat = AR.alloc([128, 128], BF16); b_rmat = Buf("rmat")
    msk_f = AR.alloc([128, 256], F32)
    msk = AR.alloc([128, 256], BF16); b_msk = Buf("msk")
    ifr = AR.alloc([128, 1], F32); b_ifr = Buf("ifr")
    dsk = AR.alloc([128, 4], F32); b_dsk = Buf("dsk")
    esink = AR.alloc([128, 8], F32); b_esink = Buf("esink")
    cw = AR.alloc([128, 132], F32); b_cw = Buf("cw")
    cb = AR.alloc([128, 44], F32); b_cb = Buf("cb")
    epsc = AR.alloc([128, 1], F32); b_epsc = Buf("epsc")
    onec2 = AR.alloc([128, 1], F32); b_onec2 = Buf("onec2")
    stat = AR.alloc([128, 64], F32)
    b_stat = [Buf(f"stat{i}") for i in range(16)]
    _st = [0]

    def next_stat():
        _st[0] = (_st[0] + 1) % 16
        return stat[:, 4 * _st[0]:4 * _st[0] + 4], b_stat[_st[0]]

    b_tmp = Buf("ldtmp")
    S.op("dve", lambda e: e.memset(epsc[:], EPS), (), [b_epsc])
    S.op("dve", lambda e: e.memset(onec2[:], 1.0), (), [b_onec2])
    dma("dsync", gains[:], gains_d.partition_broadcast(128), writes=[b_gains])
    dma("dsync", ident_f[:], ident_d[:, :], writes=[b_tmp])
    dma("dsync", rmat_f[:], rmat_d[:, :], writes=[b_tmp])
    dma("dsync", msk_f[:], msk_d[:, :], writes=[b_tmp])
    dma("dsync", ifr[:], ifr_d[:, :], writes=[b_ifr])
    dma("dsync", dsk[:], dsk_d[:, :], writes=[b_dsk])
    dma("dsync", esink[:], sink_d.partition_broadcast(128), writes=[b_esink])
    dma("dsync", cw[:], cw_d[:, :], writes=[b_cw])
    dma("dsync", cb[:], cb_d[:, :], writes=[b_cb])
    tcopy("dve", ident[:], ident_f[:], [b_tmp], [b_ident])
    tcopy("dve", rmat[:], rmat_f[:], [b_tmp], [b_rmat])
    tcopy("dve", msk[:], msk_f[:], [b_tmp], [b_msk])
    acopy(esink[:], esink[:], [b_esink], [b_esink], func=AF.Exp)
    G_MIX, G_FFN, G_FIN, G_ATT, G_SSM = 0, 1024, 2048, 3072, 3584

    ysJ = AR_YS.alloc([128, 4, 8, T_EXT // 8], F32); b_ys = [Buf(f"ys{c}") for c in range(4)]

    def norm_transpose_group(tiles, exp_set=False):
        for (src, go, xt, b_xt, hb, b_hb, hT, b_hT, col0) in tiles:
            if src is not None:
                dma("dsync", xt[:], src, writes=[b_xt])
        scs = [next_stat() for _ in tiles]
        if exp_set:
            for (src, go, xt, b_xt, hb, b_hb, hT, b_hT, col0), (sc, b_sc) in zip(tiles, scs):
                S.op("dve", ssq_dve(hb[:], xt[:], sc), [b_xt], [b_hb, b_sc])
                rstd_exp(sc, b_sc, D)
        else:
            for (src, go, xt, b_xt, hb, b_hb, hT, b_hT, col0), (sc, b_sc) in zip(tiles, scs):
                acopy(hb[:], xt[:], [b_xt], [b_hb, b_sc], func=AF.Square, accum=sc[:, 0:1])
            for (sc, b_sc) in scs:
                ts("dve", sc[:, 1:2], sc[:, 0:1], 1.0 / D, EPS, ALU.mult, ALU.add, [b_sc], [b_sc])
            for (sc, b_sc) in scs:
                acopy(sc[:, 2:3], sc[:, 1:2], [b_sc], [b_sc], func=AF.Sqrt)
            for (sc, b_sc) in scs:
                S.op("dve", lambda e, sc=sc: e.reciprocal(out=sc[:, 3:4], in_=sc[:, 2:3]), [b_sc], [b_sc])
        for (src, go, xt, b_xt, hb, b_hb, hT, b_hT, col0), (sc, b_sc) in zip(tiles, scs):
            stt(hb[:], xt[:], sc[:, 3:4], gains[:, go:go + D], ALU.mult, ALU.mult, [b_xt, b_sc, b_gains], [b_hb])
        for (src, go, xt, b_xt, hb, b_hb, hT, b_hT, col0) in tiles:
            pb, b_pb = next_psb()
            transposes([(pb[:, k * 128:(k + 1) * 128], hb[:, k * 128:(k + 1) * 128]) for k in range(8)],
                       ident[:], [b_hb, b_ident], [b_pb])
            if not exp_set and (col0 // 128) % 2 == 1:
                tcopy("dve", hT[:, :, col0:col0 + 128], pb[:].rearrange("p (k t) -> p k t", k=8), [b_pb], [b_hT])
            else:
                acopy(hT[:, :, col0:col0 + 128], pb[:].rearrange("p (k t) -> p k t", k=8), [b_pb], [b_hT])

    def norm_transpose(src_ap, src_reads, gain_off, xt, b_xt, hb, b_hb, hT, b_hT, col0, from_dram=True, stop=99):
        norm_transpose_group([(src_ap if from_dram else None, gain_off, xt, b_xt, hb, b_hb, hT, b_hT, col0)],
                             exp_set=not from_dram)

    AR_P = region(196.25, 207.8)
    prm = AR_P.alloc([128, 21, 32], F32); b_prm = Buf("prm")
    (P_ARE, P_AIM, P_DT, P_ER, P_TH, P_C, P_S, P_LR, P_LI, P_T0, P_T1, P_T2, P_CR, P_CI, P_R8, P_T3,
     P_T4, P_T5, P_T6, P_T7, P_F8) = range(21)
    pri = AR_P.alloc([128, 32], I32)
    PW = AR_P.alloc([128, 9, 2, 32], F32); b_PW = Buf("PW")
    craw = AR_P.alloc([128, 2, 512], F32); b_craw = Buf("craw")
    sidx = AR_P.alloc([128, 512], F32); b_sidx = Buf("sidx")
    onec = AR_P.alloc([128, 1], F32); b_onec = Buf("onec")

    def P(i):
        return prm[:, i, :]

    dma("dsync", P(P_ARE), are_d[:, :], writes=[b_prm])
    dma("dsync", P(P_AIM), aim_d[:, :], writes=[b_prm])
    dma("dsync", P(P_DT), ls_d[:, :], writes=[b_prm])
    dma("dsync", craw[:, 0, :], cre_d[:, :], writes=[b_craw])
    dma("dsync", craw[:, 1, :], cim_d[:, :], writes=[b_craw])
    dma("dsync", sidx[:], sidx_d.partition_broadcast(128), writes=[b_sidx])
    S.op("pool", lambda e: e.memset(onec[:], 1.0), (), [b_onec])
    R, W_ = [b_prm], [b_prm]
    acopy(P(P_DT), P(P_DT), R, W_, func=AF.Exp)
    tt("dve", P(P_T0), P(P_ARE), P(P_DT), ALU.mult, R, W_)
    acopy(P(P_ER), P(P_T0), R, W_, func=AF.Exp)
    tt("dve", P(P_TH), P(P_AIM), P(P_DT), ALU.mult, R, W_)
    ts("dve", P(P_T0), P(P_TH), 1.0 / TWO_PI, None, ALU.mult, reads=R, writes=W_)
    tcopy("dve", pri[:], P(P_T0), R, W_)
    tcopy("dve", P(P_T0), pri[:], R, W_)
    stt(P(P_T1), P(P_T0), -TWO_PI, P(P_TH), ALU.mult, ALU.add, R, W_)
    ts("dve", P(P_T1), P(P_T1), float(np.pi), float(-np.pi), ALU.min, ALU.max, R, W_)
    acopy(P(P_S), P(P_T1), R, W_, func=AF.Sin)
    acopy(P(P_T2), P(P_T1), R, W_, func=AF.Sin, scale=0.5)
    tt("dve", P(P_T2), P(P_T2), P(P_T2), ALU.mult, R, W_)
    ts("dve", P(P_C), P(P_T2), -2.0, 1.0, ALU.mult, ALU.add, R, W_)
    tt("dve", P(P_LR), P(P_ER), P(P_C), ALU.mult, R, W_)
    tt("dve", P(P_LI), P(P_ER), P(P_S), ALU.mult, R, W_)
    tt("dve", P(P_T0), P(P_ER), P(P_ER), ALU.mult, R, W_)
    tt("dve", P(P_T0), P(P_T0), P(P_T0), ALU.mult, R, W_)
    tt("dve", P(P_R8), P(P_T0), P(P_T0), ALU.mult, R, W_)
    ts("dve", P(P_T0), P(P_LR), -1.0, None, ALU.add, reads=R, writes=W_)
    tt("dve", P(P_T1), P(P_ARE), P(P_ARE), ALU.mult, R, W_)
    tt("dve", P(P_T2), P(P_AIM), P(P_AIM), ALU.mult, R, W_)
    tt("dve", P(P_T1), P(P_T1), P(P_T2), ALU.add, R, W_)
    S.op("dve", lambda e: e.reciprocal(out=P(P_T3), in_=P(P_T1)), R, W_)
    tt("dve", P(P_T1), P(P_T0), P(P_ARE), ALU.mult, R, W_)
    tt("dve", P(P_T2), P(P_LI), P(P_AIM), ALU.mult, R, W_)
    tt("dve", P(P_T1), P(P_T1), P(P_T2), ALU.add, R, W_)
    tt("dve", P(P_CR), P(P_T1), P(P_T3), ALU.mult, R, W_)
    tt("dve", P(P_T1), P(P_LI), P(P_ARE), ALU.mult, R, W_)
    tt("dve", P(P_T2), P(P_T0), P(P_AIM), ALU.mult, R, W_)
    tt("dve", P(P_T1), P(P_T1), P(P_T2), ALU.subtract, R, W_)
    tt("dve", P(P_CI), P(P_T1), P(P_T3), ALU.mult, R, W_)
    ts("dve", P(P_T4), P(P_TH), 8.0 / TWO_PI, None, ALU.mult, reads=R, writes=W_)
    tcopy("dve", pri[:], P(P_T4), R, W_)
    tcopy("dve", P(P_T5), pri[:], R, W_)
    tt("dve", P(P_F8), P(P_T4), P(P_T5), ALU.subtract, R, W_)
    RP = [b_prm, b_PW]
    S.op("dve", lambda e: e.memset(PW[:, 0, 0, :], 1.0), (), [b_PW])
    S.op("dve", lambda e: e.memset(PW[:, 0, 1, :], 0.0), (), [b_PW])
    tcopy("dve", PW[:, 1, 0, :], P(P_LR), RP, [b_PW])
    tcopy("dve", PW[:, 1, 1, :], P(P_LI), RP, [b_PW])
    for k in range(2, 9):
        ar_, ai_ = PW[:, k - 1, 0, :], PW[:, k - 1, 1, :]
        tt("dve", P(P_T4), ar_, P(P_LR), ALU.mult, RP, W_)
        tt("dve", P(P_T5), ai_, P(P_LI), ALU.mult, RP, W_)
        tt("dve", PW[:, k, 0, :], P(P_T4), P(P_T5), ALU.subtract, RP, [b_PW])
        tt("dve", P(P_T6), ar_, P(P_LI), ALU.mult, RP, W_)
        tt("dve", P(P_T7), ai_, P(P_LR), ALU.mult, RP, W_)
        tt("dve", PW[:, k, 1, :], P(P_T6), P(P_T7), ALU.add, RP, [b_PW])


    if stop <= 0:
        return finish()
    AR_U = region(55.5, 87.5)
    uJ = AR_U.alloc([128, 4, 8, T_ALL // 8], BF16); b_uT = [Buf(f"uT{c}") for c in range(4)]
    AR = region(87.5, 196.25)
    w_in_bf = AR.alloc([128, 8, 1280], BF16); b_win = Buf("w_in")
    wst = [AR.alloc([128, 1280], F32) for _ in range(2)]; b_wst = [Buf("wst0"), Buf("wst1")]
    xts = [AR.alloc([128, D], F32) for _ in range(4)]; b_xts = [Buf(f"xt{i}") for i in range(4)]
    hbs = [AR.alloc([128, D], BF16) for _ in range(4)]; b_hbs = [Buf(f"hb{i}") for i in range(4)]
    hTs = [AR.alloc([128, 8, 512], BF16) for _ in range(2)]; b_hTs = [Buf("hT0"), Buf("hT1")]
    T_RP = 2304
    AR_T = region(21.5, 55.5)
    cosT = AR_T.alloc([128, T_RP], F32); sinT = AR_T.alloc([128, T_RP], F32); b_cs = Buf("cossin")
    angb = AR.alloc([128, 512], F32); angi = AR.alloc([128, 512], I32); b_ang = Buf("ang")
    qb = [AR.alloc([128, 512], BF16) for _ in range(2)]; b_qb = [Buf("qb0"), Buf("qb1")]
    rt2 = [[AR.alloc([128, 512], F32) for _ in range(2)] for _ in range(2)]
    b_rt2 = [[Buf(f"rt{a_}{b_}") for b_ in range(2)] for a_ in range(2)]
    qst = [AR.alloc([128, 4, 512], BF16) for _ in range(2)]; b_qst = [Buf("qst0"), Buf("qst1")]
    kst = [AR.alloc([128, 512], BF16) for _ in range(2)]; b_kst = [Buf("kst0"), Buf("kst1")]
    vst = [AR.alloc([128, 4, 130], BF16) for _ in range(2)]; b_vst = [Buf("vst0"), Buf("vst1")]
    b_qsd = Buf("q_scr"); b_ksd = Buf("k_scr"); b_vsd = Buf("v_scr")

    def load_w_in():
        for kc in range(8):
            dma("dsync", wst[kc % 2][:], win_d[kc * 128:(kc + 1) * 128, :], writes=[b_wst[kc % 2]])
            if kc % 2 == 0:
                acopy(w_in_bf[:, kc, :], wst[kc % 2][:], [b_wst[kc % 2]], [b_win])
            else:
                tcopy("dve", w_in_bf[:, kc, :], wst[kc % 2][:], [b_wst[kc % 2]], [b_win])

    load_w_in()
    for blk in range(5):
        c0 = blk * 512
        nb = min(512, T_RP - c0)
        cs_, sn_b = cosT[:, c0:c0 + nb], sinT[:, c0:c0 + nb]
        dma("dsync", angb[:, 0:nb], pos_d[:, c0:c0 + nb].partition_broadcast(128), writes=[b_ang])
        ts("dve", angb[:, 0:nb], angb[:, 0:nb], ifr[:, 0:1], None, ALU.mult, reads=[b_ang, b_ifr], writes=[b_ang])
        ts("dve", sn_b, angb[:, 0:nb], 1.0 / TWO_PI, None, ALU.mult, reads=[b_ang], writes=[b_cs])
        tcopy("dve", angi[:, 0:nb], sn_b, [b_cs], [b_ang])
        tcopy("dve", sn_b, angi[:, 0:nb], [b_ang], [b_cs])
        stt(angb[:, 0:nb], sn_b, -TWO_PI, angb[:, 0:nb], ALU.mult, ALU.add, [b_ang, b_cs], [b_ang])
        ts("dve", angb[:, 0:nb], angb[:, 0:nb], float(np.pi), float(-np.pi), ALU.min, ALU.max, [b_ang], [b_ang])
        acopy(sn_b, angb[:, 0:nb], [b_ang], [b_cs], func=AF.Sin)
        acopy(cs_, angb[:, 0:nb], [b_ang], [b_cs], func=AF.Sin, scale=0.5)
        tt("dve", cs_, cs_, cs_, ALU.mult, [b_cs], [b_cs])
        ts("dve", cs_, cs_, -2.0, 1.0, ALU.mult, ALU.add, [b_cs], [b_cs])
    for i_ in range(2):
        S.op("pool", lambda e, i_=i_: e.memset(vst[i_][:], 1.0), (), [b_vst[i_]])
    if stop <= 0.2:
        return finish()
    for g4 in range(8):
        hT, b_hT = hTs[g4 % 2], b_hTs[g4 % 2]
        norm_transpose_group([(x_d[(g4 * 4 + j) * 128:(g4 * 4 + j + 1) * 128, :], G_MIX, xts[j], b_xts[j], hbs[j], b_hbs[j],
                               hT, b_hT, j * 128) for j in range(4)])
        if stop <= 0.7:
            return finish()
        for c in range(4):
            pf, b_pf = next_psf()
            mm(pf[:], [(w_in_bf[:, kc, 768 + c * 128:768 + (c + 1) * 128], hT[:, kc, :]) for kc in range(8)],
               [b_win, b_hT], [b_pf])
            if stop <= 0.75:
                return finish()
            acopy(uJ[:, c, :, g4 * 64:(g4 + 1) * 64], pf[:].rearrange("p (n j) -> p j n", j=8), [b_pf], [b_uT[c]])
            if stop <= 0.8:
                return finish()
            if stop <= 0.85:
                return finish()
        if g4 < NQKV_G:
            nt = 4 if g4 < 4 else 2
            nn = nt * 128
            cols = slice(g4 * 512, g4 * 512 + nn)
            sp = g4 % 2
            for c in range(5):
                pf, b_pf = next_psf()
                mm(pf[:, 0:nn], [(w_in_bf[:, kc, c * 128:(c + 1) * 128], hT[:, kc, 0:nn]) for kc in range(8)],
                   [b_win, b_hT], [b_pf])
                i2 = c % 2
                acopy(qb[i2][:, 0:nn], pf[:, 0:nn], [b_pf], [b_qb[i2]])
                pr_, b_pr_ = next_psf()
                mm(pr_[:, 0:nn], [(rmat[:], qb[i2][:, 0:nn])], [b_rmat, b_qb[i2]], [b_pr_])
                ra, rb_, b_ra, b_rb = rt2[i2][0], rt2[i2][1], b_rt2[i2][0], b_rt2[i2][1]
                tt("dve", ra[:, 0:nn], pf[:, 0:nn], cosT[:, cols], ALU.mult, [b_pf, b_cs], [b_ra])
                tt("dve", rb_[:, 0:nn], pr_[:, 0:nn], sinT[:, cols], ALU.mult, [b_pr_, b_cs], [b_rb])
                if c < 4:
                    tt("dve", qst[sp][:, c, 0:nn], ra[:, 0:nn], rb_[:, 0:nn], ALU.add, [b_ra, b_rb], [b_qst[sp]])
                else:
                    tt("dve", kst[sp][:, 0:nn], ra[:, 0:nn], rb_[:, 0:nn], ALU.add, [b_ra, b_rb], [b_kst[sp]])
            for j in range(nt):
                pf, b_pf = next_psf()
                mm(pf[:, 0:128], [(hT[:, kc, j * 128:(j + 1) * 128], w_in_bf[:, kc, 640:768]) for kc in range(8)],
                   [b_win, b_hT], [b_pf])
                acopy(vst[sp][:, j, :].rearrange("p (g d) -> p g d", g=2)[:, :, 0:64],
                      pf[:, 0:128].rearrange("p (g d) -> p g d", g=2), [b_pf], [b_vst[sp]])
            dma("dpool", qs_d[:, :, cols], qst[sp][:, :, 0:nn], reads=[b_qst[sp]], writes=[b_qsd])
            dma("dpool", ks_d[:, cols], kst[sp][:, 0:nn], reads=[b_kst[sp]], writes=[b_ksd])
            dma("dpool", vs_d[:, g4 * 4:g4 * 4 + nt, :], vst[sp][:, 0:nt, :], reads=[b_vst[sp]], writes=[b_vsd])
        if stop <= 0.9:
            return finish()

    if stop <= 1:
        return finish()
    S.barrier()
    AR = region(87.5, 196.25)
    NO, NA = T_EXT // 8, T_ALL // 8
    Tz = AR.alloc([128, 4, 15, 128], BF16); b_Tz = Buf("Tz")
    BbTc = AR.alloc([128, 8, 2, 8, 128], BF16); b_BbTc = Buf("BbTc")
    CbTc = AR.alloc([128, 8, 2, 16, 32], BF16); b_CbTc = Buf("CbTc")
    mB = AR.mark()
    bbar = AR.alloc([128, 2, 512], F32); b_bbar = Buf("bbar")
    braw = AR.alloc([128, 2, 512], F32); b_braw = Buf("braw")
    tmpbs = [AR.alloc([128, 2, 512], F32) for _ in range(2)]; b_tmpbs = [Buf("tmpb0"), Buf("tmpb1")]
    bpows = [AR.alloc([128, 2, 512], F32) for _ in range(2)]; b_bpows = [Buf("bpow0"), Buf("bpow1")]
    Mps = [AR.alloc([128, 2, 8, 128], BF16) for _ in range(2)]; b_Mps = [Buf("Mp0"), Buf("Mp1")]
    tmpb, b_tmpb = tmpbs[0], b_tmpbs[0]
    Cp = AR.alloc([128, 2, 8, 128], BF16); b_Cp = Buf("Cp")
    bmask = AR.alloc([128, 128], F32); b_bmask = Buf("bmask")
    diagD = AR.alloc([128, 4, 128], F32); b_diagD = Buf("diagD")
    tzt = AR.alloc([128, 128], F32); b_tzt = Buf("tzt")

    dma("dsync", braw[:, 0, :], bre_d[:, :], writes=[b_braw])
    dma("dsync", braw[:, 1, :], bim_d[:, :], writes=[b_braw])
    dma("dsync", bmask[:], bmask_d[:, :], writes=[b_bmask])
    def v3(ap2):
        return ap2.rearrange("p (a c) -> p a c", c=16)

    def bc32(ap32):
        return ap32.unsqueeze(2).to_broadcast([128, 32, 16])

    def cmul(dst, b_dst, src, b_src, cre, cim, rd, tmpb=tmpb, b_tmpb=b_tmpb):
        RB = [b_src, b_tmpb] + rd
        tt("dve", v3(tmpb[:, 0, :]), v3(src[:, 0, :]), bc32(cre), ALU.mult, RB, [b_tmpb])
        tt("dve", v3(tmpb[:, 1, :]), v3(src[:, 1, :]), bc32(cim), ALU.mult, RB, [b_tmpb])
        tt("dve", dst[:, 0, :], tmpb[:, 0, :], tmpb[:, 1, :], ALU.subtract, [b_tmpb], [b_dst])
        tt("dve", v3(tmpb[:, 0, :]), v3(src[:, 1, :]), bc32(cre), ALU.mult, RB, [b_tmpb])
        tt("dve", v3(tmpb[:, 1, :]), v3(src[:, 0, :]), bc32(cim), ALU.mult, RB, [b_tmpb])
        tt("dve", dst[:, 1, :], tmpb[:, 0, :], tmpb[:, 1, :], ALU.add, [b_tmpb], [b_dst])

    cmul(bbar, b_bbar, braw, b_braw, P(P_CR), P(P_CI), [b_prm])

    def pack(dst, b_dst, src, b_src, neg_im):
        for ri in range(2):
            for e_ in range(2):
                ps_ = slice(e_ * 64, (e_ + 1) * 64)
                s_ = src[ps_, ri, :].rearrange("p (k r c) -> p k r c", r=4, c=16)
                d_ap = dst[ps_, ri, :, :].rearrange("p k (r x) -> p k r x", x=32)[:, :, :, 16 * e_:16 * e_ + 16]
                if neg_im and ri == 1:
                    acopy(d_ap, s_, [b_src], [b_dst], scale=-1.0)
                else:
                    acopy(d_ap, s_, [b_src], [b_dst])

    for Mp, b_Mp in zip(Mps, b_Mps):
        S.op("pool", lambda e, Mp=Mp: e.memset(Mp[:], 0.0), (), [b_Mp])
    S.op("pool", lambda e: e.memset(Cp[:], 0.0), (), [b_Cp])
    pack(Cp, b_Cp, craw, b_craw, True)
    for c in range(4):
        ts("dve", diagD[:, c, :], ident_f[:], dsk[:, c:c + 1], None, ALU.mult, reads=[b_tmp, b_dsk], writes=[b_diagD])
    for k in range(8):
        bpow, b_bpow, Mp, b_Mp = bpows[k % 2], b_bpows[k % 2], Mps[k % 2], b_Mps[k % 2]
        cmul(bpow, b_bpow, bbar, b_bbar, PW[:, k, 0, :], PW[:, k, 1, :], [b_PW], tmpb=tmpbs[k % 2], b_tmpb=b_tmpbs[k % 2])
        pack(Mp, b_Mp, bpow, b_bpow, False)
        for c in range(4):
            if k == 0:
                pf, b_pf = next_psf()
                mm(pf[:, 0:128], [(Mp[:, ri, 4 * d_ + c, :], Cp[:, ri, 4 * d_ + c, :]) for d_ in range(2) for ri in range(2)],
                   [b_Mp, b_Cp], [b_pf])
                tt("dve", tzt[:], pf[:, 0:128], bmask[:], ALU.mult, [b_pf, b_bmask], [b_tzt])
                tt("dve", Tz[:, c, 7, :], tzt[:], diagD[:, c, :], ALU.add, [b_tzt, b_diagD], [b_Tz])
            else:
                for d_ in range(2):
                    pf, b_pf = next_psf()
                    mm(pf[:, 0:128], [(Mp[:, ri, 4 * d_ + c, :], Cp[:, ri, 4 * d_ + c, :]) for ri in range(2)],
                       [b_Mp, b_Cp], [b_pf])
                    idx = 7 + k if d_ == 0 else 7 - k
                    tt("dve", Tz[:, c, idx, :], pf[:, 0:128], bmask[:], ALU.mult, [b_pf, b_bmask], [b_Tz])
        for d_ in range(2):
            j = 7 - k if d_ == 0 else k
            pb, b_pb = next_psb()
            transposes([(pb[:, (ri * 4 + kk) * 128:(ri * 4 + kk + 1) * 128], Mp[:, ri, 4 * d_ + kk, :])
                        for ri in range(2) for kk in range(4)], ident[:], [b_Mp, b_ident], [b_pb])
            acopy(BbTc[:, j, :, 4 * d_:4 * d_ + 4, :], pb[:].rearrange("p (r k x) -> p r k x", r=2, k=4), [b_pb], [b_BbTc])

    if stop <= 2:
        return finish()
    S.barrier()
    AR.reset(mB)
    Xd = AR.alloc([128, 2, 16, NO], BF16); b_Xdc = [Buf(f"Xd{c}") for c in range(4)]
    Cts = [AR.alloc([128, NA], F32) for _ in range(2)]; Sns = [AR.alloc([128, NA], F32) for _ in range(2)]
    b_tabs = [Buf("tab0"), Buf("tab1")]
    bufA = AR.alloc([128, NA], F32); bufI = AR.alloc([128, NA], I32); b_bufA = Buf("bufA"); b_bufI = Buf("bufI")

    def gen_table(cq, Nd, par):
        Ct_, Sn_, b_t = Cts[par], Sns[par], b_tabs[par]
        acopy(bufA[:, 0:Nd], sidx[:, 0:Nd], [b_sidx, b_prm], [b_bufA], func=AF.Copy, scale=prm[:, P_F8, cq:cq + 1])
        acopy(bufI[:, 0:Nd], bufA[:, 0:Nd], [b_bufA], [b_bufI])
        tt("dve", bufA[:, 0:Nd], bufA[:, 0:Nd], bufI[:, 0:Nd], ALU.subtract, [b_bufA, b_bufI], [b_bufA])
        acopy(Sn_[:, 0:Nd], bufA[:, 0:Nd], [b_bufA], [b_t], func=AF.Sin, scale=6.283185)
        acopy(Ct_[:, 0:Nd], bufA[:, 0:Nd], [b_bufA], [b_t], func=AF.Sin, scale=3.141592)
        acopy(Ct_[:, 0:Nd], Ct_[:, 0:Nd], [b_t], [b_t], func=AF.Square)
        acopy(Ct_[:, 0:Nd], Ct_[:, 0:Nd], [b_t, b_onec], [b_t], func=AF.Identity, scale=-2.0, bias=onec[:, 0:1])
    Wr = AR.alloc([128, NA], BF16); Wi = AR.alloc([128, NA], BF16); b_W = Buf("W")
    Zr = AR.alloc([128, NA], BF16); Zi = AR.alloc([128, NA], BF16); b_Z = Buf("Z")
    mt = [AR.alloc([128, 512], F32) for _ in range(4)]; b_mt = [Buf(f"mt{i}") for i in range(4)]
    cl = AR.alloc([128, 2, 256], F32); b_cl = Buf("cl")
    clt = AR.alloc([128, 2, 256], F32); b_clt = Buf("clt")

    def v3h(ap2):
        return ap2.rearrange("p (a c) -> p a c", c=16)

    b_ysi = [[Buf(f"ysJ{c}_{i}") for i in range(8)] for c in range(4)]
    for c in range(4):
        for i in range(8):
            py, b_py = next_psf()
            mm(py[:, 0:NO], [(Tz[:, c, i - j + 7, :], uJ[:, c, j, 0:NO]) for j in range(8)], [b_Tz, b_uT[c]], [b_py])
            acopy(ysJ[:, c, i, :], py[:, 0:NO], [b_py], [b_ysi[c][i]])
    for d_ in range(2):
        Nd = NO if d_ == 0 else NA
        S.op("pool", lambda e: e.memset(CbTc[:], 0.0), (), [b_CbTc])
        for i in range(8):
            pw = i + 1 if d_ == 0 else 8 - i
            lr = PW[:, pw, 0, 16 * d_:16 * d_ + 16].unsqueeze(2).to_broadcast([128, 16, 16])
            li = PW[:, pw, 1, 16 * d_:16 * d_ + 16].unsqueeze(2).to_broadcast([128, 16, 16])
            c_re = v3h(craw[:, 0, 256 * d_:256 * d_ + 256]); c_im = v3h(craw[:, 1, 256 * d_:256 * d_ + 256])
            RC = [b_craw, b_PW, b_clt]
            tt("dve", v3h(clt[:, 0, :]), c_re, lr, ALU.mult, RC, [b_clt])
            tt("dve", v3h(clt[:, 1, :]), c_im, li, ALU.mult, RC, [b_clt])
            tt("dve", cl[:, 0, :], clt[:, 0, :], clt[:, 1, :], ALU.subtract, [b_clt], [b_cl])
            tt("dve", v3h(clt[:, 0, :]), c_re, li, ALU.mult, RC, [b_clt])
            tt("dve", v3h(clt[:, 1, :]), c_im, lr, ALU.mult, RC, [b_clt])
            stt(cl[:, 1, :], clt[:, 0, :], -1.0, clt[:, 1, :], ALU.mult, ALU.subtract, [b_clt], [b_cl])
            for ri in range(2):
                for e_ in range(2):
                    ps_ = slice(e_ * 64, (e_ + 1) * 64)
                    acopy(CbTc[ps_, i, ri, :, 16 * e_:16 * e_ + 16], v3h(cl[ps_, ri, :]), [b_cl], [b_CbTc])
        if d_ == 0:
            S.op("pool", lambda e: e.memset(Xd[:, :, :, 0:1], 0.0), (), list(b_Xdc))
        for q in range(16):
            cq = d_ * 16 + q
            ch, r4 = q // 4, q % 4
            rows = slice(32 * r4, 32 * r4 + 32)
            par = q % 2
            if q == 0:
                gen_table(cq, Nd, par)
            Ct, Sn, b_tab = Cts[par], Sns[par], b_tabs[par]
            pr, b_pr = next_psf()
            pi_, b_pi = next_psf()
            for ri, (pv, b_pv) in enumerate(((pr, b_pr), (pi_, b_pi))):
                def f(e, ri=ri, pv=pv, rows=rows, ch=ch, d_=d_, Nd=Nd, r4=r4):
                    ins = None
                    for j in range(8):
                        ins = e.matmul(pv[:, 0:Nd], BbTc[rows, j, ri, 4 * d_ + ch, :], uJ[rows, ch, j, 0:Nd],
                                       start=(j == 0), stop=(j == 7), tile_position=(32 * r4, 0))
                    return ins
                S.op("pe", f, [b_BbTc, b_uT[ch]], [b_pv], cost=0.3 + 8 * Nd / 2400.0)
            if d_ == 0:
                cs, sn_, wr, wi = Ct[:, 0:Nd], Sn[:, 0:Nd], Wr[:, 0:Nd], Wi[:, 0:Nd]
            else:
                cs, sn_ = Ct[:, 0:Nd][:, ::-1], Sn[:, 0:Nd][:, ::-1]
                wr, wi = Wr[:, 0:Nd][:, ::-1], Wi[:, 0:Nd][:, ::-1]
            tt("dve", mt[0][:, 0:Nd], pr[:, 0:Nd], cs, ALU.mult, [b_pr, b_tab], [b_mt[0]])
            tt("dve", mt[1][:, 0:Nd], pi_[:, 0:Nd], sn_, ALU.mult, [b_pi, b_tab], [b_mt[1]])
            tt("dve", mt[2][:, 0:Nd], pi_[:, 0:Nd], cs, ALU.mult, [b_pi, b_tab], [b_mt[2]])
            tt("dve", mt[3][:, 0:Nd], pr[:, 0:Nd], sn_, ALU.mult, [b_pr, b_tab], [b_mt[3]])
            tt("dve", wr, mt[0][:, 0:Nd], mt[1][:, 0:Nd], ALU.add, [b_mt[0], b_mt[1]], [b_W])
            tt("dve", wi, mt[2][:, 0:Nd], mt[3][:, 0:Nd], ALU.subtract, [b_mt[2], b_mt[3]], [b_W])
            if q + 1 < 16:
                gen_table(cq + 1, Nd, (q + 1) % 2)
            rho = prm[:, P_R8, cq:cq + 1].to_broadcast([128, Nd])
            S.op("dve", lambda e, rho=rho, Nd=Nd: e.tensor_tensor_scan(out=Zr[:, 0:Nd], data0=rho, data1=Wr[:, 0:Nd],
                 initial=0.0, op0=ALU.mult, op1=ALU.add), [b_W, b_prm], [b_Z], cost=0.15 + 2 * Nd / 960.0)
            S.op("dve", lambda e, rho=rho, Nd=Nd: e.tensor_tensor_scan(out=Zi[:, 0:Nd], data0=rho, data1=Wi[:, 0:Nd],
                 initial=0.0, op0=ALU.mult, op1=ALU.add), [b_W, b_prm], [b_Z], cost=0.15 + 2 * Nd / 960.0)
            if d_ == 0:
                n = NO - 1
                zr, zi, cs, sn_ = Zr[:, 0:n], Zi[:, 0:n], Ct[:, 0:n], Sn[:, 0:n]
                xr, xi = Xd[:, 0, q, 1:NO], Xd[:, 1, q, 1:NO]
            else:
                n = NO
                lo = NA - 1 - NO
                zr, zi = Zr[:, lo:lo + n][:, ::-1], Zi[:, lo:lo + n][:, ::-1]
                cs, sn_ = Ct[:, lo:lo + n][:, ::-1], Sn[:, lo:lo + n][:, ::-1]
                xr, xi = Xd[:, 0, q, 0:NO], Xd[:, 1, q, 0:NO]
            RZ = [b_Z, b_tab]
            tt("dve", mt[3][:, 0:n], zr, sn_, ALU.mult, RZ, [b_mt[3]])
            tt("dve", mt[0][:, 0:n], zr, cs, ALU.mult, RZ, [b_mt[0]])
            tt("dve", mt[1][:, 0:n], zi, sn_, ALU.mult, RZ, [b_mt[1]])
            tt("dve", mt[2][:, 0:n], zi, cs, ALU.mult, RZ, [b_mt[2]])
            tt("dve", xr, mt[0][:, 0:n], mt[1][:, 0:n], ALU.subtract, [b_mt[0], b_mt[1]], [b_Xdc[ch]])
            tt("dve", xi, mt[2][:, 0:n], mt[3][:, 0:n], ALU.add, [b_mt[2], b_mt[3]], [b_Xdc[ch]])
        for c in range(4):
            for i in range(8):
                py, b_py = next_psf()

                def f(e, c=c, i=i, py=py):
                    ins = None
                    for r4 in range(4):
                        for ri in range(2):
                            ins = e.matmul(py[32 * r4:32 * r4 + 32, 0:NO], CbTc[:, i, ri, 4 * c + r4, :],
                                           Xd[:, ri, 4 * c + r4, 0:NO], start=(ri == 0), stop=(ri == 1),
                                           tile_position=(0, 32 * r4), skip_group_check=True)
                    return ins
                S.op("pe", f, [b_CbTc, b_Xdc[c]], [b_py], cost=0.3 + 8 * NO / 2400.0)
                tt("dve", ysJ[:, c, i, :], ysJ[:, c, i, :], py[:, 0:NO], ALU.add, [b_py, b_ysi[c][i]], [b_ysi[c][i]])

    if dbg:
        for c in range(4):
            dma("dpool", dbg_d[c, :, 0:T_EXT], ysJ[:, c, :, :], reads=[b_ys[c]])

    if stop <= 3:
        return finish()
    if stop <= 4:
        return finish()
    AR_U = region(55.5, 87.5)
    qT = AR_U.alloc([128, 4, T_QKV], BF16); b_qT = Buf("qT")
    kT = AR_U.alloc([128, T_QKV], BF16); b_kT = Buf("kT")
    vaug = AR_U.alloc([128, 20, 2, 65], BF16); b_v = Buf("vaug")
    dma("dsync", kT[:, 0:T_RP], ks_d[:, 0:T_RP], reads=[b_ksd], writes=[b_kT] + b_uT)
    dma("dsync", vaug[:, 0:18, :, :].rearrange("p t g d -> p t (g d)"), vs_d[:, 0:18, :], reads=[b_vsd], writes=[b_v] + b_uT)
    for c in range(4):
        dma("dsync", qT[:, c, 0:T_EXT], qs_d[:, c, 0:T_EXT], reads=[b_qsd], writes=[b_qT] + b_uT)
    S.barrier()
    AR_H = region(173.5, 207.8)
    h2T = AR_H.alloc([128, 8, T_EXT], BF16); b_h2T = Buf("h2T")
    AR = region(87.5, 173.5)
    wout_bf = AR.alloc([128, 8, D], BF16); b_wout = Buf("wout")
    wglu_bf = AR.alloc([128, 4, 512], BF16); b_wglu = Buf("wglu")
    _w2 = AR.alloc([128, D], F32); _bw2 = Buf("wst2")
    wst2 = [_w2, _w2]; b_wst2 = [_bw2, _bw2]
    for kc in range(8):
        dma("dsync", wst2[kc % 2][:], wout_d[kc * 128:(kc + 1) * 128, :], writes=[b_wst2[kc % 2]])
        acopy(wout_bf[:, kc, :], wst2[kc % 2][:], [b_wst2[kc % 2]], [b_wout])
    for kc in range(4):
        dma("dsync", wst2[kc % 2][:, 0:512], wglu_d[kc * 128:(kc + 1) * 128, :], writes=[b_wst2[kc % 2]])
        acopy(wglu_bf[:, kc, :], wst2[kc % 2][:, 0:512], [b_wst2[kc % 2]], [b_wglu])
    PT = [[AR.alloc([128, 512], BF16) for _ in range(6)] for _ in range(2)]
    b_PT = [[Buf(f"PT{h}{i}") for i in range(6)] for h in range(2)]
    rden = [AR.alloc([128, 8], F32) for _ in range(2)]; b_rden = [Buf("rden0"), Buf("rden1")]
    attn_ = [AR.alloc([128, 512], F32) for _ in range(2)]; b_attn_ = [Buf("attn0"), Buf("attn1")]
    mixed_ = [AR.alloc([128, D], BF16) for _ in range(2)]; b_mixa = [Buf("mixa0"), Buf("mixa1")]; b_mixs = [Buf("mixs0"), Buf("mixs1")]
    mixT_ = [AR.alloc([128, 8, 128], BF16) for _ in range(2)]; b_mixT_ = [Buf("mixT0"), Buf("mixT1")]
    ysg_ = [AR.alloc([128, 4, 128], F32) for _ in range(2)]; b_ysg_ = [Buf("ysg0"), Buf("ysg1")]
    ysgb_ = [AR.alloc([128, 4, 128], BF16) for _ in range(2)]; b_ysgb_ = [Buf("ysgb0"), Buf("ysgb1")]
    sig_ = [AR.alloc([128, 4, 128], F32) for _ in range(2)]; b_sig_ = [Buf("sig0"), Buf("sig1")]
    ys2_ = [AR.alloc([128, 4, 128], BF16) for _ in range(2)]; b_ys2_ = [Buf("ys20"), Buf("ys21")]
    junk_ = [AR.alloc([128, 512], BF16) for _ in range(2)]; b_junk_ = [Buf("junk0"), Buf("junk1")]
    ysT_ = [AR.alloc([128, 512], BF16) for _ in range(2)]; b_ysT_ = [Buf("ysT0"), Buf("ysT1")]
    xc = [AR.alloc([128, D], F32) for _ in range(2)]; b_xc = [Buf("xc0"), Buf("xc1")]
    x1 = [AR.alloc([128, D], F32) for _ in range(2)]; b_x1 = [Buf("x1a"), Buf("x1b")]
    h2b = [AR.alloc([128, D], BF16) for _ in range(2)]; b_h2b = [Buf("h2b0"), Buf("h2b1")]
    b_x1d = [Buf(f"x1d{i}") for i in range(17)]
    po_ = {}

    def rstd_exp(sc, b_sc, n_feat):
        acopy(sc[:, 1:2], sc[:, 0:1], [b_sc, b_epsc], [b_sc], func=AF.Ln, scale=1.0 / n_feat, bias=epsc[:, 0:1])
        acopy(sc[:, 3:4], sc[:, 1:2], [b_sc], [b_sc], func=AF.Exp, scale=-0.5)

    def ssq_dve(junk, src, sc):
        return lambda e: e.scalar_tensor_tensor(out=junk, in0=src, scalar=1.0, in1=src, op0=ALU.mult, op1=ALU.mult,
                                                accum_out=sc[:, 0:1])

    def rms_to(dst, src, b_src_list, gain_off, junk, b_junk, b_dst):
        sc, b_sc = next_stat()
        S.op("dve", ssq_dve(junk[:, 0:512], src, sc), b_src_list, [b_junk, b_sc])
        rstd_exp(sc, b_sc, 512)
        stt(dst, src, sc[:, 3:4], gains[:, gain_off:gain_off + 512], ALU.mult, ALU.mult,
            b_src_list + [b_sc, b_gains], [b_dst])

    def s_scores(n):
        h = n % 2
        cols = slice(n * 128, (n + 1) * 128)
        for g in range(2):
            rows = slice(64 * g, 64 * g + 64)
            for kb in (n - 1, n, n + 1):
                if kb < 0:
                    continue
                slot = g * 3 + (kb - n + 1)
                pf, b_pf = next_psf()
                mm(pf[:].rearrange("p (j t) -> p j t", j=4),
                   [(kT[rows, kb * 128:(kb + 1) * 128], qT[rows, :, cols])], [b_kT, b_qT], [b_pf])
                acopy(PT[h][slot][:], pf[:], [b_pf], [b_PT[h][slot]], func=AF.Exp, scale=0.125)
                if kb != n:
                    m_ = msk[:, 0:128] if kb == n - 1 else msk[:, 128:256]
                    tt("dve", PT[h][slot][:].rearrange("p (j t) -> p j t", j=4),
                       PT[h][slot][:].rearrange("p (j t) -> p j t", j=4),
                       m_.unsqueeze(1).to_broadcast([128, 4, 128]), ALU.mult, [b_PT[h][slot], b_msk], [b_PT[h][slot]])

    def s_pv(n):
        h = n % 2
        for g in range(2):
            pts = [(kb, g * 3 + (kb - n + 1)) for kb in (n - 1, n, n + 1) if kb >= 0]
            pog, b_pog = next_psf()
            for j in range(4):
                mm(pog[:, j * 65:(j + 1) * 65],
                   [(PT[h][slot][:, j * 128:(j + 1) * 128], vaug[:, kb, g, :]) for (kb, slot) in pts],
                   [b_PT[h][s_] for (_, s_) in pts] + [b_v], [b_pog])
            o3 = pog[:, 0:260].rearrange("p (j d) -> p j d", j=4)
            tt("dve", rden[h][:, 4 * g:4 * g + 4], o3[:, :, 64], esink[:, 4 * g:4 * g + 4], ALU.add,
               [b_pog, b_esink], [b_rden[h]])
            S.op("dve", lambda e, g=g, h=h: e.reciprocal(out=rden[h][:, 4 * g:4 * g + 4], in_=rden[h][:, 4 * g:4 * g + 4]),
                 [b_rden[h]], [b_rden[h]])
            tt("dve", attn_[h][:, 256 * g:256 * g + 256].rearrange("p (j d) -> p j d", j=4), o3[:, :, 0:64],
               rden[h][:, 4 * g:4 * g + 4].unsqueeze(2).to_broadcast([128, 4, 64]), ALU.mult,
               [b_pog, b_rden[h]], [b_attn_[h]])
        rms_to(mixed_[h][:, 0:512], attn_[h][:], [b_attn_[h]], G_ATT, junk_[h], b_junk_[h], b_mixa[h])

    def s_ssm(n):
        h = n % 2
        for c in range(4):
            acopy(ysg_[h][:, c, :].rearrange("p (n i) -> p i n", i=8), ysJ[:, c, :, 16 * n:16 * n + 16], [b_ys[c]],
                  [b_ysg_[h]])
        acopy(ysgb_[h][:], ysg_[h][:], [b_ysg_[h]], [b_ysgb_[h]])
        for co in range(4):
            pf, b_pf = next_psf()
            mm(pf[:, 0:128], [(wglu_bf[:, kc, co * 128:(co + 1) * 128], ysgb_[h][:, kc, :]) for kc in range(4)],
               [b_wglu, b_ysgb_[h]], [b_pf])
            acopy(sig_[h][:, co, :], pf[:, 0:128], [b_pf], [b_sig_[h]], func=AF.Exp, scale=-1.0)
        acopy(sig_[h][:], sig_[h][:], [b_sig_[h], b_onec2], [b_sig_[h]], func=AF.Ln, bias=onec2[:, 0:1])
        acopy(sig_[h][:], sig_[h][:], [b_sig_[h]], [b_sig_[h]], func=AF.Exp, scale=-1.0)
        tt("dve", ys2_[h][:], ysg_[h][:], sig_[h][:], ALU.mult, [b_ysg_[h], b_sig_[h]], [b_ys2_[h]])
        pb, b_pb = next_psb()
        transposes([(pb[:, c * 128:(c + 1) * 128], ys2_[h][:, c, :]) for c in range(4)], ident[:],
                   [b_ys2_[h], b_ident], [b_pb])
        acopy(ysT_[h][:], pb[:, 0:512], [b_pb], [b_ysT_[h]])
        rms_to(mixed_[h][:, 512:1024], ysT_[h][:], [b_ysT_[h]], G_SSM, junk_[h], b_junk_[h], b_mixs[h])

    def s_out(n):
        h = n % 2
        pb, b_pb = next_psb()
        transposes([(pb[:, k * 128:(k + 1) * 128], mixed_[h][:, k * 128:(k + 1) * 128]) for k in range(8)],
                   ident[:], [b_mixa[h], b_mixs[h], b_ident], [b_pb])
        acopy(mixT_[h][:], pb[:].rearrange("p (k t) -> p k t", k=8), [b_pb], [b_mixT_[h]])
        dma("dsync", xc[h][:], x_d[n * 128:(n + 1) * 128, :], writes=[b_xc[h]])
        for hf in range(2):
            pf, b_pf = next_psf()
            mm(pf[:], [(mixT_[h][:, kc, :], wout_bf[:, kc, hf * 512:(hf + 1) * 512]) for kc in range(8)],
               [b_mixT_[h], b_wout], [b_pf])
            tt("dve", x1[h][:, hf * 512:(hf + 1) * 512], pf[:], xc[h][:, hf * 512:(hf + 1) * 512], ALU.add,
               [b_pf, b_xc[h]], [b_x1[h]])
        dma("dpool", x1_d[n * 128:(n + 1) * 128, :], x1[h][:], reads=[b_x1[h]], writes=[b_x1d[n]])
        norm_transpose(None, (), G_FFN, x1[h], b_x1[h], h2b[h], b_h2b[h], h2T, b_h2T, n * 128, from_dram=False)

    for c in range(4):
        acopy(ysJ[:, c, :, :], ysJ[:, c, :, :], [b_ys[c]] + b_ysi[c], [b_ys[c]], func=AF.Gelu)
    s_scores(0)
    s_ssm(0)
    for n in range(17):
        if n + 1 < 17:
            s_scores(n + 1)
        s_pv(n)
        if n + 1 < 17:
            s_ssm(n + 1)
        s_out(n)


    if stop <= 5:
        return finish()
    S.barrier()
    AR_ACT = region(21.5, 110)
    actT = AR_ACT.alloc([128, NPAIR, T_OWN], BF16); b_actT = Buf("actT")
    AR = region(110, 173.5)
    HT = T_OWN // 2
    NU = HT + 2
    up = [[AR.alloc([128, NU], F32) for _ in range(2)] for _ in range(2)]
    b_up = [[Buf(f"up{h}{g}") for g in range(2)] for h in range(2)]
    cv = [[AR.alloc([128, HT], F32) for _ in range(2)] for _ in range(2)]
    b_cv = [[Buf(f"cv{h}{g}") for g in range(2)] for h in range(2)]
    wus = [AR.alloc([128, 8, 128], F32) for _ in range(4)]; b_wus = [Buf(f"wus{i}") for i in range(4)]
    wub = [AR.alloc([128, 8, 128], BF16) for _ in range(4)]; b_wub = [Buf(f"wub{i}") for i in range(4)]
    for g in range(2):
        S.op("pool", lambda e, g=g: e.memset(up[0][g][:, 0:1], 0.0), (), [b_up[0][g]])
    def load_pair(p):
        for gv in range(2):
            wi = (p % 2) * 2 + gv
            c0 = gv * DFF + p * 128
            dma("dsync", wus[wi][:], wup_d[:, c0:c0 + 128].rearrange("(k p) f -> p k f", p=128), writes=[b_wus[wi]])
            if gv == 0:
                acopy(wub[wi][:], wus[wi][:], [b_wus[wi]], [b_wub[wi]])
            else:
                tcopy("dve", wub[wi][:], wus[wi][:], [b_wus[wi]], [b_wub[wi]])

    def stage_a(p, hf):
        for gv in range(2):
            wi = (p % 2) * 2 + gv
            ch = gv * NPAIR + p
            u_, b_u = up[hf][gv], b_up[hf][gv]
            if hf == 0:
                segs = [(0, 512, 1), (512, 512, 513), (1024, 1, 1025)]
            else:
                segs = [(1023, 1, 0), (1024, 512, 1), (1536, 512, 513), (2048, 1, 1025)]
            for si, (t0, n, dc) in enumerate(segs):
                pf, b_pf = next_psf()
                mm(pf[:, 0:n], [(wub[wi][:, kc, :], h2T[:, kc, t0:t0 + n]) for kc in range(8)],
                   [b_wub[wi], b_h2T], [b_pf])
                acopy(u_[:, dc:dc + n], pf[:, 0:n], [b_pf], [b_u])
            c_, b_c = cv[hf][gv], b_cv[hf][gv]
            acopy(c_[:], u_[:, 1:1 + HT], [b_u, b_cw, b_cb], [b_c], func=AF.Identity,
                  scale=cw[:, 44 + ch:44 + ch + 1], bias=cb[:, ch:ch + 1])
            stt(c_[:], u_[:, 0:HT], cw[:, ch:ch + 1], c_[:], ALU.mult, ALU.add, [b_u, b_cw, b_c], [b_c])
            stt(c_[:], u_[:, 2:2 + HT], cw[:, 88 + ch:88 + ch + 1], c_[:], ALU.mult, ALU.add, [b_u, b_cw, b_c], [b_c])

    def stage_b(p, hf):
        acopy(cv[hf][0][:], cv[hf][0][:], [b_cv[hf][0]], [b_cv[hf][0]], func=AF.Silu)
        tt("dve", actT[:, p, hf * HT:(hf + 1) * HT], cv[hf][0][:], cv[hf][1][:], ALU.mult,
           [b_cv[hf][0], b_cv[hf][1]], [b_actT])

    load_pair(0)
    for p in range(NPAIR):
        if p + 1 < NPAIR:
            load_pair(p + 1)
        stage_a(p, 0)
        if p > 0:
            stage_b(p - 1, 1)
        stage_a(p, 1)
        stage_b(p, 0)
    stage_b(NPAIR - 1, 1)

    S.barrier()
    AR = region(110, 207.8)
    wdn_bf = AR.alloc([128, NPAIR, D], BF16); b_wdnp = [Buf(f"wdn{p}") for p in range(NPAIR)]
    wds = [AR.alloc([128, D], F32) for _ in range(4)]; b_wds = [Buf(f"wds{i}") for i in range(4)]
    for p in range(NPAIR):
        dma("dsync", wds[p % 4][:], wdn_d[p * 128:(p + 1) * 128, :], writes=[b_wds[p % 4]])
        if p % 2 == 0:
            acopy(wdn_bf[:, p, :], wds[p % 4][:], [b_wds[p % 4]], [b_wdnp[p]])
        else:
            tcopy("dve", wdn_bf[:, p, :], wds[p % 4][:], [b_wds[p % 4]], [b_wdnp[p]])
    x1r = [AR.alloc([128, D], F32) for _ in range(2)]; b_x1r = [Buf("x1r0"), Buf("x1r1")]
    x2 = [AR.alloc([128, D], F32) for _ in range(2)]; b_x2 = [Buf("x2a"), Buf("x2b")]
    yo = [AR.alloc([128, D], F32) for _ in range(2)]; b_yo = [Buf("yo0"), Buf("yo1")]
    jk = AR.alloc([128, D], BF16); b_jk = Buf("jk")
    NE = 3
    early = [[next_psf() for _ in range(2)] for _ in range(NE)]
    for p in range(NPAIR):
        for n in range(NE):
            for hf in range(2):
                pf, b_pf = early[n][hf]
                S.op("pe", lambda e, pf=pf, p=p, n=n, hf=hf: e.matmul(
                    pf[:], actT[:, p, n * 128:(n + 1) * 128], wdn_bf[:, p, hf * 512:(hf + 1) * 512],
                    start=(p == 0), stop=(p == NPAIR - 1)), [b_actT, b_wdnp[p]], [b_pf], cost=0.25)
    for n in range(16):
        i2 = n % 2
        dma("dsync", x1r[i2][:], x1_d[n * 128:(n + 1) * 128, :], reads=[b_x1d[n]], writes=[b_x1r[i2]])
        for hf in range(2):
            if n < NE:
                pf, b_pf = early[n][hf]
            else:
                pf, b_pf = next_psf()
                mm(pf[:], [(actT[:, p, n * 128:(n + 1) * 128], wdn_bf[:, p, hf * 512:(hf + 1) * 512])
                           for p in range(NPAIR)], [b_actT] + b_wdnp, [b_pf])
            tt("dve", x2[i2][:, hf * 512:(hf + 1) * 512], pf[:], x1r[i2][:, hf * 512:(hf + 1) * 512], ALU.add,
               [b_pf, b_x1r[i2]], [b_x2[i2]])
        sc, b_sc = next_stat()
        acopy(jk[:], x2[i2][:], [b_x2[i2]], [b_jk, b_sc], func=AF.Square, accum=sc[:, 0:1])
        ts("dve", sc[:, 1:2], sc[:, 0:1], 1.0 / D, EPS, ALU.mult, ALU.add, [b_sc], [b_sc])
        acopy(sc[:, 2:3], sc[:, 1:2], [b_sc], [b_sc], func=AF.Sqrt)
        S.op("dve", lambda e, sc=sc: e.reciprocal(out=sc[:, 3:4], in_=sc[:, 2:3]), [b_sc], [b_sc])
        stt(yo[i2][:], x2[i2][:], sc[:, 3:4], gains[:, G_FIN:G_FIN + D], ALU.mult, ALU.mult,
            [b_x2[i2], b_sc, b_gains], [b_yo[i2]])
        dma("dpool", y_d[n * 128:(n + 1) * 128, :], yo[i2][:], reads=[b_yo[i2]])

    return finish()


def _consts():
    ident = np.eye(128, dtype=np.float32)
    R = np.zeros((128, 128), np.float32)
    for hh in range(2):
        for d in range(8):
            R[hh * 64 + d, hh * 64 + d + 8] = -1.0
            R[hh * 64 + d + 8, hh * 64 + d] = 1.0
    rmat = np.ascontiguousarray(R.T)
    ifr = np.zeros((128, 1), np.float32)
    base = np.power(np.float32(500000.0), -np.arange(8, dtype=np.float32) / np.float32(8.0)).astype(np.float32)
    for hh in range(2):
        for d in range(16):
            ifr[hh * 64 + d, 0] = base[d % 8]
    kk = np.arange(128)[:, None]
    qq = np.arange(128)[None, :]
    msk = np.concatenate([(kk >= qq), (kk <= qq)], axis=1).astype(np.float32)
    bm = (np.arange(128)[:, None] // 16 == np.arange(128)[None, :] // 16).astype(np.float32)
    return ident, rmat, ifr, msk, bm


_NC_CACHE = {}


def _prep_inputs(inp, dbg=False):
    f = lambda a: np.ascontiguousarray(np.asarray(a, dtype=np.float32))
    x = f(inp["x"])
    w_in = f(inp["w_in"][0])
    qcols = np.concatenate([np.r_[j * 64:(j + 1) * 64, (4 + j) * 64:(5 + j) * 64] for j in range(4)])
    w_in_r = np.ascontiguousarray(np.concatenate([w_in[:, qcols], w_in[:, 512:]], axis=1))
    gains = np.concatenate([f(inp["norm_mix_g"][0]), f(inp["norm_ffn_g"][0]), f(inp["norm_final_g"]),
                            f(inp["norm_attn_g"][0]), f(inp["norm_ssm_g"][0])])[None, :]
    ident, rmat, ifr, msk, bm = _consts()
    a_re, a_im = f(inp["a_re"][0]), f(inp["a_im"][0])
    lst = np.broadcast_to(f(inp["log_step"][0])[:, :, None], (2, 32, 64))
    b_re, b_im = f(inp["b_re"][0]), f(inp["b_im"][0])
    c_re, c_im = f(inp["c_re"][0]), f(inp["c_im"][0])
    cwf = f(inp["conv_w"][0])
    shared = dict(
        w_in=w_in_r, gains=np.ascontiguousarray(gains),
        dsk=np.ascontiguousarray(f(inp["d_skip"][0]).reshape(4, 128).T),
        w_glu=f(inp["w_glu"][0]), sink=f(inp["sink"]), w_out=f(inp["w_out"][0]), w_up=f(inp["w_up"][0]),
        cb=np.ascontiguousarray(f(inp["conv_b"][0]).reshape(44, 128).T),
        w_down=f(inp["w_down"][0]), ident=ident, rmat=rmat, ifr=ifr, msk=msk, bmask=bm,
        sidx=np.arange(512, dtype=np.float32)[None, :])

    def ep(a):
        return np.ascontiguousarray(a.reshape(2, 16, 2, 64).transpose(2, 3, 0, 1).reshape(128, 32))

    def ep_b(a):
        return np.ascontiguousarray(a.reshape(2, 16, 2, 64, 16).transpose(2, 3, 0, 1, 4).reshape(128, 512))

    def ep_c(a):
        return np.ascontiguousarray(a.reshape(2, 16, 2, 16, 64).transpose(2, 4, 0, 1, 3).reshape(128, 512))

    per_half = []
    for h in range(2):
        sl = slice(None) if h == 0 else slice(None, None, -1)
        cwh = cwf if h == 0 else cwf[::-1]
        pos = np.arange(T_QKV, dtype=np.float32) if h == 0 else (T_ALL - 1 - np.arange(T_QKV)).astype(np.float32)
        per_half.append(dict(
            are=ep(a_re[sl]), aim=ep(a_im[sl]), ls=ep(lst[sl]),
            bre=ep_b(b_re[sl]), bim=ep_b(b_im[sl]), cre=ep_c(c_re[sl]), cim=ep_c(c_im[sl]),
            cw=np.ascontiguousarray(cwh.reshape(3, 44, 128).transpose(2, 0, 1).reshape(128, 132)),
            pos=np.ascontiguousarray(pos[None, :])))
    in_maps = []
    for c in range(8):
        b, h = c // 2, c % 2
        xs = x[b] if h == 0 else x[b][::-1]
        m = dict(shared)
        m.update(per_half[h])
        m["x"] = np.ascontiguousarray(xs)
        in_maps.append(m)
    return in_maps


def kernel(**inputs):
    in_maps = _prep_inputs(inputs)
    if "nc" not in _NC_CACHE:
        _NC_CACHE["nc"] = build_program()
    nc = _NC_CACHE["nc"]
    res = run_bass_kernel_spmd(nc, in_maps, core_ids=list(range(8)))
    out = np.empty((4, T_ALL, D), np.float32)
    for c in range(8):
        b, h = c // 2, c % 2
        y = np.asarray(res.results[c]["y"], dtype=np.float32)
        if h == 0:
            out[b, :T_OWN] = y
        else:
            out[b, T_OWN:] = y[::-1]
    return out
```

```python
import numpy as np
import concourse.bass as bass
import concourse.mybir as mybir
from concourse.bass_utils import run_bass_kernel_spmd

F32 = mybir.dt.float32
BF16 = mybir.dt.bfloat16
I32 = mybir.dt.int32
ALU = mybir.AluOpType
AF = mybir.ActivationFunctionType

D = 1024
T_ALL = 4096
T_OWN = 2048
T_EXT = 2176
T_QKV = 2560
NQKV_G = 5
DFF = 2816
NPAIR = 22
EPS = 1e-6
TWO_PI = float(2.0 * np.pi)
SB_BASE = 16512
import os as _os
_C = lambda k, d: float(_os.environ.get(k, d))
C_ACT_F, C_ACT_E = _C("KS_ACT_F", 0.2), _C("KS_ACT_E", 1000.0)
C_DVE_F, C_DVE_E = _C("KS_DVE_F", 0.15), _C("KS_DVE_E", 960.0)
C_PE_F, C_PE_E = _C("KS_PE_F", 0.3), _C("KS_PE_E", 2400.0)
C_DMA_F, C_DMA_B = _C("KS_DMA_F", 2.0), _C("KS_DMA_B", 150e3)
SB_END = 229376


class Buf:
    __slots__ = ("name", "writer", "readers", "excl")

    def __init__(self, name, excl=False):
        self.name = name
        self.writer = None
        self.readers = {}
        self.excl = excl


class Stream:
    def __init__(self, name, eng_name, inc, sem):
        self.name, self.eng_name, self.inc, self.sem, self.count = name, eng_name, inc, sem, 0


class Op:
    __slots__ = ("stream", "fn", "preds", "cost", "prog", "end", "start", "sidx", "nsucc", "bar_counts")

    def __init__(self, stream, fn, preds, cost, prog):
        self.stream, self.fn, self.preds, self.cost, self.prog = stream, fn, preds, cost, prog
        self.end = self.start = None
        self.sidx = None
        self.bar_counts = None


class Sched:
    ENGS = ("tensor", "vector", "scalar", "gpsimd", "sync")
    NDS = 12
    HOP = _C("KS_HOP", 0.45)
    NDP = 4

    def __init__(self, sems):
        names = [("pe", "tensor", 1), ("dve", "vector", 1), ("act", "scalar", 1), ("pool", "gpsimd", 1)]
        names += [(f"dsync{i}", "sync", 16) for i in range(self.NDS)]
        names += [(f"dpool{i}", "gpsimd", 16) for i in range(self.NDP)]
        assert len(sems) == len(names)
        self.streams = {n: Stream(n, e, inc, s) for (n, e, inc), s in zip(names, sems)}
        self.slots = {n: Buf("slot_" + n) for n in self.streams if n.startswith("ds") or n.startswith("dp")}
        self.rr = {"dsync": 0, "dpool": 0}
        self.ops = []
        self.last_barrier = None
        self.since_barrier = []

    def op(self, stream, fn, reads=(), writes=(), cost=0.5):
        if stream in self.rr:
            k = self.rr[stream]
            self.rr[stream] = (k + 1) % (self.NDS if stream == "dsync" else self.NDP)
            stream = f"{stream}{k}"
            writes = list(writes) + [self.slots[stream]]
        ex = [b for b in reads if b.excl]
        if ex:
            reads = [b for b in reads if not b.excl]
            writes = list(writes) + ex
        preds = set()
        for b in reads:
            if b.writer is not None:
                preds.add(b.writer)
        for b in writes:
            if b.writer is not None:
                preds.add(b.writer)
            for o in b.readers.values():
                preds.add(o)
        if self.last_barrier is not None:
            preds.add(self.last_barrier)
        o = Op(stream, fn, sorted(preds, key=lambda p_: p_.prog), cost, len(self.ops))
        self.ops.append(o)
        self.since_barrier.append(o)
        for b in reads:
            b.readers[id(o)] = o
        for b in writes:
            b.writer = o
            b.readers = {}
        return o

    def barrier(self):
        if not self.since_barrier:
            return
        b = Op(None, None, list(self.since_barrier) + ([self.last_barrier] if self.last_barrier else []), 0.0, len(self.ops))
        self.ops.append(b)
        self.last_barrier = b
        self.since_barrier = []

    def schedule(self):
        import heapq
        ops = self.ops
        npred = {id(o): len(o.preds) for o in ops}
        succ = {id(o): [] for o in ops}
        for o in ops:
            for p in o.preds:
                succ[id(p)].append(o)
        eng_free = {e: 0.0 for e in self.ENGS}
        eng_of = lambda o: self.streams[o.stream].eng_name
        heap = []

        def ready_time(o):
            return max([p.end for p in o.preds], default=0.0) + self.HOP

        def push(o):
            if o.stream is None:
                o.start = o.end = ready_time(o)
                release(o)
                return
            rt = ready_time(o)
            heapq.heappush(heap, (max(rt, eng_free[eng_of(o)]), o.prog, rt, o))

        def release(o):
            for s_ in succ[id(o)]:
                npred[id(s_)] -= 1
                if npred[id(s_)] == 0:
                    push(s_)
        for o in ops:
            if npred[id(o)] == 0:
                push(o)
        nsched = 0
        while heap:
            key, prog, rt, o = heapq.heappop(heap)
            e = eng_of(o)
            st_ = max(rt, eng_free[e])
            if st_ > key + 1e-9:
                heapq.heappush(heap, (st_, prog, rt, o))
                continue
            o.start = st_
            is_dma = self.streams[o.stream].inc == 16
            o.end = st_ + o.cost
            eng_free[e] = st_ + (0.08 if is_dma else o.cost)
            nsched += 1
            release(o)
        assert all(o.start is not None for o in ops), "scheduler: unscheduled ops (cycle?)"
        self.makespan = max(o.end for o in ops)

    def schedule_hlf(self):
        import heapq
        ops = self.ops
        succ = {id(o): [] for o in ops}
        for o in ops:
            for p in o.preds:
                succ[id(p)].append(o)
        bl = {}
        for o in reversed(ops):
            m = 0.0
            for s_ in succ[id(o)]:
                v = bl[id(s_)] + self.HOP
                if v > m:
                    m = v
            bl[id(o)] = o.cost + m
        npred = {id(o): len(o.preds) for o in ops}
        eng_of = lambda o: self.streams[o.stream].eng_name
        eng_free = {e: 0.0 for e in self.ENGS}
        cand = {e: [] for e in self.ENGS}
        avail = {e: [] for e in self.ENGS}

        def rtime(o):
            return (max(p.end for p in o.preds) + self.HOP) if o.preds else 0.0

        def release(o):
            for s_ in succ[id(o)]:
                npred[id(s_)] -= 1
                if npred[id(s_)] == 0:
                    add(s_)

        def add(o):
            if o.stream is None:
                o.start = o.end = max([p.end for p in o.preds], default=0.0)
                release(o)
            else:
                heapq.heappush(cand[eng_of(o)], (rtime(o), o.prog, o))
        for o in ops:
            if npred[id(o)] == 0:
                add(o)
        left = sum(1 for o in ops if o.stream is not None)
        while left:
            best_e, best_t = None, None
            for e in self.ENGS:
                while cand[e] and cand[e][0][0] <= eng_free[e] + 1e-9:
                    rt, pg, o = heapq.heappop(cand[e])
                    heapq.heappush(avail[e], (-bl[id(o)], pg, o))
                if avail[e]:
                    t_ = eng_free[e]
                elif cand[e]:
                    t_ = max(eng_free[e], cand[e][0][0])
                else:
                    continue
                if best_t is None or t_ < best_t:
                    best_e, best_t = e, t_
            e = best_e
            if not avail[e]:
                eng_free[e] = best_t
                continue
            _, pg, o = heapq.heappop(avail[e])
            o.start = best_t
            is_dma = self.streams[o.stream].inc == 16
            o.end = best_t + o.cost
            eng_free[e] = best_t + (0.08 if is_dma else o.cost)
            left -= 1
            release(o)
        assert all(o.start is not None for o in ops)

    def emit(self, block):
        self.barrier()
        if _C("KS_MODE", 1) >= 1:
            self.schedule_hlf()
        else:
            self.schedule()
        per_eng = {e: [] for e in self.ENGS}
        for o in self.ops:
            if o.stream is not None:
                per_eng[self.streams[o.stream].eng_name].append(o)
        for e in self.ENGS:
            per_eng[e].sort(key=lambda o: (o.start, o.prog))
            for o in per_eng[e]:
                st = self.streams[o.stream]
                st.count += 1
                o.sidx = st.count
        cnt = {n: 0 for n in self.streams}
        for o in self.ops:
            if o.stream is None:
                o.bar_counts = dict(cnt)
            else:
                cnt[o.stream] += 1
        final_counts = dict(cnt)
        progs = {e: [] for e in self.ENGS}
        for e in self.ENGS:
            seen = {}
            for o in per_eng[e]:
                need = {}
                for p in o.preds:
                    if p.stream is None:
                        for sn, c in p.bar_counts.items():
                            if c and need.get(sn, 0) < c:
                                need[sn] = c
                    else:
                        if p.stream == "pe" and o.stream == "pe":
                            continue
                        if need.get(p.stream, 0) < p.sidx:
                            need[p.stream] = p.sidx
                waits = []
                for sn, c in need.items():
                    if seen.get(sn, 0) >= c:
                        continue
                    seen[sn] = c
                    waits.append((self.streams[sn].sem, c * self.streams[sn].inc))
                progs[e].append((waits, o.fn, self.streams[o.stream].sem, self.streams[o.stream].inc))
            waits = [(self.streams[sn].sem, c * self.streams[sn].inc) for sn, c in final_counts.items()
                     if c and seen.get(sn, 0) < c]
            progs[e].append((waits, None, None, 0))

        def run(engname):
            def body(eh):
                for waits, fn, sem, inc in progs[engname]:
                    for (ws, wv) in waits:
                        eh.wait_ge(ws, wv)
                    if fn is not None:
                        fn(eh).then_inc(sem, inc)
            return body
        block.tensor(run("tensor"))
        block.vector(run("vector"))
        block.scalar(run("scalar"))
        block.gpsimd(run("gpsimd"))
        block.sync(run("sync"))


def build_program(dbg=False, stop=99):
    nc = bass.Bass("TRN2", target_bir_lowering=False)

    def din(name, shape, dt=F32):
        return nc.dram_tensor(name, list(shape), dt, kind="ExternalInput").ap()

    x_d = din("x", [T_ALL, D])
    pos_d = din("pos", [1, T_QKV])
    win_d = din("w_in", [D, 1280])
    gains_d = din("gains", [1, 4096])
    are_d = din("are", [128, 32])
    aim_d = din("aim", [128, 32])
    ls_d = din("ls", [128, 32])
    bre_d = din("bre", [128, 512])
    bim_d = din("bim", [128, 512])
    cre_d = din("cre", [128, 512])
    cim_d = din("cim", [128, 512])
    dsk_d = din("dsk", [128, 4])
    wglu_d = din("w_glu", [512, 512])
    sink_d = din("sink", [1, 8])
    wout_d = din("w_out", [D, D])
    wup_d = din("w_up", [D, 2 * DFF])
    cw_d = din("cw", [128, 3 * 44])
    cb_d = din("cb", [128, 44])
    wdn_d = din("w_down", [DFF, D])
    ident_d = din("ident", [128, 128])
    rmat_d = din("rmat", [128, 128])
    ifr_d = din("ifr", [128, 1])
    msk_d = din("msk", [128, 256])
    bmask_d = din("bmask", [128, 128])
    sidx_d = din("sidx", [1, 512])
    y_d = nc.dram_tensor("y", [T_OWN, D], F32, kind="ExternalOutput").ap()
    x1_d = nc.dram_tensor("x1_scr", [T_EXT, D], F32, kind="Internal").ap()
    qs_d = nc.dram_tensor("q_scr", [128, 4, T_QKV], BF16, kind="Internal").ap()
    ks_d = nc.dram_tensor("k_scr", [128, T_QKV], BF16, kind="Internal").ap()
    vs_d = nc.dram_tensor("v_scr", [128, 20, 130], BF16, kind="Internal").ap()
    dbg_d = None
    if dbg:
        dbg_d = nc.dram_tensor("dbg", [8, 128, 2560], F32, kind="ExternalOutput").ap()

    import contextlib
    _es = contextlib.ExitStack()
    sems = [_es.enter_context(nc.semaphore(f"sem{i}")) for i in range(4 + Sched.NDS + Sched.NDP)]
    S = Sched(sems)

    _sbn = [0]
    def finish():
        with nc.Block() as block:
            S.emit(block)
        _es.close()
        return nc

    class Arena:
        def __init__(self, lo, hi):
            self.lo, self.hi, self.cur, self.n = lo, hi, lo, 0

        def alloc(self, shape, dt):
            nbytes = int(np.prod(shape[1:])) * (4 if dt in (F32, I32) else 2)
            nbytes = (nbytes + 63) // 64 * 64
            off = self.cur
            assert off + nbytes <= self.hi, (shape, off, nbytes, self.hi)
            self.cur += nbytes
            _sbn[0] += 1
            return nc.alloc_sbuf_tensor_at(f"sb{_sbn[0]}", list(shape), dt, offset=off)

        def mark(self):
            return self.cur

        def reset(self, m):
            self.cur = m

    def region(lo_kb, hi_kb):
        return Arena(SB_BASE + int(lo_kb * 1024), min(SB_END, SB_BASE + int(hi_kb * 1024)))

    AR = region(0, 21.5)
    AR_YS = region(21.5, 55.5)
    _psn = [0]

    def psum(dt=F32):
        _psn[0] += 1
        return nc.alloc_psum_tensor(f"ps{_psn[0]}", [128, 512 if dt == F32 else 1024], dt)

    PSF = [(psum(F32), Buf(f"psf{i}", excl=True)) for i in range(6)]
    PSB = [(psum(BF16), Buf(f"psb{i}", excl=True)) for i in range(2)]
    _rr = {"f": 0, "b": 0}

    def next_psf():
        _rr["f"] = (_rr["f"] + 1) % len(PSF)
        return PSF[_rr["f"]]

    def next_psb():
        _rr["b"] = (_rr["b"] + 1) % len(PSB)
        return PSB[_rr["b"]]

    def fsz(ap):
        n = 1
        for d_ in ap.shape[1:]:
            n *= int(d_)
        return n

    def dma(q, out, in_, reads=(), writes=()):
        nbytes = fsz(out) * int(out.shape[0]) * 4
        S.op(q, lambda e: e.dma_start(out=out, in_=in_), reads, writes, cost=C_DMA_F + nbytes / C_DMA_B)

    def tcopy(st, out, in_, reads=(), writes=()):
        S.op(st, lambda e: e.tensor_copy(out=out, in_=in_), reads, writes,
             cost=(C_DVE_F + fsz(out) / C_DVE_E) * (4 if st == "pool" else 1))

    def acopy(out, in_, reads=(), writes=(), func=AF.Copy, scale=1.0, bias=None, accum=None):
        def f(e):
            kw = {}
            if bias is not None:
                kw["bias"] = bias
            if accum is not None:
                kw["accum_out"] = accum
            return e.activation(out=out, in_=in_, func=func, scale=scale, **kw)
        S.op("act", f, reads, writes, cost=C_ACT_F + fsz(out) / C_ACT_E)

    def tt(st, out, in0, in1, op, reads=(), writes=()):
        S.op(st, lambda e: e.tensor_tensor(out=out, in0=in0, in1=in1, op=op), reads, writes,
             cost=(C_DVE_F + fsz(out) / C_DVE_E) * (4 if st == "pool" else 1))

    def ts(st, out, in0, s1, s2, op0, op1=None, reads=(), writes=()):
        c_ = (C_DVE_F + fsz(out) / C_DVE_E) * (4 if st == "pool" else 1)
        if op1 is None:
            S.op(st, lambda e: e.tensor_scalar(out=out, in0=in0, scalar1=s1, scalar2=None, op0=op0), reads, writes, cost=c_)
        else:
            S.op(st, lambda e: e.tensor_scalar(out=out, in0=in0, scalar1=s1, scalar2=s2, op0=op0, op1=op1), reads, writes,
                 cost=c_)

    def stt(out, in0, scalar, in1, op0, op1, reads=(), writes=()):
        S.op("dve", lambda e: e.scalar_tensor_tensor(out=out, in0=in0, scalar=scalar, in1=in1, op0=op0, op1=op1),
             reads, writes, cost=C_DVE_F + fsz(out) / (C_DVE_E * 0.83))

    def mm(out, pairs, reads=(), writes=()):
        def f(e):
            ins = None
            n = len(pairs)
            for i, (l, r) in enumerate(pairs):
                ins = e.matmul(out, l, r, start=(i == 0), stop=(i == n - 1))
            return ins
        S.op("pe", f, reads, writes, cost=C_PE_F + sum(max(64, fsz(r)) / C_PE_E + 0.01 for (_, r) in pairs))

    def transposes(outs_ins, ident, reads=(), writes=()):
        def f(e):
            ins = None
            for (o, i_) in outs_ins:
                ins = e.transpose(o, i_, ident)
            return ins
        S.op("pe", f, reads, writes, cost=0.3 + 0.1 * len(outs_ins))

    gains = AR.alloc([128, 4096], F32); b_gains = Buf("gains")
    ident_f = AR.alloc([128, 128], F32)
    ident = AR.alloc([128, 128], BF16); b_ident = Buf("ident")
    rmat_f = AR.alloc([128, 128], F32)
    rmat = AR.alloc([128, 128], BF16); b_rmat = Buf("rmat")
    msk_f = AR.alloc([128, 256], F32)
    msk = AR.alloc([128, 256], BF16); b_msk = Buf("msk")
    ifr = AR.alloc([128, 1], F32); b_ifr = Buf("ifr")
    dsk = AR.alloc([128, 4], F32); b_dsk = Buf("dsk")
    esink = AR.alloc([128, 8], F32); b_esink = Buf("esink")
    cw = AR.alloc([128, 132], F32); b_cw = Buf("cw")
    cb = AR.alloc([128, 44], F32); b_cb = Buf("cb")
    epsc = AR.alloc([128, 1], F32); b_epsc = Buf("epsc")
    onec2 = AR.alloc([128, 1], F32); b_onec2 = Buf("onec2")
    stat = AR.alloc([128, 64], F32)
    b_stat = [Buf(f"stat{i}") for i in range(16)]
    _st = [0]

    def next_stat():
        _st[0] = (_st[0] + 1) % 16
        return stat[:, 4 * _st[0]:4 * _st[0] + 4], b_stat[_st[0]]

    b_tmp = Buf("ldtmp")
    S.op("dve", lambda e: e.memset(epsc[:], EPS), (), [b_epsc])
    S.op("dve", lambda e: e.memset(onec2[:], 1.0), (), [b_onec2])
    dma("dsync", gains[:], gains_d.partition_broadcast(128), writes=[b_gains])
    dma("dsync", ident_f[:], ident_d[:, :], writes=[b_tmp])
    dma("dsync", rmat_f[:], rmat_d[:, :], writes=[b_tmp])
    dma("dsync", msk_f[:], msk_d[:, :], writes=[b_tmp])
    dma("dsync", ifr[:], ifr_d[:, :], writes=[b_ifr])
    dma("dsync", dsk[:], dsk_d[:, :], writes=[b_dsk])
    dma("dsync", esink[:], sink_d.partition_broadcast(128), writes=[b_esink])
    dma("dsync", cw[:], cw_d[:, :], writes=[b_cw])
    dma("dsync", cb[:], cb_d[:, :], writes=[b_cb])
    tcopy("dve", ident[:], ident_f[:], [b_tmp], [b_ident])
    tcopy("dve", rmat[:], rmat_f[:], [b_tmp], [b_rmat])
    tcopy("dve", msk[:], msk_f[:], [b_tmp], [b_msk])
    acopy(esink[:], esink[:], [b_esink], [b_esink], func=AF.Exp)
    G_MIX, G_FFN, G_FIN, G_ATT, G_SSM = 0, 1024, 2048, 3072, 3584

    ysJ = AR_YS.alloc([128, 4, 8, T_EXT // 8], F32); b_ys = [Buf(f"ys{c}") for c in range(4)]

    def norm_transpose_group(tiles, exp_set=False):
        for (src, go, xt, b_xt, hb, b_hb, hT, b_hT, col0) in tiles:
            if src is not None:
                dma("dsync", xt[:], src, writes=[b_xt])
        scs = [next_stat() for _ in tiles]
        if exp_set:
            for (src, go, xt, b_xt, hb, b_hb, hT, b_hT, col0), (sc, b_sc) in zip(tiles, scs):
                S.op("dve", ssq_dve(hb[:], xt[:], sc), [b_xt], [b_hb, b_sc])
                rstd_exp(sc, b_sc, D)
        else:
            for (src, go, xt, b_xt, hb, b_hb, hT, b_hT, col0), (sc, b_sc) in zip(tiles, scs):
                acopy(hb[:], xt[:], [b_xt], [b_hb, b_sc], func=AF.Square, accum=sc[:, 0:1])
            for (sc, b_sc) in scs:
                ts("dve", sc[:, 1:2], sc[:, 0:1], 1.0 / D, EPS, ALU.mult, ALU.add, [b_sc], [b_sc])
            for (sc, b_sc) in scs:
                acopy(sc[:, 2:3], sc[:, 1:2], [b_sc], [b_sc], func=AF.Sqrt)
            for (sc, b_sc) in scs:
                S.op("dve", lambda e, sc=sc: e.reciprocal(out=sc[:, 3:4], in_=sc[:, 2:3]), [b_sc], [b_sc])
        for (src, go, xt, b_xt, hb, b_hb, hT, b_hT, col0), (sc, b_sc) in zip(tiles, scs):
            stt(hb[:], xt[:], sc[:, 3:4], gains[:, go:go + D], ALU.mult, ALU.mult, [b_xt, b_sc, b_gains], [b_hb])
        for (src, go, xt, b_xt, hb, b_hb, hT, b_hT, col0) in tiles:
            pb, b_pb = next_psb()
            transposes([(pb[:, k * 128:(k + 1) * 128], hb[:, k * 128:(k + 1) * 128]) for k in range(8)],
                       ident[:], [b_hb, b_ident], [b_pb])
            if not exp_set and (col0 // 128) % 2 == 1:
                tcopy("dve", hT[:, :, col0:col0 + 128], pb[:].rearrange("p (k t) -> p k t", k=8), [b_pb], [b_hT])
            else:
                acopy(hT[:, :, col0:col0 + 128], pb[:].rearrange("p (k t) -> p k t", k=8), [b_pb], [b_hT])

    def norm_transpose(src_ap, src_reads, gain_off, xt, b_xt, hb, b_hb, hT, b_hT, col0, from_dram=True, stop=99):
        norm_transpose_group([(src_ap if from_dram else None, gain_off, xt, b_xt, hb, b_hb, hT, b_hT, col0)],
                             exp_set=not from_dram)

    AR_P = region(196.25, 207.8)
    prm = AR_P.alloc([128, 21, 32], F32); b_prm = Buf("prm")
    (P_ARE, P_AIM, P_DT, P_ER, P_TH, P_C, P_S, P_LR, P_LI, P_T0, P_T1, P_T2, P_CR, P_CI, P_R8, P_T3,
     P_T4, P_T5, P_T6, P_T7, P_F8) = range(21)
    pri = AR_P.alloc([128, 32], I32)
    PW = AR_P.alloc([128, 9, 2, 32], F32); b_PW = Buf("PW")
    craw = AR_P.alloc([128, 2, 512], F32); b_craw = Buf("craw")
    sidx = AR_P.alloc([128, 512], F32); b_sidx = Buf("sidx")
    onec = AR_P.alloc([128, 1], F32); b_onec = Buf("onec")

    def P(i):
        return prm[:, i, :]

    dma("dsync", P(P_ARE), are_d[:, :], writes=[b_prm])
    dma("dsync", P(P_AIM), aim_d[:, :], writes=[b_prm])
    dma("dsync", P(P_DT), ls_d[:, :], writes=[b_prm])
    dma("dsync", craw[:, 0, :], cre_d[:, :], writes=[b_craw])
    dma("dsync", craw[:, 1, :], cim_d[:, :], writes=[b_craw])
    dma("dsync", sidx[:], sidx_d.partition_broadcast(128), writes=[b_sidx])
    S.op("pool", lambda e: e.memset(onec[:], 1.0), (), [b_onec])
    R, W_ = [b_prm], [b_prm]
    acopy(P(P_DT), P(P_DT), R, W_, func=AF.Exp)
    tt("dve", P(P_T0), P(P_ARE), P(P_DT), ALU.mult, R, W_)
    acopy(P(P_ER), P(P_T0), R, W_, func=AF.Exp)
    tt("dve", P(P_TH), P(P_AIM), P(P_DT), ALU.mult, R, W_)
    ts("dve", P(P_T0), P(P_TH), 1.0 / TWO_PI, None, ALU.mult, reads=R, writes=W_)
    tcopy("dve", pri[:], P(P_T0), R, W_)
    tcopy("dve", P(P_T0), pri[:], R, W_)
    stt(P(P_T1), P(P_T0), -TWO_PI, P(P_TH), ALU.mult, ALU.add, R, W_)
    ts("dve", P(P_T1), P(P_T1), float(np.pi), float(-np.pi), ALU.min, ALU.max, R, W_)
    acopy(P(P_S), P(P_T1), R, W_, func=AF.Sin)
    acopy(P(P_T2), P(P_T1), R, W_, func=AF.Sin, scale=0.5)
    tt("dve", P(P_T2), P(P_T2), P(P_T2), ALU.mult, R, W_)
    ts("dve", P(P_C), P(P_T2), -2.0, 1.0, ALU.mult, ALU.add, R, W_)
    tt("dve", P(P_LR), P(P_ER), P(P_C), ALU.mult, R, W_)
    tt("dve", P(P_LI), P(P_ER), P(P_S), ALU.mult, R, W_)
    tt("dve", P(P_T0), P(P_ER), P(P_ER), ALU.mult, R, W_)
    tt("dve", P(P_T0), P(P_T0), P(P_T0), ALU.mult, R, W_)
    tt("dve", P(P_R8), P(P_T0), P(P_T0), ALU.mult, R, W_)
    ts("dve", P(P_T0), P(P_LR), -1.0, None, ALU.add, reads=R, writes=W_)
    tt("dve", P(P_T1), P(P_ARE), P(P_ARE), ALU.mult, R, W_)
    tt("dve", P(P_T2), P(P_AIM), P(P_AIM), ALU.mult, R, W_)
    tt("dve", P(P_T1), P(P_T1), P(P_T2), ALU.add, R, W_)
    S.op("dve", lambda e: e.reciprocal(out=P(P_T3), in_=P(P_T1)), R, W_)
    tt("dve", P(P_T1), P(P_T0), P(P_ARE), ALU.mult, R, W_)
    tt("dve", P(P_T2), P(P_LI), P(P_AIM), ALU.mult, R, W_)
    tt("dve", P(P_T1), P(P_T1), P(P_T2), ALU.add, R, W_)
    tt("dve", P(P_CR), P(P_T1), P(P_T3), ALU.mult, R, W_)
    tt("dve", P(P_T1), P(P_LI), P(P_ARE), ALU.mult, R, W_)
    tt("dve", P(P_T2), P(P_T0), P(P_AIM), ALU.mult, R, W_)
    tt("dve", P(P_T1), P(P_T1), P(P_T2), ALU.subtract, R, W_)
    tt("dve", P(P_CI), P(P_T1), P(P_T3), ALU.mult, R, W_)
    ts("dve", P(P_T4), P(P_TH), 8.0 / TWO_PI, None, ALU.mult, reads=R, writes=W_)
    tcopy("dve", pri[:], P(P_T4), R, W_)
    tcopy("dve", P(P_T5), pri[:], R, W_)
    tt("dve", P(P_F8), P(P_T4), P(P_T5), ALU.subtract, R, W_)
    RP = [b_prm, b_PW]
    S.op("dve", lambda e: e.memset(PW[:, 0, 0, :], 1.0), (), [b_PW])
    S.op("dve", lambda e: e.memset(PW[:, 0, 1, :], 0.0), (), [b_PW])
    tcopy("dve", PW[:, 1, 0, :], P(P_LR), RP, [b_PW])
    tcopy("dve", PW[:, 1, 1, :], P(P_LI), RP, [b_PW])
    for k in range(2, 9):
        ar_, ai_ = PW[:, k - 1, 0, :], PW[:, k - 1, 1, :]
        tt("dve", P(P_T4), ar_, P(P_LR), ALU.mult, RP, W_)
        tt("dve", P(P_T5), ai_, P(P_LI), ALU.mult, RP, W_)
        tt("dve", PW[:, k, 0, :], P(P_T4), P(P_T5), ALU.subtract, RP, [b_PW])
        tt("dve", P(P_T6), ar_, P(P_LI), ALU.mult, RP, W_)
        tt("dve", P(P_T7), ai_, P(P_LR), ALU.mult, RP, W_)
        tt("dve", PW[:, k, 1, :], P(P_T6), P(P_T7), ALU.add, RP, [b_PW])


    if stop <= 0:
        return finish()
    AR_U = region(55.5, 87.5)
    uJ = AR_U.alloc([128, 4, 8, T_ALL // 8], BF16); b_uT = [Buf(f"uT{c}") for c in range(4)]
    AR = region(87.5, 196.25)
    w_in_bf = AR.alloc([128, 8, 1280], BF16); b_win = Buf("w_in")
    wst = [AR.alloc([128, 1280], F32) for _ in range(2)]; b_wst = [Buf("wst0"), Buf("wst1")]
    xts = [AR.alloc([128, D], F32) for _ in range(4)]; b_xts = [Buf(f"xt{i}") for i in range(4)]
    hbs = [AR.alloc([128, D], BF16) for _ in range(4)]; b_hbs = [Buf(f"hb{i}") for i in range(4)]
    hTs = [AR.alloc([128, 8, 512], BF16) for _ in range(2)]; b_hTs = [Buf("hT0"), Buf("hT1")]
    T_RP = 2304
    AR_T = region(21.5, 55.5)
    cosT = AR_T.alloc([128, T_RP], F32); sinT = AR_T.alloc([128, T_RP], F32); b_cs = Buf("cossin")
    angb = AR.alloc([128, 512], F32); angi = AR.alloc([128, 512], I32); b_ang = Buf("ang")
    qb = [AR.alloc([128, 512], BF16) for _ in range(2)]; b_qb = [Buf("qb0"), Buf("qb1")]
    rt2 = [[AR.alloc([128, 512], F32) for _ in range(2)] for _ in range(2)]
    b_rt2 = [[Buf(f"rt{a_}{b_}") for b_ in range(2)] for a_ in range(2)]
    qst = [AR.alloc([128, 4, 512], BF16) for _ in range(2)]; b_qst = [Buf("qst0"), Buf("qst1")]
    kst = [AR.alloc([128, 512], BF16) for _ in range(2)]; b_kst = [Buf("kst0"), Buf("kst1")]
    vst = [AR.alloc([128, 4, 130], BF16) for _ in range(2)]; b_vst = [Buf("vst0"), Buf("vst1")]
    b_qsd = Buf("q_scr"); b_ksd = Buf("k_scr"); b_vsd = Buf("v_scr")

    def load_w_in():
        for kc in range(8):
            dma("dsync", wst[kc % 2][:], win_d[kc * 128:(kc + 1) * 128, :], writes=[b_wst[kc % 2]])
            if kc % 2 == 0:
                acopy(w_in_bf[:, kc, :], wst[kc % 2][:], [b_wst[kc % 2]], [b_win])
            else:
                tcopy("dve", w_in_bf[:, kc, :], wst[kc % 2][:], [b_wst[kc % 2]], [b_win])

    load_w_in()
    for blk in range(5):
        c0 = blk * 512
        nb = min(512, T_RP - c0)
        cs_, sn_b = cosT[:, c0:c0 + nb], sinT[:, c0:c0 + nb]
        dma("dsync", angb[:, 0:nb], pos_d[:, c0:c0 + nb].partition_broadcast(128), writes=[b_ang])
        ts("dve", angb[:, 0:nb], angb[:, 0:nb], ifr[:, 0:1], None, ALU.mult, reads=[b_ang, b_ifr], writes=[b_ang])
        ts("dve", sn_b, angb[:, 0:nb], 1.0 / TWO_PI, None, ALU.mult, reads=[b_ang], writes=[b_cs])
        tcopy("dve", angi[:, 0:nb], sn_b, [b_cs], [b_ang])
        tcopy("dve", sn_b, angi[:, 0:nb], [b_ang], [b_cs])
        stt(angb[:, 0:nb], sn_b, -TWO_PI, angb[:, 0:nb], ALU.mult, ALU.add, [b_ang, b_cs], [b_ang])
        ts("dve", angb[:, 0:nb], angb[:, 0:nb], float(np.pi), float(-np.pi), ALU.min, ALU.max, [b_ang], [b_ang])
        acopy(sn_b, angb[:, 0:nb], [b_ang], [b_cs], func=AF.Sin)
        acopy(cs_, angb[:, 0:nb], [b_ang], [b_cs], func=AF.Sin, scale=0.5)
        tt("dve", cs_, cs_, cs_, ALU.mult, [b_cs], [b_cs])
        ts("dve", cs_, cs_, -2.0, 1.0, ALU.mult, ALU.add, [b_cs], [b_cs])
    for i_ in range(2):
        S.op("pool", lambda e, i_=i_: e.memset(vst[i_][:], 1.0), (), [b_vst[i_]])
    if stop <= 0.2:
        return finish()
    for g4 in range(8):
        hT, b_hT = hTs[g4 % 2], b_hTs[g4 % 2]
        norm_transpose_group([(x_d[(g4 * 4 + j) * 128:(g4 * 4 + j + 1) * 128, :], G_MIX, xts[j], b_xts[j], hbs[j], b_hbs[j],
                               hT, b_hT, j * 128) for j in range(4)])
        if stop <= 0.7:
            return finish()
        for c in range(4):
            pf, b_pf = next_psf()
            mm(pf[:], [(w_in_bf[:, kc, 768 + c * 128:768 + (c + 1) * 128], hT[:, kc, :]) for kc in range(8)],
               [b_win, b_hT], [b_pf])
            if stop <= 0.75:
                return finish()
            acopy(uJ[:, c, :, g4 * 64:(g4 + 1) * 64], pf[:].rearrange("p (n j) -> p j n", j=8), [b_pf], [b_uT[c]])
            if stop <= 0.8:
                return finish()
            if stop <= 0.85:
                return finish()
        if g4 < NQKV_G:
            nt = 4 if g4 < 4 else 2
            nn = nt * 128
            cols = slice(g4 * 512, g4 * 512 + nn)
            sp = g4 % 2
            for c in range(5):
                pf, b_pf = next_psf()
                mm(pf[:, 0:nn], [(w_in_bf[:, kc, c * 128:(c + 1) * 128], hT[:, kc, 0:nn]) for kc in range(8)],
                   [b_win, b_hT], [b_pf])
                i2 = c % 2
                acopy(qb[i2][:, 0:nn], pf[:, 0:nn], [b_pf], [b_qb[i2]])
                pr_, b_pr_ = next_psf()
                mm(pr_[:, 0:nn], [(rmat[:], qb[i2][:, 0:nn])], [b_rmat, b_qb[i2]], [b_pr_])
                ra, rb_, b_ra, b_rb = rt2[i2][0], rt2[i2][1], b_rt2[i2][0], b_rt2[i2][1]
                tt("dve", ra[:, 0:nn], pf[:, 0:nn], cosT[:, cols], ALU.mult, [b_pf, b_cs], [b_ra])
                tt("dve", rb_[:, 0:nn], pr_[:, 0:nn], sinT[:, cols], ALU.mult, [b_pr_, b_cs], [b_rb])
                if c < 4:
                    tt("dve", qst[sp][:, c, 0:nn], ra[:, 0:nn], rb_[:, 0:nn], ALU.add, [b_ra, b_rb], [b_qst[sp]])
                else:
                    tt("dve", kst[sp][:, 0:nn], ra[:, 0:nn], rb_[:, 0:nn], ALU.add, [b_ra, b_rb], [b_kst[sp]])
            for j in range(nt):
                pf, b_pf = next_psf()
                mm(pf[:, 0:128], [(hT[:, kc, j * 128:(j + 1) * 128], w_in_bf[:, kc, 640:768]) for kc in range(8)],
                   [b_win, b_hT], [b_pf])
                acopy(vst[sp][:, j, :].rearrange("p (g d) -> p g d", g=2)[:, :, 0:64],
                      pf[:, 0:128].rearrange("p (g d) -> p g d", g=2), [b_pf], [b_vst[sp]])
            dma("dpool", qs_d[:, :, cols], qst[sp][:, :, 0:nn], reads=[b_qst[sp]], writes=[b_qsd])
            dma("dpool", ks_d[:, cols], kst[sp][:, 0:nn], reads=[b_kst[sp]], writes=[b_ksd])
            dma("dpool", vs_d[:, g4 * 4:g4 * 4 + nt, :], vst[sp][:, 0:nt, :], reads=[b_vst[sp]], writes=[b_vsd])
        if stop <= 0.9:
            return finish()

    if stop <= 1:
        return finish()
    S.barrier()
    AR = region(87.5, 196.25)
    NO, NA = T_EXT // 8, T_ALL // 8
    Tz = AR.alloc([128, 4, 15, 128], BF16); b_Tz = Buf("Tz")
    BbTc = AR.alloc([128, 8, 2, 8, 128], BF16); b_BbTc = Buf("BbTc")
    CbTc = AR.alloc([128, 8, 2, 16, 32], BF16); b_CbTc = Buf("CbTc")
    mB = AR.mark()
    bbar = AR.alloc([128, 2, 512], F32); b_bbar = Buf("bbar")
    braw = AR.alloc([128, 2, 512], F32); b_braw = Buf("braw")
    tmpbs = [AR.alloc([128, 2, 512], F32) for _ in range(2)]; b_tmpbs = [Buf("tmpb0"), Buf("tmpb1")]
    bpows = [AR.alloc([128, 2, 512], F32) for _ in range(2)]; b_bpows = [Buf("bpow0"), Buf("bpow1")]
    Mps = [AR.alloc([128, 2, 8, 128], BF16) for _ in range(2)]; b_Mps = [Buf("Mp0"), Buf("Mp1")]
    tmpb, b_tmpb = tmpbs[0], b_tmpbs[0]
    Cp = AR.alloc([128, 2, 8, 128], BF16); b_Cp = Buf("Cp")
    bmask = AR.alloc([128, 128], F32); b_bmask = Buf("bmask")
    diagD = AR.alloc([128, 4, 128], F32); b_diagD = Buf("diagD")
    tzt = AR.alloc([128, 128], F32); b_tzt = Buf("tzt")

    dma("dsync", braw[:, 0, :], bre_d[:, :], writes=[b_braw])
    dma("dsync", braw[:, 1, :], bim_d[:, :], writes=[b_braw])
    dma("dsync", bmask[:], bmask_d[:, :], writes=[b_bmask])
    def v3(ap2):
        return ap2.rearrange("p (a c) -> p a c", c=16)

    def bc32(ap32):
        return ap32.unsqueeze(2).to_broadcast([128, 32, 16])

    def cmul(dst, b_dst, src, b_src, cre, cim, rd, tmpb=tmpb, b_tmpb=b_tmpb):
        RB = [b_src, b_tmpb] + rd
        tt("dve", v3(tmpb[:, 0, :]), v3(src[:, 0, :]), bc32(cre), ALU.mult, RB, [b_tmpb])
        tt("dve", v3(tmpb[:, 1, :]), v3(src[:, 1, :]), bc32(cim), ALU.mult, RB, [b_tmpb])
        tt("dve", dst[:, 0, :], tmpb[:, 0, :], tmpb[:, 1, :], ALU.subtract, [b_tmpb], [b_dst])
        tt("dve", v3(tmpb[:, 0, :]), v3(src[:, 1, :]), bc32(cre), ALU.mult, RB, [b_tmpb])
        tt("dve", v3(tmpb[:, 1, :]), v3(src[:, 0, :]), bc32(cim), ALU.mult, RB, [b_tmpb])
        tt("dve", dst[:, 1, :], tmpb[:, 0, :], tmpb[:, 1, :], ALU.add, [b_tmpb], [b_dst])

    cmul(bbar, b_bbar, braw, b_braw, P(P_CR), P(P_CI), [b_prm])

    def pack(dst, b_dst, src, b_src, neg_im):
        for ri in range(2):
            for e_ in range(2):
                ps_ = slice(e_ * 64, (e_ + 1) * 64)
                s_ = src[ps_, ri, :].rearrange("p (k r c) -> p k r c", r=4, c=16)
                d_ap = dst[ps_, ri, :, :].rearrange("p k (r x) -> p k r x", x=32)[:, :, :, 16 * e_:16 * e_ + 16]
                if neg_im and ri == 1:
                    acopy(d_ap, s_, [b_src], [b_dst], scale=-1.0)
                else:
                    acopy(d_ap, s_, [b_src], [b_dst])

    for Mp, b_Mp in zip(Mps, b_Mps):
        S.op("pool", lambda e, Mp=Mp: e.memset(Mp[:], 0.0), (), [b_Mp])
    S.op("pool", lambda e: e.memset(Cp[:], 0.0), (), [b_Cp])
    pack(Cp, b_Cp, craw, b_craw, True)
    for c in range(4):
        ts("dve", diagD[:, c, :], ident_f[:], dsk[:, c:c + 1], None, ALU.mult, reads=[b_tmp, b_dsk], writes=[b_diagD])
    for k in range(8):
        bpow, b_bpow, Mp, b_Mp = bpows[k % 2], b_bpows[k % 2], Mps[k % 2], b_Mps[k % 2]
        cmul(bpow, b_bpow, bbar, b_bbar, PW[:, k, 0, :], PW[:, k, 1, :], [b_PW], tmpb=tmpbs[k % 2], b_tmpb=b_tmpbs[k % 2])
        pack(Mp, b_Mp, bpow, b_bpow, False)
        for c in range(4):
            if k == 0:
                pf, b_pf = next_psf()
                mm(pf[:, 0:128], [(Mp[:, ri, 4 * d_ + c, :], Cp[:, ri, 4 * d_ + c, :]) for d_ in range(2) for ri in range(2)],
                   [b_Mp, b_Cp], [b_pf])
                tt("dve", tzt[:], pf[:, 0:128], bmask[:], ALU.mult, [b_pf, b_bmask], [b_tzt])
                tt("dve", Tz[:, c, 7, :], tzt[:], diagD[:, c, :], ALU.add, [b_tzt, b_diagD], [b_Tz])
            else:
                for d_ in range(2):
                    pf, b_pf = next_psf()
                    mm(pf[:, 0:128], [(Mp[:, ri, 4 * d_ + c, :], Cp[:, ri, 4 * d_ + c, :]) for ri in range(2)],
                       [b_Mp, b_Cp], [b_pf])
                    idx = 7 + k if d_ == 0 else 7 - k
                    tt("dve", Tz[:, c, idx, :], pf[:, 0:128], bmask[:], ALU.mult, [b_pf, b_bmask], [b_Tz])
        for d_ in range(2):
            j = 7 - k if d_ == 0 else k
            pb, b_pb = next_psb()
            transposes([(pb[:, (ri * 4 + kk) * 128:(ri * 4 + kk + 1) * 128], Mp[:, ri, 4 * d_ + kk, :])
                        for ri in range(2) for kk in range(4)], ident[:], [b_Mp, b_ident], [b_pb])
            acopy(BbTc[:, j, :, 4 * d_:4 * d_ + 4, :], pb[:].rearrange("p (r k x) -> p r k x", r=2, k=4), [b_pb], [b_BbTc])

    if stop <= 2:
        return finish()
    S.barrier()
    AR.reset(mB)
    Xd = AR.alloc([128, 2, 16, NO], BF16); b_Xdc = [Buf(f"Xd{c}") for c in range(4)]
    Cts = [AR.alloc([128, NA], F32) for _ in range(2)]; Sns = [AR.alloc([128, NA], F32) for _ in range(2)]
    b_tabs = [Buf("tab0"), Buf("tab1")]
    bufA = AR.alloc([128, NA], F32); bufI = AR.alloc([128, NA], I32); b_bufA = Buf("bufA"); b_bufI = Buf("bufI")

    def gen_table(cq, Nd, par):
        Ct_, Sn_, b_t = Cts[par], Sns[par], b_tabs[par]
        acopy(bufA[:, 0:Nd], sidx[:, 0:Nd], [b_sidx, b_prm], [b_bufA], func=AF.Copy, scale=prm[:, P_F8, cq:cq + 1])
        acopy(bufI[:, 0:Nd], bufA[:, 0:Nd], [b_bufA], [b_bufI])
        tt("dve", bufA[:, 0:Nd], bufA[:, 0:Nd], bufI[:, 0:Nd], ALU.subtract, [b_bufA, b_bufI], [b_bufA])
        acopy(Sn_[:, 0:Nd], bufA[:, 0:Nd], [b_bufA], [b_t], func=AF.Sin, scale=6.283185)
        acopy(Ct_[:, 0:Nd], bufA[:, 0:Nd], [b_bufA], [b_t], func=AF.Sin, scale=3.141592)
        acopy(Ct_[:, 0:Nd], Ct_[:, 0:Nd], [b_t], [b_t], func=AF.Square)
        acopy(Ct_[:, 0:Nd], Ct_[:, 0:Nd], [b_t, b_onec], [b_t], func=AF.Identity, scale=-2.0, bias=onec[:, 0:1])
    Wr = AR.alloc([128, NA], BF16); Wi = AR.alloc([128, NA], BF16); b_W = Buf("W")
    Zr = AR.alloc([128, NA], BF16); Zi = AR.alloc([128, NA], BF16); b_Z = Buf("Z")
    mt = [AR.alloc([128, 512], F32) for _ in range(4)]; b_mt = [Buf(f"mt{i}") for i in range(4)]
    cl = AR.alloc([128, 2, 256], F32); b_cl = Buf("cl")
    clt = AR.alloc([128, 2, 256], F32); b_clt = Buf("clt")

    def v3h(ap2):
        return ap2.rearrange("p (a c) -> p a c", c=16)

    b_ysi = [[Buf(f"ysJ{c}_{i}") for i in range(8)] for c in range(4)]
    for c in range(4):
        for i in range(8):
            py, b_py = next_psf()
            mm(py[:, 0:NO], [(Tz[:, c, i - j + 7, :], uJ[:, c, j, 0:NO]) for j in range(8)], [b_Tz, b_uT[c]], [b_py])
            acopy(ysJ[:, c, i, :], py[:, 0:NO], [b_py], [b_ysi[c][i]])
    for d_ in range(2):
        Nd = NO if d_ == 0 else NA
        S.op("pool", lambda e: e.memset(CbTc[:], 0.0), (), [b_CbTc])
        for i in range(8):
            pw = i + 1 if d_ == 0 else 8 - i
            lr = PW[:, pw, 0, 16 * d_:16 * d_ + 16].unsqueeze(2).to_broadcast([128, 16, 16])
            li = PW[:, pw, 1, 16 * d_:16 * d_ + 16].unsqueeze(2).to_broadcast([128, 16, 16])
            c_re = v3h(craw[:, 0, 256 * d_:256 * d_ + 256]); c_im = v3h(craw[:, 1, 256 * d_:256 * d_ + 256])
            RC = [b_craw, b_PW, b_clt]
            tt("dve", v3h(clt[:, 0, :]), c_re, lr, ALU.mult, RC, [b_clt])
            tt("dve", v3h(clt[:, 1, :]), c_im, li, ALU.mult, RC, [b_clt])
            tt("dve", cl[:, 0, :], clt[:, 0, :], clt[:, 1, :], ALU.subtract, [b_clt], [b_cl])
            tt("dve", v3h(clt[:, 0, :]), c_re, li, ALU.mult, RC, [b_clt])
            tt("dve", v3h(clt[:, 1, :]), c_im, lr, ALU.mult, RC, [b_clt])
            stt(cl[:, 1, :], clt[:, 0, :], -1.0, clt[:, 1, :], ALU.mult, ALU.subtract, [b_clt], [b_cl])
            for ri in range(2):
                for e_ in range(2):
                    ps_ = slice(e_ * 64, (e_ + 1) * 64)
                    acopy(CbTc[ps_, i, ri, :, 16 * e_:16 * e_ + 16], v3h(cl[ps_, ri, :]), [b_cl], [b_CbTc])
        if d_ == 0:
            S.op("pool", lambda e: e.memset(Xd[:, :, :, 0:1], 0.0), (), list(b_Xdc))
        for q in range(16):
            cq = d_ * 16 + q
            ch, r4 = q // 4, q % 4
            rows = slice(32 * r4, 32 * r4 + 32)
            par = q % 2
            if q == 0:
                gen_table(cq, Nd, par)
            Ct, Sn, b_tab = Cts[par], Sns[par], b_tabs[par]
            pr, b_pr = next_psf()
            pi_, b_pi = next_psf()
            for ri, (pv, b_pv) in enumerate(((pr, b_pr), (pi_, b_pi))):
                def f(e, ri=ri, pv=pv, rows=rows, ch=ch, d_=d_, Nd=Nd, r4=r4):
                    ins = None
                    for j in range(8):
                        ins = e.matmul(pv[:, 0:Nd], BbTc[rows, j, ri, 4 * d_ + ch, :], uJ[rows, ch, j, 0:Nd],
                                       start=(j == 0), stop=(j == 7), tile_position=(32 * r4, 0))
                    return ins
                S.op("pe", f, [b_BbTc, b_uT[ch]], [b_pv], cost=0.3 + 8 * Nd / 2400.0)
            if d_ == 0:
                cs, sn_, wr, wi = Ct[:, 0:Nd], Sn[:, 0:Nd], Wr[:, 0:Nd], Wi[:, 0:Nd]
            else:
                cs, sn_ = Ct[:, 0:Nd][:, ::-1], Sn[:, 0:Nd][:, ::-1]
                wr, wi = Wr[:, 0:Nd][:, ::-1], Wi[:, 0:Nd][:, ::-1]
            tt("dve", mt[0][:, 0:Nd], pr[:, 0:Nd], cs, ALU.mult, [b_pr, b_tab], [b_mt[0]])
            tt("dve", mt[1][:, 0:Nd], pi_[:, 0:Nd], sn_, ALU.mult, [b_pi, b_tab], [b_mt[1]])
            tt("dve", mt[2][:, 0:Nd], pi_[:, 0:Nd], cs, ALU.mult, [b_pi, b_tab], [b_mt[2]])
            tt("dve", mt[3][:, 0:Nd], pr[:, 0:Nd], sn_, ALU.mult, [b_pr, b_tab], [b_mt[3]])
            tt("dve", wr, mt[0][:, 0:Nd], mt[1][:, 0:Nd], ALU.add, [b_mt[0], b_mt[1]], [b_W])
            tt("dve", wi, mt[2][:, 0:Nd], mt[3][:, 0:Nd], ALU.subtract, [b_mt[2], b_mt[3]], [b_W])
            if q + 1 < 16:
                gen_table(cq + 1, Nd, (q + 1) % 2)
            rho = prm[:, P_R8, cq:cq + 1].to_broadcast([128, Nd])
            S.op("dve", lambda e, rho=rho, Nd=Nd: e.tensor_tensor_scan(out=Zr[:, 0:Nd], data0=rho, data1=Wr[:, 0:Nd],
                 initial=0.0, op0=ALU.mult, op1=ALU.add), [b_W, b_prm], [b_Z], cost=0.15 + 2 * Nd / 960.0)
            S.op("dve", lambda e, rho=rho, Nd=Nd: e.tensor_tensor_scan(out=Zi[:, 0:Nd], data0=rho, data1=Wi[:, 0:Nd],
                 initial=0.0, op0=ALU.mult, op1=ALU.add), [b_W, b_prm], [b_Z], cost=0.15 + 2 * Nd / 960.0)
            if d_ == 0:
                n = NO - 1
                zr, zi, cs, sn_ = Zr[:, 0:n], Zi[:, 0:n], Ct[:, 0:n], Sn[:, 0:n]
                xr, xi = Xd[:, 0, q, 1:NO], Xd[:, 1, q, 1:NO]
            else:
                n = NO
                lo = NA - 1 - NO
                zr, zi = Zr[:, lo:lo + n][:, ::-1], Zi[:, lo:lo + n][:, ::-1]
                cs, sn_ = Ct[:, lo:lo + n][:, ::-1], Sn[:, lo:lo + n][:, ::-1]
                xr, xi = Xd[:, 0, q, 0:NO], Xd[:, 1, q, 0:NO]
            RZ = [b_Z, b_tab]
            tt("dve", mt[3][:, 0:n], zr, sn_, ALU.mult, RZ, [b_mt[3]])
            tt("dve", mt[0][:, 0:n], zr, cs, ALU.mult, RZ, [b_mt[0]])
            tt("dve", mt[1][:, 0:n], zi, sn_, ALU.mult, RZ, [b_mt[1]])
            tt("dve", mt[2][:, 0:n], zi, cs, ALU.mult, RZ, [b_mt[2]])
            tt("dve", xr, mt[0][:, 0:n], mt[1][:, 0:n], ALU.subtract, [b_mt[0], b_mt[1]], [b_Xdc[ch]])
            tt("dve", xi, mt[2][:, 0:n], mt[3][:, 0:n], ALU.add, [b_mt[2], b_mt[3]], [b_Xdc[ch]])
        for c in range(4):
            for i in range(8):
                py, b_py = next_psf()

                def f(e, c=c, i=i, py=py):
                    ins = None
                    for r4 in range(4):
                        for ri in range(2):
                            ins = e.matmul(py[32 * r4:32 * r4 + 32, 0:NO], CbTc[:, i, ri, 4 * c + r4, :],
                                           Xd[:, ri, 4 * c + r4, 0:NO], start=(ri == 0), stop=(ri == 1),
                                           tile_position=(0, 32 * r4), skip_group_check=True)
                    return ins
                S.op("pe", f, [b_CbTc, b_Xdc[c]], [b_py], cost=0.3 + 8 * NO / 2400.0)
                tt("dve", ysJ[:, c, i, :], ysJ[:, c, i, :], py[:, 0:NO], ALU.add, [b_py, b_ysi[c][i]], [b_ysi[c][i]])

    if dbg:
        for c in range(4):
            dma("dpool", dbg_d[c, :, 0:T_EXT], ysJ[:, c, :, :], reads=[b_ys[c]])

    if stop <= 3:
        return finish()
    if stop <= 4:
        return finish()
    AR_U = region(55.5, 87.5)
    qT = AR_U.alloc([128, 4, T_QKV], BF16); b_qT = Buf("qT")
    kT = AR_U.alloc([128, T_QKV], BF16); b_kT = Buf("kT")
    vaug = AR_U.alloc([128, 20, 2, 65], BF16); b_v = Buf("vaug")
    dma("dsync", kT[:, 0:T_RP], ks_d[:, 0:T_RP], reads=[b_ksd], writes=[b_kT] + b_uT)
    dma("dsync", vaug[:, 0:18, :, :].rearrange("p t g d -> p t (g d)"), vs_d[:, 0:18, :], reads=[b_vsd], writes=[b_v] + b_uT)
    for c in range(4):
        dma("dsync", qT[:, c, 0:T_EXT], qs_d[:, c, 0:T_EXT], reads=[b_qsd], writes=[b_qT] + b_uT)
    S.barrier()
    AR_H = region(173.5, 207.8)
    h2T = AR_H.alloc([128, 8, T_EXT], BF16); b_h2T = Buf("h2T")
    AR = region(87.5, 173.5)
    wout_bf = AR.alloc([128, 8, D], BF16); b_wout = Buf("wout")
    wglu_bf = AR.alloc([128, 4, 512], BF16); b_wglu = Buf("wglu")
    _w2 = AR.alloc([128, D], F32); _bw2 = Buf("wst2")
    wst2 = [_w2, _w2]; b_wst2 = [_bw2, _bw2]
    for kc in range(8):
        dma("dsync", wst2[kc % 2][:], wout_d[kc * 128:(kc + 1) * 128, :], writes=[b_wst2[kc % 2]])
        acopy(wout_bf[:, kc, :], wst2[kc % 2][:], [b_wst2[kc % 2]], [b_wout])
    for kc in range(4):
        dma("dsync", wst2[kc % 2][:, 0:512], wglu_d[kc * 128:(kc + 1) * 128, :], writes=[b_wst2[kc % 2]])
        acopy(wglu_bf[:, kc, :], wst2[kc % 2][:, 0:512], [b_wst2[kc % 2]], [b_wglu])
    PT = [[AR.alloc([128, 512], BF16) for _ in range(6)] for _ in range(2)]
    b_PT = [[Buf(f"PT{h}{i}") for i in range(6)] for h in range(2)]
    rden = [AR.alloc([128, 8], F32) for _ in range(2)]; b_rden = [Buf("rden0"), Buf("rden1")]
    attn_ = [AR.alloc([128, 512], F32) for _ in range(2)]; b_attn_ = [Buf("attn0"), Buf("attn1")]
    mixed_ = [AR.alloc([128, D], BF16) for _ in range(2)]; b_mixa = [Buf("mixa0"), Buf("mixa1")]; b_mixs = [Buf("mixs0"), Buf("mixs1")]
    mixT_ = [AR.alloc([128, 8, 128], BF16) for _ in range(2)]; b_mixT_ = [Buf("mixT0"), Buf("mixT1")]
    ysg_ = [AR.alloc([128, 4, 128], F32) for _ in range(2)]; b_ysg_ = [Buf("ysg0"), Buf("ysg1")]
    ysgb_ = [AR.alloc([128, 4, 128], BF16) for _ in range(2)]; b_ysgb_ = [Buf("ysgb0"), Buf("ysgb1")]
    sig_ = [AR.alloc([128, 4, 128], F32) for _ in range(2)]; b_sig_ = [Buf("sig0"), Buf("sig1")]
    ys2_ = [AR.alloc([128, 4, 128], BF16) for _ in range(2)]; b_ys2_ = [Buf("ys20"), Buf("ys21")]
    junk_ = [AR.alloc([128, 512], BF16) for _ in range(2)]; b_junk_ = [Buf("junk0"), Buf("junk1")]
    ysT_ = [AR.alloc([128, 512], BF16) for _ in range(2)]; b_ysT_ = [Buf("ysT0"), Buf("ysT1")]
    xc = [AR.alloc([128, D], F32) for _ in range(2)]; b_xc = [Buf("xc0"), Buf("xc1")]
    x1 = [AR.alloc([128, D], F32) for _ in range(2)]; b_x1 = [Buf("x1a"), Buf("x1b")]
    h2b = [AR.alloc([128, D], BF16) for _ in range(2)]; b_h2b = [Buf("h2b0"), Buf("h2b1")]
    b_x1d = [Buf(f"x1d{i}") for i in range(17)]
    po_ = {}

    def rstd_exp(sc, b_sc, n_feat):
        acopy(sc[:, 1:2], sc[:, 0:1], [b_sc, b_epsc], [b_sc], func=AF.Ln, scale=1.0 / n_feat, bias=epsc[:, 0:1])
        acopy(sc[:, 3:4], sc[:, 1:2], [b_sc], [b_sc], func=AF.Exp, scale=-0.5)

    def ssq_dve(junk, src, sc):
        return lambda e: e.scalar_tensor_tensor(out=junk, in0=src, scalar=1.0, in1=src, op0=ALU.mult, op1=ALU.mult,
                                                accum_out=sc[:, 0:1])

    def rms_to(dst, src, b_src_list, gain_off, junk, b_junk, b_dst):
        sc, b_sc = next_stat()
        S.op("dve", ssq_dve(junk[:, 0:512], src, sc), b_src_list, [b_junk, b_sc])
        rstd_exp(sc, b_sc, 512)
        stt(dst, src, sc[:, 3:4], gains[:, gain_off:gain_off + 512], ALU.mult, ALU.mult,
            b_src_list + [b_sc, b_gains], [b_dst])

    def s_scores(n):
        h = n % 2
        cols = slice(n * 128, (n + 1) * 128)
        for g in range(2):
            rows = slice(64 * g, 64 * g + 64)
            for kb in (n - 1, n, n + 1):
                if kb < 0:
                    continue
                slot = g * 3 + (kb - n + 1)
                pf, b_pf = next_psf()
                mm(pf[:].rearrange("p (j t) -> p j t", j=4),
                   [(kT[rows, kb * 128:(kb + 1) * 128], qT[rows, :, cols])], [b_kT, b_qT], [b_pf])
                acopy(PT[h][slot][:], pf[:], [b_pf], [b_PT[h][slot]], func=AF.Exp, scale=0.125)
                if kb != n:
                    m_ = msk[:, 0:128] if kb == n - 1 else msk[:, 128:256]
                    tt("dve", PT[h][slot][:].rearrange("p (j t) -> p j t", j=4),
                       PT[h][slot][:].rearrange("p (j t) -> p j t", j=4),
                       m_.unsqueeze(1).to_broadcast([128, 4, 128]), ALU.mult, [b_PT[h][slot], b_msk], [b_PT[h][slot]])

    def s_pv(n):
        h = n % 2
        for g in range(2):
            pts = [(kb, g * 3 + (kb - n + 1)) for kb in (n - 1, n, n + 1) if kb >= 0]
            pog, b_pog = next_psf()
            for j in range(4):
                mm(pog[:, j * 65:(j + 1) * 65],
                   [(PT[h][slot][:, j * 128:(j + 1) * 128], vaug[:, kb, g, :]) for (kb, slot) in pts],
                   [b_PT[h][s_] for (_, s_) in pts] + [b_v], [b_pog])
            o3 = pog[:, 0:260].rearrange("p (j d) -> p j d", j=4)
            tt("dve", rden[h][:, 4 * g:4 * g + 4], o3[:, :, 64], esink[:, 4 * g:4 * g + 4], ALU.add,
               [b_pog, b_esink], [b_rden[h]])
            S.op("dve", lambda e, g=g, h=h: e.reciprocal(out=rden[h][:, 4 * g:4 * g + 4], in_=rden[h][:, 4 * g:4 * g + 4]),
                 [b_rden[h]], [b_rden[h]])
            tt("dve", attn_[h][:, 256 * g:256 * g + 256].rearrange("p (j d) -> p j d", j=4), o3[:, :, 0:64],
               rden[h][:, 4 * g:4 * g + 4].unsqueeze(2).to_broadcast([128, 4, 64]), ALU.mult,
               [b_pog, b_rden[h]], [b_attn_[h]])
        rms_to(mixed_[h][:, 0:512], attn_[h][:], [b_attn_[h]], G_ATT, junk_[h], b_junk_[h], b_mixa[h])

    def s_ssm(n):
        h = n % 2
        for c in range(4):
            acopy(ysg_[h][:, c, :].rearrange("p (n i) -> p i n", i=8), ysJ[:, c, :, 16 * n:16 * n + 16], [b_ys[c]],
                  [b_ysg_[h]])
        acopy(ysgb_[h][:], ysg_[h][:], [b_ysg_[h]], [b_ysgb_[h]])
        for co in range(4):
            pf, b_pf = next_psf()
            mm(pf[:, 0:128], [(wglu_bf[:, kc, co * 128:(co + 1) * 128], ysgb_[h][:, kc, :]) for kc in range(4)],
               [b_wglu, b_ysgb_[h]], [b_pf])
            acopy(sig_[h][:, co, :], pf[:, 0:128], [b_pf], [b_sig_[h]], func=AF.Exp, scale=-1.0)
        acopy(sig_[h][:], sig_[h][:], [b_sig_[h], b_onec2], [b_sig_[h]], func=AF.Ln, bias=onec2[:, 0:1])
        acopy(sig_[h][:], sig_[h][:], [b_sig_[h]], [b_sig_[h]], func=AF.Exp, scale=-1.0)
        tt("dve", ys2_[h][:], ysg_[h][:], sig_[h][:], ALU.mult, [b_ysg_[h], b_sig_[h]], [b_ys2_[h]])
        pb, b_pb = next_psb()
        transposes([(pb[:, c * 128:(c + 1) * 128], ys2_[h][:, c, :]) for c in range(4)], ident[:],
                   [b_ys2_[h], b_ident], [b_pb])
        acopy(ysT_[h][:], pb[:, 0:512], [b_pb], [b_ysT_[h]])
        rms_to(mixed_[h][:, 512:1024], ysT_[h][:], [b_ysT_[h]], G_SSM, junk_[h], b_junk_[h], b_mixs[h])

    def s_out(n):
        h = n % 2
        pb, b_pb = next_psb()
        transposes([(pb[:, k * 128:(k + 1) * 128], mixed_[h][:, k * 128:(k + 1) * 128]) for k in range(8)],
                   ident[:], [b_mixa[h], b_mixs[h], b_ident], [b_pb])
        acopy(mixT_[h][:], pb[:].rearrange("p (k t) -> p k t", k=8), [b_pb], [b_mixT_[h]])
        dma("dsync", xc[h][:], x_d[n * 128:(n + 1) * 128, :], writes=[b_xc[h]])
        for hf in range(2):
            pf, b_pf = next_psf()
            mm(pf[:], [(mixT_[h][:, kc, :], wout_bf[:, kc, hf * 512:(hf + 1) * 512]) for kc in range(8)],
               [b_mixT_[h], b_wout], [b_pf])
            tt("dve", x1[h][:, hf * 512:(hf + 1) * 512], pf[:], xc[h][:, hf * 512:(hf + 1) * 512], ALU.add,
               [b_pf, b_xc[h]], [b_x1[h]])
        dma("dpool", x1_d[n * 128:(n + 1) * 128, :], x1[h][:], reads=[b_x1[h]], writes=[b_x1d[n]])
        norm_transpose(None, (), G_FFN, x1[h], b_x1[h], h2b[h], b_h2b[h], h2T, b_h2T, n * 128, from_dram=False)

    for c in range(4):
        acopy(ysJ[:, c, :, :], ysJ[:, c, :, :], [b_ys[c]] + b_ysi[c], [b_ys[c]], func=AF.Gelu)
    s_scores(0)
    s_ssm(0)
    for n in range(17):
        if n + 1 < 17:
            s_scores(n + 1)
        s_pv(n)
        if n + 1 < 17:
            s_ssm(n + 1)
        s_out(n)


    if stop <= 5:
        return finish()
    S.barrier()
    AR_ACT = region(21.5, 110)
    actT = AR_ACT.alloc([128, NPAIR, T_OWN], BF16); b_actT = Buf("actT")
    AR = region(110, 173.5)
    HT = T_OWN // 2
    NU = HT + 2
    up = [[AR.alloc([128, NU], F32) for _ in range(2)] for _ in range(2)]
    b_up = [[Buf(f"up{h}{g}") for g in range(2)] for h in range(2)]
    cv = [[AR.alloc([128, HT], F32) for _ in range(2)] for _ in range(2)]
    b_cv = [[Buf(f"cv{h}{g}") for g in range(2)] for h in range(2)]
    wus = [AR.alloc([128, 8, 128], F32) for _ in range(4)]; b_wus = [Buf(f"wus{i}") for i in range(4)]
    wub = [AR.alloc([128, 8, 128], BF16) for _ in range(4)]; b_wub = [Buf(f"wub{i}") for i in range(4)]
    for g in range(2):
        S.op("pool", lambda e, g=g: e.memset(up[0][g][:, 0:1], 0.0), (), [b_up[0][g]])
    def load_pair(p):
        for gv in range(2):
            wi = (p % 2) * 2 + gv
            c0 = gv * DFF + p * 128
            dma("dsync", wus[wi][:], wup_d[:, c0:c0 + 128].rearrange("(k p) f -> p k f", p=128), writes=[b_wus[wi]])
            if gv == 0:
                acopy(wub[wi][:], wus[wi][:], [b_wus[wi]], [b_wub[wi]])
            else:
                tcopy("dve", wub[wi][:], wus[wi][:], [b_wus[wi]], [b_wub[wi]])

    def stage_a(p, hf):
        for gv in range(2):
            wi = (p % 2) * 2 + gv
            ch = gv * NPAIR + p
            u_, b_u = up[hf][gv], b_up[hf][gv]
            if hf == 0:
                segs = [(0, 512, 1), (512, 512, 513), (1024, 1, 1025)]
            else:
                segs = [(1023, 1, 0), (1024, 512, 1), (1536, 512, 513), (2048, 1, 1025)]
            for si, (t0, n, dc) in enumerate(segs):
                pf, b_pf = next_psf()
                mm(pf[:, 0:n], [(wub[wi][:, kc, :], h2T[:, kc, t0:t0 + n]) for kc in range(8)],
                   [b_wub[wi], b_h2T], [b_pf])
                acopy(u_[:, dc:dc + n], pf[:, 0:n], [b_pf], [b_u])
            c_, b_c = cv[hf][gv], b_cv[hf][gv]
            acopy(c_[:], u_[:, 1:1 + HT], [b_u, b_cw, b_cb], [b_c], func=AF.Identity,
                  scale=cw[:, 44 + ch:44 + ch + 1], bias=cb[:, ch:ch + 1])
            stt(c_[:], u_[:, 0:HT], cw[:, ch:ch + 1], c_[:], ALU.mult, ALU.add, [b_u, b_cw, b_c], [b_c])
            stt(c_[:], u_[:, 2:2 + HT], cw[:, 88 + ch:88 + ch + 1], c_[:], ALU.mult, ALU.add, [b_u, b_cw, b_c], [b_c])

    def stage_b(p, hf):
        acopy(cv[hf][0][:], cv[hf][0][:], [b_cv[hf][0]], [b_cv[hf][0]], func=AF.Silu)
        tt("dve", actT[:, p, hf * HT:(hf + 1) * HT], cv[hf][0][:], cv[hf][1][:], ALU.mult,
           [b_cv[hf][0], b_cv[hf][1]], [b_actT])

    load_pair(0)
    for p in range(NPAIR):
        if p + 1 < NPAIR:
            load_pair(p + 1)
        stage_a(p, 0)
        if p > 0:
            stage_b(p - 1, 1)
        stage_a(p, 1)
        stage_b(p, 0)
    stage_b(NPAIR - 1, 1)

    S.barrier()
    AR = region(110, 207.8)
    wdn_bf = AR.alloc([128, NPAIR, D], BF16); b_wdnp = [Buf(f"wdn{p}") for p in range(NPAIR)]
    wds = [AR.alloc([128, D], F32) for _ in range(4)]; b_wds = [Buf(f"wds{i}") for i in range(4)]
    for p in range(NPAIR):
        dma("dsync", wds[p % 4][:], wdn_d[p * 128:(p + 1) * 128, :], writes=[b_wds[p % 4]])
        if p % 2 == 0:
            acopy(wdn_bf[:, p, :], wds[p % 4][:], [b_wds[p % 4]], [b_wdnp[p]])
        else:
            tcopy("dve", wdn_bf[:, p, :], wds[p % 4][:], [b_wds[p % 4]], [b_wdnp[p]])
    x1r = [AR.alloc([128, D], F32) for _ in range(2)]; b_x1r = [Buf("x1r0"), Buf("x1r1")]
    x2 = [AR.alloc([128, D], F32) for _ in range(2)]; b_x2 = [Buf("x2a"), Buf("x2b")]
    yo = [AR.alloc([128, D], F32) for _ in range(2)]; b_yo = [Buf("yo0"), Buf("yo1")]
    jk = AR.alloc([128, D], BF16); b_jk = Buf("jk")
    NE = 3
    early = [[next_psf() for _ in range(2)] for _ in range(NE)]
    for p in range(NPAIR):
        for n in range(NE):
            for hf in range(2):
                pf, b_pf = early[n][hf]
                S.op("pe", lambda e, pf=pf, p=p, n=n, hf=hf: e.matmul(
                    pf[:], actT[:, p, n * 128:(n + 1) * 128], wdn_bf[:, p, hf * 512:(hf + 1) * 512],
                    start=(p == 0), stop=(p == NPAIR - 1)), [b_actT, b_wdnp[p]], [b_pf], cost=0.25)
    for n in range(16):
        i2 = n % 2
        dma("dsync", x1r[i2][:], x1_d[n * 128:(n + 1) * 128, :], reads=[b_x1d[n]], writes=[b_x1r[i2]])
        for hf in range(2):
            if n < NE:
                pf, b_pf = early[n][hf]
            else:
                pf, b_pf = next_psf()
                mm(pf[:], [(actT[:, p, n * 128:(n + 1) * 128], wdn_bf[:, p, hf * 512:(hf + 1) * 512])
                           for p in range(NPAIR)], [b_actT] + b_wdnp, [b_pf])
            tt("dve", x2[i2][:, hf * 512:(hf + 1) * 512], pf[:], x1r[i2][:, hf * 512:(hf + 1) * 512], ALU.add,
               [b_pf, b_x1r[i2]], [b_x2[i2]])
        sc, b_sc = next_stat()
        acopy(jk[:], x2[i2][:], [b_x2[i2]], [b_jk, b_sc], func=AF.Square, accum=sc[:, 0:1])
        ts("dve", sc[:, 1:2], sc[:, 0:1], 1.0 / D, EPS, ALU.mult, ALU.add, [b_sc], [b_sc])
        acopy(sc[:, 2:3], sc[:, 1:2], [b_sc], [b_sc], func=AF.Sqrt)
        S.op("dve", lambda e, sc=sc: e.reciprocal(out=sc[:, 3:4], in_=sc[:, 2:3]), [b_sc], [b_sc])
        stt(yo[i2][:], x2[i2][:], sc[:, 3:4], gains[:, G_FIN:G_FIN + D], ALU.mult, ALU.mult,
            [b_x2[i2], b_sc, b_gains], [b_yo[i2]])
        dma("dpool", y_d[n * 128:(n + 1) * 128, :], yo[i2][:], reads=[b_yo[i2]])

    return finish()


def _consts():
    ident = np.eye(128, dtype=np.float32)
    R = np.zeros((128, 128), np.float32)
    for hh in range(2):
        for d in range(8):
            R[hh * 64 + d, hh * 64 + d + 8] = -1.0
            R[hh * 64 + d + 8, hh * 64 + d] = 1.0
    rmat = np.ascontiguousarray(R.T)
    ifr = np.zeros((128, 1), np.float32)
    base = np.power(np.float32(500000.0), -np.arange(8, dtype=np.float32) / np.float32(8.0)).astype(np.float32)
    for hh in range(2):
        for d in range(16):
            ifr[hh * 64 + d, 0] = base[d % 8]
    kk = np.arange(128)[:, None]
    qq = np.arange(128)[None, :]
    msk = np.concatenate([(kk >= qq), (kk <= qq)], axis=1).astype(np.float32)
    bm = (np.arange(128)[:, None] // 16 == np.arange(128)[None, :] // 16).astype(np.float32)
    return ident, rmat, ifr, msk, bm


_NC_CACHE = {}


def _prep_inputs(inp, dbg=False):
    f = lambda a: np.ascontiguousarray(np.asarray(a, dtype=np.float32))
    x = f(inp["x"])
    w_in = f(inp["w_in"][0])
    qcols = np.concatenate([np.r_[j * 64:(j + 1) * 64, (4 + j) * 64:(5 + j) * 64] for j in range(4)])
    w_in_r = np.ascontiguousarray(np.concatenate([w_in[:, qcols], w_in[:, 512:]], axis=1))
    gains = np.concatenate([f(inp["norm_mix_g"][0]), f(inp["norm_ffn_g"][0]), f(inp["norm_final_g"]),
                            f(inp["norm_attn_g"][0]), f(inp["norm_ssm_g"][0])])[None, :]
    ident, rmat, ifr, msk, bm = _consts()
    a_re, a_im = f(inp["a_re"][0]), f(inp["a_im"][0])
    lst = np.broadcast_to(f(inp["log_step"][0])[:, :, None], (2, 32, 64))
    b_re, b_im = f(inp["b_re"][0]), f(inp["b_im"][0])
    c_re, c_im = f(inp["c_re"][0]), f(inp["c_im"][0])
    cwf = f(inp["conv_w"][0])
    shared = dict(
        w_in=w_in_r, gains=np.ascontiguousarray(gains),
        dsk=np.ascontiguousarray(f(inp["d_skip"][0]).reshape(4, 128).T),
        w_glu=f(inp["w_glu"][0]), sink=f(inp["sink"]), w_out=f(inp["w_out"][0]), w_up=f(inp["w_up"][0]),
        cb=np.ascontiguousarray(f(inp["conv_b"][0]).reshape(44, 128).T),
        w_down=f(inp["w_down"][0]), ident=ident, rmat=rmat, ifr=ifr, msk=msk, bmask=bm,
        sidx=np.arange(512, dtype=np.float32)[None, :])

    def ep(a):
        return np.ascontiguousarray(a.reshape(2, 16, 2, 64).transpose(2, 3, 0, 1).reshape(128, 32))

    def ep_b(a):
        return np.ascontiguousarray(a.reshape(2, 16, 2, 64, 16).transpose(2, 3, 0, 1, 4).reshape(128, 512))

    def ep_c(a):
        return np.ascontiguousarray(a.reshape(2, 16, 2, 16, 64).transpose(2, 4, 0, 1, 3).reshape(128, 512))

    per_half = []
    for h in range(2):
        sl = slice(None) if h == 0 else slice(None, None, -1)
        cwh = cwf if h == 0 else cwf[::-1]
        pos = np.arange(T_QKV, dtype=np.float32) if h == 0 else (T_ALL - 1 - np.arange(T_QKV)).astype(np.float32)
        per_half.append(dict(
            are=ep(a_re[sl]), aim=ep(a_im[sl]), ls=ep(lst[sl]),
            bre=ep_b(b_re[sl]), bim=ep_b(b_im[sl]), cre=ep_c(c_re[sl]), cim=ep_c(c_im[sl]),
            cw=np.ascontiguousarray(cwh.reshape(3, 44, 128).transpose(2, 0, 1).reshape(128, 132)),
            pos=np.ascontiguousarray(pos[None, :])))
    in_maps = []
    for c in range(8):
        b, h = c // 2, c % 2
        xs = x[b] if h == 0 else x[b][::-1]
        m = dict(shared)
        m.update(per_half[h])
        m["x"] = np.ascontiguousarray(xs)
        in_maps.append(m)
    return in_maps


def kernel(**inputs):
    in_maps = _prep_inputs(inputs)
    if "nc" not in _NC_CACHE:
        _NC_CACHE["nc"] = build_program()
    nc = _NC_CACHE["nc"]
    res = run_bass_kernel_spmd(nc, in_maps, core_ids=list(range(8)))
    out = np.empty((4, T_ALL, D), np.float32)
    for c in range(8):
        b, h = c // 2, c % 2
        y = np.asarray(res.results[c]["y"], dtype=np.float32)
        if h == 0:
            out[b, :T_OWN] = y
        else:
            out[b, T_OWN:] = y[::-1]
    return out
```

```python
import numpy as np
import concourse.bass as bass
import concourse.mybir as mybir
from concourse.bass_utils import run_bass_kernel_spmd

F32 = mybir.dt.float32
BF16 = mybir.dt.bfloat16
I32 = mybir.dt.int32
ALU = mybir.AluOpType
AF = mybir.ActivationFunctionType

D = 1024
T_ALL = 4096
T_OWN = 2048
T_EXT = 2176
T_QKV = 2560
NQKV_G = 5
DFF = 2816
NPAIR = 22
EPS = 1e-6
TWO_PI = float(2.0 * np.pi)
SB_BASE = 16512
import os as _os
_C = lambda k, d: float(_os.environ.get(k, d))
C_ACT_F, C_ACT_E = _C("KS_ACT_F", 0.2), _C("KS_ACT_E", 850.0)
C_DVE_F, C_DVE_E = _C("KS_DVE_F", 0.15), _C("KS_DVE_E", 800.0)
C_PE_F, C_PE_E = _C("KS_PE_F", 0.3), _C("KS_PE_E", 2400.0)
C_DMA_F, C_DMA_B = _C("KS_DMA_F", 2.0), _C("KS_DMA_B", 150e3)
SB_END = 229376


class Buf:
    __slots__ = ("name", "writer", "readers", "excl")

    def __init__(self, name, excl=False):
        self.name = name
        self.writer = None
        self.readers = {}
        self.excl = excl


class Stream:
    def __init__(self, name, eng_name, inc, sem):
        self.name, self.eng_name, self.inc, self.sem, self.count = name, eng_name, inc, sem, 0


class Op:
    __slots__ = ("stream", "fn", "preds", "cost", "prog", "end", "start", "sidx", "nsucc", "bar_counts")

    def __init__(self, stream, fn, preds, cost, prog):
        self.stream, self.fn, self.preds, self.cost, self.prog = stream, fn, preds, cost, prog
        self.end = self.start = None
        self.sidx = None
        self.bar_counts = None


class Sched:
    ENGS = ("tensor", "vector", "scalar", "gpsimd", "sync")
    NDS = 12
    HOP = _C("KS_HOP", 0.45)
    NDP = 4

    def __init__(self, sems):
        names = [("pe", "tensor", 1), ("dve", "vector", 1), ("act", "scalar", 1), ("pool", "gpsimd", 1)]
        names += [(f"dsync{i}", "sync", 16) for i in range(self.NDS)]
        names += [(f"dpool{i}", "gpsimd", 16) for i in range(self.NDP)]
        assert len(sems) == len(names)
        self.streams = {n: Stream(n, e, inc, s) for (n, e, inc), s in zip(names, sems)}
        self.slots = {n: Buf("slot_" + n) for n in self.streams if n.startswith("ds") or n.startswith("dp")}
        self.rr = {"dsync": 0, "dpool": 0}
        self.ops = []
        self.last_barrier = None
        self.since_barrier = []

    def op(self, stream, fn, reads=(), writes=(), cost=0.5):
        if stream in self.rr:
            k = self.rr[stream]
            self.rr[stream] = (k + 1) % (self.NDS if stream == "dsync" else self.NDP)
            stream = f"{stream}{k}"
            writes = list(writes) + [self.slots[stream]]
        ex = [b for b in reads if b.excl]
        if ex:
            reads = [b for b in reads if not b.excl]
            writes = list(writes) + ex
        preds = set()
        for b in reads:
            if b.writer is not None:
                preds.add(b.writer)
        for b in writes:
            if b.writer is not None:
                preds.add(b.writer)
            for o in b.readers.values():
                preds.add(o)
        if self.last_barrier is not None:
            preds.add(self.last_barrier)
        o = Op(stream, fn, sorted(preds, key=lambda p_: p_.prog), cost, len(self.ops))
        self.ops.append(o)
        self.since_barrier.append(o)
        for b in reads:
            b.readers[id(o)] = o
        for b in writes:
            b.writer = o
            b.readers = {}
        return o

    def barrier(self):
        if not self.since_barrier:
            return
        b = Op(None, None, list(self.since_barrier) + ([self.last_barrier] if self.last_barrier else []), 0.0, len(self.ops))
        self.ops.append(b)
        self.last_barrier = b
        self.since_barrier = []

    def schedule(self):
        import heapq
        ops = self.ops
        npred = {id(o): len(o.preds) for o in ops}
        succ = {id(o): [] for o in ops}
        for o in ops:
            for p in o.preds:
                succ[id(p)].append(o)
        eng_free = {e: 0.0 for e in self.ENGS}
        eng_of = lambda o: self.streams[o.stream].eng_name
        heap = []

        def ready_time(o):
            return max([p.end for p in o.preds], default=0.0) + self.HOP

        def push(o):
            if o.stream is None:
                o.start = o.end = ready_time(o)
                release(o)
                return
            rt = ready_time(o)
            heapq.heappush(heap, (max(rt, eng_free[eng_of(o)]), o.prog, rt, o))

        def release(o):
            for s_ in succ[id(o)]:
                npred[id(s_)] -= 1
                if npred[id(s_)] == 0:
                    push(s_)
        for o in ops:
            if npred[id(o)] == 0:
                push(o)
        nsched = 0
        while heap:
            key, prog, rt, o = heapq.heappop(heap)
            e = eng_of(o)
            st_ = max(rt, eng_free[e])
            if st_ > key + 1e-9:
                heapq.heappush(heap, (st_, prog, rt, o))
                continue
            o.start = st_
            is_dma = self.streams[o.stream].inc == 16
            o.end = st_ + o.cost
            eng_free[e] = st_ + (0.08 if is_dma else o.cost)
            nsched += 1
            release(o)
        assert all(o.start is not None for o in ops), "scheduler: unscheduled ops (cycle?)"
        self.makespan = max(o.end for o in ops)

    def schedule_hlf(self):
        import heapq
        ops = self.ops
        succ = {id(o): [] for o in ops}
        for o in ops:
            for p in o.preds:
                succ[id(p)].append(o)
        bl = {}
        for o in reversed(ops):
            m = 0.0
            for s_ in succ[id(o)]:
                v = bl[id(s_)] + self.HOP
                if v > m:
                    m = v
            bl[id(o)] = o.cost + m
        npred = {id(o): len(o.preds) for o in ops}
        eng_of = lambda o: self.streams[o.stream].eng_name
        eng_free = {e: 0.0 for e in self.ENGS}
        cand = {e: [] for e in self.ENGS}
        avail = {e: [] for e in self.ENGS}

        def rtime(o):
            return (max(p.end for p in o.preds) + self.HOP) if o.preds else 0.0

        def release(o):
            for s_ in succ[id(o)]:
                npred[id(s_)] -= 1
                if npred[id(s_)] == 0:
                    add(s_)

        def add(o):
            if o.stream is None:
                o.start = o.end = max([p.end for p in o.preds], default=0.0)
                release(o)
            else:
                heapq.heappush(cand[eng_of(o)], (rtime(o), o.prog, o))
        for o in ops:
            if npred[id(o)] == 0:
                add(o)
        left = sum(1 for o in ops if o.stream is not None)
        while left:
            best_e, best_t = None, None
            for e in self.ENGS:
                while cand[e] and cand[e][0][0] <= eng_free[e] + 1e-9:
                    rt, pg, o = heapq.heappop(cand[e])
                    heapq.heappush(avail[e], (-bl[id(o)], pg, o))
                if avail[e]:
                    t_ = eng_free[e]
                elif cand[e]:
                    t_ = max(eng_free[e], cand[e][0][0])
                else:
                    continue
                if best_t is None or t_ < best_t:
                    best_e, best_t = e, t_
            e = best_e
            if not avail[e]:
                eng_free[e] = best_t
                continue
            _, pg, o = heapq.heappop(avail[e])
            o.start = best_t
            is_dma = self.streams[o.stream].inc == 16
            o.end = best_t + o.cost
            eng_free[e] = best_t + (0.08 if is_dma else o.cost)
            left -= 1
            release(o)
        assert all(o.start is not None for o in ops)

    def emit(self, block):
        self.barrier()
        if _C("KS_MODE", 1) >= 1:
            self.schedule_hlf()
        else:
            self.schedule()
        per_eng = {e: [] for e in self.ENGS}
        for o in self.ops:
            if o.stream is not None:
                per_eng[self.streams[o.stream].eng_name].append(o)
        for e in self.ENGS:
            per_eng[e].sort(key=lambda o: (o.start, o.prog))
            for o in per_eng[e]:
                st = self.streams[o.stream]
                st.count += 1
                o.sidx = st.count
        cnt = {n: 0 for n in self.streams}
        for o in self.ops:
            if o.stream is None:
                o.bar_counts = dict(cnt)
            else:
                cnt[o.stream] += 1
        final_counts = dict(cnt)
        progs = {e: [] for e in self.ENGS}
        for e in self.ENGS:
            seen = {}
            for o in per_eng[e]:
                need = {}
                for p in o.preds:
                    if p.stream is None:
                        for sn, c in p.bar_counts.items():
                            if c and need.get(sn, 0) < c:
                                need[sn] = c
                    else:
                        if p.stream == "pe" and o.stream == "pe":
                            continue
                        if need.get(p.stream, 0) < p.sidx:
                            need[p.stream] = p.sidx
                waits = []
                for sn, c in need.items():
                    if seen.get(sn, 0) >= c:
                        continue
                    seen[sn] = c
                    waits.append((self.streams[sn].sem, c * self.streams[sn].inc))
                progs[e].append((waits, o.fn, self.streams[o.stream].sem, self.streams[o.stream].inc))
            waits = [(self.streams[sn].sem, c * self.streams[sn].inc) for sn, c in final_counts.items()
                     if c and seen.get(sn, 0) < c]
            progs[e].append((waits, None, None, 0))

        def run(engname):
            def body(eh):
                for waits, fn, sem, inc in progs[engname]:
                    for (ws, wv) in waits:
                        eh.wait_ge(ws, wv)
                    if fn is not None:
                        fn(eh).then_inc(sem, inc)
            return body
        block.tensor(run("tensor"))
        block.vector(run("vector"))
        block.scalar(run("scalar"))
        block.gpsimd(run("gpsimd"))
        block.sync(run("sync"))


def build_program(dbg=False, stop=99):
    nc = bass.Bass("TRN2", target_bir_lowering=False)

    def din(name, shape, dt=F32):
        return nc.dram_tensor(name, list(shape), dt, kind="ExternalInput").ap()

    x_d = din("x", [T_ALL, D])
    pos_d = din("pos", [1, T_QKV])
    win_d = din("w_in", [D, 1280])
    gains_d = din("gains", [1, 4096])
    are_d = din("are", [128, 32])
    aim_d = din("aim", [128, 32])
    ls_d = din("ls", [128, 32])
    bre_d = din("bre", [128, 512])
    bim_d = din("bim", [128, 512])
    cre_d = din("cre", [128, 512])
    cim_d = din("cim", [128, 512])
    dsk_d = din("dsk", [128, 4])
    wglu_d = din("w_glu", [512, 512])
    sink_d = din("sink", [1, 8])
    wout_d = din("w_out", [D, D])
    wup_d = din("w_up", [D, 2 * DFF])
    cw_d = din("cw", [128, 3 * 44])
    cb_d = din("cb", [128, 44])
    wdn_d = din("w_down", [DFF, D])
    ident_d = din("ident", [128, 128])
    rmat_d = din("rmat", [128, 128])
    ifr_d = din("ifr", [128, 1])
    msk_d = din("msk", [128, 256])
    bmask_d = din("bmask", [128, 128])
    sidx_d = din("sidx", [1, 512])
    y_d = nc.dram_tensor("y", [T_OWN, D], F32, kind="ExternalOutput").ap()
    x1_d = nc.dram_tensor("x1_scr", [T_EXT, D], F32, kind="Internal").ap()
    qs_d = nc.dram_tensor("q_scr", [128, 4, T_QKV], BF16, kind="Internal").ap()
    ks_d = nc.dram_tensor("k_scr", [128, T_QKV], BF16, kind="Internal").ap()
    vs_d = nc.dram_tensor("v_scr", [128, 20, 130], BF16, kind="Internal").ap()
    dbg_d = None
    if dbg:
        dbg_d = nc.dram_tensor("dbg", [8, 128, 2560], F32, kind="ExternalOutput").ap()

    import contextlib
    _es = contextlib.ExitStack()
    sems = [_es.enter_context(nc.semaphore(f"sem{i}")) for i in range(4 + Sched.NDS + Sched.NDP)]
    S = Sched(sems)

    _sbn = [0]
    def finish():
        with nc.Block() as block:
            S.emit(block)
        _es.close()
        return nc

    class Arena:
        def __init__(self, lo, hi):
            self.lo, self.hi, self.cur, self.n = lo, hi, lo, 0

        def alloc(self, shape, dt):
            nbytes = int(np.prod(shape[1:])) * (4 if dt in (F32, I32) else 2)
            nbytes = (nbytes + 63) // 64 * 64
            off = self.cur
            assert off + nbytes <= self.hi, (shape, off, nbytes, self.hi)
            self.cur += nbytes
            _sbn[0] += 1
            return nc.alloc_sbuf_tensor_at(f"sb{_sbn[0]}", list(shape), dt, offset=off)

        def mark(self):
            return self.cur

        def reset(self, m):
            self.cur = m

    def region(lo_kb, hi_kb):
        return Arena(SB_BASE + int(lo_kb * 1024), min(SB_END, SB_BASE + int(hi_kb * 1024)))

    AR = region(0, 21.5)
    AR_YS = region(21.5, 55.5)
    _psn = [0]

    def psum(dt=F32):
        _psn[0] += 1
        return nc.alloc_psum_tensor(f"ps{_psn[0]}", [128, 512 if dt == F32 else 1024], dt)

    PSF = [(psum(F32), Buf(f"psf{i}", excl=True)) for i in range(6)]
    PSB = [(psum(BF16), Buf(f"psb{i}", excl=True)) for i in range(2)]
    _rr = {"f": 0, "b": 0}

    def next_psf():
        _rr["f"] = (_rr["f"] + 1) % len(PSF)
        return PSF[_rr["f"]]

    def next_psb():
        _rr["b"] = (_rr["b"] + 1) % len(PSB)
        return PSB[_rr["b"]]

    def fsz(ap):
        n = 1
        for d_ in ap.shape[1:]:
            n *= int(d_)
        return n

    def dma(q, out, in_, reads=(), writes=()):
        nbytes = fsz(out) * int(out.shape[0]) * 4
        S.op(q, lambda e: e.dma_start(out=out, in_=in_), reads, writes, cost=C_DMA_F + nbytes / C_DMA_B)

    def tcopy(st, out, in_, reads=(), writes=()):
        S.op(st, lambda e: e.tensor_copy(out=out, in_=in_), reads, writes,
             cost=(C_DVE_F + fsz(out) / C_DVE_E) * (4 if st == "pool" else 1))

    def acopy(out, in_, reads=(), writes=(), func=AF.Copy, scale=1.0, bias=None, accum=None):
        def f(e):
            kw = {}
            if bias is not None:
                kw["bias"] = bias
            if accum is not None:
                kw["accum_out"] = accum
            return e.activation(out=out, in_=in_, func=func, scale=scale, **kw)
        S.op("act", f, reads, writes, cost=C_ACT_F + fsz(out) / C_ACT_E)

    def tt(st, out, in0, in1, op, reads=(), writes=()):
        S.op(st, lambda e: e.tensor_tensor(out=out, in0=in0, in1=in1, op=op), reads, writes,
             cost=(C_DVE_F + fsz(out) / C_DVE_E) * (4 if st == "pool" else 1))

    def ts(st, out, in0, s1, s2, op0, op1=None, reads=(), writes=()):
        c_ = (C_DVE_F + fsz(out) / C_DVE_E) * (4 if st == "pool" else 1)
        if op1 is None:
            S.op(st, lambda e: e.tensor_scalar(out=out, in0=in0, scalar1=s1, scalar2=None, op0=op0), reads, writes, cost=c_)
        else:
            S.op(st, lambda e: e.tensor_scalar(out=out, in0=in0, scalar1=s1, scalar2=s2, op0=op0, op1=op1), reads, writes,
                 cost=c_)

    def stt(out, in0, scalar, in1, op0, op1, reads=(), writes=()):
        S.op("dve", lambda e: e.scalar_tensor_tensor(out=out, in0=in0, scalar=scalar, in1=in1, op0=op0, op1=op1),
             reads, writes, cost=C_DVE_F + fsz(out) / (C_DVE_E * 0.83))

    def mm(out, pairs, reads=(), writes=()):
        def f(e):
            ins = None
            n = len(pairs)
            for i, (l, r) in enumerate(pairs):
                ins = e.matmul(out, l, r, start=(i == 0), stop=(i == n - 1))
            return ins
        S.op("pe", f, reads, writes, cost=C_PE_F + sum(max(64, fsz(r)) / C_PE_E + 0.01 for (_, r) in pairs))

    def transposes(outs_ins, ident, reads=(), writes=()):
        def f(e):
            ins = None
            for (o, i_) in outs_ins:
                ins = e.transpose(o, i_, ident)
            return ins
        S.op("pe", f, reads, writes, cost=0.3 + 0.1 * len(outs_ins))

    gains = AR.alloc([128, 4096], F32); b_gains = Buf("gains")
    ident_f = AR.alloc([128, 128], F32)
    ident = AR.alloc([128, 128], BF16); b_ident = Buf("ident")
    rmat_f = AR.alloc([128, 128], F32)
    rmat = AR.alloc([128, 128], BF16); b_rmat = Buf("rmat")
    msk_f = AR.alloc([128, 256], F32)
    msk = AR.alloc([128, 256], BF16); b_msk = Buf("msk")
    ifr = AR.alloc([128, 1], F32); b_ifr = Buf("ifr")
    dsk = AR.alloc([128, 4], F32); b_dsk = Buf("dsk")
    esink = AR.alloc([128, 8], F32); b_esink = Buf("esink")
    cw = AR.alloc([128, 132], F32); b_cw = Buf("cw")
    cb = AR.alloc([128, 44], F32); b_cb = Buf("cb")
    epsc = AR.alloc([128, 1], F32); b_epsc = Buf("epsc")
    onec2 = AR.alloc([128, 1], F32); b_onec2 = Buf("onec2")
    stat = AR.alloc([128, 64], F32)
    b_stat = [Buf(f"stat{i}") for i in range(16)]
    _st = [0]

    def next_stat():
        _st[0] = (_st[0] + 1) % 16
        return stat[:, 4 * _st[0]:4 * _st[0] + 4], b_stat[_st[0]]

    b_tmp = Buf("ldtmp")
    S.op("dve", lambda e: e.memset(epsc[:], EPS), (), [b_epsc])
    S.op("dve", lambda e: e.memset(onec2[:], 1.0), (), [b_onec2])
    dma("dsync", gains[:], gains_d.partition_broadcast(128), writes=[b_gains])
    dma("dsync", ident_f[:], ident_d[:, :], writes=[b_tmp])
    dma("dsync", rmat_f[:], rmat_d[:, :], writes=[b_tmp])
    dma("dsync", msk_f[:], msk_d[:, :], writes=[b_tmp])
    dma("dsync", ifr[:], ifr_d[:, :], writes=[b_ifr])
    dma("dsync", dsk[:], dsk_d[:, :], writes=[b_dsk])
    dma("dsync", esink[:], sink_d.partition_broadcast(128), writes=[b_esink])
    dma("dsync", cw[:], cw_d[:, :], writes=[b_cw])
    dma("dsync", cb[:], cb_d[:, :], writes=[b_cb])
    tcopy("dve", ident[:], ident_f[:], [b_tmp], [b_ident])
    tcopy("dve", rmat[:], rmat_f[:], [b_tmp], [b_rmat])
    tcopy("dve", msk[:], msk_f[:], [b_tmp], [b_msk])
    acopy(esink[:], esink[:], [b_esink], [b_esink], func=AF.Exp)
    G_MIX, G_FFN, G_FIN, G_ATT, G_SSM = 0, 1024, 2048, 3072, 3584

    ysJ = AR_YS.alloc([128, 4, 8, T_EXT // 8], F32); b_ys = [Buf(f"ys{c}") for c in range(4)]

    def norm_transpose_group(tiles, exp_set=False):
        for (src, go, xt, b_xt, hb, b_hb, hT, b_hT, col0) in tiles:
            if src is not None:
                dma("dsync", xt[:], src, writes=[b_xt])
        scs = [next_stat() for _ in tiles]
        if exp_set:
            for (src, go, xt, b_xt, hb, b_hb, hT, b_hT, col0), (sc, b_sc) in zip(tiles, scs):
                S.op("dve", ssq_dve(hb[:], xt[:], sc), [b_xt], [b_hb, b_sc])
                rstd_exp(sc, b_sc, D)
        else:
            for (src, go, xt, b_xt, hb, b_hb, hT, b_hT, col0), (sc, b_sc) in zip(tiles, scs):
                acopy(hb[:], xt[:], [b_xt], [b_hb, b_sc], func=AF.Square, accum=sc[:, 0:1])
            for (sc, b_sc) in scs:
                ts("dve", sc[:, 1:2], sc[:, 0:1], 1.0 / D, EPS, ALU.mult, ALU.add, [b_sc], [b_sc])
            for (sc, b_sc) in scs:
                acopy(sc[:, 2:3], sc[:, 1:2], [b_sc], [b_sc], func=AF.Sqrt)
            for (sc, b_sc) in scs:
                S.op("dve", lambda e, sc=sc: e.reciprocal(out=sc[:, 3:4], in_=sc[:, 2:3]), [b_sc], [b_sc])
        for (src, go, xt, b_xt, hb, b_hb, hT, b_hT, col0), (sc, b_sc) in zip(tiles, scs):
            stt(hb[:], xt[:], sc[:, 3:4], gains[:, go:go + D], ALU.mult, ALU.mult, [b_xt, b_sc, b_gains], [b_hb])
        for (src, go, xt, b_xt, hb, b_hb, hT, b_hT, col0) in tiles:
            pb, b_pb = next_psb()
            transposes([(pb[:, k * 128:(k + 1) * 128], hb[:, k * 128:(k + 1) * 128]) for k in range(8)],
                       ident[:], [b_hb, b_ident], [b_pb])
            if not exp_set and (col0 // 128) % 2 == 1:
                tcopy("dve", hT[:, :, col0:col0 + 128], pb[:].rearrange("p (k t) -> p k t", k=8), [b_pb], [b_hT])
            else:
                acopy(hT[:, :, col0:col0 + 128], pb[:].rearrange("p (k t) -> p k t", k=8), [b_pb], [b_hT])

    def norm_transpose(src_ap, src_reads, gain_off, xt, b_xt, hb, b_hb, hT, b_hT, col0, from_dram=True, stop=99):
        norm_transpose_group([(src_ap if from_dram else None, gain_off, xt, b_xt, hb, b_hb, hT, b_hT, col0)],
                             exp_set=not from_dram)

    AR_P = region(196.25, 207.8)
    prm = AR_P.alloc([128, 21, 32], F32); b_prm = Buf("prm")
    (P_ARE, P_AIM, P_DT, P_ER, P_TH, P_C, P_S, P_LR, P_LI, P_T0, P_T1, P_T2, P_CR, P_CI, P_R8, P_T3,
     P_T4, P_T5, P_T6, P_T7, P_F8) = range(21)
    pri = AR_P.alloc([128, 32], I32)
    PW = AR_P.alloc([128, 9, 2, 32], F32); b_PW = Buf("PW")
    craw = AR_P.alloc([128, 2, 512], F32); b_craw = Buf("craw")
    sidx = AR_P.alloc([128, 512], F32); b_sidx = Buf("sidx")
    onec = AR_P.alloc([128, 1], F32); b_onec = Buf("onec")

    def P(i):
        return prm[:, i, :]

    dma("dsync", P(P_ARE), are_d[:, :], writes=[b_prm])
    dma("dsync", P(P_AIM), aim_d[:, :], writes=[b_prm])
    dma("dsync", P(P_DT), ls_d[:, :], writes=[b_prm])
    dma("dsync", craw[:, 0, :], cre_d[:, :], writes=[b_craw])
    dma("dsync", craw[:, 1, :], cim_d[:, :], writes=[b_craw])
    dma("dsync", sidx[:], sidx_d.partition_broadcast(128), writes=[b_sidx])
    S.op("pool", lambda e: e.memset(onec[:], 1.0), (), [b_onec])
    R, W_ = [b_prm], [b_prm]
    acopy(P(P_DT), P(P_DT), R, W_, func=AF.Exp)
    tt("dve", P(P_T0), P(P_ARE), P(P_DT), ALU.mult, R, W_)
    acopy(P(P_ER), P(P_T0), R, W_, func=AF.Exp)
    tt("dve", P(P_TH), P(P_AIM), P(P_DT), ALU.mult, R, W_)
    ts("dve", P(P_T0), P(P_TH), 1.0 / TWO_PI, None, ALU.mult, reads=R, writes=W_)
    tcopy("dve", pri[:], P(P_T0), R, W_)
    tcopy("dve", P(P_T0), pri[:], R, W_)
    stt(P(P_T1), P(P_T0), -TWO_PI, P(P_TH), ALU.mult, ALU.add, R, W_)
    ts("dve", P(P_T1), P(P_T1), float(np.pi), float(-np.pi), ALU.min, ALU.max, R, W_)
    acopy(P(P_S), P(P_T1), R, W_, func=AF.Sin)
    acopy(P(P_T2), P(P_T1), R, W_, func=AF.Sin, scale=0.5)
    tt("dve", P(P_T2), P(P_T2), P(P_T2), ALU.mult, R, W_)
    ts("dve", P(P_C), P(P_T2), -2.0, 1.0, ALU.mult, ALU.add, R, W_)
    tt("dve", P(P_LR), P(P_ER), P(P_C), ALU.mult, R, W_)
    tt("dve", P(P_LI), P(P_ER), P(P_S), ALU.mult, R, W_)
    tt("dve", P(P_T0), P(P_ER), P(P_ER), ALU.mult, R, W_)
    tt("dve", P(P_T0), P(P_T0), P(P_T0), ALU.mult, R, W_)
    tt("dve", P(P_R8), P(P_T0), P(P_T0), ALU.mult, R, W_)
    ts("dve", P(P_T0), P(P_LR), -1.0, None, ALU.add, reads=R, writes=W_)
    tt("dve", P(P_T1), P(P_ARE), P(P_ARE), ALU.mult, R, W_)
    tt("dve", P(P_T2), P(P_AIM), P(P_AIM), ALU.mult, R, W_)
    tt("dve", P(P_T1), P(P_T1), P(P_T2), ALU.add, R, W_)
    S.op("dve", lambda e: e.reciprocal(out=P(P_T3), in_=P(P_T1)), R, W_)
    tt("dve", P(P_T1), P(P_T0), P(P_ARE), ALU.mult, R, W_)
    tt("dve", P(P_T2), P(P_LI), P(P_AIM), ALU.mult, R, W_)
    tt("dve", P(P_T1), P(P_T1), P(P_T2), ALU.add, R, W_)
    tt("dve", P(P_CR), P(P_T1), P(P_T3), ALU.mult, R, W_)
    tt("dve", P(P_T1), P(P_LI), P(P_ARE), ALU.mult, R, W_)
    tt("dve", P(P_T2), P(P_T0), P(P_AIM), ALU.mult, R, W_)
    tt("dve", P(P_T1), P(P_T1), P(P_T2), ALU.subtract, R, W_)
    tt("dve", P(P_CI), P(P_T1), P(P_T3), ALU.mult, R, W_)
    ts("dve", P(P_T4), P(P_TH), 8.0 / TWO_PI, None, ALU.mult, reads=R, writes=W_)
    tcopy("dve", pri[:], P(P_T4), R, W_)
    tcopy("dve", P(P_T5), pri[:], R, W_)
    tt("dve", P(P_F8), P(P_T4), P(P_T5), ALU.subtract, R, W_)
    RP = [b_prm, b_PW]
    S.op("dve", lambda e: e.memset(PW[:, 0, 0, :], 1.0), (), [b_PW])
    S.op("dve", lambda e: e.memset(PW[:, 0, 1, :], 0.0), (), [b_PW])
    tcopy("dve", PW[:, 1, 0, :], P(P_LR), RP, [b_PW])
    tcopy("dve", PW[:, 1, 1, :], P(P_LI), RP, [b_PW])
    for k in range(2, 9):
        ar_, ai_ = PW[:, k - 1, 0, :], PW[:, k - 1, 1, :]
        tt("dve", P(P_T4), ar_, P(P_LR), ALU.mult, RP, W_)
        tt("dve", P(P_T5), ai_, P(P_LI), ALU.mult, RP, W_)
        tt("dve", PW[:, k, 0, :], P(P_T4), P(P_T5), ALU.subtract, RP, [b_PW])
        tt("dve", P(P_T6), ar_, P(P_LI), ALU.mult, RP, W_)
        tt("dve", P(P_T7), ai_, P(P_LR), ALU.mult, RP, W_)
        tt("dve", PW[:, k, 1, :], P(P_T6), P(P_T7), ALU.add, RP, [b_PW])


    if stop <= 0:
        return finish()
    AR_U = region(55.5, 87.5)
    uJ = AR_U.alloc([128, 4, 8, T_ALL // 8], BF16); b_uT = [Buf(f"uT{c}") for c in range(4)]
    AR = region(87.5, 196.25)
    w_in_bf = AR.alloc([128, 8, 1280], BF16); b_win = Buf("w_in")
    wst = [AR.alloc([128, 1280], F32) for _ in range(2)]; b_wst = [Buf("wst0"), Buf("wst1")]
    xts = [AR.alloc([128, D], F32) for _ in range(4)]; b_xts = [Buf(f"xt{i}") for i in range(4)]
    hbs = [AR.alloc([128, D], BF16) for _ in range(4)]; b_hbs = [Buf(f"hb{i}") for i in range(4)]
    hTs = [AR.alloc([128, 8, 512], BF16) for _ in range(2)]; b_hTs = [Buf("hT0"), Buf("hT1")]
    T_RP = 2304
    AR_T = region(21.5, 55.5)
    cosT = AR_T.alloc([128, T_RP], F32); sinT = AR_T.alloc([128, T_RP], F32); b_cs = Buf("cossin")
    angb = AR.alloc([128, 512], F32); angi = AR.alloc([128, 512], I32); b_ang = Buf("ang")
    qb = [AR.alloc([128, 512], BF16) for _ in range(2)]; b_qb = [Buf("qb0"), Buf("qb1")]
    rt2 = [[AR.alloc([128, 512], F32) for _ in range(2)] for _ in range(2)]
    b_rt2 = [[Buf(f"rt{a_}{b_}") for b_ in range(2)] for a_ in range(2)]
    qst = [AR.alloc([128, 4, 512], BF16) for _ in range(2)]; b_qst = [Buf("qst0"), Buf("qst1")]
    kst = [AR.alloc([128, 512], BF16) for _ in range(2)]; b_kst = [Buf("kst0"), Buf("kst1")]
    vst = [AR.alloc([128, 4, 130], BF16) for _ in range(2)]; b_vst = [Buf("vst0"), Buf("vst1")]
    b_qsd = Buf("q_scr"); b_ksd = Buf("k_scr"); b_vsd = Buf("v_scr")

    def load_w_in():
        for kc in range(8):
            dma("dsync", wst[kc % 2][:], win_d[kc * 128:(kc + 1) * 128, :], writes=[b_wst[kc % 2]])
            if kc % 2 == 0:
                acopy(w_in_bf[:, kc, :], wst[kc % 2][:], [b_wst[kc % 2]], [b_win])
            else:
                tcopy("dve", w_in_bf[:, kc, :], wst[kc % 2][:], [b_wst[kc % 2]], [b_win])

    load_w_in()
    for blk in range(5):
        c0 = blk * 512
        nb = min(512, T_RP - c0)
        cs_, sn_b = cosT[:, c0:c0 + nb], sinT[:, c0:c0 + nb]
        dma("dsync", angb[:, 0:nb], pos_d[:, c0:c0 + nb].partition_broadcast(128), writes=[b_ang])
        ts("dve", angb[:, 0:nb], angb[:, 0:nb], ifr[:, 0:1], None, ALU.mult, reads=[b_ang, b_ifr], writes=[b_ang])
        ts("dve", sn_b, angb[:, 0:nb], 1.0 / TWO_PI, None, ALU.mult, reads=[b_ang], writes=[b_cs])
        tcopy("dve", angi[:, 0:nb], sn_b, [b_cs], [b_ang])
        tcopy("dve", sn_b, angi[:, 0:nb], [b_ang], [b_cs])
        stt(angb[:, 0:nb], sn_b, -TWO_PI, angb[:, 0:nb], ALU.mult, ALU.add, [b_ang, b_cs], [b_ang])
        ts("dve", angb[:, 0:nb], angb[:, 0:nb], float(np.pi), float(-np.pi), ALU.min, ALU.max, [b_ang], [b_ang])
        acopy(sn_b, angb[:, 0:nb], [b_ang], [b_cs], func=AF.Sin)
        acopy(cs_, angb[:, 0:nb], [b_ang], [b_cs], func=AF.Sin, scale=0.5)
        tt("dve", cs_, cs_, cs_, ALU.mult, [b_cs], [b_cs])
        ts("dve", cs_, cs_, -2.0, 1.0, ALU.mult, ALU.add, [b_cs], [b_cs])
    for i_ in range(2):
        S.op("pool", lambda e, i_=i_: e.memset(vst[i_][:], 1.0), (), [b_vst[i_]])
    if stop <= 0.2:
        return finish()
    for g4 in range(8):
        hT, b_hT = hTs[g4 % 2], b_hTs[g4 % 2]
        norm_transpose_group([(x_d[(g4 * 4 + j) * 128:(g4 * 4 + j + 1) * 128, :], G_MIX, xts[j], b_xts[j], hbs[j], b_hbs[j],
                               hT, b_hT, j * 128) for j in range(4)])
        if stop <= 0.7:
            return finish()
        for c in range(4):
            pf, b_pf = next_psf()
            mm(pf[:], [(w_in_bf[:, kc, 768 + c * 128:768 + (c + 1) * 128], hT[:, kc, :]) for kc in range(8)],
               [b_win, b_hT], [b_pf])
            if stop <= 0.75:
                return finish()
            acopy(uJ[:, c, :, g4 * 64:(g4 + 1) * 64], pf[:].rearrange("p (n j) -> p j n", j=8), [b_pf], [b_uT[c]])
            if stop <= 0.8:
                return finish()
            if stop <= 0.85:
                return finish()
        if g4 < NQKV_G:
            nt = 4 if g4 < 4 else 2
            nn = nt * 128
            cols = slice(g4 * 512, g4 * 512 + nn)
            sp = g4 % 2
            for c in range(5):
                pf, b_pf = next_psf()
                mm(pf[:, 0:nn], [(w_in_bf[:, kc, c * 128:(c + 1) * 128], hT[:, kc, 0:nn]) for kc in range(8)],
                   [b_win, b_hT], [b_pf])
                i2 = c % 2
                acopy(qb[i2][:, 0:nn], pf[:, 0:nn], [b_pf], [b_qb[i2]])
                pr_, b_pr_ = next_psf()
                mm(pr_[:, 0:nn], [(rmat[:], qb[i2][:, 0:nn])], [b_rmat, b_qb[i2]], [b_pr_])
                ra, rb_, b_ra, b_rb = rt2[i2][0], rt2[i2][1], b_rt2[i2][0], b_rt2[i2][1]
                tt("dve", ra[:, 0:nn], pf[:, 0:nn], cosT[:, cols], ALU.mult, [b_pf, b_cs], [b_ra])
                tt("dve", rb_[:, 0:nn], pr_[:, 0:nn], sinT[:, cols], ALU.mult, [b_pr_, b_cs], [b_rb])
                if c < 4:
                    tt("dve", qst[sp][:, c, 0:nn], ra[:, 0:nn], rb_[:, 0:nn], ALU.add, [b_ra, b_rb], [b_qst[sp]])
                else:
                    tt("dve", kst[sp][:, 0:nn], ra[:, 0:nn], rb_[:, 0:nn], ALU.add, [b_ra, b_rb], [b_kst[sp]])
            for j in range(nt):
                pf, b_pf = next_psf()
                mm(pf[:, 0:128], [(hT[:, kc, j * 128:(j + 1) * 128], w_in_bf[:, kc, 640:768]) for kc in range(8)],
                   [b_win, b_hT], [b_pf])
                acopy(vst[sp][:, j, :].rearrange("p (g d) -> p g d", g=2)[:, :, 0:64],
                      pf[:, 0:128].rearrange("p (g d) -> p g d", g=2), [b_pf], [b_vst[sp]])
            dma("dpool", qs_d[:, :, cols], qst[sp][:, :, 0:nn], reads=[b_qst[sp]], writes=[b_qsd])
            dma("dpool", ks_d[:, cols], kst[sp][:, 0:nn], reads=[b_kst[sp]], writes=[b_ksd])
            dma("dpool", vs_d[:, g4 * 4:g4 * 4 + nt, :], vst[sp][:, 0:nt, :], reads=[b_vst[sp]], writes=[b_vsd])
        if stop <= 0.9:
            return finish()

    if stop <= 1:
        return finish()
    S.barrier()
    AR = region(87.5, 196.25)
    NO, NA = T_EXT // 8, T_ALL // 8
    Tz = AR.alloc([128, 4, 15, 128], BF16); b_Tz = Buf("Tz")
    BbTc = AR.alloc([128, 8, 2, 8, 128], BF16); b_BbTc = Buf("BbTc")
    CbTc = AR.alloc([128, 8, 2, 16, 32], BF16); b_CbTc = Buf("CbTc")
    mB = AR.mark()
    bbar = AR.alloc([128, 2, 512], F32); b_bbar = Buf("bbar")
    braw = AR.alloc([128, 2, 512], F32); b_braw = Buf("braw")
    tmpbs = [AR.alloc([128, 2, 512], F32) for _ in range(2)]; b_tmpbs = [Buf("tmpb0"), Buf("tmpb1")]
    bpows = [AR.alloc([128, 2, 512], F32) for _ in range(2)]; b_bpows = [Buf("bpow0"), Buf("bpow1")]
    Mps = [AR.alloc([128, 2, 8, 128], BF16) for _ in range(2)]; b_Mps = [Buf("Mp0"), Buf("Mp1")]
    tmpb, b_tmpb = tmpbs[0], b_tmpbs[0]
    Cp = AR.alloc([128, 2, 8, 128], BF16); b_Cp = Buf("Cp")
    bmask = AR.alloc([128, 128], F32); b_bmask = Buf("bmask")
    diagD = AR.alloc([128, 4, 128], F32); b_diagD = Buf("diagD")
    tzt = AR.alloc([128, 128], F32); b_tzt = Buf("tzt")

    dma("dsync", braw[:, 0, :], bre_d[:, :], writes=[b_braw])
    dma("dsync", braw[:, 1, :], bim_d[:, :], writes=[b_braw])
    dma("dsync", bmask[:], bmask_d[:, :], writes=[b_bmask])
    def v3(ap2):
        return ap2.rearrange("p (a c) -> p a c", c=16)

    def bc32(ap32):
        return ap32.unsqueeze(2).to_broadcast([128, 32, 16])

    def cmul(dst, b_dst, src, b_src, cre, cim, rd, tmpb=tmpb, b_tmpb=b_tmpb):
        RB = [b_src, b_tmpb] + rd
        tt("dve", v3(tmpb[:, 0, :]), v3(src[:, 0, :]), bc32(cre), ALU.mult, RB, [b_tmpb])
        tt("dve", v3(tmpb[:, 1, :]), v3(src[:, 1, :]), bc32(cim), ALU.mult, RB, [b_tmpb])
        tt("dve", dst[:, 0, :], tmpb[:, 0, :], tmpb[:, 1, :], ALU.subtract, [b_tmpb], [b_dst])
        tt("dve", v3(tmpb[:, 0, :]), v3(src[:, 1, :]), bc32(cre), ALU.mult, RB, [b_tmpb])
        tt("dve", v3(tmpb[:, 1, :]), v3(src[:, 0, :]), bc32(cim), ALU.mult, RB, [b_tmpb])
        tt("dve", dst[:, 1, :], tmpb[:, 0, :], tmpb[:, 1, :], ALU.add, [b_tmpb], [b_dst])

    cmul(bbar, b_bbar, braw, b_braw, P(P_CR), P(P_CI), [b_prm])

    def pack(dst, b_dst, src, b_src, neg_im):
        for ri in range(2):
            for e_ in range(2):
                ps_ = slice(e_ * 64, (e_ + 1) * 64)
                s_ = src[ps_, ri, :].rearrange("p (k r c) -> p k r c", r=4, c=16)
                d_ap = dst[ps_, ri, :, :].rearrange("p k (r x) -> p k r x", x=32)[:, :, :, 16 * e_:16 * e_ + 16]
                if neg_im and ri == 1:
                    acopy(d_ap, s_, [b_src], [b_dst], scale=-1.0)
                else:
                    acopy(d_ap, s_, [b_src], [b_dst])

    for Mp, b_Mp in zip(Mps, b_Mps):
        S.op("pool", lambda e, Mp=Mp: e.memset(Mp[:], 0.0), (), [b_Mp])
    S.op("pool", lambda e: e.memset(Cp[:], 0.0), (), [b_Cp])
    pack(Cp, b_Cp, craw, b_craw, True)
    for c in range(4):
        ts("dve", diagD[:, c, :], ident_f[:], dsk[:, c:c + 1], None, ALU.mult, reads=[b_tmp, b_dsk], writes=[b_diagD])
    for k in range(8):
        bpow, b_bpow, Mp, b_Mp = bpows[k % 2], b_bpows[k % 2], Mps[k % 2], b_Mps[k % 2]
        cmul(bpow, b_bpow, bbar, b_bbar, PW[:, k, 0, :], PW[:, k, 1, :], [b_PW], tmpb=tmpbs[k % 2], b_tmpb=b_tmpbs[k % 2])
        pack(Mp, b_Mp, bpow, b_bpow, False)
        for c in range(4):
            if k == 0:
                pf, b_pf = next_psf()
                mm(pf[:, 0:128], [(Mp[:, ri, 4 * d_ + c, :], Cp[:, ri, 4 * d_ + c, :]) for d_ in range(2) for ri in range(2)],
                   [b_Mp, b_Cp], [b_pf])
                tt("dve", tzt[:], pf[:, 0:128], bmask[:], ALU.mult, [b_pf, b_bmask], [b_tzt])
                tt("dve", Tz[:, c, 7, :], tzt[:], diagD[:, c, :], ALU.add, [b_tzt, b_diagD], [b_Tz])
            else:
                for d_ in range(2):
                    pf, b_pf = next_psf()
                    mm(pf[:, 0:128], [(Mp[:, ri, 4 * d_ + c, :], Cp[:, ri, 4 * d_ + c, :]) for ri in range(2)],
                       [b_Mp, b_Cp], [b_pf])
                    idx = 7 + k if d_ == 0 else 7 - k
                    tt("dve", Tz[:, c, idx, :], pf[:, 0:128], bmask[:], ALU.mult, [b_pf, b_bmask], [b_Tz])
        for d_ in range(2):
            j = 7 - k if d_ == 0 else k
            pb, b_pb = next_psb()
            transposes([(pb[:, (ri * 4 + kk) * 128:(ri * 4 + kk + 1) * 128], Mp[:, ri, 4 * d_ + kk, :])
                        for ri in range(2) for kk in range(4)], ident[:], [b_Mp, b_ident], [b_pb])
            acopy(BbTc[:, j, :, 4 * d_:4 * d_ + 4, :], pb[:].rearrange("p (r k x) -> p r k x", r=2, k=4), [b_pb], [b_BbTc])

    if stop <= 2:
        return finish()
    S.barrier()
    AR.reset(mB)
    Xd = AR.alloc([128, 2, 16, NO], BF16); b_Xdc = [Buf(f"Xd{c}") for c in range(4)]
    Cts = [AR.alloc([128, NA], F32) for _ in range(2)]; Sns = [AR.alloc([128, NA], F32) for _ in range(2)]
    b_tabs = [Buf("tab0"), Buf("tab1")]
    bufA = AR.alloc([128, NA], F32); bufI = AR.alloc([128, NA], I32); b_bufA = Buf("bufA"); b_bufI = Buf("bufI")

    def gen_table(cq, Nd, par):
        Ct_, Sn_, b_t = Cts[par], Sns[par], b_tabs[par]
        acopy(bufA[:, 0:Nd], sidx[:, 0:Nd], [b_sidx, b_prm], [b_bufA], func=AF.Copy, scale=prm[:, P_F8, cq:cq + 1])
        acopy(bufI[:, 0:Nd], bufA[:, 0:Nd], [b_bufA], [b_bufI])
        tt("dve", bufA[:, 0:Nd], bufA[:, 0:Nd], bufI[:, 0:Nd], ALU.subtract, [b_bufA, b_bufI], [b_bufA])
        acopy(Sn_[:, 0:Nd], bufA[:, 0:Nd], [b_bufA], [b_t], func=AF.Sin, scale=6.283185)
        acopy(Ct_[:, 0:Nd], bufA[:, 0:Nd], [b_bufA], [b_t], func=AF.Sin, scale=3.141592)
        acopy(Ct_[:, 0:Nd], Ct_[:, 0:Nd], [b_t], [b_t], func=AF.Square)
        acopy(Ct_[:, 0:Nd], Ct_[:, 0:Nd], [b_t, b_onec], [b_t], func=AF.Identity, scale=-2.0, bias=onec[:, 0:1])
    Wr = AR.alloc([128, NA], BF16); Wi = AR.alloc([128, NA], BF16); b_W = Buf("W")
    Zr = AR.alloc([128, NA], BF16); Zi = AR.alloc([128, NA], BF16); b_Z = Buf("Z")
    mt = [AR.alloc([128, 512], F32) for _ in range(4)]; b_mt = [Buf(f"mt{i}") for i in range(4)]
    cl = AR.alloc([128, 2, 256], F32); b_cl = Buf("cl")
    clt = AR.alloc([128, 2, 256], F32); b_clt = Buf("clt")

    def v3h(ap2):
        return ap2.rearrange("p (a c) -> p a c", c=16)

    b_ysi = [[Buf(f"ysJ{c}_{i}") for i in range(8)] for c in range(4)]
    for c in range(4):
        for i in range(8):
            py, b_py = next_psf()
            mm(py[:, 0:NO], [(Tz[:, c, i - j + 7, :], uJ[:, c, j, 0:NO]) for j in range(8)], [b_Tz, b_uT[c]], [b_py])
            acopy(ysJ[:, c, i, :], py[:, 0:NO], [b_py], [b_ysi[c][i]])
    for d_ in range(2):
        Nd = NO if d_ == 0 else NA
        S.op("pool", lambda e: e.memset(CbTc[:], 0.0), (), [b_CbTc])
        for i in range(8):
            pw = i + 1 if d_ == 0 else 8 - i
            lr = PW[:, pw, 0, 16 * d_:16 * d_ + 16].unsqueeze(2).to_broadcast([128, 16, 16])
            li = PW[:, pw, 1, 16 * d_:16 * d_ + 16].unsqueeze(2).to_broadcast([128, 16, 16])
            c_re = v3h(craw[:, 0, 256 * d_:256 * d_ + 256]); c_im = v3h(craw[:, 1, 256 * d_:256 * d_ + 256])
            RC = [b_craw, b_PW, b_clt]
            tt("dve", v3h(clt[:, 0, :]), c_re, lr, ALU.mult, RC, [b_clt])
            tt("dve", v3h(clt[:, 1, :]), c_im, li, ALU.mult, RC, [b_clt])
            tt("dve", cl[:, 0, :], clt[:, 0, :], clt[:, 1, :], ALU.subtract, [b_clt], [b_cl])
            tt("dve", v3h(clt[:, 0, :]), c_re, li, ALU.mult, RC, [b_clt])
            tt("dve", v3h(clt[:, 1, :]), c_im, lr, ALU.mult, RC, [b_clt])
            stt(cl[:, 1, :], clt[:, 0, :], -1.0, clt[:, 1, :], ALU.mult, ALU.subtract, [b_clt], [b_cl])
            for ri in range(2):
                for e_ in range(2):
                    ps_ = slice(e_ * 64, (e_ + 1) * 64)
                    acopy(CbTc[ps_, i, ri, :, 16 * e_:16 * e_ + 16], v3h(cl[ps_, ri, :]), [b_cl], [b_CbTc])
        if d_ == 0:
            S.op("pool", lambda e: e.memset(Xd[:, :, :, 0:1], 0.0), (), list(b_Xdc))
        for q in range(16):
            cq = d_ * 16 + q
            ch, r4 = q // 4, q % 4
            rows = slice(32 * r4, 32 * r4 + 32)
            par = q % 2
            if q == 0:
                gen_table(cq, Nd, par)
            Ct, Sn, b_tab = Cts[par], Sns[par], b_tabs[par]
            pr, b_pr = next_psf()
            pi_, b_pi = next_psf()
            for ri, (pv, b_pv) in enumerate(((pr, b_pr), (pi_, b_pi))):
                def f(e, ri=ri, pv=pv, rows=rows, ch=ch, d_=d_, Nd=Nd, r4=r4):
                    ins = None
                    for j in range(8):
                        ins = e.matmul(pv[:, 0:Nd], BbTc[rows, j, ri, 4 * d_ + ch, :], uJ[rows, ch, j, 0:Nd],
                                       start=(j == 0), stop=(j == 7), tile_position=(32 * r4, 0))
                    return ins
                S.op("pe", f, [b_BbTc, b_uT[ch]], [b_pv], cost=0.3 + 8 * Nd / 2400.0)
            if d_ == 0:
                cs, sn_, wr, wi = Ct[:, 0:Nd], Sn[:, 0:Nd], Wr[:, 0:Nd], Wi[:, 0:Nd]
            else:
                cs, sn_ = Ct[:, 0:Nd][:, ::-1], Sn[:, 0:Nd][:, ::-1]
                wr, wi = Wr[:, 0:Nd][:, ::-1], Wi[:, 0:Nd][:, ::-1]
            tt("dve", mt[0][:, 0:Nd], pr[:, 0:Nd], cs, ALU.mult, [b_pr, b_tab], [b_mt[0]])
            tt("dve", mt[1][:, 0:Nd], pi_[:, 0:Nd], sn_, ALU.mult, [b_pi, b_tab], [b_mt[1]])
            tt("dve", mt[2][:, 0:Nd], pi_[:, 0:Nd], cs, ALU.mult, [b_pi, b_tab], [b_mt[2]])
            tt("dve", mt[3][:, 0:Nd], pr[:, 0:Nd], sn_, ALU.mult, [b_pr, b_tab], [b_mt[3]])
            tt("dve", wr, mt[0][:, 0:Nd], mt[1][:, 0:Nd], ALU.add, [b_mt[0], b_mt[1]], [b_W])
            tt("dve", wi, mt[2][:, 0:Nd], mt[3][:, 0:Nd], ALU.subtract, [b_mt[2], b_mt[3]], [b_W])
            if q + 1 < 16:
                gen_table(cq + 1, Nd, (q + 1) % 2)
            rho = prm[:, P_R8, cq:cq + 1].to_broadcast([128, Nd])
            S.op("dve", lambda e, rho=rho, Nd=Nd: e.tensor_tensor_scan(out=Zr[:, 0:Nd], data0=rho, data1=Wr[:, 0:Nd],
                 initial=0.0, op0=ALU.mult, op1=ALU.add), [b_W, b_prm], [b_Z], cost=0.15 + 2 * Nd / 960.0)
            S.op("dve", lambda e, rho=rho, Nd=Nd: e.tensor_tensor_scan(out=Zi[:, 0:Nd], data0=rho, data1=Wi[:, 0:Nd],
                 initial=0.0, op0=ALU.mult, op1=ALU.add), [b_W, b_prm], [b_Z], cost=0.15 + 2 * Nd / 960.0)
            if d_ == 0:
                n = NO - 1
                zr, zi, cs, sn_ = Zr[:, 0:n], Zi[:, 0:n], Ct[:, 0:n], Sn[:, 0:n]
                xr, xi = Xd[:, 0, q, 1:NO], Xd[:, 1, q, 1:NO]
            else:
                n = NO
                lo = NA - 1 - NO
                zr, zi = Zr[:, lo:lo + n][:, ::-1], Zi[:, lo:lo + n][:, ::-1]
                cs, sn_ = Ct[:, lo:lo + n][:, ::-1], Sn[:, lo:lo + n][:, ::-1]
                xr, xi = Xd[:, 0, q, 0:NO], Xd[:, 1, q, 0:NO]
            RZ = [b_Z, b_tab]
            tt("dve", mt[3][:, 0:n], zr, sn_, ALU.mult, RZ, [b_mt[3]])
            tt("dve", mt[0][:, 0:n], zr, cs, ALU.mult, RZ, [b_mt[0]])
            tt("dve", mt[1][:, 0:n], zi, sn_, ALU.mult, RZ, [b_mt[1]])
            tt("dve", mt[2][:, 0:n], zi, cs, ALU.mult, RZ, [b_mt[2]])
            tt("dve", xr, mt[0][:, 0:n], mt[1][:, 0:n], ALU.subtract, [b_mt[0], b_mt[1]], [b_Xdc[ch]])
            tt("dve", xi, mt[2][:, 0:n], mt[3][:, 0:n], ALU.add, [b_mt[2], b_mt[3]], [b_Xdc[ch]])
        for c in range(4):
            for i in range(8):
                py, b_py = next_psf()

                def f(e, c=c, i=i, py=py):
                    ins = None
                    for r4 in range(4):
                        for ri in range(2):
                            ins = e.matmul(py[32 * r4:32 * r4 + 32, 0:NO], CbTc[:, i, ri, 4 * c + r4, :],
                                           Xd[:, ri, 4 * c + r4, 0:NO], start=(ri == 0), stop=(ri == 1),
                                           tile_position=(0, 32 * r4), skip_group_check=True)
                    return ins
                S.op("pe", f, [b_CbTc, b_Xdc[c]], [b_py], cost=0.3 + 8 * NO / 2400.0)
                tt("dve", ysJ[:, c, i, :], ysJ[:, c, i, :], py[:, 0:NO], ALU.add, [b_py, b_ysi[c][i]], [b_ysi[c][i]])

    if dbg:
        for c in range(4):
            dma("dpool", dbg_d[c, :, 0:T_EXT], ysJ[:, c, :, :], reads=[b_ys[c]])

    if stop <= 3:
        return finish()
    if stop <= 4:
        return finish()
    AR_U = region(55.5, 87.5)
    qT = AR_U.alloc([128, 4, T_QKV], BF16); b_qT = Buf("qT")
    kT = AR_U.alloc([128, T_QKV], BF16); b_kT = Buf("kT")
    vaug = AR_U.alloc([128, 20, 2, 65], BF16); b_v = Buf("vaug")
    dma("dsync", kT[:, 0:T_RP], ks_d[:, 0:T_RP], reads=[b_ksd], writes=[b_kT] + b_uT)
    dma("dsync", vaug[:, 0:18, :, :].rearrange("p t g d -> p t (g d)"), vs_d[:, 0:18, :], reads=[b_vsd], writes=[b_v] + b_uT)
    for c in range(4):
        dma("dsync", qT[:, c, 0:T_EXT], qs_d[:, c, 0:T_EXT], reads=[b_qsd], writes=[b_qT] + b_uT)
    S.barrier()
    AR_H = region(173.5, 207.8)
    h2T = AR_H.alloc([128, 8, T_EXT], BF16); b_h2T = Buf("h2T")
    AR = region(87.5, 173.5)
    wout_bf = AR.alloc([128, 8, D], BF16); b_wout = Buf("wout")
    wglu_bf = AR.alloc([128, 4, 512], BF16); b_wglu = Buf("wglu")
    _w2 = AR.alloc([128, D], F32); _bw2 = Buf("wst2")
    wst2 = [_w2, _w2]; b_wst2 = [_bw2, _bw2]
    for kc in range(8):
        dma("dsync", wst2[kc % 2][:], wout_d[kc * 128:(kc + 1) * 128, :], writes=[b_wst2[kc % 2]])
        acopy(wout_bf[:, kc, :], wst2[kc % 2][:], [b_wst2[kc % 2]], [b_wout])
    for kc in range(4):
        dma("dsync", wst2[kc % 2][:, 0:512], wglu_d[kc * 128:(kc + 1) * 128, :], writes=[b_wst2[kc % 2]])
        acopy(wglu_bf[:, kc, :], wst2[kc % 2][:, 0:512], [b_wst2[kc % 2]], [b_wglu])
    PT = [[AR.alloc([128, 512], BF16) for _ in range(6)] for _ in range(2)]
    b_PT = [[Buf(f"PT{h}{i}") for i in range(6)] for h in range(2)]
    rden = [AR.alloc([128, 8], F32) for _ in range(2)]; b_rden = [Buf("rden0"), Buf("rden1")]
    attn_ = [AR.alloc([128, 512], F32) for _ in range(2)]; b_attn_ = [Buf("attn0"), Buf("attn1")]
    mixed_ = [AR.alloc([128, D], BF16) for _ in range(2)]; b_mixa = [Buf("mixa0"), Buf("mixa1")]; b_mixs = [Buf("mixs0"), Buf("mixs1")]
    mixT_ = [AR.alloc([128, 8, 128], BF16) for _ in range(2)]; b_mixT_ = [Buf("mixT0"), Buf("mixT1")]
    ysg_ = [AR.alloc([128, 4, 128], F32) for _ in range(2)]; b_ysg_ = [Buf("ysg0"), Buf("ysg1")]
    ysgb_ = [AR.alloc([128, 4, 128], BF16) for _ in range(2)]; b_ysgb_ = [Buf("ysgb0"), Buf("ysgb1")]
    sig_ = [AR.alloc([128, 4, 128], F32) for _ in range(2)]; b_sig_ = [Buf("sig0"), Buf("sig1")]
    ys2_ = [AR.alloc([128, 4, 128], BF16) for _ in range(2)]; b_ys2_ = [Buf("ys20"), Buf("ys21")]
    junk_ = [AR.alloc([128, 512], BF16) for _ in range(2)]; b_junk_ = [Buf("junk0"), Buf("junk1")]
    ysT_ = [AR.alloc([128, 512], BF16) for _ in range(2)]; b_ysT_ = [Buf("ysT0"), Buf("ysT1")]
    xc = [AR.alloc([128, D], F32) for _ in range(2)]; b_xc = [Buf("xc0"), Buf("xc1")]
    x1 = [AR.alloc([128, D], F32) for _ in range(2)]; b_x1 = [Buf("x1a"), Buf("x1b")]
    h2b = [AR.alloc([128, D], BF16) for _ in range(2)]; b_h2b = [Buf("h2b0"), Buf("h2b1")]
    b_x1d = [Buf(f"x1d{i}") for i in range(17)]
    po_ = {}

    def rstd_exp(sc, b_sc, n_feat):
        acopy(sc[:, 1:2], sc[:, 0:1], [b_sc, b_epsc], [b_sc], func=AF.Ln, scale=1.0 / n_feat, bias=epsc[:, 0:1])
        acopy(sc[:, 3:4], sc[:, 1:2], [b_sc], [b_sc], func=AF.Exp, scale=-0.5)

    def ssq_dve(junk, src, sc):
        return lambda e: e.scalar_tensor_tensor(out=junk, in0=src, scalar=1.0, in1=src, op0=ALU.mult, op1=ALU.mult,
                                                accum_out=sc[:, 0:1])

    def rms_to(dst, src, b_src_list, gain_off, junk, b_junk, b_dst):
        sc, b_sc = next_stat()
        S.op("dve", ssq_dve(junk[:, 0:512], src, sc), b_src_list, [b_junk, b_sc])
        rstd_exp(sc, b_sc, 512)
        stt(dst, src, sc[:, 3:4], gains[:, gain_off:gain_off + 512], ALU.mult, ALU.mult,
            b_src_list + [b_sc, b_gains], [b_dst])

    def s_scores(n):
        h = n % 2
        cols = slice(n * 128, (n + 1) * 128)
        for g in range(2):
            rows = slice(64 * g, 64 * g + 64)
            for kb in (n - 1, n, n + 1):
                if kb < 0:
                    continue
                slot = g * 3 + (kb - n + 1)
                pf, b_pf = next_psf()
                mm(pf[:].rearrange("p (j t) -> p j t", j=4),
                   [(kT[rows, kb * 128:(kb + 1) * 128], qT[rows, :, cols])], [b_kT, b_qT], [b_pf])
                acopy(PT[h][slot][:], pf[:], [b_pf], [b_PT[h][slot]], func=AF.Exp, scale=0.125)
                if kb != n:
                    m_ = msk[:, 0:128] if kb == n - 1 else msk[:, 128:256]
                    tt("dve", PT[h][slot][:].rearrange("p (j t) -> p j t", j=4),
                       PT[h][slot][:].rearrange("p (j t) -> p j t", j=4),
                       m_.unsqueeze(1).to_broadcast([128, 4, 128]), ALU.mult, [b_PT[h][slot], b_msk], [b_PT[h][slot]])

    def s_pv(n):
        h = n % 2
        for g in range(2):
            pts = [(kb, g * 3 + (kb - n + 1)) for kb in (n - 1, n, n + 1) if kb >= 0]
            pog, b_pog = next_psf()
            for j in range(4):
                mm(pog[:, j * 65:(j + 1) * 65],
                   [(PT[h][slot][:, j * 128:(j + 1) * 128], vaug[:, kb, g, :]) for (kb, slot) in pts],
                   [b_PT[h][s_] for (_, s_) in pts] + [b_v], [b_pog])
            o3 = pog[:, 0:260].rearrange("p (j d) -> p j d", j=4)
            tt("dve", rden[h][:, 4 * g:4 * g + 4], o3[:, :, 64], esink[:, 4 * g:4 * g + 4], ALU.add,
               [b_pog, b_esink], [b_rden[h]])
            S.op("dve", lambda e, g=g, h=h: e.reciprocal(out=rden[h][:, 4 * g:4 * g + 4], in_=rden[h][:, 4 * g:4 * g + 4]),
                 [b_rden[h]], [b_rden[h]])
            tt("dve", attn_[h][:, 256 * g:256 * g + 256].rearrange("p (j d) -> p j d", j=4), o3[:, :, 0:64],
               rden[h][:, 4 * g:4 * g + 4].unsqueeze(2).to_broadcast([128, 4, 64]), ALU.mult,
               [b_pog, b_rden[h]], [b_attn_[h]])
        rms_to(mixed_[h][:, 0:512], attn_[h][:], [b_attn_[h]], G_ATT, junk_[h], b_junk_[h], b_mixa[h])

    def s_ssm(n):
        h = n % 2
        for c in range(4):
            acopy(ysg_[h][:, c, :].rearrange("p (n i) -> p i n", i=8), ysJ[:, c, :, 16 * n:16 * n + 16], [b_ys[c]],
                  [b_ysg_[h]])
        acopy(ysgb_[h][:], ysg_[h][:], [b_ysg_[h]], [b_ysgb_[h]])
        for co in range(4):
            pf, b_pf = next_psf()
            mm(pf[:, 0:128], [(wglu_bf[:, kc, co * 128:(co + 1) * 128], ysgb_[h][:, kc, :]) for kc in range(4)],
               [b_wglu, b_ysgb_[h]], [b_pf])
            acopy(sig_[h][:, co, :], pf[:, 0:128], [b_pf], [b_sig_[h]], func=AF.Exp, scale=-1.0)
        acopy(sig_[h][:], sig_[h][:], [b_sig_[h], b_onec2], [b_sig_[h]], func=AF.Ln, bias=onec2[:, 0:1])
        acopy(sig_[h][:], sig_[h][:], [b_sig_[h]], [b_sig_[h]], func=AF.Exp, scale=-1.0)
        tt("dve", ys2_[h][:], ysg_[h][:], sig_[h][:], ALU.mult, [b_ysg_[h], b_sig_[h]], [b_ys2_[h]])
        pb, b_pb = next_psb()
        transposes([(pb[:, c * 128:(c + 1) * 128], ys2_[h][:, c, :]) for c in range(4)], ident[:],
                   [b_ys2_[h], b_ident], [b_pb])
        acopy(ysT_[h][:], pb[:, 0:512], [b_pb], [b_ysT_[h]])
        rms_to(mixed_[h][:, 512:1024], ysT_[h][:], [b_ysT_[h]], G_SSM, junk_[h], b_junk_[h], b_mixs[h])

    def s_out(n):
        h = n % 2
        pb, b_pb = next_psb()
        transposes([(pb[:, k * 128:(k + 1) * 128], mixed_[h][:, k * 128:(k + 1) * 128]) for k in range(8)],
                   ident[:], [b_mixa[h], b_mixs[h], b_ident], [b_pb])
        acopy(mixT_[h][:], pb[:].rearrange("p (k t) -> p k t", k=8), [b_pb], [b_mixT_[h]])
        dma("dsync", xc[h][:], x_d[n * 128:(n + 1) * 128, :], writes=[b_xc[h]])
        for hf in range(2):
            pf, b_pf = next_psf()
            mm(pf[:], [(mixT_[h][:, kc, :], wout_bf[:, kc, hf * 512:(hf + 1) * 512]) for kc in range(8)],
               [b_mixT_[h], b_wout], [b_pf])
            tt("dve", x1[h][:, hf * 512:(hf + 1) * 512], pf[:], xc[h][:, hf * 512:(hf + 1) * 512], ALU.add,
               [b_pf, b_xc[h]], [b_x1[h]])
        dma("dpool", x1_d[n * 128:(n + 1) * 128, :], x1[h][:], reads=[b_x1[h]], writes=[b_x1d[n]])
        norm_transpose(None, (), G_FFN, x1[h], b_x1[h], h2b[h], b_h2b[h], h2T, b_h2T, n * 128, from_dram=False)

    for c in range(4):
        acopy(ysJ[:, c, :, :], ysJ[:, c, :, :], [b_ys[c]] + b_ysi[c], [b_ys[c]], func=AF.Gelu)
    s_scores(0)
    s_ssm(0)
    for n in range(17):
        if n + 1 < 17:
            s_scores(n + 1)
        s_pv(n)
        if n + 1 < 17:
            s_ssm(n + 1)
        s_out(n)


    if stop <= 5:
        return finish()
    S.barrier()
    AR_ACT = region(21.5, 110)
    actT = AR_ACT.alloc([128, NPAIR, T_OWN], BF16); b_actT = Buf("actT")
    AR = region(110, 173.5)
    HT = T_OWN // 2
    NU = HT + 2
    up = [[AR.alloc([128, NU], F32) for _ in range(2)] for _ in range(2)]
    b_up = [[Buf(f"up{h}{g}") for g in range(2)] for h in range(2)]
    cv = [[AR.alloc([128, HT], F32) for _ in range(2)] for _ in range(2)]
    b_cv = [[Buf(f"cv{h}{g}") for g in range(2)] for h in range(2)]
    wus = [AR.alloc([128, 8, 128], F32) for _ in range(4)]; b_wus = [Buf(f"wus{i}") for i in range(4)]
    wub = [AR.alloc([128, 8, 128], BF16) for _ in range(4)]; b_wub = [Buf(f"wub{i}") for i in range(4)]
    for g in range(2):
        S.op("pool", lambda e, g=g: e.memset(up[0][g][:, 0:1], 0.0), (), [b_up[0][g]])
    def load_pair(p):
        for gv in range(2):
            wi = (p % 2) * 2 + gv
            c0 = gv * DFF + p * 128
            dma("dsync", wus[wi][:], wup_d[:, c0:c0 + 128].rearrange("(k p) f -> p k f", p=128), writes=[b_wus[wi]])
            if gv == 0:
                acopy(wub[wi][:], wus[wi][:], [b_wus[wi]], [b_wub[wi]])
            else:
                tcopy("dve", wub[wi][:], wus[wi][:], [b_wus[wi]], [b_wub[wi]])

    def stage_a(p, hf):
        for gv in range(2):
            wi = (p % 2) * 2 + gv
            ch = gv * NPAIR + p
            u_, b_u = up[hf][gv], b_up[hf][gv]
            if hf == 0:
                segs = [(0, 512, 1), (512, 512, 513), (1024, 1, 1025)]
            else:
                segs = [(1023, 1, 0), (1024, 512, 1), (1536, 512, 513), (2048, 1, 1025)]
            for si, (t0, n, dc) in enumerate(segs):
                pf, b_pf = next_psf()
                mm(pf[:, 0:n], [(wub[wi][:, kc, :], h2T[:, kc, t0:t0 + n]) for kc in range(8)],
                   [b_wub[wi], b_h2T], [b_pf])
                acopy(u_[:, dc:dc + n], pf[:, 0:n], [b_pf], [b_u])
            c_, b_c = cv[hf][gv], b_cv[hf][gv]
            acopy(c_[:], u_[:, 1:1 + HT], [b_u, b_cw, b_cb], [b_c], func=AF.Identity,
                  scale=cw[:, 44 + ch:44 + ch + 1], bias=cb[:, ch:ch + 1])
            stt(c_[:], u_[:, 0:HT], cw[:, ch:ch + 1], c_[:], ALU.mult, ALU.add, [b_u, b_cw, b_c], [b_c])
            stt(c_[:], u_[:, 2:2 + HT], cw[:, 88 + ch:88 + ch + 1], c_[:], ALU.mult, ALU.add, [b_u, b_cw, b_c], [b_c])

    def stage_b(p, hf):
        acopy(cv[hf][0][:], cv[hf][0][:], [b_cv[hf][0]], [b_cv[hf][0]], func=AF.Silu)
        tt("dve", actT[:, p, hf * HT:(hf + 1) * HT], cv[hf][0][:], cv[hf][1][:], ALU.mult,
           [b_cv[hf][0], b_cv[hf][1]], [b_actT])

    load_pair(0)
    for p in range(NPAIR):
        if p + 1 < NPAIR:
            load_pair(p + 1)
        stage_a(p, 0)
        if p > 0:
            stage_b(p - 1, 1)
        stage_a(p, 1)
        stage_b(p, 0)
    stage_b(NPAIR - 1, 1)

    S.barrier()
    AR = region(110, 207.8)
    wdn_bf = AR.alloc([128, NPAIR, D], BF16); b_wdnp = [Buf(f"wdn{p}") for p in range(NPAIR)]
    wds = [AR.alloc([128, D], F32) for _ in range(4)]; b_wds = [Buf(f"wds{i}") for i in range(4)]
    for p in range(NPAIR):
        dma("dsync", wds[p % 4][:], wdn_d[p * 128:(p + 1) * 128, :], writes=[b_wds[p % 4]])
        if p % 2 == 0:
            acopy(wdn_bf[:, p, :], wds[p % 4][:], [b_wds[p % 4]], [b_wdnp[p]])
        else:
            tcopy("dve", wdn_bf[:, p, :], wds[p % 4][:], [b_wds[p % 4]], [b_wdnp[p]])
    x1r = [AR.alloc([128, D], F32) for _ in range(2)]; b_x1r = [Buf("x1r0"), Buf("x1r1")]
    x2 = [AR.alloc([128, D], F32) for _ in range(2)]; b_x2 = [Buf("x2a"), Buf("x2b")]
    yo = [AR.alloc([128, D], F32) for _ in range(2)]; b_yo = [Buf("yo0"), Buf("yo1")]
    jk = AR.alloc([128, D], BF16); b_jk = Buf("jk")
    NE = 3
    early = [[next_psf() for _ in range(2)] for _ in range(NE)]
    for p in range(NPAIR):
        for n in range(NE):
            for hf in range(2):
                pf, b_pf = early[n][hf]
                S.op("pe", lambda e, pf=pf, p=p, n=n, hf=hf: e.matmul(
                    pf[:], actT[:, p, n * 128:(n + 1) * 128], wdn_bf[:, p, hf * 512:(hf + 1) * 512],
                    start=(p == 0), stop=(p == NPAIR - 1)), [b_actT, b_wdnp[p]], [b_pf], cost=0.25)
    for n in range(16):
        i2 = n % 2
        dma("dsync", x1r[i2][:], x1_d[n * 128:(n + 1) * 128, :], reads=[b_x1d[n]], writes=[b_x1r[i2]])
        for hf in range(2):
            if n < NE:
                pf, b_pf = early[n][hf]
            else:
                pf, b_pf = next_psf()
                mm(pf[:], [(actT[:, p, n * 128:(n + 1) * 128], wdn_bf[:, p, hf * 512:(hf + 1) * 512])
                           for p in range(NPAIR)], [b_actT] + b_wdnp, [b_pf])
            tt("dve", x2[i2][:, hf * 512:(hf + 1) * 512], pf[:], x1r[i2][:, hf * 512:(hf + 1) * 512], ALU.add,
               [b_pf, b_x1r[i2]], [b_x2[i2]])
        sc, b_sc = next_stat()
        acopy(jk[:], x2[i2][:], [b_x2[i2]], [b_jk, b_sc], func=AF.Square, accum=sc[:, 0:1])
        ts("dve", sc[:, 1:2], sc[:, 0:1], 1.0 / D, EPS, ALU.mult, ALU.add, [b_sc], [b_sc])
        acopy(sc[:, 2:3], sc[:, 1:2], [b_sc], [b_sc], func=AF.Sqrt)
        S.op("dve", lambda e, sc=sc: e.reciprocal(out=sc[:, 3:4], in_=sc[:, 2:3]), [b_sc], [b_sc])
        stt(yo[i2][:], x2[i2][:], sc[:, 3:4], gains[:, G_FIN:G_FIN + D], ALU.mult, ALU.mult,
            [b_x2[i2], b_sc, b_gains], [b_yo[i2]])
        dma("dpool", y_d[n * 128:(n + 1) * 128, :], yo[i2][:], reads=[b_yo[i2]])

    return finish()


def _consts():
    ident = np.eye(128, dtype=np.float32)
    R = np.zeros((128, 128), np.float32)
    for hh in range(2):
        for d in range(8):
            R[hh * 64 + d, hh * 64 + d + 8] = -1.0
            R[hh * 64 + d + 8, hh * 64 + d] = 1.0
    rmat = np.ascontiguousarray(R.T)
    ifr = np.zeros((128, 1), np.float32)
    base = np.power(np.float32(500000.0), -np.arange(8, dtype=np.float32) / np.float32(8.0)).astype(np.float32)
    for hh in range(2):
        for d in range(16):
            ifr[hh * 64 + d, 0] = base[d % 8]
    kk = np.arange(128)[:, None]
    qq = np.arange(128)[None, :]
    msk = np.concatenate([(kk >= qq), (kk <= qq)], axis=1).astype(np.float32)
    bm = (np.arange(128)[:, None] // 16 == np.arange(128)[None, :] // 16).astype(np.float32)
    return ident, rmat, ifr, msk, bm


_NC_CACHE = {}


def _prep_inputs(inp, dbg=False):
    f = lambda a: np.ascontiguousarray(np.asarray(a, dtype=np.float32))
    x = f(inp["x"])
    w_in = f(inp["w_in"][0])
    qcols = np.concatenate([np.r_[j * 64:(j + 1) * 64, (4 + j) * 64:(5 + j) * 64] for j in range(4)])
    w_in_r = np.ascontiguousarray(np.concatenate([w_in[:, qcols], w_in[:, 512:]], axis=1))
    gains = np.concatenate([f(inp["norm_mix_g"][0]), f(inp["norm_ffn_g"][0]), f(inp["norm_final_g"]),
                            f(inp["norm_attn_g"][0]), f(inp["norm_ssm_g"][0])])[None, :]
    ident, rmat, ifr, msk, bm = _consts()
    a_re, a_im = f(inp["a_re"][0]), f(inp["a_im"][0])
    lst = np.broadcast_to(f(inp["log_step"][0])[:, :, None], (2, 32, 64))
    b_re, b_im = f(inp["b_re"][0]), f(inp["b_im"][0])
    c_re, c_im = f(inp["c_re"][0]), f(inp["c_im"][0])
    cwf = f(inp["conv_w"][0])
    shared = dict(
        w_in=w_in_r, gains=np.ascontiguousarray(gains),
        dsk=np.ascontiguousarray(f(inp["d_skip"][0]).reshape(4, 128).T),
        w_glu=f(inp["w_glu"][0]), sink=f(inp["sink"]), w_out=f(inp["w_out"][0]), w_up=f(inp["w_up"][0]),
        cb=np.ascontiguousarray(f(inp["conv_b"][0]).reshape(44, 128).T),
        w_down=f(inp["w_down"][0]), ident=ident, rmat=rmat, ifr=ifr, msk=msk, bmask=bm,
        sidx=np.arange(512, dtype=np.float32)[None, :])

    def ep(a):
        return np.ascontiguousarray(a.reshape(2, 16, 2, 64).transpose(2, 3, 0, 1).reshape(128, 32))

    def ep_b(a):
        return np.ascontiguousarray(a.reshape(2, 16, 2, 64, 16).transpose(2, 3, 0, 1, 4).reshape(128, 512))

    def ep_c(a):
        return np.ascontiguousarray(a.reshape(2, 16, 2, 16, 64).transpose(2, 4, 0, 1, 3).reshape(128, 512))

    per_half = []
    for h in range(2):
        sl = slice(None) if h == 0 else slice(None, None, -1)
        cwh = cwf if h == 0 else cwf[::-1]
        pos = np.arange(T_QKV, dtype=np.float32) if h == 0 else (T_ALL - 1 - np.arange(T_QKV)).astype(np.float32)
        per_half.append(dict(
            are=ep(a_re[sl]), aim=ep(a_im[sl]), ls=ep(lst[sl]),
            bre=ep_b(b_re[sl]), bim=ep_b(b_im[sl]), cre=ep_c(c_re[sl]), cim=ep_c(c_im[sl]),
            cw=np.ascontiguousarray(cwh.reshape(3, 44, 128).transpose(2, 0, 1).reshape(128, 132)),
            pos=np.ascontiguousarray(pos[None, :])))
    in_maps = []
    for c in range(8):
        b, h = c // 2, c % 2
        xs = x[b] if h == 0 else x[b][::-1]
        m = dict(shared)
        m.update(per_half[h])
        m["x"] = np.ascontiguousarray(xs)
        in_maps.append(m)
    return in_maps


def kernel(**inputs):
    in_maps = _prep_inputs(inputs)
    if "nc" not in _NC_CACHE:
        _NC_CACHE["nc"] = build_program()
    nc = _NC_CACHE["nc"]
    res = run_bass_kernel_spmd(nc, in_maps, core_ids=list(range(8)))
    out = np.empty((4, T_ALL, D), np.float32)
    for c in range(8):
        b, h = c // 2, c % 2
        y = np.asarray(res.results[c]["y"], dtype=np.float32)
        if h == 0:
            out[b, :T_OWN] = y
        else:
            out[b, T_OWN:] = y[::-1]
    return out
```

```python
import numpy as np
import concourse.bass as bass
import concourse.mybir as mybir
from concourse.bass_utils import run_bass_kernel_spmd

F32 = mybir.dt.float32
BF16 = mybir.dt.bfloat16
I32 = mybir.dt.int32
ALU = mybir.AluOpType
AF = mybir.ActivationFunctionType

D = 1024
T_ALL = 4096
T_OWN = 2048
T_EXT = 2176
T_QKV = 2560
NQKV_G = 5
DFF = 2816
NPAIR = 22
EPS = 1e-6
TWO_PI = float(2.0 * np.pi)
SB_BASE = 16512
import os as _os
_C = lambda k, d: float(_os.environ.get(k, d))
C_ACT_F, C_ACT_E = _C("KS_ACT_F", 0.28), _C("KS_ACT_E", 850.0)
C_DVE_F, C_DVE_E = _C("KS_DVE_F", 0.15), _C("KS_DVE_E", 800.0)
C_PE_F, C_PE_E = _C("KS_PE_F", 0.3), _C("KS_PE_E", 2400.0)
C_DMA_F, C_DMA_B = _C("KS_DMA_F", 2.0), _C("KS_DMA_B", 150e3)
SB_END = 229376


class Buf:
    __slots__ = ("name", "writer", "readers", "excl")

    def __init__(self, name, excl=False):
        self.name = name
        self.writer = None
        self.readers = {}
        self.excl = excl


class Stream:
    def __init__(self, name, eng_name, inc, sem):
        self.name, self.eng_name, self.inc, self.sem, self.count = name, eng_name, inc, sem, 0


class Op:
    __slots__ = ("stream", "fn", "preds", "cost", "prog", "end", "start", "sidx", "nsucc", "bar_counts")

    def __init__(self, stream, fn, preds, cost, prog):
        self.stream, self.fn, self.preds, self.cost, self.prog = stream, fn, preds, cost, prog
        self.end = self.start = None
        self.sidx = None
        self.bar_counts = None


class Sched:
    ENGS = ("tensor", "vector", "scalar", "gpsimd", "sync")
    NDS = 12
    HOP = _C("KS_HOP", 0.45)
    NDP = 4

    def __init__(self, sems):
        names = [("pe", "tensor", 1), ("dve", "vector", 1), ("act", "scalar", 1), ("pool", "gpsimd", 1)]
        names += [(f"dsync{i}", "sync", 16) for i in range(self.NDS)]
        names += [(f"dpool{i}", "gpsimd", 16) for i in range(self.NDP)]
        assert len(sems) == len(names)
        self.streams = {n: Stream(n, e, inc, s) for (n, e, inc), s in zip(names, sems)}
        self.slots = {n: Buf("slot_" + n) for n in self.streams if n.startswith("ds") or n.startswith("dp")}
        self.rr = {"dsync": 0, "dpool": 0}
        self.ops = []
        self.last_barrier = None
        self.since_barrier = []

    def op(self, stream, fn, reads=(), writes=(), cost=0.5):
        if stream in self.rr:
            k = self.rr[stream]
            self.rr[stream] = (k + 1) % (self.NDS if stream == "dsync" else self.NDP)
            stream = f"{stream}{k}"
            writes = list(writes) + [self.slots[stream]]
        ex = [b for b in reads if b.excl]
        if ex:
            reads = [b for b in reads if not b.excl]
            writes = list(writes) + ex
        preds = set()
        for b in reads:
            if b.writer is not None:
                preds.add(b.writer)
        for b in writes:
            if b.writer is not None:
                preds.add(b.writer)
            for o in b.readers.values():
                preds.add(o)
        if self.last_barrier is not None:
            preds.add(self.last_barrier)
        o = Op(stream, fn, sorted(preds, key=lambda p_: p_.prog), cost, len(self.ops))
        self.ops.append(o)
        self.since_barrier.append(o)
        for b in reads:
            b.readers[id(o)] = o
        for b in writes:
            b.writer = o
            b.readers = {}
        return o

    def barrier(self):
        if not self.since_barrier:
            return
        b = Op(None, None, list(self.since_barrier) + ([self.last_barrier] if self.last_barrier else []), 0.0, len(self.ops))
        self.ops.append(b)
        self.last_barrier = b
        self.since_barrier = []

    def schedule(self):
        import heapq
        ops = self.ops
        npred = {id(o): len(o.preds) for o in ops}
        succ = {id(o): [] for o in ops}
        for o in ops:
            for p in o.preds:
                succ[id(p)].append(o)
        eng_free = {e: 0.0 for e in self.ENGS}
        eng_of = lambda o: self.streams[o.stream].eng_name
        heap = []

        def ready_time(o):
            return max([p.end for p in o.preds], default=0.0) + self.HOP

        def push(o):
            if o.stream is None:
                o.start = o.end = ready_time(o)
                release(o)
                return
            rt = ready_time(o)
            heapq.heappush(heap, (max(rt, eng_free[eng_of(o)]), o.prog, rt, o))

        def release(o):
            for s_ in succ[id(o)]:
                npred[id(s_)] -= 1
                if npred[id(s_)] == 0:
                    push(s_)
        for o in ops:
            if npred[id(o)] == 0:
                push(o)
        nsched = 0
        while heap:
            key, prog, rt, o = heapq.heappop(heap)
            e = eng_of(o)
            st_ = max(rt, eng_free[e])
            if st_ > key + 1e-9:
                heapq.heappush(heap, (st_, prog, rt, o))
                continue
            o.start = st_
            is_dma = self.streams[o.stream].inc == 16
            o.end = st_ + o.cost
            eng_free[e] = st_ + (0.08 if is_dma else o.cost)
            nsched += 1
            release(o)
        assert all(o.start is not None for o in ops), "scheduler: unscheduled ops (cycle?)"
        self.makespan = max(o.end for o in ops)

    def schedule_hlf(self):
        import heapq
        ops = self.ops
        succ = {id(o): [] for o in ops}
        for o in ops:
            for p in o.preds:
                succ[id(p)].append(o)
        bl = {}
        for o in reversed(ops):
            m = 0.0
            for s_ in succ[id(o)]:
                v = bl[id(s_)] + self.HOP
                if v > m:
                    m = v
            bl[id(o)] = o.cost + m
        npred = {id(o): len(o.preds) for o in ops}
        eng_of = lambda o: self.streams[o.stream].eng_name
        eng_free = {e: 0.0 for e in self.ENGS}
        cand = {e: [] for e in self.ENGS}
        avail = {e: [] for e in self.ENGS}

        def rtime(o):
            return (max(p.end for p in o.preds) + self.HOP) if o.preds else 0.0

        def release(o):
            for s_ in succ[id(o)]:
                npred[id(s_)] -= 1
                if npred[id(s_)] == 0:
                    add(s_)

        def add(o):
            if o.stream is None:
                o.start = o.end = max([p.end for p in o.preds], default=0.0)
                release(o)
            else:
                heapq.heappush(cand[eng_of(o)], (rtime(o), o.prog, o))
        for o in ops:
            if npred[id(o)] == 0:
                add(o)
        left = sum(1 for o in ops if o.stream is not None)
        while left:
            best_e, best_t = None, None
            for e in self.ENGS:
                while cand[e] and cand[e][0][0] <= eng_free[e] + 1e-9:
                    rt, pg, o = heapq.heappop(cand[e])
                    heapq.heappush(avail[e], (-bl[id(o)], pg, o))
                if avail[e]:
                    t_ = eng_free[e]
                elif cand[e]:
                    t_ = max(eng_free[e], cand[e][0][0])
                else:
                    continue
                if best_t is None or t_ < best_t:
                    best_e, best_t = e, t_
            e = best_e
            if not avail[e]:
                eng_free[e] = best_t
                continue
            _, pg, o = heapq.heappop(avail[e])
            o.start = best_t
            is_dma = self.streams[o.stream].inc == 16
            o.end = best_t + o.cost
            eng_free[e] = best_t + (0.08 if is_dma else o.cost)
            left -= 1
            release(o)
        assert all(o.start is not None for o in ops)

    def emit(self, block):
        self.barrier()
        if _C("KS_MODE", 1) >= 1:
            self.schedule_hlf()
        else:
            self.schedule()
        per_eng = {e: [] for e in self.ENGS}
        for o in self.ops:
            if o.stream is not None:
                per_eng[self.streams[o.stream].eng_name].append(o)
        for e in self.ENGS:
            per_eng[e].sort(key=lambda o: (o.start, o.prog))
            for o in per_eng[e]:
                st = self.streams[o.stream]
                st.count += 1
                o.sidx = st.count
        cnt = {n: 0 for n in self.streams}
        for o in self.ops:
            if o.stream is None:
                o.bar_counts = dict(cnt)
            else:
                cnt[o.stream] += 1
        final_counts = dict(cnt)
        progs = {e: [] for e in self.ENGS}
        for e in self.ENGS:
            seen = {}
            for o in per_eng[e]:
                need = {}
                for p in o.preds:
                    if p.stream is None:
                        for sn, c in p.bar_counts.items():
                            if c and need.get(sn, 0) < c:
                                need[sn] = c
                    else:
                        if p.stream == "pe" and o.stream == "pe":
                            continue
                        if need.get(p.stream, 0) < p.sidx:
                            need[p.stream] = p.sidx
                waits = []
                for sn, c in need.items():
                    if seen.get(sn, 0) >= c:
                        continue
                    seen[sn] = c
                    waits.append((self.streams[sn].sem, c * self.streams[sn].inc))
                progs[e].append((waits, o.fn, self.streams[o.stream].sem, self.streams[o.stream].inc))
            waits = [(self.streams[sn].sem, c * self.streams[sn].inc) for sn, c in final_counts.items()
                     if c and seen.get(sn, 0) < c]
            progs[e].append((waits, None, None, 0))

        def run(engname):
            def body(eh):
                for waits, fn, sem, inc in progs[engname]:
                    for (ws, wv) in waits:
                        eh.wait_ge(ws, wv)
                    if fn is not None:
                        fn(eh).then_inc(sem, inc)
            return body
        block.tensor(run("tensor"))
        block.vector(run("vector"))
        block.scalar(run("scalar"))
        block.gpsimd(run("gpsimd"))
        block.sync(run("sync"))


def build_program(dbg=False, stop=99):
    nc = bass.Bass("TRN2", target_bir_lowering=False)

    def din(name, shape, dt=F32):
        return nc.dram_tensor(name, list(shape), dt, kind="ExternalInput").ap()

    x_d = din("x", [T_ALL, D])
    pos_d = din("pos", [1, T_QKV])
    win_d = din("w_in", [D, 1280])
    gains_d = din("gains", [1, 4096])
    are_d = din("are", [128, 32])
    aim_d = din("aim", [128, 32])
    ls_d = din("ls", [128, 32])
    bre_d = din("bre", [128, 512])
    bim_d = din("bim", [128, 512])
    cre_d = din("cre", [128, 512])
    cim_d = din("cim", [128, 512])
    dsk_d = din("dsk", [128, 4])
    wglu_d = din("w_glu", [512, 512])
    sink_d = din("sink", [1, 8])
    wout_d = din("w_out", [D, D])
    wup_d = din("w_up", [D, 2 * DFF])
    cw_d = din("cw", [128, 3 * 44])
    cb_d = din("cb", [128, 44])
    wdn_d = din("w_down", [DFF, D])
    ident_d = din("ident", [128, 128])
    rmat_d = din("rmat", [128, 128])
    ifr_d = din("ifr", [128, 1])
    msk_d = din("msk", [128, 256])
    bmask_d = din("bmask", [128, 128])
    sidx_d = din("sidx", [1, 512])
    y_d = nc.dram_tensor("y", [T_OWN, D], F32, kind="ExternalOutput").ap()
    x1_d = nc.dram_tensor("x1_scr", [T_EXT, D], F32, kind="Internal").ap()
    qs_d = nc.dram_tensor("q_scr", [128, 4, T_QKV], BF16, kind="Internal").ap()
    ks_d = nc.dram_tensor("k_scr", [128, T_QKV], BF16, kind="Internal").ap()
    vs_d = nc.dram_tensor("v_scr", [128, 20, 130], BF16, kind="Internal").ap()
    dbg_d = None
    if dbg:
        dbg_d = nc.dram_tensor("dbg", [8, 128, 2560], F32, kind="ExternalOutput").ap()

    import contextlib
    _es = contextlib.ExitStack()
    sems = [_es.enter_context(nc.semaphore(f"sem{i}")) for i in range(4 + Sched.NDS + Sched.NDP)]
    S = Sched(sems)

    _sbn = [0]
    def finish():
        with nc.Block() as block:
            S.emit(block)
        _es.close()
        return nc

    class Arena:
        def __init__(self, lo, hi):
            self.lo, self.hi, self.cur, self.n = lo, hi, lo, 0

        def alloc(self, shape, dt):
            nbytes = int(np.prod(shape[1:])) * (4 if dt in (F32, I32) else 2)
            nbytes = (nbytes + 63) // 64 * 64
            off = self.cur
            assert off + nbytes <= self.hi, (shape, off, nbytes, self.hi)
            self.cur += nbytes
            _sbn[0] += 1
            return nc.alloc_sbuf_tensor_at(f"sb{_sbn[0]}", list(shape), dt, offset=off)

        def mark(self):
            return self.cur

        def reset(self, m):
            self.cur = m

    def region(lo_kb, hi_kb):
        return Arena(SB_BASE + int(lo_kb * 1024), min(SB_END, SB_BASE + int(hi_kb * 1024)))

    AR = region(0, 21.5)
    AR_YS = region(21.5, 55.5)
    _psn = [0]

    def psum(dt=F32):
        _psn[0] += 1
        return nc.alloc_psum_tensor(f"ps{_psn[0]}", [128, 512 if dt == F32 else 1024], dt)

    PSF = [(psum(F32), Buf(f"psf{i}", excl=True)) for i in range(6)]
    PSB = [(psum(BF16), Buf(f"psb{i}", excl=True)) for i in range(2)]
    _rr = {"f": 0, "b": 0}

    def next_psf():
        _rr["f"] = (_rr["f"] + 1) % len(PSF)
        return PSF[_rr["f"]]

    def next_psb():
        _rr["b"] = (_rr["b"] + 1) % len(PSB)
        return PSB[_rr["b"]]

    def fsz(ap):
        n = 1
        for d_ in ap.shape[1:]:
            n *= int(d_)
        return n

    def dma(q, out, in_, reads=(), writes=()):
        nbytes = fsz(out) * int(out.shape[0]) * 4
        S.op(q, lambda e: e.dma_start(out=out, in_=in_), reads, writes, cost=C_DMA_F + nbytes / C_DMA_B)

    def tcopy(st, out, in_, reads=(), writes=()):
        S.op(st, lambda e: e.tensor_copy(out=out, in_=in_), reads, writes,
             cost=(C_DVE_F + fsz(out) / C_DVE_E) * (4 if st == "pool" else 1))

    def acopy(out, in_, reads=(), writes=(), func=AF.Copy, scale=1.0, bias=None, accum=None):
        def f(e):
            kw = {}
            if bias is not None:
                kw["bias"] = bias
            if accum is not None:
                kw["accum_out"] = accum
            return e.activation(out=out, in_=in_, func=func, scale=scale, **kw)
        S.op("act", f, reads, writes, cost=C_ACT_F + fsz(out) / C_ACT_E)

    def tt(st, out, in0, in1, op, reads=(), writes=()):
        S.op(st, lambda e: e.tensor_tensor(out=out, in0=in0, in1=in1, op=op), reads, writes,
             cost=(C_DVE_F + fsz(out) / C_DVE_E) * (4 if st == "pool" else 1))

    def ts(st, out, in0, s1, s2, op0, op1=None, reads=(), writes=()):
        c_ = (C_DVE_F + fsz(out) / C_DVE_E) * (4 if st == "pool" else 1)
        if op1 is None:
            S.op(st, lambda e: e.tensor_scalar(out=out, in0=in0, scalar1=s1, scalar2=None, op0=op0), reads, writes, cost=c_)
        else:
            S.op(st, lambda e: e.tensor_scalar(out=out, in0=in0, scalar1=s1, scalar2=s2, op0=op0, op1=op1), reads, writes,
                 cost=c_)

    def stt(out, in0, scalar, in1, op0, op1, reads=(), writes=()):
        S.op("dve", lambda e: e.scalar_tensor_tensor(out=out, in0=in0, scalar=scalar, in1=in1, op0=op0, op1=op1),
             reads, writes, cost=C_DVE_F + fsz(out) / (C_DVE_E * 0.83))

    def mm(out, pairs, reads=(), writes=()):
        def f(e):
            ins = None
            n = len(pairs)
            for i, (l, r) in enumerate(pairs):
                ins = e.matmul(out, l, r, start=(i == 0), stop=(i == n - 1))
            return ins
        S.op("pe", f, reads, writes, cost=C_PE_F + sum(max(64, fsz(r)) / C_PE_E + 0.01 for (_, r) in pairs))

    def transposes(outs_ins, ident, reads=(), writes=()):
        def f(e):
            ins = None
            for (o, i_) in outs_ins:
                ins = e.transpose(o, i_, ident)
            return ins
        S.op("pe", f, reads, writes, cost=0.3 + 0.1 * len(outs_ins))

    gains = AR.alloc([128, 4096], F32); b_gains = Buf("gains")
    ident_f = AR.alloc([128, 128], F32)
    ident = AR.alloc([128, 128], BF16); b_ident = Buf("ident")
    rmat_f = AR.alloc([128, 128], F32)
    rmat = AR.alloc([128, 128], BF16); b_rmat = Buf("rmat")
    msk_f = AR.alloc([128, 256], F32)
    msk = AR.alloc([128, 256], BF16); b_msk = Buf("msk")
    ifr = AR.alloc([128, 1], F32); b_ifr = Buf("ifr")
    dsk = AR.alloc([128, 4], F32); b_dsk = Buf("dsk")
    esink = AR.alloc([128, 8], F32); b_esink = Buf("esink")
    cw = AR.alloc([128, 132], F32); b_cw = Buf("cw")
    cb = AR.alloc([128, 44], F32); b_cb = Buf("cb")
    epsc = AR.alloc([128, 1], F32); b_epsc = Buf("epsc")
    onec2 = AR.alloc([128, 1], F32); b_onec2 = Buf("onec2")
    stat = AR.alloc([128, 64], F32)
    b_stat = [Buf(f"stat{i}") for i in range(16)]
    _st = [0]

    def next_stat():
        _st[0] = (_st[0] + 1) % 16
        return stat[:, 4 * _st[0]:4 * _st[0] + 4], b_stat[_st[0]]

    b_tmp = Buf("ldtmp")
    S.op("dve", lambda e: e.memset(epsc[:], EPS), (), [b_epsc])
    S.op("dve", lambda e: e.memset(onec2[:], 1.0), (), [b_onec2])
    dma("dsync", gains[:], gains_d.partition_broadcast(128), writes=[b_gains])
    dma("dsync", ident_f[:], ident_d[:, :], writes=[b_tmp])
    dma("dsync", rmat_f[:], rmat_d[:, :], writes=[b_tmp])
    dma("dsync", msk_f[:], msk_d[:, :], writes=[b_tmp])
    dma("dsync", ifr[:], ifr_d[:, :], writes=[b_ifr])
    dma("dsync", dsk[:], dsk_d[:, :], writes=[b_dsk])
    dma("dsync", esink[:], sink_d.partition_broadcast(128), writes=[b_esink])
    dma("dsync", cw[:], cw_d[:, :], writes=[b_cw])
    dma("dsync", cb[:], cb_d[:, :], writes=[b_cb])
    tcopy("dve", ident[:], ident_f[:], [b_tmp], [b_ident])
    tcopy("dve", rmat[:], rmat_f[:], [b_tmp], [b_rmat])
    tcopy("dve", msk[:], msk_f[:], [b_tmp], [b_msk])
    acopy(esink[:], esink[:], [b_esink], [b_esink], func=AF.Exp)
    G_MIX, G_FFN, G_FIN, G_ATT, G_SSM = 0, 1024, 2048, 3072, 3584

    ysJ = AR_YS.alloc([128, 4, 8, T_EXT // 8], F32); b_ys = [Buf(f"ys{c}") for c in range(4)]

    def norm_transpose_group(tiles, exp_set=False):
        for (src, go, xt, b_xt, hb, b_hb, hT, b_hT, col0) in tiles:
            if src is not None:
                dma("dsync", xt[:], src, writes=[b_xt])
        scs = [next_stat() for _ in tiles]
        if exp_set:
            for (src, go, xt, b_xt, hb, b_hb, hT, b_hT, col0), (sc, b_sc) in zip(tiles, scs):
                S.op("dve", ssq_dve(hb[:], xt[:], sc), [b_xt], [b_hb, b_sc])
                rstd_exp(sc, b_sc, D)
        else:
            for (src, go, xt, b_xt, hb, b_hb, hT, b_hT, col0), (sc, b_sc) in zip(tiles, scs):
                acopy(hb[:], xt[:], [b_xt], [b_hb, b_sc], func=AF.Square, accum=sc[:, 0:1])
            for (sc, b_sc) in scs:
                ts("dve", sc[:, 1:2], sc[:, 0:1], 1.0 / D, EPS, ALU.mult, ALU.add, [b_sc], [b_sc])
            for (sc, b_sc) in scs:
                acopy(sc[:, 2:3], sc[:, 1:2], [b_sc], [b_sc], func=AF.Sqrt)
            for (sc, b_sc) in scs:
                S.op("dve", lambda e, sc=sc: e.reciprocal(out=sc[:, 3:4], in_=sc[:, 2:3]), [b_sc], [b_sc])
        for (src, go, xt, b_xt, hb, b_hb, hT, b_hT, col0), (sc, b_sc) in zip(tiles, scs):
            stt(hb[:], xt[:], sc[:, 3:4], gains[:, go:go + D], ALU.mult, ALU.mult, [b_xt, b_sc, b_gains], [b_hb])
        for (src, go, xt, b_xt, hb, b_hb, hT, b_hT, col0) in tiles:
            pb, b_pb = next_psb()
            transposes([(pb[:, k * 128:(k + 1) * 128], hb[:, k * 128:(k + 1) * 128]) for k in range(8)],
                       ident[:], [b_hb, b_ident], [b_pb])
            if not exp_set and (col0 // 128) % 2 == 1:
                tcopy("dve", hT[:, :, col0:col0 + 128], pb[:].rearrange("p (k t) -> p k t", k=8), [b_pb], [b_hT])
            else:
                acopy(hT[:, :, col0:col0 + 128], pb[:].rearrange("p (k t) -> p k t", k=8), [b_pb], [b_hT])

    def norm_transpose(src_ap, src_reads, gain_off, xt, b_xt, hb, b_hb, hT, b_hT, col0, from_dram=True, stop=99):
        norm_transpose_group([(src_ap if from_dram else None, gain_off, xt, b_xt, hb, b_hb, hT, b_hT, col0)],
                             exp_set=not from_dram)

    AR_P = region(196.25, 207.8)
    prm = AR_P.alloc([128, 21, 32], F32); b_prm = Buf("prm")
    (P_ARE, P_AIM, P_DT, P_ER, P_TH, P_C, P_S, P_LR, P_LI, P_T0, P_T1, P_T2, P_CR, P_CI, P_R8, P_T3,
     P_T4, P_T5, P_T6, P_T7, P_F8) = range(21)
    pri = AR_P.alloc([128, 32], I32)
    PW = AR_P.alloc([128, 9, 2, 32], F32); b_PW = Buf("PW")
    craw = AR_P.alloc([128, 2, 512], F32); b_craw = Buf("craw")
    sidx = AR_P.alloc([128, 512], F32); b_sidx = Buf("sidx")
    onec = AR_P.alloc([128, 1], F32); b_onec = Buf("onec")

    def P(i):
        return prm[:, i, :]

    dma("dsync", P(P_ARE), are_d[:, :], writes=[b_prm])
    dma("dsync", P(P_AIM), aim_d[:, :], writes=[b_prm])
    dma("dsync", P(P_DT), ls_d[:, :], writes=[b_prm])
    dma("dsync", craw[:, 0, :], cre_d[:, :], writes=[b_craw])
    dma("dsync", craw[:, 1, :], cim_d[:, :], writes=[b_craw])
    dma("dsync", sidx[:], sidx_d.partition_broadcast(128), writes=[b_sidx])
    S.op("pool", lambda e: e.memset(onec[:], 1.0), (), [b_onec])
    R, W_ = [b_prm], [b_prm]
    acopy(P(P_DT), P(P_DT), R, W_, func=AF.Exp)
    tt("dve", P(P_T0), P(P_ARE), P(P_DT), ALU.mult, R, W_)
    acopy(P(P_ER), P(P_T0), R, W_, func=AF.Exp)
    tt("dve", P(P_TH), P(P_AIM), P(P_DT), ALU.mult, R, W_)
    ts("dve", P(P_T0), P(P_TH), 1.0 / TWO_PI, None, ALU.mult, reads=R, writes=W_)
    tcopy("dve", pri[:], P(P_T0), R, W_)
    tcopy("dve", P(P_T0), pri[:], R, W_)
    stt(P(P_T1), P(P_T0), -TWO_PI, P(P_TH), ALU.mult, ALU.add, R, W_)
    ts("dve", P(P_T1), P(P_T1), float(np.pi), float(-np.pi), ALU.min, ALU.max, R, W_)
    acopy(P(P_S), P(P_T1), R, W_, func=AF.Sin)
    acopy(P(P_T2), P(P_T1), R, W_, func=AF.Sin, scale=0.5)
    tt("dve", P(P_T2), P(P_T2), P(P_T2), ALU.mult, R, W_)
    ts("dve", P(P_C), P(P_T2), -2.0, 1.0, ALU.mult, ALU.add, R, W_)
    tt("dve", P(P_LR), P(P_ER), P(P_C), ALU.mult, R, W_)
    tt("dve", P(P_LI), P(P_ER), P(P_S), ALU.mult, R, W_)
    tt("dve", P(P_T0), P(P_ER), P(P_ER), ALU.mult, R, W_)
    tt("dve", P(P_T0), P(P_T0), P(P_T0), ALU.mult, R, W_)
    tt("dve", P(P_R8), P(P_T0), P(P_T0), ALU.mult, R, W_)
    ts("dve", P(P_T0), P(P_LR), -1.0, None, ALU.add, reads=R, writes=W_)
    tt("dve", P(P_T1), P(P_ARE), P(P_ARE), ALU.mult, R, W_)
    tt("dve", P(P_T2), P(P_AIM), P(P_AIM), ALU.mult, R, W_)
    tt("dve", P(P_T1), P(P_T1), P(P_T2), ALU.add, R, W_)
    S.op("dve", lambda e: e.reciprocal(out=P(P_T3), in_=P(P_T1)), R, W_)
    tt("dve", P(P_T1), P(P_T0), P(P_ARE), ALU.mult, R, W_)
    tt("dve", P(P_T2), P(P_LI), P(P_AIM), ALU.mult, R, W_)
    tt("dve", P(P_T1), P(P_T1), P(P_T2), ALU.add, R, W_)
    tt("dve", P(P_CR), P(P_T1), P(P_T3), ALU.mult, R, W_)
    tt("dve", P(P_T1), P(P_LI), P(P_ARE), ALU.mult, R, W_)
    tt("dve", P(P_T2), P(P_T0), P(P_AIM), ALU.mult, R, W_)
    tt("dve", P(P_T1), P(P_T1), P(P_T2), ALU.subtract, R, W_)
    tt("dve", P(P_CI), P(P_T1), P(P_T3), ALU.mult, R, W_)
    ts("dve", P(P_T4), P(P_TH), 8.0 / TWO_PI, None, ALU.mult, reads=R, writes=W_)
    tcopy("dve", pri[:], P(P_T4), R, W_)
    tcopy("dve", P(P_T5), pri[:], R, W_)
    tt("dve", P(P_F8), P(P_T4), P(P_T5), ALU.subtract, R, W_)
    RP = [b_prm, b_PW]
    S.op("dve", lambda e: e.memset(PW[:, 0, 0, :], 1.0), (), [b_PW])
    S.op("dve", lambda e: e.memset(PW[:, 0, 1, :], 0.0), (), [b_PW])
    tcopy("dve", PW[:, 1, 0, :], P(P_LR), RP, [b_PW])
    tcopy("dve", PW[:, 1, 1, :], P(P_LI), RP, [b_PW])
    for k in range(2, 9):
        ar_, ai_ = PW[:, k - 1, 0, :], PW[:, k - 1, 1, :]
        tt("dve", P(P_T4), ar_, P(P_LR), ALU.mult, RP, W_)
        tt("dve", P(P_T5), ai_, P(P_LI), ALU.mult, RP, W_)
        tt("dve", PW[:, k, 0, :], P(P_T4), P(P_T5), ALU.subtract, RP, [b_PW])
        tt("dve", P(P_T6), ar_, P(P_LI), ALU.mult, RP, W_)
        tt("dve", P(P_T7), ai_, P(P_LR), ALU.mult, RP, W_)
        tt("dve", PW[:, k, 1, :], P(P_T6), P(P_T7), ALU.add, RP, [b_PW])


    if stop <= 0:
        return finish()
    AR_U = region(55.5, 87.5)
    uJ = AR_U.alloc([128, 4, 8, T_ALL // 8], BF16); b_uT = [Buf(f"uT{c}") for c in range(4)]
    AR = region(87.5, 196.25)
    w_in_bf = AR.alloc([128, 8, 1280], BF16); b_win = Buf("w_in")
    wst = [AR.alloc([128, 1280], F32) for _ in range(2)]; b_wst = [Buf("wst0"), Buf("wst1")]
    xts = [AR.alloc([128, D], F32) for _ in range(4)]; b_xts = [Buf(f"xt{i}") for i in range(4)]
    hbs = [AR.alloc([128, D], BF16) for _ in range(4)]; b_hbs = [Buf(f"hb{i}") for i in range(4)]
    hTs = [AR.alloc([128, 8, 512], BF16) for _ in range(2)]; b_hTs = [Buf("hT0"), Buf("hT1")]
    T_RP = 2304
    AR_T = region(21.5, 55.5)
    cosT = AR_T.alloc([128, T_RP], F32); sinT = AR_T.alloc([128, T_RP], F32); b_cs = Buf("cossin")
    angb = AR.alloc([128, 512], F32); angi = AR.alloc([128, 512], I32); b_ang = Buf("ang")
    qb = [AR.alloc([128, 512], BF16) for _ in range(2)]; b_qb = [Buf("qb0"), Buf("qb1")]
    rt2 = [[AR.alloc([128, 512], F32) for _ in range(2)] for _ in range(2)]
    b_rt2 = [[Buf(f"rt{a_}{b_}") for b_ in range(2)] for a_ in range(2)]
    qst = [AR.alloc([128, 4, 512], BF16) for _ in range(2)]; b_qst = [Buf("qst0"), Buf("qst1")]
    kst = [AR.alloc([128, 512], BF16) for _ in range(2)]; b_kst = [Buf("kst0"), Buf("kst1")]
    vst = [AR.alloc([128, 4, 130], BF16) for _ in range(2)]; b_vst = [Buf("vst0"), Buf("vst1")]
    b_qsd = Buf("q_scr"); b_ksd = Buf("k_scr"); b_vsd = Buf("v_scr")

    def load_w_in():
        for kc in range(8):
            dma("dsync", wst[kc % 2][:], win_d[kc * 128:(kc + 1) * 128, :], writes=[b_wst[kc % 2]])
            if kc % 2 == 0:
                acopy(w_in_bf[:, kc, :], wst[kc % 2][:], [b_wst[kc % 2]], [b_win])
            else:
                tcopy("dve", w_in_bf[:, kc, :], wst[kc % 2][:], [b_wst[kc % 2]], [b_win])

    load_w_in()
    for blk in range(5):
        c0 = blk * 512
        nb = min(512, T_RP - c0)
        cs_, sn_b = cosT[:, c0:c0 + nb], sinT[:, c0:c0 + nb]
        dma("dsync", angb[:, 0:nb], pos_d[:, c0:c0 + nb].partition_broadcast(128), writes=[b_ang])
        ts("dve", angb[:, 0:nb], angb[:, 0:nb], ifr[:, 0:1], None, ALU.mult, reads=[b_ang, b_ifr], writes=[b_ang])
        ts("dve", sn_b, angb[:, 0:nb], 1.0 / TWO_PI, None, ALU.mult, reads=[b_ang], writes=[b_cs])
        tcopy("dve", angi[:, 0:nb], sn_b, [b_cs], [b_ang])
        tcopy("dve", sn_b, angi[:, 0:nb], [b_ang], [b_cs])
        stt(angb[:, 0:nb], sn_b, -TWO_PI, angb[:, 0:nb], ALU.mult, ALU.add, [b_ang, b_cs], [b_ang])
        ts("dve", angb[:, 0:nb], angb[:, 0:nb], float(np.pi), float(-np.pi), ALU.min, ALU.max, [b_ang], [b_ang])
        acopy(sn_b, angb[:, 0:nb], [b_ang], [b_cs], func=AF.Sin)
        acopy(cs_, angb[:, 0:nb], [b_ang], [b_cs], func=AF.Sin, scale=0.5)
        tt("dve", cs_, cs_, cs_, ALU.mult, [b_cs], [b_cs])
        ts("dve", cs_, cs_, -2.0, 1.0, ALU.mult, ALU.add, [b_cs], [b_cs])
    for i_ in range(2):
        S.op("pool", lambda e, i_=i_: e.memset(vst[i_][:], 1.0), (), [b_vst[i_]])
    if stop <= 0.2:
        return finish()
    for g4 in range(8):
        hT, b_hT = hTs[g4 % 2], b_hTs[g4 % 2]
        norm_transpose_group([(x_d[(g4 * 4 + j) * 128:(g4 * 4 + j + 1) * 128, :], G_MIX, xts[j], b_xts[j], hbs[j], b_hbs[j],
                               hT, b_hT, j * 128) for j in range(4)])
        if stop <= 0.7:
            return finish()
        for c in range(4):
            pf, b_pf = next_psf()
            mm(pf[:], [(w_in_bf[:, kc, 768 + c * 128:768 + (c + 1) * 128], hT[:, kc, :]) for kc in range(8)],
               [b_win, b_hT], [b_pf])
            if stop <= 0.75:
                return finish()
            acopy(uJ[:, c, :, g4 * 64:(g4 + 1) * 64], pf[:].rearrange("p (n j) -> p j n", j=8), [b_pf], [b_uT[c]])
            if stop <= 0.8:
                return finish()
            if stop <= 0.85:
                return finish()
        if g4 < NQKV_G:
            nt = 4 if g4 < 4 else 2
            nn = nt * 128
            cols = slice(g4 * 512, g4 * 512 + nn)
            sp = g4 % 2
            for c in range(5):
                pf, b_pf = next_psf()
                mm(pf[:, 0:nn], [(w_in_bf[:, kc, c * 128:(c + 1) * 128], hT[:, kc, 0:nn]) for kc in range(8)],
                   [b_win, b_hT], [b_pf])
                i2 = c % 2
                acopy(qb[i2][:, 0:nn], pf[:, 0:nn], [b_pf], [b_qb[i2]])
                pr_, b_pr_ = next_psf()
                mm(pr_[:, 0:nn], [(rmat[:], qb[i2][:, 0:nn])], [b_rmat, b_qb[i2]], [b_pr_])
                ra, rb_, b_ra, b_rb = rt2[i2][0], rt2[i2][1], b_rt2[i2][0], b_rt2[i2][1]
                tt("dve", ra[:, 0:nn], pf[:, 0:nn], cosT[:, cols], ALU.mult, [b_pf, b_cs], [b_ra])
                tt("dve", rb_[:, 0:nn], pr_[:, 0:nn], sinT[:, cols], ALU.mult, [b_pr_, b_cs], [b_rb])
                if c < 4:
                    tt("dve", qst[sp][:, c, 0:nn], ra[:, 0:nn], rb_[:, 0:nn], ALU.add, [b_ra, b_rb], [b_qst[sp]])
                else:
                    tt("dve", kst[sp][:, 0:nn], ra[:, 0:nn], rb_[:, 0:nn], ALU.add, [b_ra, b_rb], [b_kst[sp]])
            for j in range(nt):
                pf, b_pf = next_psf()
                mm(pf[:, 0:128], [(hT[:, kc, j * 128:(j + 1) * 128], w_in_bf[:, kc, 640:768]) for kc in range(8)],
                   [b_win, b_hT], [b_pf])
                acopy(vst[sp][:, j, :].rearrange("p (g d) -> p g d", g=2)[:, :, 0:64],
                      pf[:, 0:128].rearrange("p (g d) -> p g d", g=2), [b_pf], [b_vst[sp]])
            dma("dpool", qs_d[:, :, cols], qst[sp][:, :, 0:nn], reads=[b_qst[sp]], writes=[b_qsd])
            dma("dpool", ks_d[:, cols], kst[sp][:, 0:nn], reads=[b_kst[sp]], writes=[b_ksd])
            dma("dpool", vs_d[:, g4 * 4:g4 * 4 + nt, :], vst[sp][:, 0:nt, :], reads=[b_vst[sp]], writes=[b_vsd])
        if stop <= 0.9:
            return finish()

    if stop <= 1:
        return finish()
    S.barrier()
    AR = region(87.5, 196.25)
    NO, NA = T_EXT // 8, T_ALL // 8
    Tz = AR.alloc([128, 4, 15, 128], BF16); b_Tz = Buf("Tz")
    BbTc = AR.alloc([128, 8, 2, 8, 128], BF16); b_BbTc = Buf("BbTc")
    CbTc = AR.alloc([128, 8, 2, 16, 32], BF16); b_CbTc = Buf("CbTc")
    mB = AR.mark()
    bbar = AR.alloc([128, 2, 512], F32); b_bbar = Buf("bbar")
    braw = AR.alloc([128, 2, 512], F32); b_braw = Buf("braw")
    tmpbs = [AR.alloc([128, 2, 512], F32) for _ in range(2)]; b_tmpbs = [Buf("tmpb0"), Buf("tmpb1")]
    bpows = [AR.alloc([128, 2, 512], F32) for _ in range(2)]; b_bpows = [Buf("bpow0"), Buf("bpow1")]
    Mps = [AR.alloc([128, 2, 8, 128], BF16) for _ in range(2)]; b_Mps = [Buf("Mp0"), Buf("Mp1")]
    tmpb, b_tmpb = tmpbs[0], b_tmpbs[0]
    Cp = AR.alloc([128, 2, 8, 128], BF16); b_Cp = Buf("Cp")
    bmask = AR.alloc([128, 128], F32); b_bmask = Buf("bmask")
    diagD = AR.alloc([128, 4, 128], F32); b_diagD = Buf("diagD")
    tzt = AR.alloc([128, 128], F32); b_tzt = Buf("tzt")

    dma("dsync", braw[:, 0, :], bre_d[:, :], writes=[b_braw])
    dma("dsync", braw[:, 1, :], bim_d[:, :], writes=[b_braw])
    dma("dsync", bmask[:], bmask_d[:, :], writes=[b_bmask])
    def v3(ap2):
        return ap2.rearrange("p (a c) -> p a c", c=16)

    def bc32(ap32):
        return ap32.unsqueeze(2).to_broadcast([128, 32, 16])

    def cmul(dst, b_dst, src, b_src, cre, cim, rd, tmpb=tmpb, b_tmpb=b_tmpb):
        RB = [b_src, b_tmpb] + rd
        tt("dve", v3(tmpb[:, 0, :]), v3(src[:, 0, :]), bc32(cre), ALU.mult, RB, [b_tmpb])
        tt("dve", v3(tmpb[:, 1, :]), v3(src[:, 1, :]), bc32(cim), ALU.mult, RB, [b_tmpb])
        tt("dve", dst[:, 0, :], tmpb[:, 0, :], tmpb[:, 1, :], ALU.subtract, [b_tmpb], [b_dst])
        tt("dve", v3(tmpb[:, 0, :]), v3(src[:, 1, :]), bc32(cre), ALU.mult, RB, [b_tmpb])
        tt("dve", v3(tmpb[:, 1, :]), v3(src[:, 0, :]), bc32(cim), ALU.mult, RB, [b_tmpb])
        tt("dve", dst[:, 1, :], tmpb[:, 0, :], tmpb[:, 1, :], ALU.add, [b_tmpb], [b_dst])

    cmul(bbar, b_bbar, braw, b_braw, P(P_CR), P(P_CI), [b_prm])

    def pack(dst, b_dst, src, b_src, neg_im):
        for ri in range(2):
            for e_ in range(2):
                ps_ = slice(e_ * 64, (e_ + 1) * 64)
                s_ = src[ps_, ri, :].rearrange("p (k r c) -> p k r c", r=4, c=16)
                d_ap = dst[ps_, ri, :, :].rearrange("p k (r x) -> p k r x", x=32)[:, :, :, 16 * e_:16 * e_ + 16]
                if neg_im and ri == 1:
                    acopy(d_ap, s_, [b_src], [b_dst], scale=-1.0)
                else:
                    acopy(d_ap, s_, [b_src], [b_dst])

    for Mp, b_Mp in zip(Mps, b_Mps):
        S.op("pool", lambda e, Mp=Mp: e.memset(Mp[:], 0.0), (), [b_Mp])
    S.op("pool", lambda e: e.memset(Cp[:], 0.0), (), [b_Cp])
    pack(Cp, b_Cp, craw, b_craw, True)
    for c in range(4):
        ts("dve", diagD[:, c, :], ident_f[:], dsk[:, c:c + 1], None, ALU.mult, reads=[b_tmp, b_dsk], writes=[b_diagD])
    for k in range(8):
        bpow, b_bpow, Mp, b_Mp = bpows[k % 2], b_bpows[k % 2], Mps[k % 2], b_Mps[k % 2]
        cmul(bpow, b_bpow, bbar, b_bbar, PW[:, k, 0, :], PW[:, k, 1, :], [b_PW], tmpb=tmpbs[k % 2], b_tmpb=b_tmpbs[k % 2])
        pack(Mp, b_Mp, bpow, b_bpow, False)
        for c in range(4):
            if k == 0:
                pf, b_pf = next_psf()
                mm(pf[:, 0:128], [(Mp[:, ri, 4 * d_ + c, :], Cp[:, ri, 4 * d_ + c, :]) for d_ in range(2) for ri in range(2)],
                   [b_Mp, b_Cp], [b_pf])
                tt("dve", tzt[:], pf[:, 0:128], bmask[:], ALU.mult, [b_pf, b_bmask], [b_tzt])
                tt("dve", Tz[:, c, 7, :], tzt[:], diagD[:, c, :], ALU.add, [b_tzt, b_diagD], [b_Tz])
            else:
                for d_ in range(2):
                    pf, b_pf = next_psf()
                    mm(pf[:, 0:128], [(Mp[:, ri, 4 * d_ + c, :], Cp[:, ri, 4 * d_ + c, :]) for ri in range(2)],
                       [b_Mp, b_Cp], [b_pf])
                    idx = 7 + k if d_ == 0 else 7 - k
                    tt("dve", Tz[:, c, idx, :], pf[:, 0:128], bmask[:], ALU.mult, [b_pf, b_bmask], [b_Tz])
        for d_ in range(2):
            j = 7 - k if d_ == 0 else k
            pb, b_pb = next_psb()
            transposes([(pb[:, (ri * 4 + kk) * 128:(ri * 4 + kk + 1) * 128], Mp[:, ri, 4 * d_ + kk, :])
                        for ri in range(2) for kk in range(4)], ident[:], [b_Mp, b_ident], [b_pb])
            acopy(BbTc[:, j, :, 4 * d_:4 * d_ + 4, :], pb[:].rearrange("p (r k x) -> p r k x", r=2, k=4), [b_pb], [b_BbTc])

    if stop <= 2:
        return finish()
    S.barrier()
    AR.reset(mB)
    Xd = AR.alloc([128, 2, 16, NO], BF16); b_Xdc = [Buf(f"Xd{c}") for c in range(4)]
    Cts = [AR.alloc([128, NA], F32) for _ in range(2)]; Sns = [AR.alloc([128, NA], F32) for _ in range(2)]
    b_tabs = [Buf("tab0"), Buf("tab1")]
    bufA = AR.alloc([128, NA], F32); bufI = AR.alloc([128, NA], I32); b_bufA = Buf("bufA"); b_bufI = Buf("bufI")

    def gen_table(cq, Nd, par):
        Ct_, Sn_, b_t = Cts[par], Sns[par], b_tabs[par]
        acopy(bufA[:, 0:Nd], sidx[:, 0:Nd], [b_sidx, b_prm], [b_bufA], func=AF.Copy, scale=prm[:, P_F8, cq:cq + 1])
        acopy(bufI[:, 0:Nd], bufA[:, 0:Nd], [b_bufA], [b_bufI])
        tt("dve", bufA[:, 0:Nd], bufA[:, 0:Nd], bufI[:, 0:Nd], ALU.subtract, [b_bufA, b_bufI], [b_bufA])
        acopy(Sn_[:, 0:Nd], bufA[:, 0:Nd], [b_bufA], [b_t], func=AF.Sin, scale=6.283185)
        acopy(Ct_[:, 0:Nd], bufA[:, 0:Nd], [b_bufA], [b_t], func=AF.Sin, scale=3.141592)
        acopy(Ct_[:, 0:Nd], Ct_[:, 0:Nd], [b_t], [b_t], func=AF.Square)
        acopy(Ct_[:, 0:Nd], Ct_[:, 0:Nd], [b_t, b_onec], [b_t], func=AF.Identity, scale=-2.0, bias=onec[:, 0:1])
    Wr = AR.alloc([128, NA], BF16); Wi = AR.alloc([128, NA], BF16); b_W = Buf("W")
    Zr = AR.alloc([128, NA], BF16); Zi = AR.alloc([128, NA], BF16); b_Z = Buf("Z")
    mt = [AR.alloc([128, 512], F32) for _ in range(4)]; b_mt = [Buf(f"mt{i}") for i in range(4)]
    cl = AR.alloc([128, 2, 256], F32); b_cl = Buf("cl")
    clt = AR.alloc([128, 2, 256], F32); b_clt = Buf("clt")

    def v3h(ap2):
        return ap2.rearrange("p (a c) -> p a c", c=16)

    b_ysi = [[Buf(f"ysJ{c}_{i}") for i in range(8)] for c in range(4)]
    for c in range(4):
        for i in range(8):
            py, b_py = next_psf()
            mm(py[:, 0:NO], [(Tz[:, c, i - j + 7, :], uJ[:, c, j, 0:NO]) for j in range(8)], [b_Tz, b_uT[c]], [b_py])
            acopy(ysJ[:, c, i, :], py[:, 0:NO], [b_py], [b_ysi[c][i]])
    for d_ in range(2):
        Nd = NO if d_ == 0 else NA
        S.op("pool", lambda e: e.memset(CbTc[:], 0.0), (), [b_CbTc])
        for i in range(8):
            pw = i + 1 if d_ == 0 else 8 - i
            lr = PW[:, pw, 0, 16 * d_:16 * d_ + 16].unsqueeze(2).to_broadcast([128, 16, 16])
            li = PW[:, pw, 1, 16 * d_:16 * d_ + 16].unsqueeze(2).to_broadcast([128, 16, 16])
            c_re = v3h(craw[:, 0, 256 * d_:256 * d_ + 256]); c_im = v3h(craw[:, 1, 256 * d_:256 * d_ + 256])
            RC = [b_craw, b_PW, b_clt]
            tt("dve", v3h(clt[:, 0, :]), c_re, lr, ALU.mult, RC, [b_clt])
            tt("dve", v3h(clt[:, 1, :]), c_im, li, ALU.mult, RC, [b_clt])
            tt("dve", cl[:, 0, :], clt[:, 0, :], clt[:, 1, :], ALU.subtract, [b_clt], [b_cl])
            tt("dve", v3h(clt[:, 0, :]), c_re, li, ALU.mult, RC, [b_clt])
            tt("dve", v3h(clt[:, 1, :]), c_im, lr, ALU.mult, RC, [b_clt])
            stt(cl[:, 1, :], clt[:, 0, :], -1.0, clt[:, 1, :], ALU.mult, ALU.subtract, [b_clt], [b_cl])
            for ri in range(2):
                for e_ in range(2):
                    ps_ = slice(e_ * 64, (e_ + 1) * 64)
                    acopy(CbTc[ps_, i, ri, :, 16 * e_:16 * e_ + 16], v3h(cl[ps_, ri, :]), [b_cl], [b_CbTc])
        if d_ == 0:
            S.op("pool", lambda e: e.memset(Xd[:, :, :, 0:1], 0.0), (), list(b_Xdc))
        for q in range(16):
            cq = d_ * 16 + q
            ch, r4 = q // 4, q % 4
            rows = slice(32 * r4, 32 * r4 + 32)
            par = q % 2
            if q == 0:
                gen_table(cq, Nd, par)
            Ct, Sn, b_tab = Cts[par], Sns[par], b_tabs[par]
            pr, b_pr = next_psf()
            pi_, b_pi = next_psf()
            for ri, (pv, b_pv) in enumerate(((pr, b_pr), (pi_, b_pi))):
                def f(e, ri=ri, pv=pv, rows=rows, ch=ch, d_=d_, Nd=Nd, r4=r4):
                    ins = None
                    for j in range(8):
                        ins = e.matmul(pv[:, 0:Nd], BbTc[rows, j, ri, 4 * d_ + ch, :], uJ[rows, ch, j, 0:Nd],
                                       start=(j == 0), stop=(j == 7), tile_position=(32 * r4, 0))
                    return ins
                S.op("pe", f, [b_BbTc, b_uT[ch]], [b_pv], cost=0.3 + 8 * Nd / 2400.0)
            if d_ == 0:
                cs, sn_, wr, wi = Ct[:, 0:Nd], Sn[:, 0:Nd], Wr[:, 0:Nd], Wi[:, 0:Nd]
            else:
                cs, sn_ = Ct[:, 0:Nd][:, ::-1], Sn[:, 0:Nd][:, ::-1]
                wr, wi = Wr[:, 0:Nd][:, ::-1], Wi[:, 0:Nd][:, ::-1]
            tt("dve", mt[0][:, 0:Nd], pr[:, 0:Nd], cs, ALU.mult, [b_pr, b_tab], [b_mt[0]])
            tt("dve", mt[1][:, 0:Nd], pi_[:, 0:Nd], sn_, ALU.mult, [b_pi, b_tab], [b_mt[1]])
            tt("dve", mt[2][:, 0:Nd], pi_[:, 0:Nd], cs, ALU.mult, [b_pi, b_tab], [b_mt[2]])
            tt("dve", mt[3][:, 0:Nd], pr[:, 0:Nd], sn_, ALU.mult, [b_pr, b_tab], [b_mt[3]])
            tt("dve", wr, mt[0][:, 0:Nd], mt[1][:, 0:Nd], ALU.add, [b_mt[0], b_mt[1]], [b_W])
            tt("dve", wi, mt[2][:, 0:Nd], mt[3][:, 0:Nd], ALU.subtract, [b_mt[2], b_mt[3]], [b_W])
            if q + 1 < 16:
                gen_table(cq + 1, Nd, (q + 1) % 2)
            rho = prm[:, P_R8, cq:cq + 1].to_broadcast([128, Nd])
            S.op("dve", lambda e, rho=rho, Nd=Nd: e.tensor_tensor_scan(out=Zr[:, 0:Nd], data0=rho, data1=Wr[:, 0:Nd],
                 initial=0.0, op0=ALU.mult, op1=ALU.add), [b_W, b_prm], [b_Z], cost=0.15 + 2 * Nd / 960.0)
            S.op("dve", lambda e, rho=rho, Nd=Nd: e.tensor_tensor_scan(out=Zi[:, 0:Nd], data0=rho, data1=Wi[:, 0:Nd],
                 initial=0.0, op0=ALU.mult, op1=ALU.add), [b_W, b_prm], [b_Z], cost=0.15 + 2 * Nd / 960.0)
            if d_ == 0:
                n = NO - 1
                zr, zi, cs, sn_ = Zr[:, 0:n], Zi[:, 0:n], Ct[:, 0:n], Sn[:, 0:n]
                xr, xi = Xd[:, 0, q, 1:NO], Xd[:, 1, q, 1:NO]
            else:
                n = NO
                lo = NA - 1 - NO
                zr, zi = Zr[:, lo:lo + n][:, ::-1], Zi[:, lo:lo + n][:, ::-1]
                cs, sn_ = Ct[:, lo:lo + n][:, ::-1], Sn[:, lo:lo + n][:, ::-1]
                xr, xi = Xd[:, 0, q, 0:NO], Xd[:, 1, q, 0:NO]
            RZ = [b_Z, b_tab]
            tt("dve", mt[3][:, 0:n], zr, sn_, ALU.mult, RZ, [b_mt[3]])
            tt("dve", mt[0][:, 0:n], zr, cs, ALU.mult, RZ, [b_mt[0]])
            tt("dve", mt[1][:, 0:n], zi, sn_, ALU.mult, RZ, [b_mt[1]])
            tt("dve", mt[2][:, 0:n], zi, cs, ALU.mult, RZ, [b_mt[2]])
            tt("dve", xr, mt[0][:, 0:n], mt[1][:, 0:n], ALU.subtract, [b_mt[0], b_mt[1]], [b_Xdc[ch]])
            tt("dve", xi, mt[2][:, 0:n], mt[3][:, 0:n], ALU.add, [b_mt[2], b_mt[3]], [b_Xdc[ch]])
        for c in range(4):
            for i in range(8):
                py, b_py = next_psf()

                def f(e, c=c, i=i, py=py):
                    ins = None
                    for r4 in range(4):
                        for ri in range(2):
                            ins = e.matmul(py[32 * r4:32 * r4 + 32, 0:NO], CbTc[:, i, ri, 4 * c + r4, :],
                                           Xd[:, ri, 4 * c + r4, 0:NO], start=(ri == 0), stop=(ri == 1),
                                           tile_position=(0, 32 * r4), skip_group_check=True)
                    return ins
                S.op("pe", f, [b_CbTc, b_Xdc[c]], [b_py], cost=0.3 + 8 * NO / 2400.0)
                tt("dve", ysJ[:, c, i, :], ysJ[:, c, i, :], py[:, 0:NO], ALU.add, [b_py, b_ysi[c][i]], [b_ysi[c][i]])

    if dbg:
        for c in range(4):
            dma("dpool", dbg_d[c, :, 0:T_EXT], ysJ[:, c, :, :], reads=[b_ys[c]])

    if stop <= 3:
        return finish()
    if stop <= 4:
        return finish()
    AR_U = region(55.5, 87.5)
    qT = AR_U.alloc([128, 4, T_QKV], BF16); b_qT = Buf("qT")
    kT = AR_U.alloc([128, T_QKV], BF16); b_kT = Buf("kT")
    vaug = AR_U.alloc([128, 20, 2, 65], BF16); b_v = Buf("vaug")
    dma("dsync", kT[:, 0:T_RP], ks_d[:, 0:T_RP], reads=[b_ksd], writes=[b_kT] + b_uT)
    dma("dsync", vaug[:, 0:18, :, :].rearrange("p t g d -> p t (g d)"), vs_d[:, 0:18, :], reads=[b_vsd], writes=[b_v] + b_uT)
    for c in range(4):
        dma("dsync", qT[:, c, 0:T_EXT], qs_d[:, c, 0:T_EXT], reads=[b_qsd], writes=[b_qT] + b_uT)
    S.barrier()
    AR_H = region(173.5, 207.8)
    h2T = AR_H.alloc([128, 8, T_EXT], BF16); b_h2T = Buf("h2T")
    AR = region(87.5, 173.5)
    wout_bf = AR.alloc([128, 8, D], BF16); b_wout = Buf("wout")
    wglu_bf = AR.alloc([128, 4, 512], BF16); b_wglu = Buf("wglu")
    _w2 = AR.alloc([128, D], F32); _bw2 = Buf("wst2")
    wst2 = [_w2, _w2]; b_wst2 = [_bw2, _bw2]
    for kc in range(8):
        dma("dsync", wst2[kc % 2][:], wout_d[kc * 128:(kc + 1) * 128, :], writes=[b_wst2[kc % 2]])
        acopy(wout_bf[:, kc, :], wst2[kc % 2][:], [b_wst2[kc % 2]], [b_wout])
    for kc in range(4):
        dma("dsync", wst2[kc % 2][:, 0:512], wglu_d[kc * 128:(kc + 1) * 128, :], writes=[b_wst2[kc % 2]])
        acopy(wglu_bf[:, kc, :], wst2[kc % 2][:, 0:512], [b_wst2[kc % 2]], [b_wglu])
    PT = [[AR.alloc([128, 512], BF16) for _ in range(6)] for _ in range(2)]
    b_PT = [[Buf(f"PT{h}{i}") for i in range(6)] for h in range(2)]
    rden = [AR.alloc([128, 8], F32) for _ in range(2)]; b_rden = [Buf("rden0"), Buf("rden1")]
    attn_ = [AR.alloc([128, 512], F32) for _ in range(2)]; b_attn_ = [Buf("attn0"), Buf("attn1")]
    mixed_ = [AR.alloc([128, D], BF16) for _ in range(2)]; b_mixa = [Buf("mixa0"), Buf("mixa1")]; b_mixs = [Buf("mixs0"), Buf("mixs1")]
    mixT_ = [AR.alloc([128, 8, 128], BF16) for _ in range(2)]; b_mixT_ = [Buf("mixT0"), Buf("mixT1")]
    ysg_ = [AR.alloc([128, 4, 128], F32) for _ in range(2)]; b_ysg_ = [Buf("ysg0"), Buf("ysg1")]
    ysgb_ = [AR.alloc([128, 4, 128], BF16) for _ in range(2)]; b_ysgb_ = [Buf("ysgb0"), Buf("ysgb1")]
    sig_ = [AR.alloc([128, 4, 128], F32) for _ in range(2)]; b_sig_ = [Buf("sig0"), Buf("sig1")]
    ys2_ = [AR.alloc([128, 4, 128], BF16) for _ in range(2)]; b_ys2_ = [Buf("ys20"), Buf("ys21")]
    junk_ = [AR.alloc([128, 512], BF16) for _ in range(2)]; b_junk_ = [Buf("junk0"), Buf("junk1")]
    ysT_ = [AR.alloc([128, 512], BF16) for _ in range(2)]; b_ysT_ = [Buf("ysT0"), Buf("ysT1")]
    xc = [AR.alloc([128, D], F32) for _ in range(2)]; b_xc = [Buf("xc0"), Buf("xc1")]
    x1 = [AR.alloc([128, D], F32) for _ in range(2)]; b_x1 = [Buf("x1a"), Buf("x1b")]
    h2b = [AR.alloc([128, D], BF16) for _ in range(2)]; b_h2b = [Buf("h2b0"), Buf("h2b1")]
    b_x1d = [Buf(f"x1d{i}") for i in range(17)]
    po_ = {}

    def rstd_exp(sc, b_sc, n_feat):
        acopy(sc[:, 1:2], sc[:, 0:1], [b_sc, b_epsc], [b_sc], func=AF.Ln, scale=1.0 / n_feat, bias=epsc[:, 0:1])
        acopy(sc[:, 3:4], sc[:, 1:2], [b_sc], [b_sc], func=AF.Exp, scale=-0.5)

    def ssq_dve(junk, src, sc):
        return lambda e: e.scalar_tensor_tensor(out=junk, in0=src, scalar=1.0, in1=src, op0=ALU.mult, op1=ALU.mult,
                                                accum_out=sc[:, 0:1])

    def rms_to(dst, src, b_src_list, gain_off, junk, b_junk, b_dst):
        sc, b_sc = next_stat()
        S.op("dve", ssq_dve(junk[:, 0:512], src, sc), b_src_list, [b_junk, b_sc])
        rstd_exp(sc, b_sc, 512)
        stt(dst, src, sc[:, 3:4], gains[:, gain_off:gain_off + 512], ALU.mult, ALU.mult,
            b_src_list + [b_sc, b_gains], [b_dst])

    def s_scores(n):
        h = n % 2
        cols = slice(n * 128, (n + 1) * 128)
        for g in range(2):
            rows = slice(64 * g, 64 * g + 64)
            for kb in (n - 1, n, n + 1):
                if kb < 0:
                    continue
                slot = g * 3 + (kb - n + 1)
                pf, b_pf = next_psf()
                mm(pf[:].rearrange("p (j t) -> p j t", j=4),
                   [(kT[rows, kb * 128:(kb + 1) * 128], qT[rows, :, cols])], [b_kT, b_qT], [b_pf])
                acopy(PT[h][slot][:], pf[:], [b_pf], [b_PT[h][slot]], func=AF.Exp, scale=0.125)
                if kb != n:
                    m_ = msk[:, 0:128] if kb == n - 1 else msk[:, 128:256]
                    tt("dve", PT[h][slot][:].rearrange("p (j t) -> p j t", j=4),
                       PT[h][slot][:].rearrange("p (j t) -> p j t", j=4),
                       m_.unsqueeze(1).to_broadcast([128, 4, 128]), ALU.mult, [b_PT[h][slot], b_msk], [b_PT[h][slot]])

    def s_pv(n):
        h = n % 2
        for g in range(2):
            pts = [(kb, g * 3 + (kb - n + 1)) for kb in (n - 1, n, n + 1) if kb >= 0]
            pog, b_pog = next_psf()
            for j in range(4):
                mm(pog[:, j * 65:(j + 1) * 65],
                   [(PT[h][slot][:, j * 128:(j + 1) * 128], vaug[:, kb, g, :]) for (kb, slot) in pts],
                   [b_PT[h][s_] for (_, s_) in pts] + [b_v], [b_pog])
            o3 = pog[:, 0:260].rearrange("p (j d) -> p j d", j=4)
            tt("dve", rden[h][:, 4 * g:4 * g + 4], o3[:, :, 64], esink[:, 4 * g:4 * g + 4], ALU.add,
               [b_pog, b_esink], [b_rden[h]])
            S.op("dve", lambda e, g=g, h=h: e.reciprocal(out=rden[h][:, 4 * g:4 * g + 4], in_=rden[h][:, 4 * g:4 * g + 4]),
                 [b_rden[h]], [b_rden[h]])
            tt("dve", attn_[h][:, 256 * g:256 * g + 256].rearrange("p (j d) -> p j d", j=4), o3[:, :, 0:64],
               rden[h][:, 4 * g:4 * g + 4].unsqueeze(2).to_broadcast([128, 4, 64]), ALU.mult,
               [b_pog, b_rden[h]], [b_attn_[h]])
        rms_to(mixed_[h][:, 0:512], attn_[h][:], [b_attn_[h]], G_ATT, junk_[h], b_junk_[h], b_mixa[h])

    def s_ssm(n):
        h = n % 2
        for c in range(4):
            acopy(ysg_[h][:, c, :].rearrange("p (n i) -> p i n", i=8), ysJ[:, c, :, 16 * n:16 * n + 16], [b_ys[c]],
                  [b_ysg_[h]])
        acopy(ysgb_[h][:], ysg_[h][:], [b_ysg_[h]], [b_ysgb_[h]])
        for co in range(4):
            pf, b_pf = next_psf()
            mm(pf[:, 0:128], [(wglu_bf[:, kc, co * 128:(co + 1) * 128], ysgb_[h][:, kc, :]) for kc in range(4)],
               [b_wglu, b_ysgb_[h]], [b_pf])
            acopy(sig_[h][:, co, :], pf[:, 0:128], [b_pf], [b_sig_[h]], func=AF.Exp, scale=-1.0)
        acopy(sig_[h][:], sig_[h][:], [b_sig_[h], b_onec2], [b_sig_[h]], func=AF.Ln, bias=onec2[:, 0:1])
        acopy(sig_[h][:], sig_[h][:], [b_sig_[h]], [b_sig_[h]], func=AF.Exp, scale=-1.0)
        tt("dve", ys2_[h][:], ysg_[h][:], sig_[h][:], ALU.mult, [b_ysg_[h], b_sig_[h]], [b_ys2_[h]])
        pb, b_pb = next_psb()
        transposes([(pb[:, c * 128:(c + 1) * 128], ys2_[h][:, c, :]) for c in range(4)], ident[:],
                   [b_ys2_[h], b_ident], [b_pb])
        acopy(ysT_[h][:], pb[:, 0:512], [b_pb], [b_ysT_[h]])
        rms_to(mixed_[h][:, 512:1024], ysT_[h][:], [b_ysT_[h]], G_SSM, junk_[h], b_junk_[h], b_mixs[h])

    def s_out(n):
        h = n % 2
        pb, b_pb = next_psb()
        transposes([(pb[:, k * 128:(k + 1) * 128], mixed_[h][:, k * 128:(k + 1) * 128]) for k in range(8)],
                   ident[:], [b_mixa[h], b_mixs[h], b_ident], [b_pb])
        acopy(mixT_[h][:], pb[:].rearrange("p (k t) -> p k t", k=8), [b_pb], [b_mixT_[h]])
        dma("dsync", xc[h][:], x_d[n * 128:(n + 1) * 128, :], writes=[b_xc[h]])
        for hf in range(2):
            pf, b_pf = next_psf()
            mm(pf[:], [(mixT_[h][:, kc, :], wout_bf[:, kc, hf * 512:(hf + 1) * 512]) for kc in range(8)],
               [b_mixT_[h], b_wout], [b_pf])
            tt("dve", x1[h][:, hf * 512:(hf + 1) * 512], pf[:], xc[h][:, hf * 512:(hf + 1) * 512], ALU.add,
               [b_pf, b_xc[h]], [b_x1[h]])
        dma("dpool", x1_d[n * 128:(n + 1) * 128, :], x1[h][:], reads=[b_x1[h]], writes=[b_x1d[n]])
        norm_transpose(None, (), G_FFN, x1[h], b_x1[h], h2b[h], b_h2b[h], h2T, b_h2T, n * 128, from_dram=False)

    for c in range(4):
        acopy(ysJ[:, c, :, :], ysJ[:, c, :, :], [b_ys[c]] + b_ysi[c], [b_ys[c]], func=AF.Gelu)
    s_scores(0)
    s_ssm(0)
    for n in range(17):
        if n + 1 < 17:
            s_scores(n + 1)
        s_pv(n)
        if n + 1 < 17:
            s_ssm(n + 1)
        s_out(n)


    if stop <= 5:
        return finish()
    S.barrier()
    AR_ACT = region(21.5, 110)
    actT = AR_ACT.alloc([128, NPAIR, T_OWN], BF16); b_actT = Buf("actT")
    AR = region(110, 173.5)
    HT = T_OWN // 2
    NU = HT + 2
    up = [[AR.alloc([128, NU], F32) for _ in range(2)] for _ in range(2)]
    b_up = [[Buf(f"up{h}{g}") for g in range(2)] for h in range(2)]
    cv = [[AR.alloc([128, HT], F32) for _ in range(2)] for _ in range(2)]
    b_cv = [[Buf(f"cv{h}{g}") for g in range(2)] for h in range(2)]
    wus = [AR.alloc([128, 8, 128], F32) for _ in range(4)]; b_wus = [Buf(f"wus{i}") for i in range(4)]
    wub = [AR.alloc([128, 8, 128], BF16) for _ in range(4)]; b_wub = [Buf(f"wub{i}") for i in range(4)]
    for g in range(2):
        S.op("pool", lambda e, g=g: e.memset(up[0][g][:, 0:1], 0.0), (), [b_up[0][g]])
    def load_pair(p):
        for gv in range(2):
            wi = (p % 2) * 2 + gv
            c0 = gv * DFF + p * 128
            dma("dsync", wus[wi][:], wup_d[:, c0:c0 + 128].rearrange("(k p) f -> p k f", p=128), writes=[b_wus[wi]])
            if gv == 0:
                acopy(wub[wi][:], wus[wi][:], [b_wus[wi]], [b_wub[wi]])
            else:
                tcopy("dve", wub[wi][:], wus[wi][:], [b_wus[wi]], [b_wub[wi]])

    def stage_a(p, hf):
        for gv in range(2):
            wi = (p % 2) * 2 + gv
            ch = gv * NPAIR + p
            u_, b_u = up[hf][gv], b_up[hf][gv]
            if hf == 0:
                segs = [(0, 512, 1), (512, 512, 513), (1024, 1, 1025)]
            else:
                segs = [(1023, 1, 0), (1024, 512, 1), (1536, 512, 513), (2048, 1, 1025)]
            for si, (t0, n, dc) in enumerate(segs):
                pf, b_pf = next_psf()
                mm(pf[:, 0:n], [(wub[wi][:, kc, :], h2T[:, kc, t0:t0 + n]) for kc in range(8)],
                   [b_wub[wi], b_h2T], [b_pf])
                acopy(u_[:, dc:dc + n], pf[:, 0:n], [b_pf], [b_u])
            c_, b_c = cv[hf][gv], b_cv[hf][gv]
            acopy(c_[:], u_[:, 1:1 + HT], [b_u, b_cw, b_cb], [b_c], func=AF.Identity,
                  scale=cw[:, 44 + ch:44 + ch + 1], bias=cb[:, ch:ch + 1])
            stt(c_[:], u_[:, 0:HT], cw[:, ch:ch + 1], c_[:], ALU.mult, ALU.add, [b_u, b_cw, b_c], [b_c])
            stt(c_[:], u_[:, 2:2 + HT], cw[:, 88 + ch:88 + ch + 1], c_[:], ALU.mult, ALU.add, [b_u, b_cw, b_c], [b_c])

    def stage_b(p, hf):
        acopy(cv[hf][0][:], cv[hf][0][:], [b_cv[hf][0]], [b_cv[hf][0]], func=AF.Silu)
        tt("dve", actT[:, p, hf * HT:(hf + 1) * HT], cv[hf][0][:], cv[hf][1][:], ALU.mult,
           [b_cv[hf][0], b_cv[hf][1]], [b_actT])

    load_pair(0)
    for p in range(NPAIR):
        if p + 1 < NPAIR:
            load_pair(p + 1)
        stage_a(p, 0)
        if p > 0:
            stage_b(p - 1, 1)
        stage_a(p, 1)
        stage_b(p, 0)
    stage_b(NPAIR - 1, 1)

    S.barrier()
    AR = region(110, 207.8)
    wdn_bf = AR.alloc([128, NPAIR, D], BF16); b_wdnp = [Buf(f"wdn{p}") for p in range(NPAIR)]
    wds = [AR.alloc([128, D], F32) for _ in range(4)]; b_wds = [Buf(f"wds{i}") for i in range(4)]
    for p in range(NPAIR):
        dma("dsync", wds[p % 4][:], wdn_d[p * 128:(p + 1) * 128, :], writes=[b_wds[p % 4]])
        if p % 2 == 0:
            acopy(wdn_bf[:, p, :], wds[p % 4][:], [b_wds[p % 4]], [b_wdnp[p]])
        else:
            tcopy("dve", wdn_bf[:, p, :], wds[p % 4][:], [b_wds[p % 4]], [b_wdnp[p]])
    x1r = [AR.alloc([128, D], F32) for _ in range(2)]; b_x1r = [Buf("x1r0"), Buf("x1r1")]
    x2 = [AR.alloc([128, D], F32) for _ in range(2)]; b_x2 = [Buf("x2a"), Buf("x2b")]
    yo = [AR.alloc([128, D], F32) for _ in range(2)]; b_yo = [Buf("yo0"), Buf("yo1")]
    jk = AR.alloc([128, D], BF16); b_jk = Buf("jk")
    NE = 3
    early = [[next_psf() for _ in range(2)] for _ in range(NE)]
    for p in range(NPAIR):
        for n in range(NE):
            for hf in range(2):
                pf, b_pf = early[n][hf]
                S.op("pe", lambda e, pf=pf, p=p, n=n, hf=hf: e.matmul(
                    pf[:], actT[:, p, n * 128:(n + 1) * 128], wdn_bf[:, p, hf * 512:(hf + 1) * 512],
                    start=(p == 0), stop=(p == NPAIR - 1)), [b_actT, b_wdnp[p]], [b_pf], cost=0.25)
    for n in range(16):
        i2 = n % 2
        dma("dsync", x1r[i2][:], x1_d[n * 128:(n + 1) * 128, :], reads=[b_x1d[n]], writes=[b_x1r[i2]])
        for hf in range(2):
            if n < NE:
                pf, b_pf = early[n][hf]
            else:
                pf, b_pf = next_psf()
                mm(pf[:], [(actT[:, p, n * 128:(n + 1) * 128], wdn_bf[:, p, hf * 512:(hf + 1) * 512])
                           for p in range(NPAIR)], [b_actT] + b_wdnp, [b_pf])
            tt("dve", x2[i2][:, hf * 512:(hf + 1) * 512], pf[:], x1r[i2][:, hf * 512:(hf + 1) * 512], ALU.add,
               [b_pf, b_x1r[i2]], [b_x2[i2]])
        sc, b_sc = next_stat()
        acopy(jk[:], x2[i2][:], [b_x2[i2]], [b_jk, b_sc], func=AF.Square, accum=sc[:, 0:1])
        ts("dve", sc[:, 1:2], sc[:, 0:1], 1.0 / D, EPS, ALU.mult, ALU.add, [b_sc], [b_sc])
        acopy(sc[:, 2:3], sc[:, 1:2], [b_sc], [b_sc], func=AF.Sqrt)
        S.op("dve", lambda e, sc=sc: e.reciprocal(out=sc[:, 3:4], in_=sc[:, 2:3]), [b_sc], [b_sc])
        stt(yo[i2][:], x2[i2][:], sc[:, 3:4], gains[:, G_FIN:G_FIN + D], ALU.mult, ALU.mult,
            [b_x2[i2], b_sc, b_gains], [b_yo[i2]])
        dma("dpool", y_d[n * 128:(n + 1) * 128, :], yo[i2][:], reads=[b_yo[i2]])

    return finish()


def _consts():
    ident = np.eye(128, dtype=np.float32)
    R = np.zeros((128, 128), np.float32)
    for hh in range(2):
        for d in range(8):
            R[hh * 64 + d, hh * 64 + d + 8] = -1.0
            R[hh * 64 + d + 8, hh * 64 + d] = 1.0
    rmat = np.ascontiguousarray(R.T)
    ifr = np.zeros((128, 1), np.float32)
    base = np.power(np.float32(500000.0), -np.arange(8, dtype=np.float32) / np.float32(8.0)).astype(np.float32)
    for hh in range(2):
        for d in range(16):
            ifr[hh * 64 + d, 0] = base[d % 8]
    kk = np.arange(128)[:, None]
    qq = np.arange(128)[None, :]
    msk = np.concatenate([(kk >= qq), (kk <= qq)], axis=1).astype(np.float32)
    bm = (np.arange(128)[:, None] // 16 == np.arange(128)[None, :] // 16).astype(np.float32)
    return ident, rmat, ifr, msk, bm


_NC_CACHE = {}


def _prep_inputs(inp, dbg=False):
    f = lambda a: np.ascontiguousarray(np.asarray(a, dtype=np.float32))
    x = f(inp["x"])
    w_in = f(inp["w_in"][0])
    qcols = np.concatenate([np.r_[j * 64:(j + 1) * 64, (4 + j) * 64:(5 + j) * 64] for j in range(4)])
    w_in_r = np.ascontiguousarray(np.concatenate([w_in[:, qcols], w_in[:, 512:]], axis=1))
    gains = np.concatenate([f(inp["norm_mix_g"][0]), f(inp["norm_ffn_g"][0]), f(inp["norm_final_g"]),
                            f(inp["norm_attn_g"][0]), f(inp["norm_ssm_g"][0])])[None, :]
    ident, rmat, ifr, msk, bm = _consts()
    a_re, a_im = f(inp["a_re"][0]), f(inp["a_im"][0])
    lst = np.broadcast_to(f(inp["log_step"][0])[:, :, None], (2, 32, 64))
    b_re, b_im = f(inp["b_re"][0]), f(inp["b_im"][0])
    c_re, c_im = f(inp["c_re"][0]), f(inp["c_im"][0])
    cwf = f(inp["conv_w"][0])
    shared = dict(
        w_in=w_in_r, gains=np.ascontiguousarray(gains),
        dsk=np.ascontiguousarray(f(inp["d_skip"][0]).reshape(4, 128).T),
        w_glu=f(inp["w_glu"][0]), sink=f(inp["sink"]), w_out=f(inp["w_out"][0]), w_up=f(inp["w_up"][0]),
        cb=np.ascontiguousarray(f(inp["conv_b"][0]).reshape(44, 128).T),
        w_down=f(inp["w_down"][0]), ident=ident, rmat=rmat, ifr=ifr, msk=msk, bmask=bm,
        sidx=np.arange(512, dtype=np.float32)[None, :])

    def ep(a):
        return np.ascontiguousarray(a.reshape(2, 16, 2, 64).transpose(2, 3, 0, 1).reshape(128, 32))

    def ep_b(a):
        return np.ascontiguousarray(a.reshape(2, 16, 2, 64, 16).transpose(2, 3, 0, 1, 4).reshape(128, 512))

    def ep_c(a):
        return np.ascontiguousarray(a.reshape(2, 16, 2, 16, 64).transpose(2, 4, 0, 1, 3).reshape(128, 512))

    per_half = []
    for h in range(2):
        sl = slice(None) if h == 0 else slice(None, None, -1)
        cwh = cwf if h == 0 else cwf[::-1]
        pos = np.arange(T_QKV, dtype=np.float32) if h == 0 else (T_ALL - 1 - np.arange(T_QKV)).astype(np.float32)
        per_half.append(dict(
            are=ep(a_re[sl]), aim=ep(a_im[sl]), ls=ep(lst[sl]),
            bre=ep_b(b_re[sl]), bim=ep_b(b_im[sl]), cre=ep_c(c_re[sl]), cim=ep_c(c_im[sl]),
            cw=np.ascontiguousarray(cwh.reshape(3, 44, 128).transpose(2, 0, 1).reshape(128, 132)),
            pos=np.ascontiguousarray(pos[None, :])))
    in_maps = []
    for c in range(8):
        b, h = c // 2, c % 2
        xs = x[b] if h == 0 else x[b][::-1]
        m = dict(shared)
        m.update(per_half[h])
        m["x"] = np.ascontiguousarray(xs)
        in_maps.append(m)
    return in_maps


def kernel(**inputs):
    in_maps = _prep_inputs(inputs)
    if "nc" not in _NC_CACHE:
        _NC_CACHE["nc"] = build_program()
    nc = _NC_CACHE["nc"]
    res = run_bass_kernel_spmd(nc, in_maps, core_ids=list(range(8)))
    out = np.empty((4, T_ALL, D), np.float32)
    for c in range(8):
        b, h = c // 2, c % 2
        y = np.asarray(res.results[c]["y"], dtype=np.float32)
        if h == 0:
            out[b, :T_OWN] = y
        else:
            out[b, T_OWN:] = y[::-1]
    return out
```
